# Optimizing a Trainium2 kernel written in Bass

```python
import math
import jax, jax.numpy as jnp
from jax import lax
import numpy as np

D_MODEL = 1024
BATCH = 8
SEQ = 2048
DEPTH = 2

N_EVEN = (DEPTH + 1) // 2
N_ODD = DEPTH // 2

ATTN_HEADS = 4
ATTN_QK_DIM = 64
ATTN_V_DIM = 2 * ATTN_QK_DIM
ATTN_QK_WIDTH = ATTN_HEADS * 2 * ATTN_QK_DIM
ATTN_WIDTH = ATTN_HEADS * ATTN_V_DIM
ROPE_THETA = 10000.0
Q_BLOCK = 128

S5_WIDTH = D_MODEL - ATTN_WIDTH
S5_GROUP = 16
S5_GROUPS = S5_WIDTH // S5_GROUP
S5_STATE = 64
S5_DT_MIN = 1e-3
S5_DT_MAX = 1e-1

EVEN_IN_WIDTH = 2 * ATTN_QK_WIDTH + ATTN_WIDTH + S5_WIDTH

HGRN_HEADS = 8
HGRN_KEY = D_MODEL // HGRN_HEADS
HGRN_VAL = D_MODEL // HGRN_HEADS
HGRN_CHUNK = 64
ODD_IN_WIDTH = 5 * D_MODEL

D_FF = 4 * D_MODEL
EPS = 1e-6

kernel_name = 'hybrid_diffattn_s5_hgrn2_encoder'


def rmsnorm(x, g):
    xf = x.astype(jnp.float32)
    y = xf * lax.rsqrt(jnp.mean(xf * xf, axis=-1, keepdims=True) + EPS)
    return (y * g.astype(jnp.float32)).astype(x.dtype)


def rope_tables(seq_len, dim, dtype):
    inv = ROPE_THETA ** (-jnp.arange(0, dim, 2, dtype=jnp.float32) / dim)
    ang = jnp.arange(seq_len, dtype=jnp.float32)[:, None] * inv[None, :]
    return jnp.cos(ang).astype(dtype), jnp.sin(ang).astype(dtype)


def apply_rope(t, cos, sin):
    t1, t2 = jnp.split(t, 2, axis=-1)
    return jnp.concatenate([t1 * cos - t2 * sin, t2 * cos + t1 * sin], axis=-1)


def diff_attention(h, lam, subln_g, lambda_init, cos, sin):
    bsz, seq, _ = h.shape
    q, k, v = jnp.split(h, [ATTN_QK_WIDTH, 2 * ATTN_QK_WIDTH], axis=-1)

    def split_qk(t):
        return t.reshape(bsz, seq, ATTN_HEADS, 2, ATTN_QK_DIM).transpose(3, 0, 2, 1, 4)

    q = apply_rope(split_qk(q), cos, sin) * (ATTN_QK_DIM ** -0.5)
    k = apply_rope(split_qk(k), cos, sin)
    v = v.reshape(bsz, seq, ATTN_HEADS, ATTN_V_DIM).transpose(0, 2, 1, 3)
    lam32 = lam.astype(jnp.float32)
    lam_val = (jnp.exp(jnp.sum(lam32[0] * lam32[1])) - jnp.exp(jnp.sum(lam32[2] * lam32[3]))
               + lambda_init)
    n_blk = seq // Q_BLOCK
    q_blocks = q.reshape(2, bsz, ATTN_HEADS, n_blk, Q_BLOCK, ATTN_QK_DIM).transpose(3, 0, 1, 2, 4, 5)

    def attend(qb):
        s = jnp.einsum('cbhqd,cbhkd->cbhqk', qb, k).astype(jnp.float32)
        p = jax.nn.softmax(s, axis=-1)
        w = p[0] - lam_val * p[1]
        return jnp.einsum('bhqk,bhkv->bhqv', w.astype(v.dtype), v)

    o = lax.map(attend, q_blocks)
    o = o.transpose(1, 2, 0, 3, 4).reshape(bsz, ATTN_HEADS, seq, ATTN_V_DIM)
    o = rmsnorm(o, subln_g) * (1.0 - lambda_init)
    return o.transpose(0, 2, 1, 3).reshape(bsz, seq, ATTN_WIDTH)


def _complex_affine_combine(e1, e2):
    a1r, a1i, b1r, b1i = e1
    a2r, a2i, b2r, b2i = e2
    return (a2r * a1r - a2i * a1i,
            a2r * a1i + a2i * a1r,
            a2r * b1r - a2i * b1i + b2r,
            a2r * b1i + a2i * b1r + b2i)


def s5_mixer(u, lam_re, lam_im, log_step, b_re, b_im, c_re, c_im, d_skip, w_glu, b_glu):
    bsz, seq, _ = u.shape
    ug = u.reshape(bsz, seq, S5_GROUPS, S5_GROUP)
    y = u * d_skip
    for direction in range(2):
        lr = jnp.minimum(lam_re[direction], -1e-4)
        li = lam_im[direction]
        dt = jnp.exp(log_step[direction])[:, None]
        mag = jnp.exp(lr * dt)
        ar = mag * jnp.cos(li * dt)
        ai = mag * jnp.sin(li * dt)
        den = lr * lr + li * li
        cr = ((ar - 1.0) * lr + ai * li) / den
        ci = (ai * lr - (ar - 1.0) * li) / den
        bbr = cr[..., None] * b_re[direction] - ci[..., None] * b_im[direction]
        bbi = cr[..., None] * b_im[direction] + ci[..., None] * b_re[direction]
        bur = jnp.einsum('blgn,gpn->blgp', ug, bbr)
        bui = jnp.einsum('blgn,gpn->blgp', ug, bbi)
        elems = (jnp.broadcast_to(ar, bur.shape), jnp.broadcast_to(ai, bur.shape), bur, bui)
        _, _, xr, xi = lax.associative_scan(_complex_affine_combine, elems,
                                            reverse=(direction == 1), axis=1)
        y_dir = (jnp.einsum('blgp,gnp->blgn', xr, c_re[direction])
                 - jnp.einsum('blgp,gnp->blgn', xi, c_im[direction]))
        y = y + y_dir.reshape(bsz, seq, S5_WIDTH)
    y = jax.nn.gelu(y)
    return y * jax.nn.sigmoid(y @ w_glu + b_glu)


def gated_linear_recurrence(q, k, log_f, v):
    bsz, heads, seq, dk = q.shape
    dv = v.shape[-1]
    n = seq // HGRN_CHUNK

    def chunk(t):
        return t.reshape(bsz, heads, n, HGRN_CHUNK, t.shape[-1])

    q, k, log_f, v = chunk(q), chunk(k), chunk(log_f), chunk(v)
    b = jnp.cumsum(log_f, axis=3)
    b_last = b[:, :, :, -1:, :]
    q_dec = q * jnp.exp(b)
    k_inv = k * jnp.exp(-b)
    scores = jnp.einsum('bhnck,bhnsk->bhncs', q_dec, k_inv)
    mask = jnp.tril(jnp.ones((HGRN_CHUNK, HGRN_CHUNK), dtype=bool))
    scores = jnp.where(mask, scores, 0.0)
    o_intra = jnp.einsum('bhncs,bhnsv->bhncv', scores, v)
    chunk_kv = jnp.einsum('bhnck,bhncv->bhnkv', k * jnp.exp(b_last - b), v)
    chunk_decay = jnp.exp(b_last[:, :, :, 0, :])

    def step(state, inp):
        dec, kv = inp
        return dec[..., None] * state + kv, state

    s0 = jnp.zeros((bsz, heads, dk, dv), jnp.float32)
    _, states = lax.scan(step, s0, (jnp.moveaxis(chunk_decay, 2, 0), jnp.moveaxis(chunk_kv, 2, 0)))
    o_inter = jnp.einsum('bhnck,nbhkv->bhncv', q_dec, states)
    return (o_intra + o_inter).reshape(bsz, heads, seq, dv)


def hgrn2_mixer(h, lb, norm_g):
    bsz, seq, _ = h.shape
    q, i, f_fwd, f_bwd, g = jnp.split(h, 5, axis=-1)

    def heads(t):
        return t.astype(jnp.float32).reshape(bsz, seq, HGRN_HEADS, -1).transpose(0, 2, 1, 3)

    lb32 = lb.astype(jnp.float32)
    qh, ih = heads(q), heads(i)
    o = jnp.zeros((bsz, HGRN_HEADS, seq, HGRN_VAL), jnp.float32)
    for f_logit, reverse in ((f_fwd, False), (f_bwd, True)):
        fl = f_logit.astype(jnp.float32)
        f = lb32 + (1.0 - lb32) * jax.nn.sigmoid(fl)
        k = (1.0 - lb32) * jax.nn.sigmoid(-fl)
        args = (qh, heads(k), heads(jnp.log(f)), ih)
        if reverse:
            args = tuple(jnp.flip(a, axis=2) for a in args)
            o = o + jnp.flip(gated_linear_recurrence(*args), axis=2)
        else:
            o = o + gated_linear_recurrence(*args)
    o = o.transpose(0, 2, 1, 3)
    o = rmsnorm(o, norm_g.reshape(HGRN_HEADS, HGRN_VAL)).reshape(bsz, seq, D_MODEL)
    return (o * jax.nn.sigmoid(g.astype(jnp.float32))).astype(h.dtype)


def setup_inputs(seed: int = 0) -> dict:
    key = jax.random.key(seed)
    ks = jax.random.split(key, 24)
    f32 = jnp.float32

    def nrm(k, shape, scale):
        return jax.random.normal(k, shape, f32) * scale

    n_idx = jnp.arange(S5_STATE, dtype=f32)
    s5_shape = (N_EVEN, 2, S5_GROUPS, S5_STATE)
    return {
        'x': nrm(ks[0], (BATCH, SEQ, D_MODEL), 1.0),
        'norm_mix_g': 1.0 + nrm(ks[1], (DEPTH, D_MODEL), 0.02),
        'norm_mlp_g': 1.0 + nrm(ks[2], (DEPTH, D_MODEL), 0.02),
        'final_norm_g': 1.0 + nrm(ks[3], (D_MODEL,), 0.02),
        'w_ff_in': nrm(ks[4], (DEPTH, D_MODEL, D_FF), D_MODEL ** -0.5),
        'w_ff_out': nrm(ks[5], (DEPTH, D_FF, D_MODEL), D_FF ** -0.5),
        'w_in_even': nrm(ks[6], (N_EVEN, D_MODEL, EVEN_IN_WIDTH), D_MODEL ** -0.5),
        'w_out_even': nrm(ks[7], (N_EVEN, ATTN_WIDTH + S5_WIDTH, D_MODEL), (ATTN_WIDTH + S5_WIDTH) ** -0.5),
        'diff_lambda': nrm(ks[8], (N_EVEN, 4, ATTN_QK_DIM), 0.1),
        'diff_subln_g': 1.0 + nrm(ks[9], (N_EVEN, ATTN_V_DIM), 0.02),
        's5_lam_re': -0.5 + nrm(ks[10], s5_shape, 0.01),
        's5_lam_im': math.pi * n_idx + nrm(ks[11], s5_shape, 0.01),
        's5_log_step': jax.random.uniform(ks[12], (N_EVEN, 2, S5_GROUPS), f32,
                                          math.log(S5_DT_MIN), math.log(S5_DT_MAX)),
        's5_b_re': nrm(ks[13], (N_EVEN, 2, S5_GROUPS, S5_STATE, S5_GROUP), (2.0 * S5_GROUP) ** -0.5),
        's5_b_im': nrm(ks[14], (N_EVEN, 2, S5_GROUPS, S5_STATE, S5_GROUP), (2.0 * S5_GROUP) ** -0.5),
        's5_c_re': nrm(ks[15], (N_EVEN, 2, S5_GROUPS, S5_GROUP, S5_STATE), S5_STATE ** -0.5),
        's5_c_im': nrm(ks[16], (N_EVEN, 2, S5_GROUPS, S5_GROUP, S5_STATE), S5_STATE ** -0.5),
        's5_d': nrm(ks[17], (N_EVEN, S5_WIDTH), 1.0),
        's5_w_glu': nrm(ks[18], (N_EVEN, S5_WIDTH, S5_WIDTH), S5_WIDTH ** -0.5),
        's5_b_glu': nrm(ks[19], (N_EVEN, S5_WIDTH), 0.01),
        'w_in_odd': nrm(ks[20], (N_ODD, D_MODEL, ODD_IN_WIDTH), D_MODEL ** -0.5),
        'w_out_odd': nrm(ks[21], (N_ODD, D_MODEL, D_MODEL), D_MODEL ** -0.5),
        'hgrn_norm_g': 1.0 + nrm(ks[22], (N_ODD, D_MODEL), 0.02),
        'hgrn_lb_logits': nrm(ks[23], (DEPTH, D_MODEL), 0.1),
    }


def reference(x, norm_mix_g, norm_mlp_g, final_norm_g, w_ff_in, w_ff_out, w_in_even, w_out_even,
              diff_lambda, diff_subln_g, s5_lam_re, s5_lam_im, s5_log_step, s5_b_re, s5_b_im,
              s5_c_re, s5_c_im, s5_d, s5_w_glu, s5_b_glu, w_in_odd, w_out_odd, hgrn_norm_g,
              hgrn_lb_logits):
    seq = x.shape[1]
    cos, sin = rope_tables(seq, ATTN_QK_DIM, x.dtype)
    lb_soft = jax.nn.softmax(hgrn_lb_logits.astype(jnp.float32), axis=0)
    lb_table = jnp.cumsum(lb_soft, axis=0) - lb_soft[0:1]
    for l in range(DEPTH):
        h = rmsnorm(x, norm_mix_g[l])
        if l % 2 == 0:
            e = l // 2
            proj = h @ w_in_even[e]
            attn_in = proj[..., :2 * ATTN_QK_WIDTH + ATTN_WIDTH]
            u = proj[..., 2 * ATTN_QK_WIDTH + ATTN_WIDTH:]
            lambda_init = 0.8 - 0.6 * math.exp(-0.3 * l)
            a_out = diff_attention(attn_in, diff_lambda[e], diff_subln_g[e], lambda_init, cos, sin)
            b_out = s5_mixer(u, s5_lam_re[e], s5_lam_im[e], s5_log_step[e], s5_b_re[e], s5_b_im[e],
                             s5_c_re[e], s5_c_im[e], s5_d[e], s5_w_glu[e], s5_b_glu[e])
            x = x + jnp.concatenate([a_out, b_out], axis=-1) @ w_out_even[e]
        else:
            o_i = l // 2
            proj = h @ w_in_odd[o_i]
            x = x + hgrn2_mixer(proj, lb_table[l], hgrn_norm_g[o_i]) @ w_out_odd[o_i]
        h = rmsnorm(x, norm_mlp_g[l])
        x = x + jnp.square(jax.nn.relu(h @ w_ff_in[l])) @ w_ff_out[l]
    return rmsnorm(x, final_norm_g)
```

```python
import math
from contextlib import ExitStack

import numpy as np
import concourse.bass as bass
import concourse.mybir as mybir
from concourse.bass_utils import run_bass_kernel_spmd

F32 = mybir.dt.float32
BF16 = mybir.dt.bfloat16
AF = mybir.ActivationFunctionType
ALU = mybir.AluOpType
AX = mybir.AxisListType

ENGS = ("pe", "act", "dve", "pool", "sp")

L = 2048
D = 1024
NTB = 4
EPS = 1e-6
LAMBDA_INIT0 = 0.8 - 0.6 * math.exp(-0.3 * 0)


class Op:
    __slots__ = ("eng", "fn", "deps", "is_dma", "dkey", "signal", "sem", "val", "idx")


class Prog:
    def __init__(self, nc):
        self.nc = nc
        self.ops = []
        self.last_w = {}
        self.readers = {}
        self.stack = ExitStack()
        self.pending_barrier = {}

    def sb(self, name, shape, dt):
        return self.stack.enter_context(self.nc.sbuf_tensor("sb_" + name, list(shape), dt))

    def ps(self, name, shape, dt=F32):
        return self.stack.enter_context(self.nc.psum_tensor("pp_" + name, list(shape), dt))

    def barrier(self):
        deps = set()
        last = {}
        for o in self.ops:
            if o.is_dma:
                deps.add(o.idx)
            else:
                last[o.eng] = o.idx
        deps.update(last.values())
        self.pending_barrier = {e: set(deps) for e in ENGS}
        self.last_w = {}
        self.readers = {}

    def op(self, eng, fn, r=(), w=(), dma=False, dkey=None):
        o = Op()
        o.eng, o.fn, o.is_dma, o.signal = eng, fn, dma, False
        o.idx = len(self.ops)
        deps = set()
        for k in list(r) + list(w):
            if k in self.last_w:
                deps.add(self.last_w[k])
        for k in w:
            for rd in self.readers.get(k, ()):
                deps.add(rd)
        if self.pending_barrier.get(eng):
            deps |= self.pending_barrier[eng]
            self.pending_barrier[eng] = set()
        deps.discard(o.idx)
        o.deps = deps
        o.dkey = (dkey if dkey is not None else (w[0] if len(w) else r[0])) if dma else None
        for k in r:
            self.readers.setdefault(k, []).append(o.idx)
        for k in w:
            self.last_w[k] = o.idx
            self.readers[k] = []
        self.ops.append(o)
        return o.idx

    def _skip(self, od, o):
        return (not od.is_dma) and (not o.is_dma) and od.eng == o.eng and o.eng == "pe"

    def emit(self, final_eng="sp"):
        nc = self.nc
        ops = self.ops
        final_deps = [o.idx for o in ops if o.is_dma]
        for o in ops:
            for d in o.deps:
                if not self._skip(ops[d], o):
                    ops[d].signal = True
        for d in final_deps:
            ops[d].signal = True
        sems = {}

        def get_sem(key):
            if key not in sems:
                sems[key] = self.stack.enter_context(nc.semaphore("s%d" % len(sems)))
            return sems[key]

        cnt = {}
        for o in ops:
            if not o.signal:
                continue
            key = ("dma", o.dkey) if o.is_dma else ("eng", o.eng)
            inc = 16 if o.is_dma else 1
            cnt[key] = cnt.get(key, 0) + inc
            o.sem = get_sem(key)
            o.val = cnt[key]
            o.dkey = key
        self.n_sems = len(sems)
        per_eng = {e: [] for e in ENGS}
        for o in ops:
            per_eng[o.eng].append(o)

        def run(eng_name, e):
            waited = {}
            for o in per_eng[eng_name]:
                need = {}
                for d in o.deps:
                    od = ops[d]
                    if (not od.signal) or self._skip(od, o):
                        continue
                    if waited.get(od.dkey, 0) >= od.val:
                        continue
                    if need.get(od.dkey, (None, 0))[1] < od.val:
                        need[od.dkey] = (od.sem, od.val)
                for k, (sem, val) in need.items():
                    e.wait_ge(sem, val)
                    waited[k] = val
                ins = o.fn(e)
                if o.signal:
                    ins.then_inc(o.sem, 16 if o.is_dma else 1)
            if eng_name == final_eng:
                need = {}
                for d in final_deps:
                    od = ops[d]
                    if waited.get(od.dkey, 0) >= od.val:
                        continue
                    if need.get(od.dkey, (None, 0))[1] < od.val:
                        need[od.dkey] = (od.sem, od.val)
                for k, (sem, val) in need.items():
                    e.wait_ge(sem, val)

        with nc.Block() as block:
            @block.tensor
            def _(e):
                run("pe", e)

            @block.scalar
            def _(e):
                run("act", e)

            @block.vector
            def _(e):
                run("dve", e)

            @block.gpsimd
            def _(e):
                run("pool", e)

            @block.sync
            def _(e):
                run("sp", e)
        self.stack.close()


class Arena:
    def __init__(self, P, nbytes):
        self.P = P
        self.nbytes = nbytes
        self.base = P.sb("arena", [128, nbytes // 2], BF16)
        self.live = []

    def phase(self):
        self.P.barrier()
        self.live = []

    def at(self, name, off, shape, dt):
        n = 1
        for s in shape:
            n *= s
        nb = n * (4 if dt == F32 else 2)
        assert off % 4 == 0 and off + nb <= self.nbytes, (name, off, nb, self.nbytes)
        for (a, b, nm) in self.live:
            assert off >= b or off + nb <= a, ("arena overlap", name, nm)
        self.live.append((off, off + nb, name))
        ap = self.base[:, off // 2:(off + nb) // 2]
        if dt == F32:
            ap = ap.bitcast(F32)
        if len(shape) > 1:
            names = "abcd"[:len(shape)]
            pat = "p (" + " ".join(names) + ") -> p " + " ".join(names)
            ap = ap.rearrange(pat, **{names[i]: shape[i] for i in range(len(shape))})
        return ap

    def drop(self, name):
        self.live = [x for x in self.live if x[2] != name]


KB = 1024


def build(cfg):
    parts = cfg["parts"]
    nc = bass.Bass("TRN2", target_bir_lowering=False)

    def din(name, shape):
        return nc.dram_tensor(name, list(shape), F32, kind="ExternalInput").ap()

    x_d = din("x", [L, D])
    out_d = nc.dram_tensor("out", [L, D], F32, kind="ExternalOutput").ap()
    gains_d = din("gains", [128, 5, 8])
    w_ff_in_d = din("w_ff_in", [2, D, 4 * D])
    w_ff_out_d = din("w_ff_out", [2, 4 * D, D])
    ident_d = din("ident", [128, 128])
    w_in_odd_d = din("w_in_odd", [D, 5 * D])
    w_in_even_d = din("w_in_even", [D, 2 * D])
    w_out_even_d = din("w_out_even", [D, D])
    cos_d = din("rope_cos", [128, L])
    sin_d = din("rope_sin", [128, L])
    dl_d = din("diff_lambda", [1, 256])
    sg_d = din("subln_g", [128, 1])
    w_glu_d = din("w_glu", [512, 512])
    s5s_d = din("s5s", [128, 3, 32])
    s5b_d = din("s5b", [128, 2, 32, 16])
    s5c_d = din("s5c", [128, 2, 32, 16])
    s5d_d = din("s5d", [128, 4])
    s5bg_d = din("s5bg", [128, 4])
    s5m_d = din("s5m", [128, 2, 128])
    w_out_odd_d = din("w_out_odd", [D, D])
    lbl_d = din("lbl", [128, 2, 8])
    hg_d = din("hg", [128, 8])
    hmask_d = din("hmask", [128, 2, 128])

    P = Prog(nc)
    op = P.op

    xT = P.sb("xT", [128, 8, L], F32)
    WB = [P.sb("wb%d" % i, [128, 4096], BF16) for i in range(4)]
    ident_f = P.sb("ident_f", [128, 128], F32)
    ident_b = P.sb("ident_b", [128, 128], BF16)
    ones_b = P.sb("ones_b", [128, 128], BF16)
    gains = P.sb("gains", [128, 5, 8], F32)
    PS = [P.ps("ps%d" % i, [128, 512], F32) for i in range(8)]
    lbl = P.sb("lbl", [128, 2, 8], F32)
    dl = P.sb("dl", [128, 256], F32)
    dlp = P.sb("dlp", [128, 128], F32)
    lam = P.sb("lam", [128, 4], F32)
    sg = P.sb("sg", [128, 1], F32)
    rr_g = P.sb("rr_g", [128, 8], F32)
    hg = P.sb("hg", [128, 8], F32)
    hmask = P.sb("hmask", [128, 2, 128], F32)
    lb = P.sb("lb", [128, 8], F32)
    oml = P.sb("oml", [128, 8], F32)
    noml = P.sb("noml", [128, 8], F32)
    AR = Arena(P, 104 * KB)

    def psk(i):
        return ("ps", i)

    dbg = cfg.get("dbg", False)
    dbg_tab = cfg.setdefault("dbg_tab", {})
    xT_flat = xT[:].rearrange("p a b -> p (a b)")
    dbg_off = [0]

    def dbg_put(name, ap, rkeys):
        if not dbg:
            return
        shp = list(ap.shape)
        n = 1
        for v_ in shp[1:]:
            n *= v_
        o = dbg_off[0]
        dst = xT_flat[:, o:o + n]
        if len(shp) == 3:
            dst = dst.rearrange("p (a b) -> p a b", b=shp[2])
        op("dve", lambda e: e.tensor_copy(out=dst, in_=ap), r=list(rkeys), w=["dbg"])
        dbg_tab[name] = (o, shp)
        dbg_off[0] = o + n

    def xk(c, tb):
        return ("xT", c, tb)

    def hk(c, tb):
        return ("hT", c, tb)

    XT_ALL = [xk(c, tb) for c in range(8) for tb in range(4)]
    HT_ALL = [hk(c, tb) for c in range(8) for tb in range(4)]

    op("sp", lambda e: e.dma_start(out=ident_f[:], in_=ident_d), w=["ident_f"], dma=True)
    op("dve", lambda e: e.tensor_copy(out=ident_b[:], in_=ident_f[:]), r=["ident_f"], w=["ident_b"])
    op("pool", lambda e: e.memset(ones_b[:], 1.0), w=["ones_b"])
    op("sp", lambda e: e.dma_start(out=gains[:], in_=gains_d), w=["gains"], dma=True)

    AR.phase()
    xin = [AR.at("xin%d" % i, i * 4 * KB, [D], F32) for i in range(2)]
    ev = 0
    for t in range(16):
        b = xin[t % 2]
        op("sp", lambda e, b=b, t=t: e.dma_start(out=b, in_=x_d[t * 128:(t + 1) * 128, :]), w=[("xin", t % 2)], dma=True)
        for half in range(2):
            bank = (2 * t + half) % 4
            for j in range(4):
                c = half * 4 + j
                op("pe", lambda e, bank=bank, j=j, c=c, b=b: e.transpose(out=PS[bank][:, j * 128:(j + 1) * 128], in_=b[:, c * 128:(c + 1) * 128], identity=ident_f[:]),
                   r=[("xin", t % 2), "ident_f"], w=[psk(bank)])
            dst = xT[:, half * 4:(half + 1) * 4, t * 128:(t + 1) * 128]
            src = PS[bank][:].rearrange("p (a b) -> p a b", b=128)
            wk = [xk(half * 4 + j, t // 4) for j in range(4)]
            if ev % 2 == 0:
                op("dve", lambda e, dst=dst, src=src: e.tensor_copy(out=dst, in_=src), r=[psk(bank)], w=wk)
            else:
                op("act", lambda e, dst=dst, src=src: e.activation(out=dst, in_=src, func=AF.Copy), r=[psk(bank)], w=wk)
            ev += 1

    def rmsnorm_rstd(rstd, sq):
        for c in range(8):
            s = sq[c % 2]
            op("act", lambda e, s=s, c=c: e.activation(out=s, in_=xT[:, c, :], func=AF.Square),
               r=[xk(c, tb) for tb in range(4)], w=[("sq", c % 2)])
            for tb in range(4):
                op("pe", lambda e, s=s, c=c, tb=tb: e.matmul(PS[tb][:], lhsT=ones_b[:], rhs=s[:, tb * 512:(tb + 1) * 512], start=(c == 0), stop=(c == 7)),
                   r=[("sq", c % 2), "ones_b"], w=[psk(tb)])
        for tb in range(4):
            sl = rstd[:, tb * 512:(tb + 1) * 512]
            op("act", lambda e, sl=sl, tb=tb: e.activation(out=sl, in_=PS[tb][:], func=AF.Sqrt, scale=1.0 / D, bias=EPS),
               r=[psk(tb)], w=[("rstd", tb)])
            op("dve", lambda e, sl=sl: e.reciprocal(out=sl, in_=sl), r=[("rstd", tb)], w=[("rstd", tb)])

    def norm_to_hT(gi, hT, rstd, sq):
        rmsnorm_rstd(rstd, sq)
        for c in range(8):
            for tb in range(4):
                op("dve", lambda e, c=c, tb=tb: e.scalar_tensor_tensor(
                    out=hT[:, c, tb * 512:(tb + 1) * 512], in0=xT[:, c, tb * 512:(tb + 1) * 512], scalar=gains[:, gi, c:c + 1],
                    in1=rstd[:, tb * 512:(tb + 1) * 512], op0=ALU.mult, op1=ALU.mult),
                   r=[xk(c, tb), ("rstd", tb), "gains"], w=[hk(c, tb)])

    def wload(slot, src, r=()):
        a, b = src.shape[1], src.shape[2]
        dst = WB[slot][:, 0:a * b].rearrange("p (a b) -> p a b", b=b)
        op("pool", lambda e: e.dma_start(out=dst, in_=src), r=list(r), w=[("wb", slot)], dma=True)
        return dst

    def ffn(l, hT, actT, rl):
        w_in = w_ff_in_d[l].rearrange("(c p) f -> p c f", p=128)
        w_out = w_ff_out_d[l].rearrange("(c p) d -> p c d", p=128)
        views = {}

        def load(fg):
            views[fg] = (wload((2 * fg) % 4, w_in[:, :, fg * 512:(fg + 1) * 512]),
                         wload((2 * fg + 1) % 4, w_out[:, fg * 4:(fg + 1) * 4, :]))

        load(0)
        load(1)
        k = 0
        for fg in range(8):
            wi, wo = views[fg]
            sa, sbk = (2 * fg) % 4, (2 * fg + 1) % 4
            at = actT[fg % 2]
            for tb in range(4):
                for fc in range(4):
                    bank = k % 3
                    for c in range(8):
                        op("pe", lambda e, bank=bank, wi=wi, c=c, fc=fc, tb=tb: e.matmul(
                            PS[bank][:], lhsT=wi[:, c, fc * 128:(fc + 1) * 128], rhs=hT[:, c, tb * 512:(tb + 1) * 512], start=(c == 0), stop=(c == 7)),
                           r=[("wb", sa), hk(c, tb)], w=[psk(bank)])
                    r_ = rl[k % 2]
                    op("act", lambda e, bank=bank, r_=r_: e.activation(out=r_, in_=PS[bank][:], func=AF.Relu), r=[psk(bank)], w=[("rl", k % 2)])
                    op("act", lambda e, r_=r_, at=at, fc=fc, tb=tb: e.activation(out=at[:, fc, tb * 512:(tb + 1) * 512], in_=r_, func=AF.Square),
                       r=[("rl", k % 2)], w=[("actT", fg % 2, fc, tb)])
                    k += 1
            kk = 0
            for tb in range(4):
                for dc in range(8):
                    bank = 3 + kk % 3
                    for fc in range(4):
                        op("pe", lambda e, bank=bank, wo=wo, fc=fc, dc=dc, tb=tb, at=at: e.matmul(
                            PS[bank][:], lhsT=wo[:, fc, dc * 128:(dc + 1) * 128], rhs=at[:, fc, tb * 512:(tb + 1) * 512], start=(fc == 0), stop=(fc == 3)),
                           r=[("wb", sbk), ("actT", fg % 2, fc, tb)], w=[psk(bank)])
                    xs = xT[:, dc, tb * 512:(tb + 1) * 512]
                    op("dve", lambda e, xs=xs, bank=bank: e.tensor_tensor(out=xs, in0=xs, in1=PS[bank][:], op=ALU.add), r=[psk(bank), xk(dc, tb)], w=[xk(dc, tb)])
                    kk += 1
            if fg + 2 < 8:
                load(fg + 2)

    def ffn_phase(l, gi):
        AR.phase()
        hT = AR.at("hT", 0, [8, L], BF16)
        actT = [AR.at("actT%d" % i, 32 * KB + i * 16 * KB, [4, L], BF16) for i in range(2)]
        rl = [AR.at("rl%d" % i, 64 * KB + i * 2 * KB, [512], F32) for i in range(2)]
        sq = [AR.at("sq%d" % i, 68 * KB + i * 4 * KB, [L], BF16) for i in range(2)]
        rstd = AR.at("rstd", 76 * KB, [L], F32)
        norm_to_hT(gi, hT, rstd, sq)
        ffn(l, hT, actT, rl)


    def attn_phase(gi):
        w_in = w_in_even_d.rearrange("(c p) f -> p c f", p=128)
        w_out = w_out_even_d.rearrange("(c p) d -> p c d", p=128)
        T0 = 81 * KB
        AR.phase()
        hT = AR.at("hT", 0, [8, L], BF16)
        sq = [AR.at("sq%d" % i, T0 + i * 4 * KB, [L], BF16) for i in range(2)]
        rstd = AR.at("rstd", T0 + 8 * KB, [L], F32)
        norm_to_hT(gi, hT, rstd, sq)
        op("sp", lambda e: e.dma_start(out=dl[:], in_=bass.AP(dl_d.tensor, 0, [[0, 128], [1, 256]])), w=["dl"], dma=True)
        op("sp", lambda e: e.dma_start(out=sg[:], in_=sg_d), w=["sg"], dma=True)
        op("dve", lambda e: e.tensor_tensor(out=dlp[:, 0:64], in0=dl[:, 0:64], in1=dl[:, 64:128], op=ALU.mult), r=["dl"], w=["dlp"])
        op("dve", lambda e: e.tensor_tensor(out=dlp[:, 64:128], in0=dl[:, 128:192], in1=dl[:, 192:256], op=ALU.mult), r=["dl", "dlp"], w=["dlp"])
        op("dve", lambda e: e.reduce_sum(out=lam[:, 0:2], in_=dlp[:].rearrange("p (a b) -> p a b", b=64), axis=AX.X), r=["dlp"], w=["lam"])
        op("act", lambda e: e.activation(out=lam[:, 0:2], in_=lam[:, 0:2], func=AF.Exp), r=["lam"], w=["lam"])
        op("dve", lambda e: e.tensor_tensor(out=lam[:, 2:3], in0=lam[:, 1:2], in1=lam[:, 0:1], op=ALU.subtract), r=["lam"], w=["lam"])
        op("dve", lambda e: e.tensor_scalar(out=lam[:, 3:4], in0=lam[:, 2:3], scalar1=-LAMBDA_INIT0, scalar2=None, op0=ALU.add), r=["lam"], w=["lam"])
        op("dve", lambda e: e.tensor_scalar(out=sg[:], in0=sg[:], scalar1=1.0 - LAMBDA_INIT0, scalar2=None, op0=ALU.mult), r=["sg"], w=["sg"])
        nlam = lam[:, 3:4]
        if cfg.get("attn_stop", 9) <= 1:
            return

        AR.phase()
        hT = AR.at("hT", 0, [8, L], BF16)
        qT = AR.at("qT", 32 * KB, [4, L], BF16)
        kT = AR.at("kT", 48 * KB, [4, L], BF16)
        vaug = AR.at("vaug", 64 * KB, [16, 4, 130], BF16)
        t1 = [AR.at("t1_%d" % i, T0 + i * 2 * KB, [512], F32) for i in range(2)]
        t2 = [AR.at("t2_%d" % i, T0 + 4 * KB + i * 2 * KB, [512], F32) for i in range(2)]
        cosb = [AR.at("cos%d" % i, T0 + 8 * KB + i * 2 * KB, [512], F32) for i in range(2)]
        sinb = [AR.at("sin%d" % i, T0 + 12 * KB + i * 2 * KB, [512], F32) for i in range(2)]

        def load_group(slot, g):
            return wload(slot, w_in[:, :, g * 512:(g + 1) * 512])

        def rotate(sa, sr):
            a = WB[sa][:].rearrange("p (cb two j) -> p cb two j", two=2, j=32)
            r_ = WB[sr][:].rearrange("p (cb two j) -> p cb two j", two=2, j=32)
            op("dve", lambda e: e.tensor_scalar(out=r_[:, :, 0, :], in0=a[:, :, 1, :], scalar1=-1.0, scalar2=None, op0=ALU.mult), r=[("wb", sa)], w=[("wb", sr)])
            op("dve", lambda e: e.tensor_copy(out=r_[:, :, 1, :], in_=a[:, :, 0, :]), r=[("wb", sa)], w=[("wb", sr)])
            return WB[sr][:].rearrange("p (a b) -> p a b", b=512)

        wq = load_group(0, 0)
        wk = load_group(2, 1)
        wqr = rotate(0, 1)
        wkr = rotate(2, 3)
        op("pool", lambda e: e.memset(vaug[:, :, :, 128:130], 1.0), w=["vaug1"])
        kctr = [0]

        def rope_proj(tb, wA, wR, sA, sR, dstT, h, dkey):
            i = kctr[0] % 2
            kctr[0] += 1
            ba, bb = 2 * i, 2 * i + 1
            for c in range(8):
                op("pe", lambda e, c=c: e.matmul(PS[ba][:], lhsT=wA[:, c, h * 128:(h + 1) * 128], rhs=hT[:, c, tb * 512:(tb + 1) * 512], start=(c == 0), stop=(c == 7)),
                   r=[("wb", sA), hk(c, tb)], w=[psk(ba)])
            for c in range(8):
                op("pe", lambda e, c=c: e.matmul(PS[bb][:], lhsT=wR[:, c, h * 128:(h + 1) * 128], rhs=hT[:, c, tb * 512:(tb + 1) * 512], start=(c == 0), stop=(c == 7)),
                   r=[("wb", sR), hk(c, tb)], w=[psk(bb)])
            op("dve", lambda e: e.tensor_tensor(out=t1[i], in0=PS[ba][:], in1=cosb[tb % 2], op=ALU.mult), r=[psk(ba), ("cos", tb % 2)], w=[("t1", i)])
            op("dve", lambda e: e.tensor_tensor(out=t2[i], in0=PS[bb][:], in1=sinb[tb % 2], op=ALU.mult), r=[psk(bb), ("sin", tb % 2)], w=[("t2", i)])
            op("pool", lambda e: e.tensor_tensor(out=dstT[:, h, tb * 512:(tb + 1) * 512], in0=t1[i], in1=t2[i], op=ALU.add), r=[("t1", i), ("t2", i)], w=[(dkey, h, tb)])

        def do_tb(tb):
            op("sp", lambda e: e.dma_start(out=cosb[tb % 2], in_=cos_d[:, tb * 512:(tb + 1) * 512]), w=[("cos", tb % 2)], dma=True)
            op("sp", lambda e: e.dma_start(out=sinb[tb % 2], in_=sin_d[:, tb * 512:(tb + 1) * 512]), w=[("sin", tb % 2)], dma=True)
            for h in range(4):
                rope_proj(tb, wq, wqr, 0, 1, qT, h, "qT")
                rope_proj(tb, wk, wkr, 2, 3, kT, h, "kT")

        for tb in range(4):
            do_tb(tb)
        wv = load_group(0, 2)

        def do_v(j):
            bank = 4 + j % 2
            for c in range(8):
                op("pe", lambda e, c=c: e.matmul(PS[bank][:], lhsT=hT[:, c, j * 128:(j + 1) * 128], rhs=wv[:, c, :], start=(c == 0), stop=(c == 7)),
                   r=[("wb", 0), hk(c, j // 4)], w=[psk(bank)])
            op("act", lambda e: e.activation(out=vaug[:, j, :, 0:128], in_=PS[bank][:].rearrange("p (a b) -> p a b", b=128), func=AF.Copy), r=[psk(bank)], w=[("vaug", j)])

        for j in range(16):
            do_v(j)
        wo = wload(2, w_out[:, 0:4, :])
        dbg_put("qT0", qT[:, 0, 0:256], [("qT", 0, 0)])
        dbg_put("kT0", kT[:, 0, 0:256], [("kT", 0, 0)])
        dbg_put("vaug0", vaug[:, 0, :, :], [("vaug", 0), "vaug1"])
        dbg_put("lam", lam[:, 0:4], ["lam"])
        if cfg.get("attn_stop", 9) <= 2:
            return

        AR.phase()
        hT = AR.at("hT", 0, [8, L], BF16)
        qT = AR.at("qT", 32 * KB, [4, L], BF16)
        kT = AR.at("kT", 48 * KB, [4, L], BF16)
        vaug = AR.at("vaug", 64 * KB, [16, 4, 130], BF16)
        aT = AR.at("aT", T0, [4, L], BF16)
        PT = [AR.at("PT%d" % i, T0 + 16 * KB + i * KB, [512], BF16) for i in range(4)]
        SM = T0 + 20 * KB
        rr = rr_g[:]
        tt_ = [AR.at("tt%d" % i, SM + i * 512, [128], F32) for i in range(2)]
        oo = [AR.at("oo%d" % i, SM + 1024 + i * 512, [128], F32) for i in range(2)]
        onb = [AR.at("onb%d" % i, SM + 2048 + i * 256, [128], BF16) for i in range(4)]
        pctr = [0]
        cctr = [0]
        pending = []

        def accv(comp, qs):
            bank = 2 + 2 * comp + qs // 2
            off = (qs % 2) * 130
            return bank, PS[bank][:, off:off + 129]

        def attend(h, qb):
            for comp in range(2):
                for kt in range(16):
                    sbank = (kt % 2) if comp == 0 else (6 + kt % 2)
                    pi = pctr[0] % 4
                    pctr[0] += 1
                    op("pe", lambda e, comp=comp, kt=kt, sbank=sbank: e.matmul(PS[sbank][:], lhsT=kT[comp * 64:(comp + 1) * 64, h, kt * 128:(kt + 1) * 128],
                                                                         rhs=qT[comp * 64:(comp + 1) * 64, h, qb * 512:(qb + 1) * 512], start=True, stop=True),
                       r=[("kT", h, kt // 4), ("qT", h, qb)], w=[psk(sbank)])
                    op("act", lambda e, sbank=sbank, pi=pi: e.activation(out=PT[pi], in_=PS[sbank][:], func=AF.Exp, scale=0.125), r=[psk(sbank)], w=[("PT", pi)])
                    for qs in range(4):
                        bank, av = accv(comp, qs)
                        first = (kt == 0 and qs % 2 == 0)
                        op("pe", lambda e, av=av, pi=pi, qs=qs, kt=kt, first=first: e.matmul(av, lhsT=PT[pi][:, qs * 128:(qs + 1) * 128], rhs=vaug[:, kt, h, 0:129],
                                                                                 start=first, stop=(kt == 15), skip_group_check=True),
                           r=[("PT", pi), ("vaug", kt), "vaug1"], w=[psk(bank)])
                if comp == 0 and pending:
                    for f in pending:
                        f()
                    del pending[:]
            if dbg and h == 0 and qb == 0:
                dbg_put("acc0", PS[2][:, 0:260], [psk(2)])
                dbg_put("acc1", PS[4][:, 0:260], [psk(4)])
                dbg_put("acc0b", PS[3][:, 0:260], [psk(3)])
            for qs in range(4 if cfg.get("attn_stop", 9) > 3 else 0):
                i = cctr[0] % 2
                cctr[0] += 1
                b0, a0 = accv(0, qs)
                b1, a1 = accv(1, qs)
                op("dve", lambda e, a0=a0: e.reciprocal(out=rr[:, 0:1], in_=a0[:, 128:129]), r=[psk(b0)], w=["rr"])
                op("dve", lambda e, a1=a1: e.reciprocal(out=rr[:, 1:2], in_=a1[:, 128:129]), r=[psk(b1), "rr"], w=["rr"])
                op("dve", lambda e: e.tensor_scalar(out=rr[:, 2:3], in0=rr[:, 1:2], scalar1=nlam, scalar2=None, op0=ALU.mult), r=["rr", "lam"], w=["rr"])
                op("dve", lambda e, a1=a1, i=i: e.tensor_scalar(out=tt_[i], in0=a1[:, 0:128], scalar1=rr[:, 2:3], scalar2=None, op0=ALU.mult), r=[psk(b1), "rr"], w=[("tt", i)])
                op("dve", lambda e, a0=a0, i=i: e.scalar_tensor_tensor(out=oo[i], in0=a0[:, 0:128], scalar=rr[:, 0:1], in1=tt_[i], op0=ALU.mult, op1=ALU.add),
                   r=[psk(b0), "rr", ("tt", i)], w=[("oo", i)])
                op("act", lambda e, i=i: e.activation(out=tt_[i], in_=oo[i], func=AF.Square, accum_out=rr[:, 4:5]), r=[("oo", i), "rr"], w=["rr", ("tt", i)])
                op("act", lambda e: e.activation(out=rr[:, 4:5], in_=rr[:, 4:5], func=AF.Sqrt, scale=1.0 / 128, bias=EPS), r=["rr"], w=["rr"])
                op("dve", lambda e: e.reciprocal(out=rr[:, 5:6], in_=rr[:, 4:5]), r=["rr"], w=["rr"])
                op("dve", lambda e, i=i, qs=qs: e.tensor_scalar(out=onb[qs], in0=oo[i], scalar1=rr[:, 5:6], scalar2=None, op0=ALU.mult), r=[("oo", i), "rr"], w=[("onb", qs)])

                def fin(qs=qs):
                    pb = PS[0][:].bitcast(BF16)
                    op("pe", lambda e: e.transpose(out=pb[:, 0:128], in_=onb[qs], identity=ident_b[:]), r=[("onb", qs), "ident_b"], w=[psk(0)])
                    c0 = qb * 512 + qs * 128
                    op("dve", lambda e: e.tensor_scalar(out=aT[:, h, c0:c0 + 128], in0=pb[:, 0:128], scalar1=sg[:, 0:1], scalar2=None, op0=ALU.mult),
                       r=[psk(0), "sg"], w=[("aT", h, qb)])
                pending.append(fin)

        for h in range(4):
            for qb in range(4):
                if cfg.get("attn_stop", 9) == 3 and (h, qb) != (0, 0):
                    continue
                attend(h, qb)
        for f in pending:
            f()
        del pending[:]
        dbg_put("aT", aT[:, :, 0:512], [("aT", h_, 0) for h_ in range(4)])

        def oproj(tb, dc, kq):
            bank = kq % 2
            for hc in range(4):
                op("pe", lambda e, hc=hc: e.matmul(PS[bank][:], lhsT=wo[:, hc, dc * 128:(dc + 1) * 128], rhs=aT[:, hc, tb * 512:(tb + 1) * 512], start=(hc == 0), stop=(hc == 3)),
                   r=[("wb", 2), ("aT", hc, tb)], w=[psk(bank)])
            xs = xT[:, dc, tb * 512:(tb + 1) * 512]
            op("dve", lambda e: e.tensor_tensor(out=xs, in0=xs, in1=PS[bank][:], op=ALU.add), r=[psk(bank), xk(dc, tb)], w=[xk(dc, tb)])

        if not dbg:
            kq = 0
            for tb in range(4):
                for dc in range(8):
                    oproj(tb, dc, kq)
                    kq += 1


    def s5_phase():
        w_in = w_in_even_d.rearrange("(c p) f -> p c f", p=128)
        w_out = w_out_even_d.rearrange("(c p) d -> p c d", p=128)
        AR.phase()
        hT = AR.at("hT", 0, [8, L], BF16)
        if not ("l0attn" in parts or "l0mix" in parts):
            sq = [AR.at("sq%d" % i, 81 * KB + i * 4 * KB, [L], BF16) for i in range(2)]
            rstd = AR.at("rstd", 89 * KB, [L], F32)
            norm_to_hT(0, hT, rstd, sq)
        uT = AR.at("uT", 32 * KB, [4, L], BF16)
        wu = wload(0, w_in[:, :, 1536:2048])

        def uproj(tb, cc, kq):
            bank = kq % 4
            for c in range(8):
                op("pe", lambda e, c=c: e.matmul(PS[bank][:], lhsT=wu[:, c, cc * 128:(cc + 1) * 128], rhs=hT[:, c, tb * 512:(tb + 1) * 512], start=(c == 0), stop=(c == 7)),
                   r=[("wb", 0), hk(c, tb)], w=[psk(bank)])
            op("act", lambda e: e.activation(out=uT[:, cc, tb * 512:(tb + 1) * 512], in_=PS[bank][:], func=AF.Copy), r=[psk(bank)], w=[("uT", cc)])

        kq = 0
        for tb in range(4):
            for cc in range(4):
                uproj(tb, cc, kq)
                kq += 1
        wglu = wload(1, w_glu_d.rearrange("(c p) f -> p c f", p=128))
        wo2 = wload(2, w_out[:, 4:8, :])

        AR.phase()
        VX = AR.at("VX", 0, [2, 32, 256], BF16)
        uT = AR.at("uT", 32 * KB, [4, L], BF16)
        U = AR.at("U", 48 * KB, [32, 256], BF16)
        sm_off = [64 * KB]

        def sm(name, shape=(32,)):
            n = 1
            for v_ in shape:
                n *= v_
            t_ = AR.at(name, sm_off[0], list(shape), F32)
            sm_off[0] += n * 4
            return t_

        bm = sm("bm", (2, 32, 16))
        names = ["lr", "dt", "lrdt", "ang", "mg", "sn", "cs", "r_", "i_", "t0", "t1", "t2", "den", "am1", "cr", "ci", "vr", "vi"]
        sv = {n_: sm(n_) for n_ in names}
        par = sm("par", (3, 32))
        cm = sm("cm", (2, 32, 16))
        bb = sm("bb", (2, 32, 16))
        pw = sm("pw", (2, 32, 8))
        pwi = sm("pwi", (2, 32, 8))
        P1s = sm("P1s", (2, 32))
        P2s = sm("P2s", (2, 32))
        Xst = sm("Xst", (2, 32))
        S1 = sm("S1", (2, 32))
        T1 = sm("T1", (2, 32))
        T2 = sm("T2", (2, 32))
        dsk = sm("dsk", (4,))
        bgl = sm("bgl", (4,))
        SM_END = sm_off[0]
        assert SM_END <= 85 * KB, SM_END
        AB = [AR.at("AB%d" % i, 85 * KB + i * 2 * KB, [8, 128], BF16) for i in range(2)]
        ABT = [AR.at("ABT%d" % i, 89 * KB + i * 2 * KB, [8, 128], BF16) for i in range(2)]
        tA = AR.at("tA", 93 * KB, [512], F32)
        tB = AR.at("tB", 95 * KB, [512], F32)
        Zt = AR.at("Zt", 97 * KB, [8, 240], BF16)
        s5m = AR.at("s5m", 97 * KB + 3840, [2, 128], F32)
        PP = ["pp"]

        def vop(fn, eng="dve", r=(), w=()):
            op(eng, fn, r=PP + list(r), w=PP + list(w))

        def tt(o_, a_, b_, o, **kw):
            vop(lambda e: e.tensor_tensor(out=o_, in0=a_, in1=b_, op=o), **kw)

        def ts(o_, a_, s1, o1, s2=None, o2=None, **kw):
            if o2 is None:
                vop(lambda e: e.tensor_scalar(out=o_, in0=a_, scalar1=s1, scalar2=None, op0=o1), **kw)
            else:
                vop(lambda e: e.tensor_scalar(out=o_, in0=a_, scalar1=s1, scalar2=s2, op0=o1, op1=o2), **kw)

        def cmul(outr, outi, ar_, ai_, br_, bi_, x1, x2, **kw):
            tt(x1, ar_, br_, ALU.mult, **kw)
            tt(x2, ai_, bi_, ALU.mult, **kw)
            tt(outr, x1, x2, ALU.subtract, **kw)
            tt(x1, ar_, bi_, ALU.mult, **kw)
            tt(x2, ai_, br_, ALU.mult, **kw)
            tt(outi, x1, x2, ALU.add, **kw)

        op("sp", lambda e: e.dma_start(out=par, in_=s5s_d), w=PP, dma=True, dkey="s5s")
        op("sp", lambda e: e.dma_start(out=bm, in_=s5b_d), w=PP, dma=True, dkey="s5b")
        op("sp", lambda e: e.dma_start(out=cm, in_=s5c_d), w=PP, dma=True, dkey="s5c")
        op("sp", lambda e: e.dma_start(out=dsk, in_=s5d_d), w=PP, dma=True, dkey="s5d")
        op("sp", lambda e: e.dma_start(out=bgl, in_=s5bg_d), w=PP, dma=True, dkey="s5bg")
        op("sp", lambda e: e.dma_start(out=s5m, in_=s5m_d), w=["s5m"], dma=True)
        op("pool", lambda e: e.memset(Zt, 0.0), w=["Zt"])
        op("pool", lambda e: e.tensor_copy(out=Zt[:, :, 112:128], in_=ident_b[:].rearrange("p (a b) -> p a b", b=16)), r=["ident_b"], w=["Zt"])
        lam_re, lam_im, lstep = par[:, 0, :], par[:, 1, :], par[:, 2, :]
        v_ = sv
        ts(v_["lr"], lam_re, -1e-4, ALU.min)
        vop(lambda e: e.activation(out=v_["dt"], in_=lstep, func=AF.Exp), eng="act")
        tt(v_["lrdt"], v_["lr"], v_["dt"], ALU.mult)
        tt(v_["ang"], lam_im, v_["dt"], ALU.mult)
        vop(lambda e: e.activation(out=v_["mg"], in_=v_["lrdt"], func=AF.Exp, scale=1.0 / 32), eng="act")
        vop(lambda e: e.activation(out=v_["sn"], in_=v_["ang"], func=AF.Sin, scale=1.0 / 32), eng="act")
        ts(v_["t0"], v_["ang"], 1.0 / 32, ALU.mult, math.pi / 2, ALU.add)
        vop(lambda e: e.activation(out=v_["cs"], in_=v_["t0"], func=AF.Sin), eng="act")
        tt(v_["r_"], v_["mg"], v_["cs"], ALU.mult)
        tt(v_["i_"], v_["mg"], v_["sn"], ALU.mult)
        for _ in range(5):
            tt(v_["t0"], v_["r_"], v_["r_"], ALU.mult)
            tt(v_["t1"], v_["i_"], v_["i_"], ALU.mult)
            tt(v_["t2"], v_["r_"], v_["i_"], ALU.mult)
            tt(v_["r_"], v_["t0"], v_["t1"], ALU.subtract)
            ts(v_["i_"], v_["t2"], 2.0, ALU.mult)
        ar, ai = v_["r_"], v_["i_"]
        tt(v_["t0"], v_["lr"], v_["lr"], ALU.mult)
        tt(v_["t1"], lam_im, lam_im, ALU.mult)
        tt(v_["den"], v_["t0"], v_["t1"], ALU.add)
        vop(lambda e: e.reciprocal(out=v_["den"], in_=v_["den"]))
        ts(v_["am1"], ar, -1.0, ALU.add)
        tt(v_["t0"], v_["am1"], v_["lr"], ALU.mult)
        tt(v_["t1"], ai, lam_im, ALU.mult)
        tt(v_["t0"], v_["t0"], v_["t1"], ALU.add)
        tt(v_["cr"], v_["t0"], v_["den"], ALU.mult)
        tt(v_["t0"], ai, v_["lr"], ALU.mult)
        tt(v_["t1"], v_["am1"], lam_im, ALU.mult)
        tt(v_["t0"], v_["t0"], v_["t1"], ALU.subtract)
        tt(v_["ci"], v_["t0"], v_["den"], ALU.mult)
        tt(v_["t0"], ar, ar, ALU.mult)
        tt(v_["t1"], ai, ai, ALU.mult)
        tt(v_["t0"], v_["t0"], v_["t1"], ALU.add)
        vop(lambda e: e.reciprocal(out=v_["t0"], in_=v_["t0"]))
        tt(v_["vr"], ar, v_["t0"], ALU.mult)
        tt(v_["t1"], ai, v_["t0"], ALU.mult)
        ts(v_["vi"], v_["t1"], -1.0, ALU.mult)
        x1 = tA[:, 0:128].rearrange("p (a b) -> p a b", b=4)
        x2 = tB[:, 0:128].rearrange("p (a b) -> p a b", b=4)
        for (tab, br_, bi_) in ((pw, ar, ai), (pwi, v_["vr"], v_["vi"])):
            tr_, ti_ = tab[:, 0, :, :], tab[:, 1, :, :]
            vop(lambda e, tr_=tr_, br_=br_: e.tensor_copy(out=tr_[:, :, 0:1], in_=br_.unsqueeze(2)))
            vop(lambda e, ti_=ti_, bi_=bi_: e.tensor_copy(out=ti_[:, :, 0:1], in_=bi_.unsqueeze(2)))
            for n_ in (1, 2, 4):
                cmul(tr_[:, :, n_:2 * n_], ti_[:, :, n_:2 * n_], tr_[:, :, 0:n_], ti_[:, :, 0:n_],
                     tr_[:, :, n_ - 1:n_].broadcast_to([128, 32, n_]), ti_[:, :, n_ - 1:n_].broadcast_to([128, 32, n_]), x1[:, :, 0:n_], x2[:, :, 0:n_])
        cmul(bb[:, 0, :, :], bb[:, 1, :, :], v_["cr"].unsqueeze(2).broadcast_to([128, 32, 16]), v_["ci"].unsqueeze(2).broadcast_to([128, 32, 16]),
             bm[:, 0, :, :], bm[:, 1, :, :], tA.rearrange("p (a b) -> p a b", b=16), tB.rearrange("p (a b) -> p a b", b=16))
        vop(lambda e: e.tensor_copy(out=P1s[:, 0, :], in_=pw[:, 0, :, 7]))
        vop(lambda e: e.tensor_copy(out=P1s[:, 1, :], in_=pw[:, 0, :, 7]))
        ts(P2s[:, 0, :], pw[:, 1, :, 7], -1.0, ALU.mult)
        vop(lambda e: e.tensor_copy(out=P2s[:, 1, :], in_=pw[:, 1, :, 7]))

        def gen_AB(ABt, g0, gl0):
            for (lo, hi, rev) in ((0, 64, False), (64, 128, True)):
                sl = slice(None, None, -1) if rev else slice(None)
                pr = pwi[lo:hi, 0, g0:g0 + 4, sl].unsqueeze(3).broadcast_to([64, 4, 8, 16])
                pi = pwi[lo:hi, 1, g0:g0 + 4, sl].unsqueeze(3).broadcast_to([64, 4, 8, 16])
                br_ = bb[lo:hi, 0, g0:g0 + 4, :].unsqueeze(2).broadcast_to([64, 4, 8, 16])
                bi_ = bb[lo:hi, 1, g0:g0 + 4, :].unsqueeze(2).broadcast_to([64, 4, 8, 16])
                o_r = ABt[0][lo:hi, gl0:gl0 + 4, :].rearrange("p g (s m) -> p g s m", m=16)
                o_i = ABt[1][lo:hi, gl0:gl0 + 4, :].rearrange("p g (s m) -> p g s m", m=16)
                ta = tA[lo:hi, :].rearrange("p (g s m) -> p g s m", s=8, m=16)
                tb_ = tB[lo:hi, :].rearrange("p (g s m) -> p g s m", s=8, m=16)
                cmul(o_r, o_i, pr, pi, br_, bi_, ta, tb_, w=["AB"])

        def gen_CA(g0, gl0, CAf, CAb):
            for (lo, hi, rev, dst) in ((0, 64, False, CAf), (64, 128, True, CAb)):
                sl = slice(None, None, -1) if rev else slice(None)
                pr = pw[lo:hi, 0, g0:g0 + 4, sl].unsqueeze(3).broadcast_to([64, 4, 8, 16])
                pi = pw[lo:hi, 1, g0:g0 + 4, sl].unsqueeze(3).broadcast_to([64, 4, 8, 16])
                c_r = cm[lo:hi, 0, g0:g0 + 4, :].unsqueeze(2).broadcast_to([64, 4, 8, 16])
                c_i = cm[lo:hi, 1, g0:g0 + 4, :].unsqueeze(2).broadcast_to([64, 4, 8, 16])
                o_r = dst[0][lo:hi, gl0:gl0 + 4, :].rearrange("p g (s m) -> p g s m", m=16)
                o_i = dst[1][lo:hi, gl0:gl0 + 4, :].rearrange("p g (s m) -> p g s m", m=16)
                ta = tA[lo:hi, :].rearrange("p (g s m) -> p g s m", s=8, m=16)
                tb_ = tB[lo:hi, :].rearrange("p (g s m) -> p g s m", s=8, m=16)
                kw = dict(w=["CA"])
                tt(ta, c_r, pr, ALU.mult, **kw)
                tt(tb_, c_i, pi, ALU.mult, **kw)
                tt(o_r, ta, tb_, ALU.subtract, **kw)
                tt(ta, c_r, pi, ALU.mult, **kw)
                tt(tb_, c_i, pr, ALU.mult, **kw)
                vop(lambda e, o_i=o_i, ta=ta, tb_=tb_: e.scalar_tensor_tensor(out=o_i, in0=ta, scalar=-1.0, in1=tb_, op0=ALU.mult, op1=ALU.subtract), **kw)

        def shuffle_group(cc, gl):
            g = 8 * cc + gl
            bank = g % 2
            for s_ in range(8):
                op("pe", lambda e, s_=s_: e.matmul(PS[bank][:, 0:256], lhsT=Zt[:, gl, (7 - s_) * 16:(7 - s_) * 16 + 128],
                                                   rhs=uT[:, cc, :].rearrange("p (b s) -> p b s", s=8)[:, :, s_], start=(s_ == 0), stop=(s_ == 7)),
                   r=["Zt", ("uT", cc)], w=[psk(bank)])
            op("act", lambda e: e.activation(out=U[:, g, :], in_=PS[bank][:, 0:256], func=AF.Copy), r=[psk(bank)], w=[("U", g)])

        def abt_chunk():
            for ri in range(2):
                bank = 2 + ri
                pb = PS[bank][:].bitcast(BF16)
                for gl in range(8):
                    op("pe", lambda e, gl=gl, pb=pb, ri=ri: e.transpose(out=pb[:, gl * 128:(gl + 1) * 128], in_=AB[ri][:, gl, :], identity=ident_b[:]), r=["AB", "pp", "ident_b"], w=[psk(bank)])
                op("dve", lambda e, pb=pb, ri=ri: e.tensor_copy(out=ABT[ri], in_=pb.rearrange("p (a b) -> p a b", b=128)), r=[psk(bank)], w=["ABT"])

        def vprime_group(cc, gl):
            g = 8 * cc + gl
            for ri in range(2):
                bank = 4 + ri
                op("pe", lambda e, ri=ri, bank=bank: e.matmul(PS[bank][:, 0:256], lhsT=ABT[ri][:, gl, :], rhs=U[:, g, :], start=True, stop=True), r=["ABT", ("U", g)], w=[psk(bank)])
                op("act", lambda e, ri=ri, bank=bank: e.activation(out=VX[:, ri, g, :], in_=PS[bank][:, 0:256], func=AF.Copy), r=[psk(bank)], w=[("VX", g)])

        for cc in range(4):
            for gl in range(8):
                shuffle_group(cc, gl)
            gen_AB(AB, 8 * cc, 0)
            gen_AB(AB, 8 * cc + 4, 4)
            abt_chunk()
            for gl in range(8):
                vprime_group(cc, gl)
        dbg_put("U0", U[:, 0, 0:64], [("U", 0)])
        dbg_put("VX0", VX[:, :, 0, 0:64], [("VX", 0)])
        dbg_put("pw", pw[:, :, 0, :], PP)
        dbg_put("pwi", pwi[:, :, 0, :], PP)
        dbg_put("bb", bb[:, :, 0, :], PP)

        VXK = [("VX", g) for g in range(32)]
        op("dve", lambda e: e.memset(Xst, 0.0), w=["sc0x", "sc64x"])

        def scan_step(lo, hi, b):
            key = "sc%d" % lo
            xs, s1, t1_, t2_ = Xst[lo:hi], S1[lo:hi], T1[lo:hi], T2[lo:hi]
            vx = VX[lo:hi, :, :, b]
            s1sw = bass.AP(s1.tensor, s1.offset + 32, [list(s1.ap[0]), [-32, 2], [1, 32]])
            op("dve", lambda e: e.tensor_tensor(out=s1, in0=xs, in1=vx, op=ALU.add), r=VXK + [key + "x"], w=[key + "s"])
            op("dve", lambda e: e.tensor_tensor(out=t1_, in0=P1s[lo:hi], in1=s1, op=ALU.mult), r=[key + "s", "pp"], w=[key + "a"])
            op("dve", lambda e: e.tensor_tensor(out=t2_, in0=P2s[lo:hi], in1=s1sw, op=ALU.mult), r=[key + "s", "pp"], w=[key + "b"])
            op("dve", lambda e: e.tensor_tensor(out=xs, in0=t1_, in1=t2_, op=ALU.add), r=[key + "a", key + "b"], w=[key + "x"])
            op("pool", lambda e: e.tensor_copy(out=vx, in_=xs), r=[key + "x"], w=[key + "c"])

        for b in range(256):
            scan_step(0, 64, b)
            scan_step(64, 128, 255 - b)
        dbg_put("X0", VX[:, :, 0, 0:64], ["sc0c", "sc64c"])

        AR.phase()
        AR.at("keep", 0, [85 * KB // 2], BF16)
        AB2 = [AR.at("AB2_%d" % i, 85 * KB + i * 2 * KB, [8, 128], BF16) for i in range(2)]
        CAf = [AR.at("CAf%d" % i, 89 * KB + i * 2 * KB, [8, 128], BF16) for i in range(2)]
        AR.at("keep2", 93 * KB, [(104 - 93) * KB // 2], BF16)
        CAb = [bm.rearrange("p a b c -> p (a b c)")[:, i * 512:(i + 1) * 512].bitcast(BF16).rearrange("p (a b) -> p a b", b=128) for i in range(2)]
        W0 = sv["lr"].tensor and AR.base[:, (64 * KB + 4096) // 2:(64 * KB + 4096 + 2048) // 2].rearrange("p (a b) -> p a b", b=128)
        for i in range(2):
            op("pool", lambda e, i=i: e.memset(CAf[i][64:128], 0.0), w=["CA"])
            op("pool", lambda e, i=i: e.memset(CAb[i][0:64], 0.0), w=["CA"])
        mF_ = s5m[:, 0, :]
        mB_ = s5m[:, 1, :]

        def w0_chunk():
            for half in range(2):
                pf, pb_ = PS[2], PS[3]
                for jj in range(4):
                    gl = half * 4 + jj
                    o1 = pf[:, jj * 128:(jj + 1) * 128]
                    o2 = pb_[:, jj * 128:(jj + 1) * 128]
                    op("pe", lambda e, gl=gl, o1=o1: e.matmul(o1, lhsT=AB2[0][:, gl, :], rhs=CAf[0][:, gl, :], start=True, stop=False), r=["AB", "CA", "pp"], w=[psk(2)])
                    op("pe", lambda e, gl=gl, o1=o1: e.matmul(o1, lhsT=AB2[1][:, gl, :], rhs=CAf[1][:, gl, :], start=False, stop=True), r=["AB", "CA", "pp"], w=[psk(2)])
                    op("pe", lambda e, gl=gl, o2=o2: e.matmul(o2, lhsT=AB2[0][:, gl, :], rhs=CAb[0][:, gl, :], start=True, stop=False), r=["AB", "CA", "pp"], w=[psk(3)])
                    op("pe", lambda e, gl=gl, o2=o2: e.matmul(o2, lhsT=AB2[1][:, gl, :], rhs=CAb[1][:, gl, :], start=False, stop=True), r=["AB", "CA", "pp"], w=[psk(3)])
                t3 = tA.rearrange("p (a b) -> p a b", b=128)
                op("dve", lambda e, t3=t3: e.tensor_tensor(out=t3, in0=PS[2][:].rearrange("p (a b) -> p a b", b=128), in1=mF_.unsqueeze(1).broadcast_to([128, 4, 128]), op=ALU.mult),
                   r=[psk(2), "s5m", "pp"], w=["pp"])
                op("dve", lambda e: e.tensor_tensor(out=tB.rearrange("p (a b) -> p a b", b=128), in0=PS[3][:].rearrange("p (a b) -> p a b", b=128), in1=mB_.unsqueeze(1).broadcast_to([128, 4, 128]), op=ALU.mult),
                   r=[psk(3), "s5m", "pp"], w=["pp"])
                op("dve", lambda e, half=half: e.tensor_tensor(out=W0[:, half * 4:(half + 1) * 4, :], in0=tA.rearrange("p (a b) -> p a b", b=128), in1=tB.rearrange("p (a b) -> p a b", b=128), op=ALU.add),
                   r=["pp"], w=["W0", "pp"])

        SCK = ["sc0c", "sc64c"]

        def y_group(cc, gl):
            g = 8 * cc + gl
            bank = 4 + g % 2
            o_ = PS[bank]
            rk = ["W0", "CA", ("U", g), ("VX", g)] + SCK
            op("pe", lambda e: e.matmul(o_[:, 0:256], lhsT=W0[:, gl, :], rhs=U[:, g, :], start=True, stop=False), r=rk, w=[psk(bank)])
            op("pe", lambda e: e.matmul(o_[:, 1:256], lhsT=CAf[0][:, gl, :], rhs=VX[:, 0, g, 0:255], start=False, stop=False), r=rk, w=[psk(bank)])
            op("pe", lambda e: e.matmul(o_[:, 1:256], lhsT=CAf[1][:, gl, :], rhs=VX[:, 1, g, 0:255], start=False, stop=False), r=rk, w=[psk(bank)])
            op("pe", lambda e: e.matmul(o_[:, 0:255], lhsT=CAb[0][:, gl, :], rhs=VX[:, 0, g, 1:256], start=False, stop=False), r=rk, w=[psk(bank)])
            op("pe", lambda e: e.matmul(o_[:, 0:255], lhsT=CAb[1][:, gl, :], rhs=VX[:, 1, g, 1:256], start=False, stop=True), r=rk, w=[psk(bank)])
            op("act", lambda e: e.activation(out=U[:, g, :], in_=o_[:, 0:256], func=AF.Copy), r=[psk(bank)], w=[("U", g)])

        def unshuffle(cc, tl):
            bank = tl % 2
            for gl in range(8):
                g = 8 * cc + gl
                op("pe", lambda e, gl=gl, g=g: e.matmul(PS[bank][:, 0:256], lhsT=Zt[:, tl, (7 - gl) * 16:(7 - gl) * 16 + 128], rhs=U[:, g, :], start=(gl == 0), stop=(gl == 7)),
                   r=["Zt", ("U", g)], w=[psk(bank)])
            uv = uT[:, cc, :].rearrange("p (b s) -> p b s", s=8)[:, :, tl]
            op("dve", lambda e: e.scalar_tensor_tensor(out=uv, in0=uv, scalar=dsk[:, cc:cc + 1], in1=PS[bank][:, 0:256], op0=ALU.mult, op1=ALU.add),
               r=[psk(bank), ("uT", cc), "pp"], w=[("uT", cc)])

        for cc in range(4):
            for hh in range(2):
                gen_AB(AB2, 8 * cc + 4 * hh, 4 * hh)
                gen_CA(8 * cc + 4 * hh, 4 * hh, CAf, CAb)
            w0_chunk()
            for gl in range(8):
                y_group(cc, gl)
            for tl in range(8):
                unshuffle(cc, tl)
        dbg_put("y", uT[:, :, 0:512], [("uT", cc) for cc in range(4)])

        AR.phase()
        AR.at("uT", 32 * KB, [4, L], BF16)
        bT = AR.at("bT", 0, [4, L], BF16)
        g1 = AR.at("g1", 16 * KB, [L], F32)
        g2 = AR.at("g2", 24 * KB, [L], F32)
        CG = math.sqrt(2.0 / math.pi)

        def gelu_chunk(cc):
            y_ = uT[:, cc, :]
            op("act", lambda e: e.activation(out=g1, in_=y_, func=AF.Square), r=[("uT", cc)], w=["g1"])
            op("dve", lambda e: e.tensor_scalar(out=g1, in0=g1, scalar1=0.044715, scalar2=1.0, op0=ALU.mult, op1=ALU.add), r=["g1"], w=["g1"])
            op("dve", lambda e: e.tensor_tensor(out=g1, in0=g1, in1=y_, op=ALU.mult), r=["g1", ("uT", cc)], w=["g1"])
            op("act", lambda e: e.activation(out=g2, in_=g1, func=AF.Sigmoid, scale=2.0 * CG), r=["g1"], w=["g2"])
            op("pool", lambda e: e.tensor_tensor(out=y_, in0=y_, in1=g2, op=ALU.mult), r=["g2", ("uT", cc)], w=[("uT", cc)])

        for cc in range(4):
            gelu_chunk(cc)

        def glu(tb, oc, kq):
            bank = kq % 2
            for kc in range(4):
                op("pe", lambda e, kc=kc: e.matmul(PS[bank][:], lhsT=wglu[:, kc, oc * 128:(oc + 1) * 128], rhs=uT[:, kc, tb * 512:(tb + 1) * 512], start=(kc == 0), stop=(kc == 3)),
                   r=[("wb", 1), ("uT", kc)], w=[psk(bank)])
            gsl = g1[:, (kq % 4) * 512:(kq % 4 + 1) * 512]
            op("act", lambda e: e.activation(out=gsl, in_=PS[bank][:], func=AF.Sigmoid, bias=bgl[:, oc:oc + 1]), r=[psk(bank), "pp"], w=[("gs", kq % 4)])
            op("dve", lambda e: e.tensor_tensor(out=bT[:, oc, tb * 512:(tb + 1) * 512], in0=uT[:, oc, tb * 512:(tb + 1) * 512], in1=gsl, op=ALU.mult),
               r=[("gs", kq % 4), ("uT", oc)], w=[("bT", oc, tb)])

        P.barrier()
        kq = 0
        for tb in range(4):
            for oc in range(4):
                glu(tb, oc, kq)
                kq += 1
        dbg_put("bT", bT[:, :, 0:512], [("bT", oc, 0) for oc in range(4)])

        def oproj2(tb, dc, kq):
            bank = 2 + kq % 2
            for hc in range(4):
                op("pe", lambda e, hc=hc: e.matmul(PS[bank][:], lhsT=wo2[:, hc, dc * 128:(dc + 1) * 128], rhs=bT[:, hc, tb * 512:(tb + 1) * 512], start=(hc == 0), stop=(hc == 3)),
                   r=[("wb", 2), ("bT", hc, tb)], w=[psk(bank)])
            xs = xT[:, dc, tb * 512:(tb + 1) * 512]
            op("dve", lambda e: e.tensor_tensor(out=xs, in0=xs, in1=PS[bank][:], op=ALU.add), r=[psk(bank), xk(dc, tb)], w=[xk(dc, tb)])

        if not dbg:
            kq = 0
            for tb in range(4):
                for dc in range(8):
                    oproj2(tb, dc, kq)
                    kq += 1

    def hgrn_phase(gi):
        w_in = w_in_odd_d.rearrange("(c p) f -> p c f", p=128)
        w_out = w_out_odd_d.rearrange("(h p) d -> p h d", p=128)
        AR.phase()
        hT = AR.at("hT", 0, [8, L], BF16)
        sq = [AR.at("sq%d" % i, 44 * KB + i * 4 * KB, [L], BF16) for i in range(2)]
        rstd = AR.at("rstd", 52 * KB, [L], F32)
        norm_to_hT(gi, hT, rstd, sq)
        AR.phase()
        hT = AR.at("hT", 0, [8, L], BF16)
        qT = AR.at("qT", 32 * KB, [L], BF16)
        sigG = AR.at("sigG", 36 * KB, [L], BF16)
        vtok = AR.at("vtok", 40 * KB, [16, 128], BF16)
        A = AR.at("A", 44 * KB, [L], F32)
        B = AR.at("B", 52 * KB, [L], F32)
        kk = AR.at("kk", 60 * KB, [L], BF16)
        qdec = AR.at("qdec", 64 * KB, [L], BF16)
        kinv = AR.at("kinv", 68 * KB, [L], BF16)
        kend = AR.at("kend", 72 * KB, [16, 128], BF16)
        scT = AR.at("scT", 76 * KB, [16, 128], BF16)
        Sb = AR.at("Sb", 80 * KB, [32, 128], BF16)
        oacc = AR.at("oacc", 88 * KB, [16, 128], F32)
        mF = AR.at("mF", 96 * KB, [L], BF16)
        dec = AR.at("dec", 100 * KB, [32], F32)
        ssq = AR.at("ssq", 100 * KB + 128, [16], F32)
        Sst = [AR.at("Sst%d" % i, 100 * KB + 256 + i * 512, [128], F32) for i in range(2)]
        on_tok = AR.base[:, 44 * KB // 2: 48 * KB // 2].rearrange("p (a b) -> p a b", b=128)
        mT = AR.base[:, 52 * KB // 2: 56 * KB // 2]

        op("sp", lambda e: e.dma_start(out=lbl[:], in_=lbl_d), w=["lbl"], dma=True)
        op("sp", lambda e: e.dma_start(out=hg[:], in_=hg_d), w=["hg"], dma=True)
        op("sp", lambda e: e.dma_start(out=hmask[:], in_=hmask_d), w=["hmask"], dma=True)
        op("dve", lambda e: e.tensor_tensor(out=lb[:], in0=lbl[:, 1, :], in1=lbl[:, 0, :], op=ALU.subtract), r=["lbl"], w=["lb"])
        op("act", lambda e: e.activation(out=lb[:], in_=lb[:], func=AF.Sigmoid), r=["lb"], w=["lb"])
        op("dve", lambda e: e.tensor_scalar(out=oml[:], in0=lb[:], scalar1=-1.0, scalar2=1.0, op0=ALU.mult, op1=ALU.add), r=["lb"], w=["oml"])
        op("dve", lambda e: e.tensor_scalar(out=noml[:], in0=oml[:], scalar1=-1.0, scalar2=None, op0=ALU.mult), r=["oml"], w=["noml"])
        op("pool", lambda e: e.memset(mF, 1.0), w=["mF"])
        op("pool", lambda e: e.memset(mF.rearrange("p (a b) -> p a b", b=64)[:, :, 0:1], 0.0), w=["mF"])

        def load_head(h):
            sa, sb_ = (0, 1) if h % 2 == 0 else (2, 3)
            secs = []
            for i, sec in enumerate((0, 1, 2, 3)):
                dst = WB[sa][:, i * 1024:(i + 1) * 1024].rearrange("p (a b) -> p a b", b=128)
                op("pool", lambda e, dst=dst, sec=sec, h=h: e.dma_start(out=dst, in_=w_in[:, :, sec * 1024 + h * 128: sec * 1024 + (h + 1) * 128]),
                   w=[("wb", sa)], dma=True, dkey=("wbs", sa, i))
                secs.append(dst)
            dst = WB[sb_][:, 0:1024].rearrange("p (a b) -> p a b", b=128)
            op("pool", lambda e, dst=dst, h=h: e.dma_start(out=dst, in_=w_in[:, :, 4 * 1024 + h * 128: 4 * 1024 + (h + 1) * 128]),
               w=[("wb", sb_)], dma=True, dkey=("wbs", sb_, 0))
            secs.append(dst)
            wo = WB[sb_][:, 1024:2048]
            op("pool", lambda e, wo=wo, h=h: e.dma_start(out=wo, in_=w_out[:, h, :]), w=[("wb", sb_)], dma=True, dkey=("wbs", sb_, 1))
            return secs, wo, sa, sb_

        def proj_fm(wsec, slot, consume):
            for tb in range(4):
                bank = tb
                for c in range(8):
                    op("pe", lambda e, bank=bank, c=c, tb=tb: e.matmul(PS[bank][:], lhsT=wsec[:, c, :], rhs=hT[:, c, tb * 512:(tb + 1) * 512], start=(c == 0), stop=(c == 7)),
                       r=[("wb", slot), hk(c, tb)], w=[psk(bank)])
                consume(tb, bank)

        def do_head(h, cur):
            (wq, wi_, wff, wfb, wg), wo, sa, sb_ = cur
            proj_fm(wq, sa, lambda tb, bank: op("act", lambda e: e.activation(out=qT[:, tb * 512:(tb + 1) * 512], in_=PS[bank][:], func=AF.Copy), r=[psk(bank)], w=["qT"]))
            proj_fm(wg, sb_, lambda tb, bank: op("act", lambda e: e.activation(out=sigG[:, tb * 512:(tb + 1) * 512], in_=PS[bank][:], func=AF.Sigmoid), r=[psk(bank)], w=["sigG"]))
            for q4 in range(4):
                bank = 4 + q4 % 2
                for jj in range(4):
                    j = q4 * 4 + jj
                    for c in range(8):
                        op("pe", lambda e, bank=bank, jj=jj, j=j, c=c: e.matmul(PS[bank][:, jj * 128:(jj + 1) * 128], lhsT=hT[:, c, j * 128:(j + 1) * 128], rhs=wi_[:, c, :], start=(c == 0), stop=(c == 7)),
                           r=[("wb", sa), hk(c, j // 4)], w=[psk(bank)])
                op("dve", lambda e, bank=bank, q4=q4: e.tensor_copy(out=vtok[:, q4 * 4:(q4 + 1) * 4, :], in_=PS[bank][:].rearrange("p (a b) -> p a b", b=128)), r=[psk(bank)], w=["vtok"])
            if h == 0:
                dbg_put("qT", qT[:, 0:256], ["qT"])
                dbg_put("sigG", sigG[:, 0:256], ["sigG"])
                dbg_put("vtok", vtok[:, 0:2, :], ["vtok"])
            def do_dir(d):
                wf = wff if d == 0 else wfb
                proj_fm(wf, sa, lambda tb, bank: op("act", lambda e: e.activation(out=A[:, tb * 512:(tb + 1) * 512], in_=PS[bank][:], func=AF.Sigmoid), r=[psk(bank)], w=["hgA"]))
                op("dve", lambda e: e.tensor_scalar(out=kk, in0=A, scalar1=noml[:, h:h + 1], scalar2=oml[:, h:h + 1], op0=ALU.mult, op1=ALU.add), r=["hgA", "noml", "oml"], w=["kk"])
                op("dve", lambda e: e.tensor_scalar(out=A, in0=A, scalar1=oml[:, h:h + 1], scalar2=lb[:, h:h + 1], op0=ALU.mult, op1=ALU.add), r=["hgA", "lb", "oml"], w=["hgA"])
                if h == 0 and d == 0:
                    dbg_put("k", kk[:, 0:256], ["kk"])
                    dbg_put("f", A[:, 0:256], ["hgA"])
                op("act", lambda e: e.activation(out=A, in_=A, func=AF.Ln), r=["hgA"], w=["hgA"])
                if d == 0:
                    op("dve", lambda e: e.tensor_tensor_scan(out=B, data0=mF, data1=A, initial=0.0, op0=ALU.mult, op1=ALU.add), r=["hgA", "mF"], w=["hgB"])
                else:
                    op("dve", lambda e: e.tensor_tensor_scan(out=B[:, ::-1], data0=mF, data1=A[:, ::-1], initial=0.0, op0=ALU.mult, op1=ALU.add), r=["hgA", "mF"], w=["hgB"])
                if h == 0 and d == 0:
                    dbg_put("b", B[:, 0:256], ["hgB"])
                op("act", lambda e: e.activation(out=A, in_=B, func=AF.Exp), r=["hgB"], w=["hgA"])
                op("pool", lambda e: e.tensor_tensor(out=qdec, in0=qT, in1=A, op=ALU.mult), r=["hgA", "qT"], w=["qdec"])
                op("act", lambda e: e.activation(out=A, in_=B, func=AF.Exp, scale=-1.0), r=["hgB", "qdec"], w=["hgA"])
                op("pool", lambda e: e.tensor_tensor(out=A, in0=kk, in1=A, op=ALU.mult), r=["hgA", "kk"], w=["hgA"])
                op("act", lambda e: e.activation(out=kinv, in_=A, func=AF.Copy), r=["hgA"], w=["kinv"])
                bl = B.rearrange("p (a b) -> p a b", b=64)[:, :, 63:64] if d == 0 else B.rearrange("p (a b) -> p a b", b=64)[:, :, 0:1]
                op("act", lambda e, bl=bl: e.activation(out=dec.rearrange("p (a b) -> p a b", b=1), in_=bl, func=AF.Exp), r=["hgB"], w=["dec"])
                dec_bc = bass.AP(dec.tensor, dec.offset, [list(dec.ap[0]), [1, 32], [0, 64]])
                op("pool", lambda e, dec_bc=dec_bc: e.tensor_tensor(out=kk.rearrange("p (a b) -> p a b", b=64), in0=A.rearrange("p (a b) -> p a b", b=64), in1=dec_bc, op=ALU.mult),
                   r=["hgA", "dec"], w=["kk"])
                if h == 0 and d == 0:
                    dbg_put("qdec", qdec[:, 0:256], ["qdec"])
                    dbg_put("kinv", kinv[:, 0:256], ["kinv"])
                    dbg_put("dec", dec, ["dec"])
                    dbg_put("kendT", kk[:, 0:256], ["kk"])
                for half in range(2):
                    bank = 4 + half
                    pb = PS[bank][:].bitcast(BF16)
                    for jj in range(8):
                        j = half * 8 + jj
                        op("pe", lambda e, pb=pb, jj=jj, j=j: e.transpose(out=pb[:, jj * 128:(jj + 1) * 128], in_=kk[:, j * 128:(j + 1) * 128], identity=ident_b[:]),
                           r=["kk", "ident_b"], w=[psk(bank)])
                    op("act", lambda e, pb=pb, half=half: e.activation(out=kend[:, half * 8:(half + 1) * 8, :], in_=pb.rearrange("p (a b) -> p a b", b=128), func=AF.Copy),
                       r=[psk(bank)], w=["kend"])
                for q4 in range(4):
                    bank = q4 % 4
                    for jj in range(4):
                        j = q4 * 4 + jj
                        op("pe", lambda e, bank=bank, jj=jj, j=j: e.matmul(PS[bank][:, jj * 128:(jj + 1) * 128], lhsT=kinv[:, j * 128:(j + 1) * 128], rhs=qdec[:, j * 128:(j + 1) * 128], start=True, stop=True),
                           r=["kinv", "qdec"], w=[psk(bank)])
                    hm_ = hmask[:, d, :]
                    mk = bass.AP(hm_.tensor, hm_.offset, [list(hm_.ap[0]), [0, 4], [1, 128]])
                    op("dve", lambda e, bank=bank, q4=q4, mk=mk: e.tensor_tensor(out=scT[:, q4 * 4:(q4 + 1) * 4, :], in0=PS[bank][:].rearrange("p (a b) -> p a b", b=128), in1=mk, op=ALU.mult),
                       r=[psk(bank), "hmask"], w=["scT"])
                op("pool", lambda e: e.memset(Sst[0], 0.0), w=[("Sst", 0)])
                order = list(range(32)) if d == 0 else list(range(31, -1, -1))
                for n, ci in enumerate(order):
                    cur, new = Sst[n % 2], Sst[(n + 1) % 2]
                    par = ci % 2
                    bank = 3 if par == 0 else 7
                    op("act", lambda e, cur=cur, ci=ci: e.activation(out=Sb[:, ci, :], in_=cur, func=AF.Copy), r=[("Sst", n % 2)], w=[("Sb", ci)])
                    op("pe", lambda e, bank=bank, ci=ci, par=par: e.matmul(PS[bank][:, 0:128], lhsT=kend[par * 64:(par + 1) * 64, ci // 2, :], rhs=vtok[par * 64:(par + 1) * 64, ci // 2, :], start=True, stop=True),
                       r=["kend", "vtok"], w=[psk(bank)])
                    op("dve", lambda e, bank=bank, cur=cur, new=new, ci=ci: e.scalar_tensor_tensor(out=new, in0=cur, scalar=dec[:, ci:ci + 1], in1=PS[bank][:, 0:128], op0=ALU.mult, op1=ALU.add),
                       r=[psk(bank), ("Sst", n % 2), "dec"], w=[("Sst", (n + 1) % 2)])
                if h == 0 and d == 0:
                    dbg_put("kend", kend[:, 0:2, :], ["kend"])
                    dbg_put("scT", scT[:, 0:2, :], ["scT"])
                    dbg_put("Sb", Sb[:, 0:4, :], [("Sb", i_) for i_ in range(4)])
                for q4 in range(4):
                    bank = 4 + q4 % 2
                    for jj in range(4):
                        j = q4 * 4 + jj
                        o_ = PS[bank][:, jj * 128:(jj + 1) * 128]
                        op("pe", lambda e, o_=o_, j=j: e.matmul(o_, lhsT=scT[:, j, :], rhs=vtok[:, j, :], start=True, stop=False), r=["scT", "vtok"], w=[psk(bank)])
                        op("pe", lambda e, bank=bank, jj=jj, j=j: e.matmul(PS[bank][0:64, jj * 128:(jj + 1) * 128], lhsT=qdec[:, j * 128:j * 128 + 64], rhs=Sb[:, 2 * j, :], start=False, stop=False),
                           r=["qdec", ("Sb", 2 * j)], w=[psk(bank)])
                        op("pe", lambda e, bank=bank, jj=jj, j=j: e.matmul(PS[bank][64:128, jj * 128:(jj + 1) * 128], lhsT=qdec[:, j * 128 + 64:j * 128 + 128], rhs=Sb[:, 2 * j + 1, :], start=False, stop=True),
                           r=["qdec", ("Sb", 2 * j + 1)], w=[psk(bank)])
                    ov = oacc[:, q4 * 4:(q4 + 1) * 4, :]
                    pv = PS[bank][:].rearrange("p (a b) -> p a b", b=128)
                    if d == 0:
                        op("dve", lambda e, ov=ov, pv=pv: e.tensor_copy(out=ov, in_=pv), r=[psk(bank)], w=["oacc"])
                    else:
                        op("dve", lambda e, ov=ov, pv=pv: e.tensor_tensor(out=ov, in0=ov, in1=pv, op=ALU.add), r=[psk(bank), "oacc"], w=["oacc"])
                if h == 0:
                    dbg_put("oacc%d" % d, oacc[:, 0:2, :], ["oacc"])

            do_dir(0)
            do_dir(1)
            for j in range(16):
                op("act", lambda e, j=j: e.activation(out=on_tok[:, j, :], in_=oacc[:, j, :], func=AF.Square, accum_out=ssq[:, j:j + 1]), r=["oacc", "hgB"], w=["hgA", "ssq"])
            op("act", lambda e: e.activation(out=ssq, in_=ssq, func=AF.Sqrt, scale=1.0 / 128, bias=EPS), r=["ssq"], w=["ssq"])
            op("dve", lambda e: e.reciprocal(out=ssq, in_=ssq), r=["ssq"], w=["ssq"])
            for j in range(16):
                op("dve", lambda e, j=j: e.tensor_scalar(out=on_tok[:, j, :], in0=oacc[:, j, :], scalar1=ssq[:, j:j + 1], scalar2=None, op0=ALU.mult), r=["oacc", "ssq", "hgA"], w=["hgA"])
            for half in range(2):
                bank = half
                pb = PS[bank][:].bitcast(BF16)
                for jj in range(8):
                    j = half * 8 + jj
                    op("pe", lambda e, pb=pb, jj=jj, j=j: e.transpose(out=pb[:, jj * 128:(jj + 1) * 128], in_=on_tok[:, j, :], identity=ident_b[:]), r=["hgA", "ident_b"], w=[psk(bank)])
                op("dve", lambda e, pb=pb, half=half: e.scalar_tensor_tensor(out=mT[:, half * 1024:(half + 1) * 1024], in0=pb, scalar=hg[:, h:h + 1], in1=sigG[:, half * 1024:(half + 1) * 1024], op0=ALU.mult, op1=ALU.mult),
                   r=[psk(bank), "hg", "sigG", "hgA"], w=["hgB"])
            if h == 0:
                dbg_put("mT", mT[:, 0:256], ["hgB"])
            if dbg:
                return
            kq = 0
            for tb in range(4):
                for dc in range(8):
                    bank = 2 + kq % 2
                    op("pe", lambda e, bank=bank, dc=dc, tb=tb: e.matmul(PS[bank][:], lhsT=wo[:, dc * 128:(dc + 1) * 128], rhs=mT[:, tb * 512:(tb + 1) * 512], start=True, stop=True),
                       r=[("wb", sb_), "hgB"], w=[psk(bank)])
                    xs = xT[:, dc, tb * 512:(tb + 1) * 512]
                    op("dve", lambda e, xs=xs, bank=bank: e.tensor_tensor(out=xs, in0=xs, in1=PS[bank][:], op=ALU.add), r=[psk(bank), xk(dc, tb)], w=[xk(dc, tb)])
                    kq += 1

        nxt = load_head(0)
        for h in range(8):
            cur = nxt
            if h + 1 < 8:
                nxt = load_head(h + 1)
            do_head(h, cur)
            if dbg:
                break

    if "l0attn" in parts or "l0mix" in parts:
        attn_phase(0)
    if "l0s5" in parts or "l0mix" in parts:
        s5_phase()
    if "l0ffn" in parts:
        ffn_phase(0, 1)
    if "l1mix" in parts:
        hgrn_phase(2)
    if "l1ffn" in parts:
        ffn_phase(1, 3)

    if dbg:
        P.barrier()
        op("sp", lambda e: e.dma_start(out=out_d.rearrange("(p a) d -> p (a d)", p=128), in_=xT_flat), dma=True, dkey="dbgout")
        P.emit()
        return nc
    AR.phase()
    sq = [AR.at("sq%d" % i, i * 4 * KB, [L], BF16) for i in range(2)]
    rstd = AR.at("rstd", 8 * KB, [L], F32)
    stage = [AR.at("stage%d" % i, 16 * KB + i * 4 * KB, [D], F32) for i in range(2)]
    ftmp = [AR.at("ftmp%d" % i, 24 * KB + i * 512, [128], F32) for i in range(4)]
    do_final = "final" in parts
    if do_final:
        rmsnorm_rstd(rstd, sq)
    k = 0
    for t in range(16):
        st = stage[t % 2]
        for half in range(2):
            bank = (2 * t + half) % 4
            for j in range(4):
                c = half * 4 + j
                src = xT[:, c, t * 128:(t + 1) * 128]
                if do_final:
                    ft = ftmp[k % 4]
                    op("dve", lambda e, ft=ft, src=src, c=c, t=t: e.scalar_tensor_tensor(
                        out=ft, in0=src, scalar=gains[:, 4, c:c + 1], in1=rstd[:, t * 128:(t + 1) * 128], op0=ALU.mult, op1=ALU.mult),
                       r=[xk(c, t // 4), ("rstd", t // 4), "gains"], w=[("ftmp", k % 4)])
                    op("pe", lambda e, bank=bank, j=j, ft=ft: e.transpose(out=PS[bank][:, j * 128:(j + 1) * 128], in_=ft, identity=ident_f[:]),
                       r=[("ftmp", k % 4), "ident_f"], w=[psk(bank)])
                    k += 1
                else:
                    op("pe", lambda e, bank=bank, j=j, src=src: e.transpose(out=PS[bank][:, j * 128:(j + 1) * 128], in_=src, identity=ident_f[:]),
                       r=[xk(c, t // 4), "ident_f"], w=[psk(bank)])
            dst = st[:, half * 512:(half + 1) * 512]
            op("act", lambda e, dst=dst, bank=bank: e.activation(out=dst, in_=PS[bank][:], func=AF.Copy), r=[psk(bank)], w=[("stage", t % 2, half)])
        op("sp", lambda e, st=st, t=t: e.dma_start(out=out_d[t * 128:(t + 1) * 128, :], in_=st), r=[("stage", t % 2, 0), ("stage", t % 2, 1)], dma=True, dkey=("st", t % 2))
    P.emit()
    return nc


def host_consts():
    c = {}
    c["ident"] = np.eye(128, dtype=np.float32)
    inv = (10000.0 ** (-np.arange(0, 64, 2, dtype=np.float32) / np.float32(64))).astype(np.float32)
    ang = (np.arange(L, dtype=np.float32)[None, :] * inv[:, None]).astype(np.float32)
    c["rope_cos"] = np.ascontiguousarray(np.tile(np.cos(ang).astype(np.float32), (4, 1)))
    c["rope_sin"] = np.ascontiguousarray(np.tile(np.sin(ang).astype(np.float32), (4, 1)))
    s_ = np.arange(128)[:, None]
    c_ = np.arange(128)[None, :]
    same = (s_ // 64) == (c_ // 64)
    hm = np.stack([(same & (s_ <= c_)), (same & (s_ >= c_))], axis=1).astype(np.float32)
    c["hmask"] = np.ascontiguousarray(hm)
    sl_ = (np.arange(128) // 16)[:, None]
    tl_ = (np.arange(128) // 16)[None, :]
    c["s5m"] = np.ascontiguousarray(np.stack([(tl_ >= sl_), (tl_ <= sl_)], axis=1).astype(np.float32))
    return c


def make_in_maps(inputs, parts=None):
    consts = host_consts()
    gains = np.concatenate([inputs["norm_mix_g"][0:1], inputs["norm_mlp_g"][0:1], inputs["norm_mix_g"][1:2],
                            inputs["norm_mlp_g"][1:2], inputs["final_norm_g"][None, :]], axis=0).astype(np.float32)
    shared = dict(consts)
    shared["gains"] = np.ascontiguousarray(gains.reshape(5, 8, 128).transpose(2, 0, 1))
    shared["w_in_odd"] = np.ascontiguousarray(inputs["w_in_odd"][0], dtype=np.float32)
    shared["w_out_odd"] = np.ascontiguousarray(inputs["w_out_odd"][0], dtype=np.float32)
    shared["lbl"] = np.ascontiguousarray(np.asarray(inputs["hgrn_lb_logits"], dtype=np.float32).reshape(2, 8, 128).transpose(2, 0, 1))
    shared["hg"] = np.ascontiguousarray(np.asarray(inputs["hgrn_norm_g"][0], dtype=np.float32).reshape(8, 128).T)
    shared["w_in_even"] = np.ascontiguousarray(inputs["w_in_even"][0], dtype=np.float32)
    shared["w_out_even"] = np.ascontiguousarray(inputs["w_out_even"][0], dtype=np.float32)
    shared["diff_lambda"] = np.ascontiguousarray(np.asarray(inputs["diff_lambda"][0], dtype=np.float32).reshape(1, 256))
    shared["subln_g"] = np.ascontiguousarray(np.asarray(inputs["diff_subln_g"][0], dtype=np.float32).reshape(128, 1))
    f32 = np.float32
    lre = np.asarray(inputs["s5_lam_re"][0], f32); lim = np.asarray(inputs["s5_lam_im"][0], f32); lst = np.asarray(inputs["s5_log_step"][0], f32)
    s5s = np.stack([lre.transpose(0, 2, 1).reshape(128, 32), lim.transpose(0, 2, 1).reshape(128, 32),
                    np.repeat(lst[:, None, :], 64, axis=1).reshape(128, 32)], axis=1)
    shared["s5s"] = np.ascontiguousarray(s5s, dtype=f32)
    shared["s5b"] = np.ascontiguousarray(np.stack([np.asarray(inputs[k][0], f32).transpose(0, 2, 1, 3).reshape(128, 32, 16) for k in ("s5_b_re", "s5_b_im")], axis=1))
    shared["s5c"] = np.ascontiguousarray(np.stack([np.asarray(inputs[k][0], f32).transpose(0, 3, 1, 2).reshape(128, 32, 16) for k in ("s5_c_re", "s5_c_im")], axis=1))
    shared["s5d"] = np.ascontiguousarray(np.asarray(inputs["s5_d"][0], f32).reshape(4, 128).T)
    shared["s5bg"] = np.ascontiguousarray(np.asarray(inputs["s5_b_glu"][0], f32).reshape(4, 128).T)
    shared["w_glu"] = np.ascontiguousarray(inputs["s5_w_glu"][0], dtype=f32)
    shared["w_ff_in"] = np.ascontiguousarray(inputs["w_ff_in"], dtype=np.float32)
    shared["w_ff_out"] = np.ascontiguousarray(inputs["w_ff_out"], dtype=np.float32)
    x = np.asarray(inputs["x"], dtype=np.float32)
    maps = []
    for b in range(x.shape[0]):
        m = dict(shared)
        m["x"] = np.ascontiguousarray(x[b])
        maps.append(m)
    return maps


ALL_PARTS = {"l0mix", "l0ffn", "l1mix", "l1ffn", "final"}
_NC_CACHE = {}


def kernel(**inputs):
    inputs = {k: np.asarray(v) for k, v in inputs.items()}
    key = "full"
    if key not in _NC_CACHE:
        _NC_CACHE[key] = build({"parts": ALL_PARTS})
    nc = _NC_CACHE[key]
    maps = make_in_maps(inputs)
    res = run_bass_kernel_spmd(nc, maps, core_ids=list(range(8)))
    out = np.stack([np.asarray(r["out"]) for r in res.results], axis=0)
    return out.astype(np.float32)
```

```python
import math
from contextlib import ExitStack

import numpy as np
import concourse.bass as bass
import concourse.mybir as mybir
from concourse.bass_utils import run_bass_kernel_spmd

F32 = mybir.dt.float32
BF16 = mybir.dt.bfloat16
AF = mybir.ActivationFunctionType
ALU = mybir.AluOpType
AX = mybir.AxisListType

ENGS = ("pe", "act", "dve", "pool", "sp")

L = 2048
D = 1024
NTB = 4
EPS = 1e-6
LAMBDA_INIT0 = 0.8 - 0.6 * math.exp(-0.3 * 0)


class Op:
    __slots__ = ("eng", "fn", "deps", "is_dma", "dkey", "signal", "sem", "val", "idx")


class Prog:
    def __init__(self, nc):
        self.nc = nc
        self.ops = []
        self.last_w = {}
        self.readers = {}
        self.stack = ExitStack()
        self.pending_barrier = {}

    def sb(self, name, shape, dt):
        return self.stack.enter_context(self.nc.sbuf_tensor("sb_" + name, list(shape), dt))

    def ps(self, name, shape, dt=F32):
        return self.stack.enter_context(self.nc.psum_tensor("pp_" + name, list(shape), dt))

    def barrier(self):
        deps = set()
        last = {}
        for o in self.ops:
            if o.is_dma:
                deps.add(o.idx)
            else:
                last[o.eng] = o.idx
        deps.update(last.values())
        self.pending_barrier = {e: set(deps) for e in ENGS}
        self.last_w = {}
        self.readers = {}

    def op(self, eng, fn, r=(), w=(), dma=False, dkey=None):
        o = Op()
        o.eng, o.fn, o.is_dma, o.signal = eng, fn, dma, False
        o.idx = len(self.ops)
        deps = set()
        for k in list(r) + list(w):
            if k in self.last_w:
                deps.add(self.last_w[k])
        for k in w:
            for rd in self.readers.get(k, ()):
                deps.add(rd)
        if self.pending_barrier.get(eng):
            deps |= self.pending_barrier[eng]
            self.pending_barrier[eng] = set()
        deps.discard(o.idx)
        o.deps = deps
        o.dkey = (dkey if dkey is not None else (w[0] if len(w) else r[0])) if dma else None
        for k in r:
            self.readers.setdefault(k, []).append(o.idx)
        for k in w:
            self.last_w[k] = o.idx
            self.readers[k] = []
        self.ops.append(o)
        return o.idx

    def _skip(self, od, o):
        return (not od.is_dma) and (not o.is_dma) and od.eng == o.eng and o.eng == "pe"

    def emit(self, final_eng="sp"):
        nc = self.nc
        ops = self.ops
        final_deps = [o.idx for o in ops if o.is_dma]
        for o in ops:
            for d in o.deps:
                if not self._skip(ops[d], o):
                    ops[d].signal = True
        for d in final_deps:
            ops[d].signal = True
        sems = {}

        def get_sem(key):
            if key not in sems:
                sems[key] = self.stack.enter_context(nc.semaphore("s%d" % len(sems)))
            return sems[key]

        cnt = {}
        for o in ops:
            if not o.signal:
                continue
            key = ("dma", o.dkey) if o.is_dma else ("eng", o.eng)
            inc = 16 if o.is_dma else 1
            cnt[key] = cnt.get(key, 0) + inc
            o.sem = get_sem(key)
            o.val = cnt[key]
            o.dkey = key
        self.n_sems = len(sems)
        per_eng = {e: [] for e in ENGS}
        for o in ops:
            per_eng[o.eng].append(o)

        def run(eng_name, e):
            waited = {}
            for o in per_eng[eng_name]:
                need = {}
                for d in o.deps:
                    od = ops[d]
                    if (not od.signal) or self._skip(od, o):
                        continue
                    if waited.get(od.dkey, 0) >= od.val:
                        continue
                    if need.get(od.dkey, (None, 0))[1] < od.val:
                        need[od.dkey] = (od.sem, od.val)
                for k, (sem, val) in need.items():
                    e.wait_ge(sem, val)
                    waited[k] = val
                ins = o.fn(e)
                if o.signal:
                    ins.then_inc(o.sem, 16 if o.is_dma else 1)
            if eng_name == final_eng:
                need = {}
                for d in final_deps:
                    od = ops[d]
                    if waited.get(od.dkey, 0) >= od.val:
                        continue
                    if need.get(od.dkey, (None, 0))[1] < od.val:
                        need[od.dkey] = (od.sem, od.val)
                for k, (sem, val) in need.items():
                    e.wait_ge(sem, val)

        with nc.Block() as block:
            @block.tensor
            def _(e):
                run("pe", e)

            @block.scalar
            def _(e):
                run("act", e)

            @block.vector
            def _(e):
                run("dve", e)

            @block.gpsimd
            def _(e):
                run("pool", e)

            @block.sync
            def _(e):
                run("sp", e)
        self.stack.close()


class Arena:
    def __init__(self, P, nbytes):
        self.P = P
        self.nbytes = nbytes
        self.base = P.sb("arena", [128, nbytes // 2], BF16)
        self.live = []

    def phase(self):
        self.P.barrier()
        self.live = []

    def at(self, name, off, shape, dt):
        n = 1
        for s in shape:
            n *= s
        nb = n * (4 if dt == F32 else 2)
        assert off % 4 == 0 and off + nb <= self.nbytes, (name, off, nb, self.nbytes)
        for (a, b, nm) in self.live:
            assert off >= b or off + nb <= a, ("arena overlap", name, nm)
        self.live.append((off, off + nb, name))
        ap = self.base[:, off // 2:(off + nb) // 2]
        if dt == F32:
            ap = ap.bitcast(F32)
        if len(shape) > 1:
            names = "abcd"[:len(shape)]
            pat = "p (" + " ".join(names) + ") -> p " + " ".join(names)
            ap = ap.rearrange(pat, **{names[i]: shape[i] for i in range(len(shape))})
        return ap

    def drop(self, name):
        self.live = [x for x in self.live if x[2] != name]


KB = 1024


def build(cfg):
    parts = cfg["parts"]
    nc = bass.Bass("TRN2", target_bir_lowering=False)

    def din(name, shape):
        return nc.dram_tensor(name, list(shape), F32, kind="ExternalInput").ap()

    x_d = din("x", [L, D])
    out_d = nc.dram_tensor("out", [L, D], F32, kind="ExternalOutput").ap()
    gains_d = din("gains", [128, 5, 8])
    w_ff_in_d = din("w_ff_in", [2, D, 4 * D])
    w_ff_out_d = din("w_ff_out", [2, 4 * D, D])
    ident_d = din("ident", [128, 128])
    w_in_odd_d = din("w_in_odd", [D, 5 * D])
    w_in_even_d = din("w_in_even", [D, 2 * D])
    w_out_even_d = din("w_out_even", [D, D])
    cos_d = din("rope_cos", [128, L])
    sin_d = din("rope_sin", [128, L])
    dl_d = din("diff_lambda", [1, 256])
    sg_d = din("subln_g", [128, 1])
    w_glu_d = din("w_glu", [512, 512])
    s5s_d = din("s5s", [128, 3, 32])
    s5b_d = din("s5b", [128, 2, 32, 16])
    s5c_d = din("s5c", [128, 2, 32, 16])
    s5d_d = din("s5d", [128, 4])
    s5bg_d = din("s5bg", [128, 4])
    s5m_d = din("s5m", [128, 2, 128])
    w_out_odd_d = din("w_out_odd", [D, D])
    lbl_d = din("lbl", [128, 2, 8])
    hg_d = din("hg", [128, 8])
    hmask_d = din("hmask", [128, 2, 128])

    P = Prog(nc)
    op = P.op

    xT = P.sb("xT", [128, 8, L], F32)
    WB = [P.sb("wb%d" % i, [128, 4096], BF16) for i in range(4)]
    ident_f = P.sb("ident_f", [128, 128], F32)
    ident_b = P.sb("ident_b", [128, 128], BF16)
    ones_b = P.sb("ones_b", [128, 128], BF16)
    ones_f = P.sb("ones_f", [128, 128], F32)
    gains = P.sb("gains", [128, 5, 8], F32)
    PS = [P.ps("ps%d" % i, [128, 512], F32) for i in range(8)]
    lbl = P.sb("lbl", [128, 2, 8], F32)
    dl = P.sb("dl", [128, 256], F32)
    dlp = P.sb("dlp", [128, 128], F32)
    lam = P.sb("lam", [128, 4], F32)
    sg = P.sb("sg", [128, 1], F32)
    rr_g = P.sb("rr_g", [128, 8], F32)
    hg = P.sb("hg", [128, 8], F32)
    hmask = P.sb("hmask", [128, 2, 128], F32)
    lb = P.sb("lb", [128, 8], F32)
    oml = P.sb("oml", [128, 8], F32)
    noml = P.sb("noml", [128, 8], F32)
    AR = Arena(P, 104 * KB)

    def psk(i):
        return ("ps", i)

    dbg = cfg.get("dbg", False)
    dbg_tab = cfg.setdefault("dbg_tab", {})
    xT_flat = xT[:].rearrange("p a b -> p (a b)")
    dbg_off = [0]

    def dbg_put(name, ap, rkeys):
        if not dbg:
            return
        shp = list(ap.shape)
        n = 1
        for v_ in shp[1:]:
            n *= v_
        o = dbg_off[0]
        dst = xT_flat[:, o:o + n]
        if len(shp) == 3:
            dst = dst.rearrange("p (a b) -> p a b", b=shp[2])
        op("dve", lambda e: e.tensor_copy(out=dst, in_=ap), r=list(rkeys), w=["dbg"])
        dbg_tab[name] = (o, shp)
        dbg_off[0] = o + n

    def xk(c, tb):
        return ("xT", c, tb)

    def hk(c, tb):
        return ("hT", c, tb)

    XT_ALL = [xk(c, tb) for c in range(8) for tb in range(4)]
    HT_ALL = [hk(c, tb) for c in range(8) for tb in range(4)]

    op("sp", lambda e: e.dma_start(out=ident_f[:], in_=ident_d), w=["ident_f"], dma=True)
    op("dve", lambda e: e.tensor_copy(out=ident_b[:], in_=ident_f[:]), r=["ident_f"], w=["ident_b"])
    op("pool", lambda e: e.memset(ones_b[:], 1.0), w=["ones_b"])
    op("pool", lambda e: e.memset(ones_f[:], 1.0), w=["ones_f"])
    op("sp", lambda e: e.dma_start(out=gains[:], in_=gains_d), w=["gains"], dma=True)

    AR.phase()
    xin = [AR.at("xin%d" % i, i * 4 * KB, [D], F32) for i in range(2)]
    ev = 0
    for t in range(16):
        b = xin[t % 2]
        op("sp", lambda e, b=b, t=t: e.dma_start(out=b, in_=x_d[t * 128:(t + 1) * 128, :]), w=[("xin", t % 2)], dma=True)
        for half in range(2):
            bank = (2 * t + half) % 4
            for j in range(4):
                c = half * 4 + j
                op("pe", lambda e, bank=bank, j=j, c=c, b=b: e.transpose(out=PS[bank][:, j * 128:(j + 1) * 128], in_=b[:, c * 128:(c + 1) * 128], identity=ident_f[:]),
                   r=[("xin", t % 2), "ident_f"], w=[psk(bank)])
            dst = xT[:, half * 4:(half + 1) * 4, t * 128:(t + 1) * 128]
            src = PS[bank][:].rearrange("p (a b) -> p a b", b=128)
            wk = [xk(half * 4 + j, t // 4) for j in range(4)]
            if ev % 2 == 0:
                op("dve", lambda e, dst=dst, src=src: e.tensor_copy(out=dst, in_=src), r=[psk(bank)], w=wk)
            else:
                op("act", lambda e, dst=dst, src=src: e.activation(out=dst, in_=src, func=AF.Copy), r=[psk(bank)], w=wk)
            ev += 1

    def rmsnorm_rstd(rstd, sq):
        for c in range(8):
            s = sq[c % 2]
            op("act", lambda e, s=s, c=c: e.activation(out=s, in_=xT[:, c, :], func=AF.Square),
               r=[xk(c, tb) for tb in range(4)], w=[("sq", c % 2)])
            for tb in range(4):
                op("pe", lambda e, s=s, c=c, tb=tb: e.matmul(PS[tb][:], lhsT=ones_b[:], rhs=s[:, tb * 512:(tb + 1) * 512], start=(c == 0), stop=(c == 7)),
                   r=[("sq", c % 2), "ones_b"], w=[psk(tb)])
        for tb in range(4):
            sl = rstd[:, tb * 512:(tb + 1) * 512]
            op("act", lambda e, sl=sl, tb=tb: e.activation(out=sl, in_=PS[tb][:], func=AF.Sqrt, scale=1.0 / D, bias=EPS),
               r=[psk(tb)], w=[("rstd", tb)])
            op("dve", lambda e, sl=sl: e.reciprocal(out=sl, in_=sl), r=[("rstd", tb)], w=[("rstd", tb)])

    def norm_to_hT(gi, hT, rstd, sq):
        rmsnorm_rstd(rstd, sq)
        for c in range(8):
            for tb in range(4):
                op("dve", lambda e, c=c, tb=tb: e.scalar_tensor_tensor(
                    out=hT[:, c, tb * 512:(tb + 1) * 512], in0=xT[:, c, tb * 512:(tb + 1) * 512], scalar=gains[:, gi, c:c + 1],
                    in1=rstd[:, tb * 512:(tb + 1) * 512], op0=ALU.mult, op1=ALU.mult),
                   r=[xk(c, tb), ("rstd", tb), "gains"], w=[hk(c, tb)])

    def wload(slot, src, r=()):
        a, b = src.shape[1], src.shape[2]
        dst = WB[slot][:, 0:a * b].rearrange("p (a b) -> p a b", b=b)
        op("pool", lambda e: e.dma_start(out=dst, in_=src), r=list(r), w=[("wb", slot)], dma=True)
        return dst

    def ffn(l, hT, actT, rl):
        w_in = w_ff_in_d[l].rearrange("(c p) f -> p c f", p=128)
        w_out = w_ff_out_d[l].rearrange("(c p) d -> p c d", p=128)
        views = {}

        def load(fg):
            views[fg] = (wload((2 * fg) % 4, w_in[:, :, fg * 512:(fg + 1) * 512]),
                         wload((2 * fg + 1) % 4, w_out[:, fg * 4:(fg + 1) * 4, :]))

        load(0)
        load(1)
        k = 0
        for fg in range(8):
            wi, wo = views[fg]
            sa, sbk = (2 * fg) % 4, (2 * fg + 1) % 4
            at = actT[fg % 2]
            for tb in range(4):
                for fc in range(4):
                    bank = k % 3
                    for c in range(8):
                        op("pe", lambda e, bank=bank, wi=wi, c=c, fc=fc, tb=tb: e.matmul(
                            PS[bank][:], lhsT=wi[:, c, fc * 128:(fc + 1) * 128], rhs=hT[:, c, tb * 512:(tb + 1) * 512], start=(c == 0), stop=(c == 7)),
                           r=[("wb", sa), hk(c, tb)], w=[psk(bank)])
                    r_ = rl[k % 2]
                    op("act", lambda e, bank=bank, r_=r_: e.activation(out=r_, in_=PS[bank][:], func=AF.Relu), r=[psk(bank)], w=[("rl", k % 2)])
                    op("act", lambda e, r_=r_, at=at, fc=fc, tb=tb: e.activation(out=at[:, fc, tb * 512:(tb + 1) * 512], in_=r_, func=AF.Square),
                       r=[("rl", k % 2)], w=[("actT", fg % 2, fc, tb)])
                    k += 1
            kk = 0
            for tb in range(4):
                for dc in range(8):
                    bank = 3 + kk % 3
                    for fc in range(4):
                        op("pe", lambda e, bank=bank, wo=wo, fc=fc, dc=dc, tb=tb, at=at: e.matmul(
                            PS[bank][:], lhsT=wo[:, fc, dc * 128:(dc + 1) * 128], rhs=at[:, fc, tb * 512:(tb + 1) * 512], start=(fc == 0), stop=(fc == 3)),
                           r=[("wb", sbk), ("actT", fg % 2, fc, tb)], w=[psk(bank)])
                    xs = xT[:, dc, tb * 512:(tb + 1) * 512]
                    op("dve", lambda e, xs=xs, bank=bank: e.tensor_tensor(out=xs, in0=xs, in1=PS[bank][:], op=ALU.add), r=[psk(bank), xk(dc, tb)], w=[xk(dc, tb)])
                    kk += 1
            if fg + 2 < 8:
                load(fg + 2)

    def ffn_phase(l, gi):
        AR.phase()
        hT = AR.at("hT", 0, [8, L], BF16)
        actT = [AR.at("actT%d" % i, 32 * KB + i * 16 * KB, [4, L], BF16) for i in range(2)]
        rl = [AR.at("rl%d" % i, 64 * KB + i * 2 * KB, [512], F32) for i in range(2)]
        sq = [AR.at("sq%d" % i, 68 * KB + i * 4 * KB, [L], BF16) for i in range(2)]
        rstd = AR.at("rstd", 76 * KB, [L], F32)
        norm_to_hT(gi, hT, rstd, sq)
        ffn(l, hT, actT, rl)


    def attn_phase(gi):
        w_in = w_in_even_d.rearrange("(c p) f -> p c f", p=128)
        w_out = w_out_even_d.rearrange("(c p) d -> p c d", p=128)
        T0 = 81 * KB
        AR.phase()
        hT = AR.at("hT", 0, [8, L], BF16)
        sq = [AR.at("sq%d" % i, T0 + i * 4 * KB, [L], BF16) for i in range(2)]
        rstd = AR.at("rstd", T0 + 8 * KB, [L], F32)
        norm_to_hT(gi, hT, rstd, sq)
        op("sp", lambda e: e.dma_start(out=dl[:], in_=bass.AP(dl_d.tensor, 0, [[0, 128], [1, 256]])), w=["dl"], dma=True)
        op("sp", lambda e: e.dma_start(out=sg[:], in_=sg_d), w=["sg"], dma=True)
        op("dve", lambda e: e.tensor_tensor(out=dlp[:, 0:64], in0=dl[:, 0:64], in1=dl[:, 64:128], op=ALU.mult), r=["dl"], w=["dlp"])
        op("dve", lambda e: e.tensor_tensor(out=dlp[:, 64:128], in0=dl[:, 128:192], in1=dl[:, 192:256], op=ALU.mult), r=["dl", "dlp"], w=["dlp"])
        op("dve", lambda e: e.reduce_sum(out=lam[:, 0:2], in_=dlp[:].rearrange("p (a b) -> p a b", b=64), axis=AX.X), r=["dlp"], w=["lam"])
        op("act", lambda e: e.activation(out=lam[:, 0:2], in_=lam[:, 0:2], func=AF.Exp), r=["lam"], w=["lam"])
        op("dve", lambda e: e.tensor_tensor(out=lam[:, 2:3], in0=lam[:, 1:2], in1=lam[:, 0:1], op=ALU.subtract), r=["lam"], w=["lam"])
        op("dve", lambda e: e.tensor_scalar(out=lam[:, 3:4], in0=lam[:, 2:3], scalar1=-LAMBDA_INIT0, scalar2=None, op0=ALU.add), r=["lam"], w=["lam"])
        op("dve", lambda e: e.tensor_scalar(out=sg[:], in0=sg[:], scalar1=1.0 - LAMBDA_INIT0, scalar2=None, op0=ALU.mult), r=["sg"], w=["sg"])
        nlam = lam[:, 3:4]
        if cfg.get("attn_stop", 9) <= 1:
            return

        AR.phase()
        hT = AR.at("hT", 0, [8, L], BF16)
        qT = AR.at("qT", 32 * KB, [4, L], BF16)
        kT = AR.at("kT", 48 * KB, [4, L], BF16)
        vaug = AR.at("vaug", 64 * KB, [16, 4, 130], BF16)
        t1 = [AR.at("t1_%d" % i, T0 + i * 2 * KB, [512], F32) for i in range(2)]
        t2 = [AR.at("t2_%d" % i, T0 + 4 * KB + i * 2 * KB, [512], F32) for i in range(2)]
        cosb = [AR.at("cos%d" % i, T0 + 8 * KB + i * 2 * KB, [512], F32) for i in range(2)]
        sinb = [AR.at("sin%d" % i, T0 + 12 * KB + i * 2 * KB, [512], F32) for i in range(2)]

        def load_group(slot, g):
            return wload(slot, w_in[:, :, g * 512:(g + 1) * 512])

        def rotate(sa, sr):
            a = WB[sa][:].rearrange("p (cb two j) -> p cb two j", two=2, j=32)
            r_ = WB[sr][:].rearrange("p (cb two j) -> p cb two j", two=2, j=32)
            op("dve", lambda e: e.tensor_scalar(out=r_[:, :, 0, :], in0=a[:, :, 1, :], scalar1=-1.0, scalar2=None, op0=ALU.mult), r=[("wb", sa)], w=[("wb", sr)])
            op("dve", lambda e: e.tensor_copy(out=r_[:, :, 1, :], in_=a[:, :, 0, :]), r=[("wb", sa)], w=[("wb", sr)])
            return WB[sr][:].rearrange("p (a b) -> p a b", b=512)

        wq = load_group(0, 0)
        wk = load_group(2, 1)
        wqr = rotate(0, 1)
        wkr = rotate(2, 3)
        op("pool", lambda e: e.memset(vaug[:, :, :, 128:130], 1.0), w=["vaug1"])
        kctr = [0]

        def rope_proj(tb, wA, wR, sA, sR, dstT, h, dkey):
            i = kctr[0] % 2
            kctr[0] += 1
            ba, bb = 2 * i, 2 * i + 1
            for c in range(8):
                op("pe", lambda e, c=c: e.matmul(PS[ba][:], lhsT=wA[:, c, h * 128:(h + 1) * 128], rhs=hT[:, c, tb * 512:(tb + 1) * 512], start=(c == 0), stop=(c == 7)),
                   r=[("wb", sA), hk(c, tb)], w=[psk(ba)])
            for c in range(8):
                op("pe", lambda e, c=c: e.matmul(PS[bb][:], lhsT=wR[:, c, h * 128:(h + 1) * 128], rhs=hT[:, c, tb * 512:(tb + 1) * 512], start=(c == 0), stop=(c == 7)),
                   r=[("wb", sR), hk(c, tb)], w=[psk(bb)])
            op("dve", lambda e: e.tensor_tensor(out=t1[i], in0=PS[ba][:], in1=cosb[tb % 2], op=ALU.mult), r=[psk(ba), ("cos", tb % 2)], w=[("t1", i)])
            op("dve", lambda e: e.tensor_tensor(out=t2[i], in0=PS[bb][:], in1=sinb[tb % 2], op=ALU.mult), r=[psk(bb), ("sin", tb % 2)], w=[("t2", i)])
            op("pool", lambda e: e.tensor_tensor(out=dstT[:, h, tb * 512:(tb + 1) * 512], in0=t1[i], in1=t2[i], op=ALU.add), r=[("t1", i), ("t2", i)], w=[(dkey, h, tb)])

        def do_tb(tb):
            op("sp", lambda e: e.dma_start(out=cosb[tb % 2], in_=cos_d[:, tb * 512:(tb + 1) * 512]), w=[("cos", tb % 2)], dma=True)
            op("sp", lambda e: e.dma_start(out=sinb[tb % 2], in_=sin_d[:, tb * 512:(tb + 1) * 512]), w=[("sin", tb % 2)], dma=True)
            for h in range(4):
                rope_proj(tb, wq, wqr, 0, 1, qT, h, "qT")
                rope_proj(tb, wk, wkr, 2, 3, kT, h, "kT")

        for tb in range(4):
            do_tb(tb)
        wv = load_group(0, 2)

        def do_v(j):
            bank = 4 + j % 2
            for c in range(8):
                op("pe", lambda e, c=c: e.matmul(PS[bank][:], lhsT=hT[:, c, j * 128:(j + 1) * 128], rhs=wv[:, c, :], start=(c == 0), stop=(c == 7)),
                   r=[("wb", 0), hk(c, j // 4)], w=[psk(bank)])
            op("act", lambda e: e.activation(out=vaug[:, j, :, 0:128], in_=PS[bank][:].rearrange("p (a b) -> p a b", b=128), func=AF.Copy), r=[psk(bank)], w=[("vaug", j)])

        for j in range(16):
            do_v(j)
        wo = wload(2, w_out[:, 0:4, :])
        dbg_put("qT0", qT[:, 0, 0:256], [("qT", 0, 0)])
        dbg_put("kT0", kT[:, 0, 0:256], [("kT", 0, 0)])
        dbg_put("vaug0", vaug[:, 0, :, :], [("vaug", 0), "vaug1"])
        dbg_put("lam", lam[:, 0:4], ["lam"])
        if cfg.get("attn_stop", 9) <= 2:
            return

        AR.phase()
        hT = AR.at("hT", 0, [8, L], BF16)
        qT = AR.at("qT", 32 * KB, [4, L], BF16)
        kT = AR.at("kT", 48 * KB, [4, L], BF16)
        vaug = AR.at("vaug", 64 * KB, [16, 4, 130], BF16)
        aT = AR.at("aT", T0, [4, L], BF16)
        PT = [AR.at("PT%d" % i, T0 + 16 * KB + i * KB, [512], BF16) for i in range(4)]
        SM = T0 + 20 * KB
        rr = rr_g[:]
        tt_ = [AR.at("tt%d" % i, SM + i * 512, [128], F32) for i in range(2)]
        oo = [AR.at("oo%d" % i, SM + 1024 + i * 512, [128], F32) for i in range(2)]
        onb = [AR.at("onb%d" % i, SM + 2048 + i * 256, [128], BF16) for i in range(4)]
        pctr = [0]
        cctr = [0]
        pending = []

        def accv(comp, qs):
            bank = 2 + 2 * comp + qs // 2
            off = (qs % 2) * 130
            return bank, PS[bank][:, off:off + 129]

        its = [(h_, qb_, comp_, kt_) for h_ in range(4) for qb_ in range(4) for comp_ in range(2) for kt_ in range(16)]
        if cfg.get("attn_stop", 9) == 3:
            its = [x_ for x_ in its if x_[0] == 0 and x_[1] == 0]

        SRING = (0, 1, 6)
        accs = [WB[1][:].bitcast(F32)[:, 512 * (1 + c_):512 * (2 + c_)] for c_ in range(2)]

        qpad = [[WB[0][:, (comp_ * 2 + j_) * 512:(comp_ * 2 + j_ + 1) * 512] for j_ in range(2)] for comp_ in range(2)]
        op("pool", lambda e: e.memset(WB[0][:, 0:2048], 0.0), r=[("wb", 0)], w=[("wb", 0)] + [("qpad", c_, j_) for c_ in range(2) for j_ in range(2)])

        def emit_qpad(gi):
            h, qb = gi // 4, gi % 4
            j = gi % 2
            op("pool", lambda e: e.tensor_copy(out=qpad[0][j][0:64, :], in_=qT[0:64, h, qb * 512:(qb + 1) * 512]), r=[("qT", h, qb)], w=[("qpad", 0, j)])
            op("pool", lambda e: e.tensor_copy(out=qpad[1][j][64:128, :], in_=qT[64:128, h, qb * 512:(qb + 1) * 512]), r=[("qT", h, qb)], w=[("qpad", 1, j)])

        def emit_S(i):
            h, qb, comp, kt = its[i]
            sbank = SRING[i % 3]
            j = (h * 4 + qb) % 2
            op("pe", lambda e: e.matmul(PS[sbank][:], lhsT=kT[:, h, kt * 128:(kt + 1) * 128], rhs=qpad[comp][j], start=True, stop=True),
               r=[("kT", h, kt // 4), ("qpad", comp, j)], w=[psk(sbank)])

        def emit_exp_pv(i):
            h, qb, comp, kt = its[i]
            sbank = SRING[i % 3]
            pi = i % 4
            op("act", lambda e: e.activation(out=PT[pi], in_=PS[sbank][:], func=AF.Exp, scale=0.125), r=[psk(sbank)], w=[("PT", pi)])
            bo, bs = 2 + 2 * comp, 3 + 2 * comp
            op("pe", lambda e: e.matmul(PS[bo][:], lhsT=vaug[:, kt, h, 0:128], rhs=PT[pi], start=(kt == 0), stop=(kt == 15)),
               r=[("PT", pi), ("vaug", kt)], w=[psk(bo)])
            op("pe", lambda e: e.matmul(PS[bs][:], lhsT=ones_b[:], rhs=PT[pi], start=(kt == 0), stop=(kt == 15)),
               r=[("PT", pi), "ones_b"], w=[psk(bs)])

        def attend_tail(h, qb):
            fo = WB[3][:].bitcast(F32)
            r0, r1, t_, o_ = fo[:, 0:512], fo[:, 512:1024], fo[:, 1024:1536], fo[:, 1536:2048]
            sqb = WB[1][:, 0:512]
            K3 = [("wb", 3)]
            if dbg and h == 0 and qb == 0:
                dbg_put("acc0", PS[2][:, 0:256], [psk(2)])
                dbg_put("acc1", PS[4][:, 0:256], [psk(4)])
            if cfg.get("attn_stop", 9) <= 3:
                return
            op("dve", lambda e: e.reciprocal(out=r0, in_=PS[3][:]), r=[psk(3)], w=K3)
            op("dve", lambda e: e.reciprocal(out=r1, in_=PS[5][:]), r=[psk(5)] + K3, w=K3)
            op("dve", lambda e: e.tensor_tensor(out=t_, in0=PS[4][:], in1=r1, op=ALU.mult), r=[psk(4)] + K3, w=K3)
            op("dve", lambda e: e.tensor_tensor(out=o_, in0=PS[2][:], in1=r0, op=ALU.mult), r=[psk(2)] + K3, w=K3)
            op("dve", lambda e: e.scalar_tensor_tensor(out=o_, in0=t_, scalar=nlam, in1=o_, op0=ALU.mult, op1=ALU.add), r=K3 + ["lam"], w=K3)
            op("act", lambda e: e.activation(out=sqb, in_=o_, func=AF.Square), r=K3, w=[("wb", 1)])

            def fin():
                op("pe", lambda e: e.matmul(PS[7][:], lhsT=ones_b[:], rhs=sqb, start=True, stop=True), r=[("wb", 1), "ones_b"], w=[psk(7)])
                op("act", lambda e: e.activation(out=r0, in_=PS[7][:], func=AF.Sqrt, scale=1.0 / 128, bias=EPS), r=[psk(7)] + K3, w=K3)
                op("dve", lambda e: e.reciprocal(out=r0, in_=r0), r=K3, w=K3)
                op("dve", lambda e: e.scalar_tensor_tensor(out=aT[:, h, qb * 512:(qb + 1) * 512], in0=o_, scalar=sg[:, 0:1], in1=r0, op0=ALU.mult, op1=ALU.mult),
                   r=K3 + ["sg"], w=[("aT", h, qb)])
            pending.append(fin)

        emit_qpad(0)
        LOOK = 2
        for i in range(LOOK):
            emit_S(i)
        for i in range(len(its)):
            if its[i][2] == 0 and its[i][3] == 0:
                gi_ = its[i][0] * 4 + its[i][1]
                if gi_ + 1 < 16 and cfg.get("attn_stop", 9) != 3:
                    emit_qpad(gi_ + 1)
            if i + LOOK < len(its):
                emit_S(i + LOOK)
            emit_exp_pv(i)
            h_, qb_, comp_, kt_ = its[i]
            if kt_ == 15 and comp_ == 0 and pending:
                for f in pending:
                    f()
                del pending[:]
            if kt_ == 15 and comp_ == 1:
                attend_tail(h_, qb_)
        for f in pending:
            f()
        del pending[:]
        dbg_put("aT", aT[:, :, 0:512], [("aT", h_, 0) for h_ in range(4)])

        def oproj(tb, dc, kq):
            bank = kq % 2
            for hc in range(4):
                op("pe", lambda e, hc=hc: e.matmul(PS[bank][:], lhsT=wo[:, hc, dc * 128:(dc + 1) * 128], rhs=aT[:, hc, tb * 512:(tb + 1) * 512], start=(hc == 0), stop=(hc == 3)),
                   r=[("wb", 2), ("aT", hc, tb)], w=[psk(bank)])
            xs = xT[:, dc, tb * 512:(tb + 1) * 512]
            op("dve", lambda e: e.tensor_tensor(out=xs, in0=xs, in1=PS[bank][:], op=ALU.add), r=[psk(bank), xk(dc, tb)], w=[xk(dc, tb)])

        if not dbg:
            kq = 0
            for tb in range(4):
                for dc in range(8):
                    oproj(tb, dc, kq)
                    kq += 1


    def s5_phase():
        w_in = w_in_even_d.rearrange("(c p) f -> p c f", p=128)
        w_out = w_out_even_d.rearrange("(c p) d -> p c d", p=128)
        AR.phase()
        hT = AR.at("hT", 0, [8, L], BF16)
        if not ("l0attn" in parts or "l0mix" in parts):
            sq = [AR.at("sq%d" % i, 81 * KB + i * 4 * KB, [L], BF16) for i in range(2)]
            rstd = AR.at("rstd", 89 * KB, [L], F32)
            norm_to_hT(0, hT, rstd, sq)
        uT = AR.at("uT", 32 * KB, [4, L], BF16)
        wu = wload(0, w_in[:, :, 1536:2048])

        def uproj(tb, cc, kq):
            bank = kq % 4
            for c in range(8):
                op("pe", lambda e, c=c: e.matmul(PS[bank][:], lhsT=wu[:, c, cc * 128:(cc + 1) * 128], rhs=hT[:, c, tb * 512:(tb + 1) * 512], start=(c == 0), stop=(c == 7)),
                   r=[("wb", 0), hk(c, tb)], w=[psk(bank)])
            op("act", lambda e: e.activation(out=uT[:, cc, tb * 512:(tb + 1) * 512], in_=PS[bank][:], func=AF.Copy), r=[psk(bank)], w=[("uT", cc)])

        kq = 0
        for tb in range(4):
            for cc in range(4):
                uproj(tb, cc, kq)
                kq += 1
        wglu = wload(1, w_glu_d.rearrange("(c p) f -> p c f", p=128))
        wo2 = wload(2, w_out[:, 4:8, :])

        AR.phase()
        VX = AR.at("VX", 0, [2, 32, 256], BF16)
        uT = AR.at("uT", 32 * KB, [4, L], BF16)
        U = AR.at("U", 48 * KB, [32, 256], BF16)
        sm_off = [64 * KB]

        def sm(name, shape=(32,)):
            n = 1
            for v_ in shape:
                n *= v_
            t_ = AR.at(name, sm_off[0], list(shape), F32)
            sm_off[0] += n * 4
            return t_

        bm = sm("bm", (2, 32, 16))
        names = ["lr", "dt", "lrdt", "ang", "mg", "sn", "cs", "r_", "i_", "t0", "t1", "t2", "den", "am1", "cr", "ci", "vr", "vi"]
        sv = {n_: sm(n_) for n_ in names}
        par = sm("par", (3, 32))
        cm = sm("cm", (2, 32, 16))
        bb = sm("bb", (2, 32, 16))
        pw = sm("pw", (2, 32, 8))
        pwi = sm("pwi", (2, 32, 8))
        P1s = sm("P1s", (2, 32))
        P2s = sm("P2s", (2, 32))
        Xst = sm("Xst", (2, 32))
        S1 = sm("S1", (2, 32))
        T1 = sm("T1", (2, 32))
        T2 = sm("T2", (2, 32))
        dsk = sm("dsk", (4,))
        bgl = sm("bgl", (4,))
        SM_END = sm_off[0]
        assert SM_END <= 85 * KB, SM_END
        AB = [AR.at("AB%d" % i, 85 * KB + i * 2 * KB, [8, 128], BF16) for i in range(2)]
        ABT = [AR.at("ABT%d" % i, 89 * KB + i * 2 * KB, [8, 128], BF16) for i in range(2)]
        tA = AR.at("tA", 93 * KB, [512], F32)
        tB = AR.at("tB", 95 * KB, [512], F32)
        Zt = AR.at("Zt", 97 * KB, [8, 240], BF16)
        s5m = AR.at("s5m", 97 * KB + 3840, [2, 128], F32)
        PP = ["pp"]

        def vop(fn, eng="dve", r=(), w=()):
            op(eng, fn, r=PP + list(r), w=PP + list(w))

        def tt(o_, a_, b_, o, **kw):
            vop(lambda e: e.tensor_tensor(out=o_, in0=a_, in1=b_, op=o), **kw)

        def ts(o_, a_, s1, o1, s2=None, o2=None, **kw):
            if o2 is None:
                vop(lambda e: e.tensor_scalar(out=o_, in0=a_, scalar1=s1, scalar2=None, op0=o1), **kw)
            else:
                vop(lambda e: e.tensor_scalar(out=o_, in0=a_, scalar1=s1, scalar2=s2, op0=o1, op1=o2), **kw)

        def cmul(outr, outi, ar_, ai_, br_, bi_, x1, x2, **kw):
            tt(x1, ar_, br_, ALU.mult, **kw)
            tt(x2, ai_, bi_, ALU.mult, **kw)
            tt(outr, x1, x2, ALU.subtract, **kw)
            tt(x1, ar_, bi_, ALU.mult, **kw)
            tt(x2, ai_, br_, ALU.mult, **kw)
            tt(outi, x1, x2, ALU.add, **kw)

        op("sp", lambda e: e.dma_start(out=par, in_=s5s_d), w=PP, dma=True, dkey="s5s")
        op("sp", lambda e: e.dma_start(out=bm, in_=s5b_d), w=PP, dma=True, dkey="s5b")
        op("sp", lambda e: e.dma_start(out=cm, in_=s5c_d), w=PP, dma=True, dkey="s5c")
        op("sp", lambda e: e.dma_start(out=dsk, in_=s5d_d), w=PP, dma=True, dkey="s5d")
        op("sp", lambda e: e.dma_start(out=bgl, in_=s5bg_d), w=PP, dma=True, dkey="s5bg")
        op("sp", lambda e: e.dma_start(out=s5m, in_=s5m_d), w=["s5m"], dma=True)
        op("pool", lambda e: e.memset(Zt, 0.0), w=["Zt"])
        op("pool", lambda e: e.tensor_copy(out=Zt[:, :, 112:128], in_=ident_b[:].rearrange("p (a b) -> p a b", b=16)), r=["ident_b"], w=["Zt"])
        lam_re, lam_im, lstep = par[:, 0, :], par[:, 1, :], par[:, 2, :]
        v_ = sv
        ts(v_["lr"], lam_re, -1e-4, ALU.min)
        vop(lambda e: e.activation(out=v_["dt"], in_=lstep, func=AF.Exp), eng="act")
        tt(v_["lrdt"], v_["lr"], v_["dt"], ALU.mult)
        tt(v_["ang"], lam_im, v_["dt"], ALU.mult)
        vop(lambda e: e.activation(out=v_["mg"], in_=v_["lrdt"], func=AF.Exp, scale=1.0 / 32), eng="act")
        vop(lambda e: e.activation(out=v_["sn"], in_=v_["ang"], func=AF.Sin, scale=1.0 / 32), eng="act")
        ts(v_["t0"], v_["ang"], 1.0 / 32, ALU.mult, math.pi / 2, ALU.add)
        vop(lambda e: e.activation(out=v_["cs"], in_=v_["t0"], func=AF.Sin), eng="act")
        tt(v_["r_"], v_["mg"], v_["cs"], ALU.mult)
        tt(v_["i_"], v_["mg"], v_["sn"], ALU.mult)
        for _ in range(5):
            tt(v_["t0"], v_["r_"], v_["r_"], ALU.mult)
            tt(v_["t1"], v_["i_"], v_["i_"], ALU.mult)
            tt(v_["t2"], v_["r_"], v_["i_"], ALU.mult)
            tt(v_["r_"], v_["t0"], v_["t1"], ALU.subtract)
            ts(v_["i_"], v_["t2"], 2.0, ALU.mult)
        ar, ai = v_["r_"], v_["i_"]
        tt(v_["t0"], v_["lr"], v_["lr"], ALU.mult)
        tt(v_["t1"], lam_im, lam_im, ALU.mult)
        tt(v_["den"], v_["t0"], v_["t1"], ALU.add)
        vop(lambda e: e.reciprocal(out=v_["den"], in_=v_["den"]))
        ts(v_["am1"], ar, -1.0, ALU.add)
        tt(v_["t0"], v_["am1"], v_["lr"], ALU.mult)
        tt(v_["t1"], ai, lam_im, ALU.mult)
        tt(v_["t0"], v_["t0"], v_["t1"], ALU.add)
        tt(v_["cr"], v_["t0"], v_["den"], ALU.mult)
        tt(v_["t0"], ai, v_["lr"], ALU.mult)
        tt(v_["t1"], v_["am1"], lam_im, ALU.mult)
        tt(v_["t0"], v_["t0"], v_["t1"], ALU.subtract)
        tt(v_["ci"], v_["t0"], v_["den"], ALU.mult)
        tt(v_["t0"], ar, ar, ALU.mult)
        tt(v_["t1"], ai, ai, ALU.mult)
        tt(v_["t0"], v_["t0"], v_["t1"], ALU.add)
        vop(lambda e: e.reciprocal(out=v_["t0"], in_=v_["t0"]))
        tt(v_["vr"], ar, v_["t0"], ALU.mult)
        tt(v_["t1"], ai, v_["t0"], ALU.mult)
        ts(v_["vi"], v_["t1"], -1.0, ALU.mult)
        x1 = tA[:, 0:128].rearrange("p (a b) -> p a b", b=4)
        x2 = tB[:, 0:128].rearrange("p (a b) -> p a b", b=4)
        for (tab, br_, bi_) in ((pw, ar, ai), (pwi, v_["vr"], v_["vi"])):
            tr_, ti_ = tab[:, 0, :, :], tab[:, 1, :, :]
            vop(lambda e, tr_=tr_, br_=br_: e.tensor_copy(out=tr_[:, :, 0:1], in_=br_.unsqueeze(2)))
            vop(lambda e, ti_=ti_, bi_=bi_: e.tensor_copy(out=ti_[:, :, 0:1], in_=bi_.unsqueeze(2)))
            for n_ in (1, 2, 4):
                cmul(tr_[:, :, n_:2 * n_], ti_[:, :, n_:2 * n_], tr_[:, :, 0:n_], ti_[:, :, 0:n_],
                     tr_[:, :, n_ - 1:n_].broadcast_to([128, 32, n_]), ti_[:, :, n_ - 1:n_].broadcast_to([128, 32, n_]), x1[:, :, 0:n_], x2[:, :, 0:n_])
        cmul(bb[:, 0, :, :], bb[:, 1, :, :], v_["cr"].unsqueeze(2).broadcast_to([128, 32, 16]), v_["ci"].unsqueeze(2).broadcast_to([128, 32, 16]),
             bm[:, 0, :, :], bm[:, 1, :, :], tA.rearrange("p (a b) -> p a b", b=16), tB.rearrange("p (a b) -> p a b", b=16))
        vop(lambda e: e.tensor_copy(out=P1s[:, 0, :], in_=pw[:, 0, :, 7]))
        vop(lambda e: e.tensor_copy(out=P1s[:, 1, :], in_=pw[:, 0, :, 7]))
        ts(P2s[:, 0, :], pw[:, 1, :, 7], -1.0, ALU.mult)
        vop(lambda e: e.tensor_copy(out=P2s[:, 1, :], in_=pw[:, 1, :, 7]))

        def gen_AB(ABt, g0, gl0):
            for (lo, hi, rev) in ((0, 64, False), (64, 128, True)):
                sl = slice(None, None, -1) if rev else slice(None)
                pr = pwi[lo:hi, 0, g0:g0 + 4, sl].unsqueeze(3).broadcast_to([64, 4, 8, 16])
                pi = pwi[lo:hi, 1, g0:g0 + 4, sl].unsqueeze(3).broadcast_to([64, 4, 8, 16])
                br_ = bb[lo:hi, 0, g0:g0 + 4, :].unsqueeze(2).broadcast_to([64, 4, 8, 16])
                bi_ = bb[lo:hi, 1, g0:g0 + 4, :].unsqueeze(2).broadcast_to([64, 4, 8, 16])
                o_r = ABt[0][lo:hi, gl0:gl0 + 4, :].rearrange("p g (s m) -> p g s m", m=16)
                o_i = ABt[1][lo:hi, gl0:gl0 + 4, :].rearrange("p g (s m) -> p g s m", m=16)
                ta = tA[lo:hi, :].rearrange("p (g s m) -> p g s m", s=8, m=16)
                tb_ = tB[lo:hi, :].rearrange("p (g s m) -> p g s m", s=8, m=16)
                cmul(o_r, o_i, pr, pi, br_, bi_, ta, tb_, w=["AB"])

        def gen_CA(g0, gl0, CAf, CAb):
            for (lo, hi, rev, dst) in ((0, 64, False, CAf), (64, 128, True, CAb)):
                sl = slice(None, None, -1) if rev else slice(None)
                pr = pw[lo:hi, 0, g0:g0 + 4, sl].unsqueeze(3).broadcast_to([64, 4, 8, 16])
                pi = pw[lo:hi, 1, g0:g0 + 4, sl].unsqueeze(3).broadcast_to([64, 4, 8, 16])
                c_r = cm[lo:hi, 0, g0:g0 + 4, :].unsqueeze(2).broadcast_to([64, 4, 8, 16])
                c_i = cm[lo:hi, 1, g0:g0 + 4, :].unsqueeze(2).broadcast_to([64, 4, 8, 16])
                o_r = dst[0][lo:hi, gl0:gl0 + 4, :].rearrange("p g (s m) -> p g s m", m=16)
                o_i = dst[1][lo:hi, gl0:gl0 + 4, :].rearrange("p g (s m) -> p g s m", m=16)
                ta = tA[lo:hi, :].rearrange("p (g s m) -> p g s m", s=8, m=16)
                tb_ = tB[lo:hi, :].rearrange("p (g s m) -> p g s m", s=8, m=16)
                kw = dict(w=["CA"])
                tt(ta, c_r, pr, ALU.mult, **kw)
                tt(tb_, c_i, pi, ALU.mult, **kw)
                tt(o_r, ta, tb_, ALU.subtract, **kw)
                tt(ta, c_r, pi, ALU.mult, **kw)
                tt(tb_, c_i, pr, ALU.mult, **kw)
                vop(lambda e, o_i=o_i, ta=ta, tb_=tb_: e.scalar_tensor_tensor(out=o_i, in0=ta, scalar=-1.0, in1=tb_, op0=ALU.mult, op1=ALU.subtract), **kw)

        def shuffle_group(cc, gl):
            g = 8 * cc + gl
            bank = g % 2
            for s_ in range(8):
                op("pe", lambda e, s_=s_: e.matmul(PS[bank][:, 0:256], lhsT=Zt[:, gl, (7 - s_) * 16:(7 - s_) * 16 + 128],
                                                   rhs=uT[:, cc, :].rearrange("p (b s) -> p b s", s=8)[:, :, s_], start=(s_ == 0), stop=(s_ == 7)),
                   r=["Zt", ("uT", cc)], w=[psk(bank)])
            op("act", lambda e: e.activation(out=U[:, g, :], in_=PS[bank][:, 0:256], func=AF.Copy), r=[psk(bank)], w=[("U", g)])

        def abt_chunk():
            for ri in range(2):
                bank = 2 + ri
                pb = PS[bank][:].bitcast(BF16)
                for gl in range(8):
                    op("pe", lambda e, gl=gl, pb=pb, ri=ri: e.transpose(out=pb[:, gl * 128:(gl + 1) * 128], in_=AB[ri][:, gl, :], identity=ident_b[:]), r=["AB", "pp", "ident_b"], w=[psk(bank)])
                op("dve", lambda e, pb=pb, ri=ri: e.tensor_copy(out=ABT[ri], in_=pb.rearrange("p (a b) -> p a b", b=128)), r=[psk(bank)], w=["ABT"])

        def vprime_group(cc, gl):
            g = 8 * cc + gl
            for ri in range(2):
                bank = 4 + ri
                op("pe", lambda e, ri=ri, bank=bank: e.matmul(PS[bank][:, 0:256], lhsT=ABT[ri][:, gl, :], rhs=U[:, g, :], start=True, stop=True), r=["ABT", ("U", g)], w=[psk(bank)])
                op("act", lambda e, ri=ri, bank=bank: e.activation(out=VX[:, ri, g, :], in_=PS[bank][:, 0:256], func=AF.Copy), r=[psk(bank)], w=[("VX", g)])

        for cc in range(4):
            for gl in range(8):
                shuffle_group(cc, gl)
            gen_AB(AB, 8 * cc, 0)
            gen_AB(AB, 8 * cc + 4, 4)
            abt_chunk()
            for gl in range(8):
                vprime_group(cc, gl)
        dbg_put("U0", U[:, 0, 0:64], [("U", 0)])
        dbg_put("VX0", VX[:, :, 0, 0:64], [("VX", 0)])
        dbg_put("pw", pw[:, :, 0, :], PP)
        dbg_put("pwi", pwi[:, :, 0, :], PP)
        dbg_put("bb", bb[:, :, 0, :], PP)

        VXK = [("VX", g) for g in range(32)]
        op("dve", lambda e: e.memset(Xst, 0.0), w=["sc0x", "sc64x"])

        def scan_step(lo, hi, b):
            key = "sc%d" % lo
            xs, s1, t1_, t2_ = Xst[lo:hi], S1[lo:hi], T1[lo:hi], T2[lo:hi]
            vx = VX[lo:hi, :, :, b]
            s1sw = bass.AP(s1.tensor, s1.offset + 32, [list(s1.ap[0]), [-32, 2], [1, 32]])
            op("dve", lambda e: e.tensor_tensor(out=s1, in0=xs, in1=vx, op=ALU.add), r=VXK + [key + "x"], w=[key + "s"])
            op("dve", lambda e: e.tensor_tensor(out=t1_, in0=P1s[lo:hi], in1=s1, op=ALU.mult), r=[key + "s", "pp"], w=[key + "a"])
            op("dve", lambda e: e.tensor_tensor(out=t2_, in0=P2s[lo:hi], in1=s1sw, op=ALU.mult), r=[key + "s", "pp"], w=[key + "b"])
            op("dve", lambda e: e.tensor_tensor(out=xs, in0=t1_, in1=t2_, op=ALU.add), r=[key + "a", key + "b"], w=[key + "x"])
            op("pool", lambda e: e.tensor_copy(out=vx, in_=xs), r=[key + "x"], w=[key + "c"])

        for b in range(256):
            scan_step(0, 64, b)
            scan_step(64, 128, 255 - b)
        dbg_put("X0", VX[:, :, 0, 0:64], ["sc0c", "sc64c"])

        AR.phase()
        AR.at("keep", 0, [85 * KB // 2], BF16)
        AB2 = [AR.at("AB2_%d" % i, 85 * KB + i * 2 * KB, [8, 128], BF16) for i in range(2)]
        CAf = [AR.at("CAf%d" % i, 89 * KB + i * 2 * KB, [8, 128], BF16) for i in range(2)]
        AR.at("keep2", 93 * KB, [(104 - 93) * KB // 2], BF16)
        CAb = [bm.rearrange("p a b c -> p (a b c)")[:, i * 512:(i + 1) * 512].bitcast(BF16).rearrange("p (a b) -> p a b", b=128) for i in range(2)]
        W0 = sv["lr"].tensor and AR.base[:, (64 * KB + 4096) // 2:(64 * KB + 4096 + 2048) // 2].rearrange("p (a b) -> p a b", b=128)
        for i in range(2):
            op("pool", lambda e, i=i: e.memset(CAf[i][64:128], 0.0), w=["CA"])
            op("pool", lambda e, i=i: e.memset(CAb[i][0:64], 0.0), w=["CA"])
        mF_ = s5m[:, 0, :]
        mB_ = s5m[:, 1, :]

        def w0_chunk():
            for half in range(2):
                pf, pb_ = PS[2], PS[3]
                for jj in range(4):
                    gl = half * 4 + jj
                    o1 = pf[:, jj * 128:(jj + 1) * 128]
                    o2 = pb_[:, jj * 128:(jj + 1) * 128]
                    op("pe", lambda e, gl=gl, o1=o1: e.matmul(o1, lhsT=AB2[0][:, gl, :], rhs=CAf[0][:, gl, :], start=True, stop=False), r=["AB", "CA", "pp"], w=[psk(2)])
                    op("pe", lambda e, gl=gl, o1=o1: e.matmul(o1, lhsT=AB2[1][:, gl, :], rhs=CAf[1][:, gl, :], start=False, stop=True), r=["AB", "CA", "pp"], w=[psk(2)])
                    op("pe", lambda e, gl=gl, o2=o2: e.matmul(o2, lhsT=AB2[0][:, gl, :], rhs=CAb[0][:, gl, :], start=True, stop=False), r=["AB", "CA", "pp"], w=[psk(3)])
                    op("pe", lambda e, gl=gl, o2=o2: e.matmul(o2, lhsT=AB2[1][:, gl, :], rhs=CAb[1][:, gl, :], start=False, stop=True), r=["AB", "CA", "pp"], w=[psk(3)])
                t3 = tA.rearrange("p (a b) -> p a b", b=128)
                op("dve", lambda e, t3=t3: e.tensor_tensor(out=t3, in0=PS[2][:].rearrange("p (a b) -> p a b", b=128), in1=mF_.unsqueeze(1).broadcast_to([128, 4, 128]), op=ALU.mult),
                   r=[psk(2), "s5m", "pp"], w=["pp"])
                op("dve", lambda e: e.tensor_tensor(out=tB.rearrange("p (a b) -> p a b", b=128), in0=PS[3][:].rearrange("p (a b) -> p a b", b=128), in1=mB_.unsqueeze(1).broadcast_to([128, 4, 128]), op=ALU.mult),
                   r=[psk(3), "s5m", "pp"], w=["pp"])
                op("dve", lambda e, half=half: e.tensor_tensor(out=W0[:, half * 4:(half + 1) * 4, :], in0=tA.rearrange("p (a b) -> p a b", b=128), in1=tB.rearrange("p (a b) -> p a b", b=128), op=ALU.add),
                   r=["pp"], w=["W0", "pp"])

        SCK = ["sc0c", "sc64c"]

        def y_group(cc, gl):
            g = 8 * cc + gl
            bank = 4 + g % 2
            o_ = PS[bank]
            rk = ["W0", "CA", ("U", g), ("VX", g)] + SCK
            op("pe", lambda e: e.matmul(o_[:, 0:256], lhsT=W0[:, gl, :], rhs=U[:, g, :], start=True, stop=False), r=rk, w=[psk(bank)])
            op("pe", lambda e: e.matmul(o_[:, 1:256], lhsT=CAf[0][:, gl, :], rhs=VX[:, 0, g, 0:255], start=False, stop=False), r=rk, w=[psk(bank)])
            op("pe", lambda e: e.matmul(o_[:, 1:256], lhsT=CAf[1][:, gl, :], rhs=VX[:, 1, g, 0:255], start=False, stop=False), r=rk, w=[psk(bank)])
            op("pe", lambda e: e.matmul(o_[:, 0:255], lhsT=CAb[0][:, gl, :], rhs=VX[:, 0, g, 1:256], start=False, stop=False), r=rk, w=[psk(bank)])
            op("pe", lambda e: e.matmul(o_[:, 0:255], lhsT=CAb[1][:, gl, :], rhs=VX[:, 1, g, 1:256], start=False, stop=True), r=rk, w=[psk(bank)])
            op("act", lambda e: e.activation(out=U[:, g, :], in_=o_[:, 0:256], func=AF.Copy), r=[psk(bank)], w=[("U", g)])

        def unshuffle(cc, tl):
            bank = tl % 2
            for gl in range(8):
                g = 8 * cc + gl
                op("pe", lambda e, gl=gl, g=g: e.matmul(PS[bank][:, 0:256], lhsT=Zt[:, tl, (7 - gl) * 16:(7 - gl) * 16 + 128], rhs=U[:, g, :], start=(gl == 0), stop=(gl == 7)),
                   r=["Zt", ("U", g)], w=[psk(bank)])
            uv = uT[:, cc, :].rearrange("p (b s) -> p b s", s=8)[:, :, tl]
            op("dve", lambda e: e.scalar_tensor_tensor(out=uv, in0=uv, scalar=dsk[:, cc:cc + 1], in1=PS[bank][:, 0:256], op0=ALU.mult, op1=ALU.add),
               r=[psk(bank), ("uT", cc), "pp"], w=[("uT", cc)])

        for cc in range(4):
            for hh in range(2):
                gen_AB(AB2, 8 * cc + 4 * hh, 4 * hh)
                gen_CA(8 * cc + 4 * hh, 4 * hh, CAf, CAb)
            w0_chunk()
            for gl in range(8):
                y_group(cc, gl)
            for tl in range(8):
                unshuffle(cc, tl)
        dbg_put("y", uT[:, :, 0:512], [("uT", cc) for cc in range(4)])

        AR.phase()
        AR.at("uT", 32 * KB, [4, L], BF16)
        bT = AR.at("bT", 0, [4, L], BF16)
        g1 = AR.at("g1", 16 * KB, [L], F32)
        g2 = AR.at("g2", 24 * KB, [L], F32)
        CG = math.sqrt(2.0 / math.pi)

        def gelu_chunk(cc):
            y_ = uT[:, cc, :]
            op("act", lambda e: e.activation(out=g1, in_=y_, func=AF.Square), r=[("uT", cc)], w=["g1"])
            op("dve", lambda e: e.tensor_scalar(out=g1, in0=g1, scalar1=0.044715, scalar2=1.0, op0=ALU.mult, op1=ALU.add), r=["g1"], w=["g1"])
            op("dve", lambda e: e.tensor_tensor(out=g1, in0=g1, in1=y_, op=ALU.mult), r=["g1", ("uT", cc)], w=["g1"])
            op("act", lambda e: e.activation(out=g2, in_=g1, func=AF.Sigmoid, scale=2.0 * CG), r=["g1"], w=["g2"])
            op("pool", lambda e: e.tensor_tensor(out=y_, in0=y_, in1=g2, op=ALU.mult), r=["g2", ("uT", cc)], w=[("uT", cc)])

        for cc in range(4):
            gelu_chunk(cc)

        def glu(tb, oc, kq):
            bank = kq % 2
            for kc in range(4):
                op("pe", lambda e, kc=kc: e.matmul(PS[bank][:], lhsT=wglu[:, kc, oc * 128:(oc + 1) * 128], rhs=uT[:, kc, tb * 512:(tb + 1) * 512], start=(kc == 0), stop=(kc == 3)),
                   r=[("wb", 1), ("uT", kc)], w=[psk(bank)])
            gsl = g1[:, (kq % 4) * 512:(kq % 4 + 1) * 512]
            op("act", lambda e: e.activation(out=gsl, in_=PS[bank][:], func=AF.Sigmoid, bias=bgl[:, oc:oc + 1]), r=[psk(bank), "pp"], w=[("gs", kq % 4)])
            op("dve", lambda e: e.tensor_tensor(out=bT[:, oc, tb * 512:(tb + 1) * 512], in0=uT[:, oc, tb * 512:(tb + 1) * 512], in1=gsl, op=ALU.mult),
               r=[("gs", kq % 4), ("uT", oc)], w=[("bT", oc, tb)])

        P.barrier()
        kq = 0
        for tb in range(4):
            for oc in range(4):
                glu(tb, oc, kq)
                kq += 1
        dbg_put("bT", bT[:, :, 0:512], [("bT", oc, 0) for oc in range(4)])

        def oproj2(tb, dc, kq):
            bank = 2 + kq % 2
            for hc in range(4):
                op("pe", lambda e, hc=hc: e.matmul(PS[bank][:], lhsT=wo2[:, hc, dc * 128:(dc + 1) * 128], rhs=bT[:, hc, tb * 512:(tb + 1) * 512], start=(hc == 0), stop=(hc == 3)),
                   r=[("wb", 2), ("bT", hc, tb)], w=[psk(bank)])
            xs = xT[:, dc, tb * 512:(tb + 1) * 512]
            op("dve", lambda e: e.tensor_tensor(out=xs, in0=xs, in1=PS[bank][:], op=ALU.add), r=[psk(bank), xk(dc, tb)], w=[xk(dc, tb)])

        if not dbg:
            kq = 0
            for tb in range(4):
                for dc in range(8):
                    oproj2(tb, dc, kq)
                    kq += 1

    def hgrn_phase(gi):
        w_in = w_in_odd_d.rearrange("(c p) f -> p c f", p=128)
        w_out = w_out_odd_d.rearrange("(h p) d -> p h d", p=128)
        AR.phase()
        hT = AR.at("hT", 0, [8, L], BF16)
        sq = [AR.at("sq%d" % i, 44 * KB + i * 4 * KB, [L], BF16) for i in range(2)]
        rstd = AR.at("rstd", 52 * KB, [L], F32)
        norm_to_hT(gi, hT, rstd, sq)
        AR.phase()
        hT = AR.at("hT", 0, [8, L], BF16)
        qT = AR.at("qT", 32 * KB, [L], BF16)
        sigG = AR.at("sigG", 36 * KB, [L], BF16)
        vtok = AR.at("vtok", 40 * KB, [16, 128], BF16)
        A = AR.at("A", 44 * KB, [L], F32)
        B = AR.at("B", 52 * KB, [L], F32)
        kk = AR.at("kk", 60 * KB, [L], BF16)
        qdec = AR.at("qdec", 64 * KB, [L], BF16)
        kinv = AR.at("kinv", 68 * KB, [L], BF16)
        kend = AR.at("kend", 72 * KB, [16, 128], BF16)
        scT = AR.at("scT", 76 * KB, [16, 128], BF16)
        Sb = AR.at("Sb", 80 * KB, [32, 128], BF16)
        oacc = AR.at("oacc", 88 * KB, [16, 128], F32)
        mF = AR.at("mF", 96 * KB, [L], BF16)
        dec = AR.at("dec", 100 * KB, [32], F32)
        ssq = AR.at("ssq", 100 * KB + 128, [16], F32)
        Sst = [AR.at("Sst%d" % i, 100 * KB + 256 + i * 512, [128], F32) for i in range(2)]
        on_tok = AR.base[:, 44 * KB // 2: 48 * KB // 2].rearrange("p (a b) -> p a b", b=128)
        mT = AR.base[:, 52 * KB // 2: 56 * KB // 2]

        op("sp", lambda e: e.dma_start(out=lbl[:], in_=lbl_d), w=["lbl"], dma=True)
        op("sp", lambda e: e.dma_start(out=hg[:], in_=hg_d), w=["hg"], dma=True)
        op("sp", lambda e: e.dma_start(out=hmask[:], in_=hmask_d), w=["hmask"], dma=True)
        op("dve", lambda e: e.tensor_tensor(out=lb[:], in0=lbl[:, 1, :], in1=lbl[:, 0, :], op=ALU.subtract), r=["lbl"], w=["lb"])
        op("act", lambda e: e.activation(out=lb[:], in_=lb[:], func=AF.Sigmoid), r=["lb"], w=["lb"])
        op("dve", lambda e: e.tensor_scalar(out=oml[:], in0=lb[:], scalar1=-1.0, scalar2=1.0, op0=ALU.mult, op1=ALU.add), r=["lb"], w=["oml"])
        op("dve", lambda e: e.tensor_scalar(out=noml[:], in0=oml[:], scalar1=-1.0, scalar2=None, op0=ALU.mult), r=["oml"], w=["noml"])
        op("pool", lambda e: e.memset(mF, 1.0), w=["mF"])
        op("pool", lambda e: e.memset(mF.rearrange("p (a b) -> p a b", b=64)[:, :, 0:1], 0.0), w=["mF"])

        def load_head(h):
            sa, sb_ = (0, 1) if h % 2 == 0 else (2, 3)
            secs = []
            for i, sec in enumerate((0, 1, 2, 3)):
                dst = WB[sa][:, i * 1024:(i + 1) * 1024].rearrange("p (a b) -> p a b", b=128)
                op("pool", lambda e, dst=dst, sec=sec, h=h: e.dma_start(out=dst, in_=w_in[:, :, sec * 1024 + h * 128: sec * 1024 + (h + 1) * 128]),
                   w=[("wb", sa)], dma=True, dkey=("wbs", sa, i))
                secs.append(dst)
            dst = WB[sb_][:, 0:1024].rearrange("p (a b) -> p a b", b=128)
            op("pool", lambda e, dst=dst, h=h: e.dma_start(out=dst, in_=w_in[:, :, 4 * 1024 + h * 128: 4 * 1024 + (h + 1) * 128]),
               w=[("wb", sb_)], dma=True, dkey=("wbs", sb_, 0))
            secs.append(dst)
            wo = WB[sb_][:, 1024:2048]
            op("pool", lambda e, wo=wo, h=h: e.dma_start(out=wo, in_=w_out[:, h, :]), w=[("wb", sb_)], dma=True, dkey=("wbs", sb_, 1))
            return secs, wo, sa, sb_

        def proj_fm(wsec, slot, consume):
            for tb in range(4):
                bank = tb
                for c in range(8):
                    op("pe", lambda e, bank=bank, c=c, tb=tb: e.matmul(PS[bank][:], lhsT=wsec[:, c, :], rhs=hT[:, c, tb * 512:(tb + 1) * 512], start=(c == 0), stop=(c == 7)),
                       r=[("wb", slot), hk(c, tb)], w=[psk(bank)])
                consume(tb, bank)

        def do_head(h, cur):
            (wq, wi_, wff, wfb, wg), wo, sa, sb_ = cur
            proj_fm(wq, sa, lambda tb, bank: op("act", lambda e: e.activation(out=qT[:, tb * 512:(tb + 1) * 512], in_=PS[bank][:], func=AF.Copy), r=[psk(bank)], w=["qT"]))
            proj_fm(wg, sb_, lambda tb, bank: op("act", lambda e: e.activation(out=sigG[:, tb * 512:(tb + 1) * 512], in_=PS[bank][:], func=AF.Sigmoid), r=[psk(bank)], w=["sigG"]))
            for q4 in range(4):
                bank = 4 + q4 % 2
                for jj in range(4):
                    j = q4 * 4 + jj
                    for c in range(8):
                        op("pe", lambda e, bank=bank, jj=jj, j=j, c=c: e.matmul(PS[bank][:, jj * 128:(jj + 1) * 128], lhsT=hT[:, c, j * 128:(j + 1) * 128], rhs=wi_[:, c, :], start=(c == 0), stop=(c == 7)),
                           r=[("wb", sa), hk(c, j // 4)], w=[psk(bank)])
                op("dve", lambda e, bank=bank, q4=q4: e.tensor_copy(out=vtok[:, q4 * 4:(q4 + 1) * 4, :], in_=PS[bank][:].rearrange("p (a b) -> p a b", b=128)), r=[psk(bank)], w=["vtok"])
            if h == 0:
                dbg_put("qT", qT[:, 0:256], ["qT"])
                dbg_put("sigG", sigG[:, 0:256], ["sigG"])
                dbg_put("vtok", vtok[:, 0:2, :], ["vtok"])
            def do_dir(d):
                wf = wff if d == 0 else wfb
                proj_fm(wf, sa, lambda tb, bank: op("act", lambda e: e.activation(out=A[:, tb * 512:(tb + 1) * 512], in_=PS[bank][:], func=AF.Sigmoid), r=[psk(bank)], w=["hgA"]))
                op("dve", lambda e: e.tensor_scalar(out=kk, in0=A, scalar1=noml[:, h:h + 1], scalar2=oml[:, h:h + 1], op0=ALU.mult, op1=ALU.add), r=["hgA", "noml", "oml"], w=["kk"])
                op("dve", lambda e: e.tensor_scalar(out=A, in0=A, scalar1=oml[:, h:h + 1], scalar2=lb[:, h:h + 1], op0=ALU.mult, op1=ALU.add), r=["hgA", "lb", "oml"], w=["hgA"])
                if h == 0 and d == 0:
                    dbg_put("k", kk[:, 0:256], ["kk"])
                    dbg_put("f", A[:, 0:256], ["hgA"])
                op("act", lambda e: e.activation(out=A, in_=A, func=AF.Ln), r=["hgA"], w=["hgA"])
                if d == 0:
                    op("dve", lambda e: e.tensor_tensor_scan(out=B, data0=mF, data1=A, initial=0.0, op0=ALU.mult, op1=ALU.add), r=["hgA", "mF"], w=["hgB"])
                else:
                    op("dve", lambda e: e.tensor_tensor_scan(out=B[:, ::-1], data0=mF, data1=A[:, ::-1], initial=0.0, op0=ALU.mult, op1=ALU.add), r=["hgA", "mF"], w=["hgB"])
                if h == 0 and d == 0:
                    dbg_put("b", B[:, 0:256], ["hgB"])
                op("act", lambda e: e.activation(out=A, in_=B, func=AF.Exp), r=["hgB"], w=["hgA"])
                op("pool", lambda e: e.tensor_tensor(out=qdec, in0=qT, in1=A, op=ALU.mult), r=["hgA", "qT"], w=["qdec"])
                op("act", lambda e: e.activation(out=A, in_=B, func=AF.Exp, scale=-1.0), r=["hgB", "qdec"], w=["hgA"])
                op("pool", lambda e: e.tensor_tensor(out=A, in0=kk, in1=A, op=ALU.mult), r=["hgA", "kk"], w=["hgA"])
                op("act", lambda e: e.activation(out=kinv, in_=A, func=AF.Copy), r=["hgA"], w=["kinv"])
                bl = B.rearrange("p (a b) -> p a b", b=64)[:, :, 63:64] if d == 0 else B.rearrange("p (a b) -> p a b", b=64)[:, :, 0:1]
                op("act", lambda e, bl=bl: e.activation(out=dec.rearrange("p (a b) -> p a b", b=1), in_=bl, func=AF.Exp), r=["hgB"], w=["dec"])
                dec_bc = bass.AP(dec.tensor, dec.offset, [list(dec.ap[0]), [1, 32], [0, 64]])
                op("pool", lambda e, dec_bc=dec_bc: e.tensor_tensor(out=kk.rearrange("p (a b) -> p a b", b=64), in0=A.rearrange("p (a b) -> p a b", b=64), in1=dec_bc, op=ALU.mult),
                   r=["hgA", "dec"], w=["kk"])
                if h == 0 and d == 0:
                    dbg_put("qdec", qdec[:, 0:256], ["qdec"])
                    dbg_put("kinv", kinv[:, 0:256], ["kinv"])
                    dbg_put("dec", dec, ["dec"])
                    dbg_put("kendT", kk[:, 0:256], ["kk"])
                for half in range(2):
                    bank = 4 + half
                    pb = PS[bank][:].bitcast(BF16)
                    for jj in range(8):
                        j = half * 8 + jj
                        op("pe", lambda e, pb=pb, jj=jj, j=j: e.transpose(out=pb[:, jj * 128:(jj + 1) * 128], in_=kk[:, j * 128:(j + 1) * 128], identity=ident_b[:]),
                           r=["kk", "ident_b"], w=[psk(bank)])
                    op("act", lambda e, pb=pb, half=half: e.activation(out=kend[:, half * 8:(half + 1) * 8, :], in_=pb.rearrange("p (a b) -> p a b", b=128), func=AF.Copy),
                       r=[psk(bank)], w=["kend"])
                for q4 in range(4):
                    bank = q4 % 4
                    for jj in range(4):
                        j = q4 * 4 + jj
                        op("pe", lambda e, bank=bank, jj=jj, j=j: e.matmul(PS[bank][:, jj * 128:(jj + 1) * 128], lhsT=kinv[:, j * 128:(j + 1) * 128], rhs=qdec[:, j * 128:(j + 1) * 128], start=True, stop=True),
                           r=["kinv", "qdec"], w=[psk(bank)])
                    hm_ = hmask[:, d, :]
                    mk = bass.AP(hm_.tensor, hm_.offset, [list(hm_.ap[0]), [0, 4], [1, 128]])
                    op("dve", lambda e, bank=bank, q4=q4, mk=mk: e.tensor_tensor(out=scT[:, q4 * 4:(q4 + 1) * 4, :], in0=PS[bank][:].rearrange("p (a b) -> p a b", b=128), in1=mk, op=ALU.mult),
                       r=[psk(bank), "hmask"], w=["scT"])
                op("pool", lambda e: e.memset(Sst[0], 0.0), w=[("Sst", 0)])
                order = list(range(32)) if d == 0 else list(range(31, -1, -1))
                for n, ci in enumerate(order):
                    cur, new = Sst[n % 2], Sst[(n + 1) % 2]
                    par = ci % 2
                    bank = 3 if par == 0 else 7
                    op("act", lambda e, cur=cur, ci=ci: e.activation(out=Sb[:, ci, :], in_=cur, func=AF.Copy), r=[("Sst", n % 2)], w=[("Sb", ci)])
                    op("pe", lambda e, bank=bank, ci=ci, par=par: e.matmul(PS[bank][:, 0:128], lhsT=kend[par * 64:(par + 1) * 64, ci // 2, :], rhs=vtok[par * 64:(par + 1) * 64, ci // 2, :], start=True, stop=True),
                       r=["kend", "vtok"], w=[psk(bank)])
                    op("dve", lambda e, bank=bank, cur=cur, new=new, ci=ci: e.scalar_tensor_tensor(out=new, in0=cur, scalar=dec[:, ci:ci + 1], in1=PS[bank][:, 0:128], op0=ALU.mult, op1=ALU.add),
                       r=[psk(bank), ("Sst", n % 2), "dec"], w=[("Sst", (n + 1) % 2)])
                if h == 0 and d == 0:
                    dbg_put("kend", kend[:, 0:2, :], ["kend"])
                    dbg_put("scT", scT[:, 0:2, :], ["scT"])
                    dbg_put("Sb", Sb[:, 0:4, :], [("Sb", i_) for i_ in range(4)])
                for q4 in range(4):
                    bank = 4 + q4 % 2
                    for jj in range(4):
                        j = q4 * 4 + jj
                        o_ = PS[bank][:, jj * 128:(jj + 1) * 128]
                        op("pe", lambda e, o_=o_, j=j: e.matmul(o_, lhsT=scT[:, j, :], rhs=vtok[:, j, :], start=True, stop=False), r=["scT", "vtok"], w=[psk(bank)])
                        op("pe", lambda e, bank=bank, jj=jj, j=j: e.matmul(PS[bank][0:64, jj * 128:(jj + 1) * 128], lhsT=qdec[:, j * 128:j * 128 + 64], rhs=Sb[:, 2 * j, :], start=False, stop=False),
                           r=["qdec", ("Sb", 2 * j)], w=[psk(bank)])
                        op("pe", lambda e, bank=bank, jj=jj, j=j: e.matmul(PS[bank][64:128, jj * 128:(jj + 1) * 128], lhsT=qdec[:, j * 128 + 64:j * 128 + 128], rhs=Sb[:, 2 * j + 1, :], start=False, stop=True),
                           r=["qdec", ("Sb", 2 * j + 1)], w=[psk(bank)])
                    ov = oacc[:, q4 * 4:(q4 + 1) * 4, :]
                    pv = PS[bank][:].rearrange("p (a b) -> p a b", b=128)
                    if d == 0:
                        op("dve", lambda e, ov=ov, pv=pv: e.tensor_copy(out=ov, in_=pv), r=[psk(bank)], w=["oacc"])
                    else:
                        op("dve", lambda e, ov=ov, pv=pv: e.tensor_tensor(out=ov, in0=ov, in1=pv, op=ALU.add), r=[psk(bank), "oacc"], w=["oacc"])
                if h == 0:
                    dbg_put("oacc%d" % d, oacc[:, 0:2, :], ["oacc"])

            do_dir(0)
            do_dir(1)
            for j in range(16):
                op("act", lambda e, j=j: e.activation(out=on_tok[:, j, :], in_=oacc[:, j, :], func=AF.Square, accum_out=ssq[:, j:j + 1]), r=["oacc", "hgB"], w=["hgA", "ssq"])
            op("act", lambda e: e.activation(out=ssq, in_=ssq, func=AF.Sqrt, scale=1.0 / 128, bias=EPS), r=["ssq"], w=["ssq"])
            op("dve", lambda e: e.reciprocal(out=ssq, in_=ssq), r=["ssq"], w=["ssq"])
            for j in range(16):
                op("dve", lambda e, j=j: e.tensor_scalar(out=on_tok[:, j, :], in0=oacc[:, j, :], scalar1=ssq[:, j:j + 1], scalar2=None, op0=ALU.mult), r=["oacc", "ssq", "hgA"], w=["hgA"])
            for half in range(2):
                bank = half
                pb = PS[bank][:].bitcast(BF16)
                for jj in range(8):
                    j = half * 8 + jj
                    op("pe", lambda e, pb=pb, jj=jj, j=j: e.transpose(out=pb[:, jj * 128:(jj + 1) * 128], in_=on_tok[:, j, :], identity=ident_b[:]), r=["hgA", "ident_b"], w=[psk(bank)])
                op("dve", lambda e, pb=pb, half=half: e.scalar_tensor_tensor(out=mT[:, half * 1024:(half + 1) * 1024], in0=pb, scalar=hg[:, h:h + 1], in1=sigG[:, half * 1024:(half + 1) * 1024], op0=ALU.mult, op1=ALU.mult),
                   r=[psk(bank), "hg", "sigG", "hgA"], w=["hgB"])
            if h == 0:
                dbg_put("mT", mT[:, 0:256], ["hgB"])
            if dbg:
                return
            kq = 0
            for tb in range(4):
                for dc in range(8):
                    bank = 2 + kq % 2
                    op("pe", lambda e, bank=bank, dc=dc, tb=tb: e.matmul(PS[bank][:], lhsT=wo[:, dc * 128:(dc + 1) * 128], rhs=mT[:, tb * 512:(tb + 1) * 512], start=True, stop=True),
                       r=[("wb", sb_), "hgB"], w=[psk(bank)])
                    xs = xT[:, dc, tb * 512:(tb + 1) * 512]
                    op("dve", lambda e, xs=xs, bank=bank: e.tensor_tensor(out=xs, in0=xs, in1=PS[bank][:], op=ALU.add), r=[psk(bank), xk(dc, tb)], w=[xk(dc, tb)])
                    kq += 1

        nxt = load_head(0)
        for h in range(8):
            cur = nxt
            if h + 1 < 8:
                nxt = load_head(h + 1)
            do_head(h, cur)
            if dbg:
                break

    if "l0attn" in parts or "l0mix" in parts:
        attn_phase(0)
    if "l0s5" in parts or "l0mix" in parts:
        s5_phase()
    if "l0ffn" in parts:
        ffn_phase(0, 1)
    if "l1mix" in parts:
        hgrn_phase(2)
    if "l1ffn" in parts:
        ffn_phase(1, 3)

    if dbg:
        P.barrier()
        op("sp", lambda e: e.dma_start(out=out_d.rearrange("(p a) d -> p (a d)", p=128), in_=xT_flat), dma=True, dkey="dbgout")
        P.emit()
        return nc
    AR.phase()
    sq = [AR.at("sq%d" % i, i * 4 * KB, [L], BF16) for i in range(2)]
    rstd = AR.at("rstd", 8 * KB, [L], F32)
    stage = [AR.at("stage%d" % i, 16 * KB + i * 4 * KB, [D], F32) for i in range(2)]
    ftmp = [AR.at("ftmp%d" % i, 24 * KB + i * 512, [128], F32) for i in range(4)]
    do_final = "final" in parts
    if do_final:
        rmsnorm_rstd(rstd, sq)
    k = 0
    for t in range(16):
        st = stage[t % 2]
        for half in range(2):
            bank = (2 * t + half) % 4
            for j in range(4):
                c = half * 4 + j
                src = xT[:, c, t * 128:(t + 1) * 128]
                if do_final:
                    ft = ftmp[k % 4]
                    op("dve", lambda e, ft=ft, src=src, c=c, t=t: e.scalar_tensor_tensor(
                        out=ft, in0=src, scalar=gains[:, 4, c:c + 1], in1=rstd[:, t * 128:(t + 1) * 128], op0=ALU.mult, op1=ALU.mult),
                       r=[xk(c, t // 4), ("rstd", t // 4), "gains"], w=[("ftmp", k % 4)])
                    op("pe", lambda e, bank=bank, j=j, ft=ft: e.transpose(out=PS[bank][:, j * 128:(j + 1) * 128], in_=ft, identity=ident_f[:]),
                       r=[("ftmp", k % 4), "ident_f"], w=[psk(bank)])
                    k += 1
                else:
                    op("pe", lambda e, bank=bank, j=j, src=src: e.transpose(out=PS[bank][:, j * 128:(j + 1) * 128], in_=src, identity=ident_f[:]),
                       r=[xk(c, t // 4), "ident_f"], w=[psk(bank)])
            dst = st[:, half * 512:(half + 1) * 512]
            op("act", lambda e, dst=dst, bank=bank: e.activation(out=dst, in_=PS[bank][:], func=AF.Copy), r=[psk(bank)], w=[("stage", t % 2, half)])
        op("sp", lambda e, st=st, t=t: e.dma_start(out=out_d[t * 128:(t + 1) * 128, :], in_=st), r=[("stage", t % 2, 0), ("stage", t % 2, 1)], dma=True, dkey=("st", t % 2))
    P.emit()
    return nc


def host_consts():
    c = {}
    c["ident"] = np.eye(128, dtype=np.float32)
    inv = (10000.0 ** (-np.arange(0, 64, 2, dtype=np.float32) / np.float32(64))).astype(np.float32)
    ang = (np.arange(L, dtype=np.float32)[None, :] * inv[:, None]).astype(np.float32)
    c["rope_cos"] = np.ascontiguousarray(np.tile(np.cos(ang).astype(np.float32), (4, 1)))
    c["rope_sin"] = np.ascontiguousarray(np.tile(np.sin(ang).astype(np.float32), (4, 1)))
    s_ = np.arange(128)[:, None]
    c_ = np.arange(128)[None, :]
    same = (s_ // 64) == (c_ // 64)
    hm = np.stack([(same & (s_ <= c_)), (same & (s_ >= c_))], axis=1).astype(np.float32)
    c["hmask"] = np.ascontiguousarray(hm)
    sl_ = (np.arange(128) // 16)[:, None]
    tl_ = (np.arange(128) // 16)[None, :]
    c["s5m"] = np.ascontiguousarray(np.stack([(tl_ >= sl_), (tl_ <= sl_)], axis=1).astype(np.float32))
    return c


def make_in_maps(inputs, parts=None):
    consts = host_consts()
    gains = np.concatenate([inputs["norm_mix_g"][0:1], inputs["norm_mlp_g"][0:1], inputs["norm_mix_g"][1:2],
                            inputs["norm_mlp_g"][1:2], inputs["final_norm_g"][None, :]], axis=0).astype(np.float32)
    shared = dict(consts)
    shared["gains"] = np.ascontiguousarray(gains.reshape(5, 8, 128).transpose(2, 0, 1))
    shared["w_in_odd"] = np.ascontiguousarray(inputs["w_in_odd"][0], dtype=np.float32)
    shared["w_out_odd"] = np.ascontiguousarray(inputs["w_out_odd"][0], dtype=np.float32)
    shared["lbl"] = np.ascontiguousarray(np.asarray(inputs["hgrn_lb_logits"], dtype=np.float32).reshape(2, 8, 128).transpose(2, 0, 1))
    shared["hg"] = np.ascontiguousarray(np.asarray(inputs["hgrn_norm_g"][0], dtype=np.float32).reshape(8, 128).T)
    shared["w_in_even"] = np.ascontiguousarray(inputs["w_in_even"][0], dtype=np.float32)
    shared["w_out_even"] = np.ascontiguousarray(inputs["w_out_even"][0], dtype=np.float32)
    shared["diff_lambda"] = np.ascontiguousarray(np.asarray(inputs["diff_lambda"][0], dtype=np.float32).reshape(1, 256))
    shared["subln_g"] = np.ascontiguousarray(np.asarray(inputs["diff_subln_g"][0], dtype=np.float32).reshape(128, 1))
    f32 = np.float32
    lre = np.asarray(inputs["s5_lam_re"][0], f32); lim = np.asarray(inputs["s5_lam_im"][0], f32); lst = np.asarray(inputs["s5_log_step"][0], f32)
    s5s = np.stack([lre.transpose(0, 2, 1).reshape(128, 32), lim.transpose(0, 2, 1).reshape(128, 32),
                    np.repeat(lst[:, None, :], 64, axis=1).reshape(128, 32)], axis=1)
    shared["s5s"] = np.ascontiguousarray(s5s, dtype=f32)
    shared["s5b"] = np.ascontiguousarray(np.stack([np.asarray(inputs[k][0], f32).transpose(0, 2, 1, 3).reshape(128, 32, 16) for k in ("s5_b_re", "s5_b_im")], axis=1))
    shared["s5c"] = np.ascontiguousarray(np.stack([np.asarray(inputs[k][0], f32).transpose(0, 3, 1, 2).reshape(128, 32, 16) for k in ("s5_c_re", "s5_c_im")], axis=1))
    shared["s5d"] = np.ascontiguousarray(np.asarray(inputs["s5_d"][0], f32).reshape(4, 128).T)
    shared["s5bg"] = np.ascontiguousarray(np.asarray(inputs["s5_b_glu"][0], f32).reshape(4, 128).T)
    shared["w_glu"] = np.ascontiguousarray(inputs["s5_w_glu"][0], dtype=f32)
    shared["w_ff_in"] = np.ascontiguousarray(inputs["w_ff_in"], dtype=np.float32)
    shared["w_ff_out"] = np.ascontiguousarray(inputs["w_ff_out"], dtype=np.float32)
    x = np.asarray(inputs["x"], dtype=np.float32)
    maps = []
    for b in range(x.shape[0]):
        m = dict(shared)
        m["x"] = np.ascontiguousarray(x[b])
        maps.append(m)
    return maps


ALL_PARTS = {"l0mix", "l0ffn", "l1mix", "l1ffn", "final"}
_NC_CACHE = {}


def kernel(**inputs):
    inputs = {k: np.asarray(v) for k, v in inputs.items()}
    key = "full"
    if key not in _NC_CACHE:
        _NC_CACHE[key] = build({"parts": ALL_PARTS})
    nc = _NC_CACHE[key]
    maps = make_in_maps(inputs)
    res = run_bass_kernel_spmd(nc, maps, core_ids=list(range(8)))
    out = np.stack([np.asarray(r["out"]) for r in res.results], axis=0)
    return out.astype(np.float32)
```

```python
import math
from contextlib import ExitStack

import numpy as np
import concourse.bass as bass
import concourse.mybir as mybir
from concourse.bass_utils import run_bass_kernel_spmd

F32 = mybir.dt.float32
BF16 = mybir.dt.bfloat16
AF = mybir.ActivationFunctionType
ALU = mybir.AluOpType
AX = mybir.AxisListType

ENGS = ("pe", "act", "dve", "pool", "sp")

L = 2048
D = 1024
NTB = 4
EPS = 1e-6
LAMBDA_INIT0 = 0.8 - 0.6 * math.exp(-0.3 * 0)


class Op:
    __slots__ = ("eng", "fn", "deps", "is_dma", "dkey", "signal", "sem", "val", "idx")


class Prog:
    def __init__(self, nc):
        self.nc = nc
        self.ops = []
        self.last_w = {}
        self.readers = {}
        self.stack = ExitStack()
        self.pending_barrier = {}

    def sb(self, name, shape, dt):
        return self.stack.enter_context(self.nc.sbuf_tensor("sb_" + name, list(shape), dt))

    def ps(self, name, shape, dt=F32):
        return self.stack.enter_context(self.nc.psum_tensor("pp_" + name, list(shape), dt))

    def barrier(self):
        deps = set()
        last = {}
        for o in self.ops:
            if o.is_dma:
                deps.add(o.idx)
            else:
                last[o.eng] = o.idx
        deps.update(last.values())
        self.pending_barrier = {e: set(deps) for e in ENGS}
        self.last_w = {}
        self.readers = {}

    def op(self, eng, fn, r=(), w=(), dma=False, dkey=None):
        o = Op()
        o.eng, o.fn, o.is_dma, o.signal = eng, fn, dma, False
        o.idx = len(self.ops)
        deps = set()
        for k in list(r) + list(w):
            if k in self.last_w:
                deps.add(self.last_w[k])
        for k in w:
            for rd in self.readers.get(k, ()):
                deps.add(rd)
        if self.pending_barrier.get(eng):
            deps |= self.pending_barrier[eng]
            self.pending_barrier[eng] = set()
        deps.discard(o.idx)
        o.deps = deps
        o.dkey = (dkey if dkey is not None else (w[0] if len(w) else r[0])) if dma else None
        for k in r:
            self.readers.setdefault(k, []).append(o.idx)
        for k in w:
            self.last_w[k] = o.idx
            self.readers[k] = []
        self.ops.append(o)
        return o.idx

    def _skip(self, od, o):
        return (not od.is_dma) and (not o.is_dma) and od.eng == o.eng and o.eng == "pe"

    def emit(self, final_eng="sp"):
        nc = self.nc
        ops = self.ops
        final_deps = [o.idx for o in ops if o.is_dma]
        for o in ops:
            for d in o.deps:
                if not self._skip(ops[d], o):
                    ops[d].signal = True
        for d in final_deps:
            ops[d].signal = True
        sems = {}

        def get_sem(key):
            if key not in sems:
                sems[key] = self.stack.enter_context(nc.semaphore("s%d" % len(sems)))
            return sems[key]

        cnt = {}
        for o in ops:
            if not o.signal:
                continue
            key = ("dma", o.dkey) if o.is_dma else ("eng", o.eng)
            inc = 16 if o.is_dma else 1
            cnt[key] = cnt.get(key, 0) + inc
            o.sem = get_sem(key)
            o.val = cnt[key]
            o.dkey = key
        self.n_sems = len(sems)
        per_eng = {e: [] for e in ENGS}
        for o in ops:
            per_eng[o.eng].append(o)

        def run(eng_name, e):
            waited = {}
            for o in per_eng[eng_name]:
                need = {}
                for d in o.deps:
                    od = ops[d]
                    if (not od.signal) or self._skip(od, o):
                        continue
                    if waited.get(od.dkey, 0) >= od.val:
                        continue
                    if need.get(od.dkey, (None, 0))[1] < od.val:
                        need[od.dkey] = (od.sem, od.val)
                for k, (sem, val) in need.items():
                    e.wait_ge(sem, val)
                    waited[k] = val
                ins = o.fn(e)
                if o.signal:
                    ins.then_inc(o.sem, 16 if o.is_dma else 1)
            if eng_name == final_eng:
                need = {}
                for d in final_deps:
                    od = ops[d]
                    if waited.get(od.dkey, 0) >= od.val:
                        continue
                    if need.get(od.dkey, (None, 0))[1] < od.val:
                        need[od.dkey] = (od.sem, od.val)
                for k, (sem, val) in need.items():
                    e.wait_ge(sem, val)

        with nc.Block() as block:
            @block.tensor
            def _(e):
                run("pe", e)

            @block.scalar
            def _(e):
                run("act", e)

            @block.vector
            def _(e):
                run("dve", e)

            @block.gpsimd
            def _(e):
                run("pool", e)

            @block.sync
            def _(e):
                run("sp", e)
        self.stack.close()


class Arena:
    def __init__(self, P, nbytes):
        self.P = P
        self.nbytes = nbytes
        self.base = P.sb("arena", [128, nbytes // 2], BF16)
        self.live = []

    def phase(self):
        self.P.barrier()
        self.live = []

    def at(self, name, off, shape, dt):
        n = 1
        for s in shape:
            n *= s
        nb = n * (4 if dt == F32 else 2)
        assert off % 4 == 0 and off + nb <= self.nbytes, (name, off, nb, self.nbytes)
        for (a, b, nm) in self.live:
            assert off >= b or off + nb <= a, ("arena overlap", name, nm)
        self.live.append((off, off + nb, name))
        ap = self.base[:, off // 2:(off + nb) // 2]
        if dt == F32:
            ap = ap.bitcast(F32)
        if len(shape) > 1:
            names = "abcd"[:len(shape)]
            pat = "p (" + " ".join(names) + ") -> p " + " ".join(names)
            ap = ap.rearrange(pat, **{names[i]: shape[i] for i in range(len(shape))})
        return ap

    def drop(self, name):
        self.live = [x for x in self.live if x[2] != name]


KB = 1024


def build(cfg):
    parts = cfg["parts"]
    nc = bass.Bass("TRN2", target_bir_lowering=False)

    def din(name, shape):
        return nc.dram_tensor(name, list(shape), F32, kind="ExternalInput").ap()

    x_d = din("x", [L, D])
    out_d = nc.dram_tensor("out", [L, D], F32, kind="ExternalOutput").ap()
    gains_d = din("gains", [128, 5, 8])
    w_ff_in_d = din("w_ff_in", [2, D, 4 * D])
    w_ff_out_d = din("w_ff_out", [2, 4 * D, D])
    ident_d = din("ident", [128, 128])
    w_in_odd_d = din("w_in_odd", [D, 5 * D])
    w_in_even_d = din("w_in_even", [D, 2 * D])
    w_out_even_d = din("w_out_even", [D, D])
    cos_d = din("rope_cos", [128, L])
    sin_d = din("rope_sin", [128, L])
    dl_d = din("diff_lambda", [1, 256])
    sg_d = din("subln_g", [128, 1])
    w_glu_d = din("w_glu", [512, 512])
    s5s_d = din("s5s", [128, 3, 32])
    s5b_d = din("s5b", [128, 2, 32, 16])
    s5c_d = din("s5c", [128, 2, 32, 16])
    s5d_d = din("s5d", [128, 4])
    s5bg_d = din("s5bg", [128, 4])
    s5m_d = din("s5m", [128, 2, 128])
    w_out_odd_d = din("w_out_odd", [D, D])
    lbl_d = din("lbl", [128, 2, 8])
    hg_d = din("hg", [128, 8])
    hmask_d = din("hmask", [128, 2, 128])

    P = Prog(nc)
    op = P.op

    xT = P.sb("xT", [128, 8, L], F32)
    WB = [P.sb("wb%d" % i, [128, 4096], BF16) for i in range(4)]
    ident_f = P.sb("ident_f", [128, 128], F32)
    ident_b = P.sb("ident_b", [128, 128], BF16)
    ones_b = P.sb("ones_b", [128, 128], BF16)
    ones_f = P.sb("ones_f", [128, 128], F32)
    gains = P.sb("gains", [128, 5, 8], F32)
    PS = [P.ps("ps%d" % i, [128, 512], F32) for i in range(8)]
    lbl = P.sb("lbl", [128, 2, 8], F32)
    dl = P.sb("dl", [128, 256], F32)
    dlp = P.sb("dlp", [128, 128], F32)
    lam = P.sb("lam", [128, 4], F32)
    sg = P.sb("sg", [128, 1], F32)
    rr_g = P.sb("rr_g", [128, 8], F32)
    hg = P.sb("hg", [128, 8], F32)
    hmask = P.sb("hmask", [128, 2, 128], F32)
    lb = P.sb("lb", [128, 8], F32)
    oml = P.sb("oml", [128, 8], F32)
    noml = P.sb("noml", [128, 8], F32)
    AR = Arena(P, 104 * KB)

    def psk(i):
        return ("ps", i)

    dbg = cfg.get("dbg", False)
    dbg_tab = cfg.setdefault("dbg_tab", {})
    xT_flat = xT[:].rearrange("p a b -> p (a b)")
    dbg_off = [0]

    def dbg_put(name, ap, rkeys):
        if not dbg:
            return
        shp = list(ap.shape)
        n = 1
        for v_ in shp[1:]:
            n *= v_
        o = dbg_off[0]
        dst = xT_flat[:, o:o + n]
        if len(shp) == 3:
            dst = dst.rearrange("p (a b) -> p a b", b=shp[2])
        op("dve", lambda e: e.tensor_copy(out=dst, in_=ap), r=list(rkeys), w=["dbg"])
        dbg_tab[name] = (o, shp)
        dbg_off[0] = o + n

    def xk(c, tb):
        return ("xT", c, tb)

    def hk(c, tb):
        return ("hT", c, tb)

    XT_ALL = [xk(c, tb) for c in range(8) for tb in range(4)]
    HT_ALL = [hk(c, tb) for c in range(8) for tb in range(4)]

    op("sp", lambda e: e.dma_start(out=ident_f[:], in_=ident_d), w=["ident_f"], dma=True)
    op("dve", lambda e: e.tensor_copy(out=ident_b[:], in_=ident_f[:]), r=["ident_f"], w=["ident_b"])
    op("pool", lambda e: e.memset(ones_b[:], 1.0), w=["ones_b"])
    op("pool", lambda e: e.memset(ones_f[:], 1.0), w=["ones_f"])
    op("sp", lambda e: e.dma_start(out=gains[:], in_=gains_d), w=["gains"], dma=True)

    AR.phase()
    xin = [AR.at("xin%d" % i, i * 4 * KB, [D], F32) for i in range(2)]
    ev = 0
    for t in range(16):
        b = xin[t % 2]
        op("sp", lambda e, b=b, t=t: e.dma_start(out=b, in_=x_d[t * 128:(t + 1) * 128, :]), w=[("xin", t % 2)], dma=True)
        for half in range(2):
            bank = (2 * t + half) % 4
            for j in range(4):
                c = half * 4 + j
                op("pe", lambda e, bank=bank, j=j, c=c, b=b: e.transpose(out=PS[bank][:, j * 128:(j + 1) * 128], in_=b[:, c * 128:(c + 1) * 128], identity=ident_f[:]),
                   r=[("xin", t % 2), "ident_f"], w=[psk(bank)])
            dst = xT[:, half * 4:(half + 1) * 4, t * 128:(t + 1) * 128]
            src = PS[bank][:].rearrange("p (a b) -> p a b", b=128)
            wk = [xk(half * 4 + j, t // 4) for j in range(4)]
            if ev % 2 == 0:
                op("dve", lambda e, dst=dst, src=src: e.tensor_copy(out=dst, in_=src), r=[psk(bank)], w=wk)
            else:
                op("act", lambda e, dst=dst, src=src: e.activation(out=dst, in_=src, func=AF.Copy), r=[psk(bank)], w=wk)
            ev += 1

    def rmsnorm_rstd(rstd, sq):
        for c in range(8):
            s = sq[c % 2]
            op("act", lambda e, s=s, c=c: e.activation(out=s, in_=xT[:, c, :], func=AF.Square),
               r=[xk(c, tb) for tb in range(4)], w=[("sq", c % 2)])
            for tb in range(4):
                op("pe", lambda e, s=s, c=c, tb=tb: e.matmul(PS[tb][:], lhsT=ones_b[:], rhs=s[:, tb * 512:(tb + 1) * 512], start=(c == 0), stop=(c == 7)),
                   r=[("sq", c % 2), "ones_b"], w=[psk(tb)])
        for tb in range(4):
            sl = rstd[:, tb * 512:(tb + 1) * 512]
            op("act", lambda e, sl=sl, tb=tb: e.activation(out=sl, in_=PS[tb][:], func=AF.Sqrt, scale=1.0 / D, bias=EPS),
               r=[psk(tb)], w=[("rstd", tb)])
            op("dve", lambda e, sl=sl: e.reciprocal(out=sl, in_=sl), r=[("rstd", tb)], w=[("rstd", tb)])

    def norm_to_hT(gi, hT, rstd, sq):
        rmsnorm_rstd(rstd, sq)
        for c in range(8):
            for tb in range(4):
                op("dve", lambda e, c=c, tb=tb: e.scalar_tensor_tensor(
                    out=hT[:, c, tb * 512:(tb + 1) * 512], in0=xT[:, c, tb * 512:(tb + 1) * 512], scalar=gains[:, gi, c:c + 1],
                    in1=rstd[:, tb * 512:(tb + 1) * 512], op0=ALU.mult, op1=ALU.mult),
                   r=[xk(c, tb), ("rstd", tb), "gains"], w=[hk(c, tb)])

    def wload(slot, src, r=()):
        a, b = src.shape[1], src.shape[2]
        dst = WB[slot][:, 0:a * b].rearrange("p (a b) -> p a b", b=b)
        op("pool", lambda e: e.dma_start(out=dst, in_=src), r=list(r), w=[("wb", slot)], dma=True)
        return dst

    def ffn(l, hT, actT, rl):
        w_in = w_ff_in_d[l].rearrange("(c p) f -> p c f", p=128)
        w_out = w_ff_out_d[l].rearrange("(c p) d -> p c d", p=128)
        views = {}

        def load(fg):
            views[fg] = (wload((2 * fg) % 4, w_in[:, :, fg * 512:(fg + 1) * 512]),
                         wload((2 * fg + 1) % 4, w_out[:, fg * 4:(fg + 1) * 4, :]))

        load(0)
        load(1)
        k = 0
        for fg in range(8):
            wi, wo = views[fg]
            sa, sbk = (2 * fg) % 4, (2 * fg + 1) % 4
            at = actT[fg % 2]
            for tb in range(4):
                for fc in range(4):
                    bank = k % 3
                    for c in range(8):
                        op("pe", lambda e, bank=bank, wi=wi, c=c, fc=fc, tb=tb: e.matmul(
                            PS[bank][:], lhsT=wi[:, c, fc * 128:(fc + 1) * 128], rhs=hT[:, c, tb * 512:(tb + 1) * 512], start=(c == 0), stop=(c == 7)),
                           r=[("wb", sa), hk(c, tb)], w=[psk(bank)])
                    r_ = rl[k % 2]
                    op("act", lambda e, bank=bank, r_=r_: e.activation(out=r_, in_=PS[bank][:], func=AF.Relu), r=[psk(bank)], w=[("rl", k % 2)])
                    op("act", lambda e, r_=r_, at=at, fc=fc, tb=tb: e.activation(out=at[:, fc, tb * 512:(tb + 1) * 512], in_=r_, func=AF.Square),
                       r=[("rl", k % 2)], w=[("actT", fg % 2, fc, tb)])
                    k += 1
            kk = 0
            for tb in range(4):
                for dc in range(8):
                    bank = 3 + kk % 3
                    for fc in range(4):
                        op("pe", lambda e, bank=bank, wo=wo, fc=fc, dc=dc, tb=tb, at=at: e.matmul(
                            PS[bank][:], lhsT=wo[:, fc, dc * 128:(dc + 1) * 128], rhs=at[:, fc, tb * 512:(tb + 1) * 512], start=(fc == 0), stop=(fc == 3)),
                           r=[("wb", sbk), ("actT", fg % 2, fc, tb)], w=[psk(bank)])
                    xs = xT[:, dc, tb * 512:(tb + 1) * 512]
                    op("dve", lambda e, xs=xs, bank=bank: e.tensor_tensor(out=xs, in0=xs, in1=PS[bank][:], op=ALU.add), r=[psk(bank), xk(dc, tb)], w=[xk(dc, tb)])
                    kk += 1
            if fg + 2 < 8:
                load(fg + 2)

    def ffn_phase(l, gi):
        AR.phase()
        hT = AR.at("hT", 0, [8, L], BF16)
        actT = [AR.at("actT%d" % i, 32 * KB + i * 16 * KB, [4, L], BF16) for i in range(2)]
        rl = [AR.at("rl%d" % i, 64 * KB + i * 2 * KB, [512], F32) for i in range(2)]
        sq = [AR.at("sq%d" % i, 68 * KB + i * 4 * KB, [L], BF16) for i in range(2)]
        rstd = AR.at("rstd", 76 * KB, [L], F32)
        norm_to_hT(gi, hT, rstd, sq)
        ffn(l, hT, actT, rl)


    def attn_phase(gi):
        w_in = w_in_even_d.rearrange("(c p) f -> p c f", p=128)
        w_out = w_out_even_d.rearrange("(c p) d -> p c d", p=128)
        T0 = 81 * KB
        AR.phase()
        hT = AR.at("hT", 0, [8, L], BF16)
        sq = [AR.at("sq%d" % i, T0 + i * 4 * KB, [L], BF16) for i in range(2)]
        rstd = AR.at("rstd", T0 + 8 * KB, [L], F32)
        norm_to_hT(gi, hT, rstd, sq)
        op("sp", lambda e: e.dma_start(out=dl[:], in_=bass.AP(dl_d.tensor, 0, [[0, 128], [1, 256]])), w=["dl"], dma=True)
        op("sp", lambda e: e.dma_start(out=sg[:], in_=sg_d), w=["sg"], dma=True)
        op("dve", lambda e: e.tensor_tensor(out=dlp[:, 0:64], in0=dl[:, 0:64], in1=dl[:, 64:128], op=ALU.mult), r=["dl"], w=["dlp"])
        op("dve", lambda e: e.tensor_tensor(out=dlp[:, 64:128], in0=dl[:, 128:192], in1=dl[:, 192:256], op=ALU.mult), r=["dl", "dlp"], w=["dlp"])
        op("dve", lambda e: e.reduce_sum(out=lam[:, 0:2], in_=dlp[:].rearrange("p (a b) -> p a b", b=64), axis=AX.X), r=["dlp"], w=["lam"])
        op("act", lambda e: e.activation(out=lam[:, 0:2], in_=lam[:, 0:2], func=AF.Exp), r=["lam"], w=["lam"])
        op("dve", lambda e: e.tensor_tensor(out=lam[:, 2:3], in0=lam[:, 1:2], in1=lam[:, 0:1], op=ALU.subtract), r=["lam"], w=["lam"])
        op("dve", lambda e: e.tensor_scalar(out=lam[:, 3:4], in0=lam[:, 2:3], scalar1=-LAMBDA_INIT0, scalar2=None, op0=ALU.add), r=["lam"], w=["lam"])
        op("dve", lambda e: e.tensor_scalar(out=sg[:], in0=sg[:], scalar1=1.0 - LAMBDA_INIT0, scalar2=None, op0=ALU.mult), r=["sg"], w=["sg"])
        nlam = lam[:, 3:4]
        if cfg.get("attn_stop", 9) <= 1:
            return

        AR.phase()
        hT = AR.at("hT", 0, [8, L], BF16)
        qT = AR.at("qT", 32 * KB, [4, L], BF16)
        kT = AR.at("kT", 48 * KB, [4, L], BF16)
        vaug = AR.at("vaug", 64 * KB, [16, 4, 130], BF16)
        t1 = [AR.at("t1_%d" % i, T0 + i * 2 * KB, [512], F32) for i in range(2)]
        t2 = [AR.at("t2_%d" % i, T0 + 4 * KB + i * 2 * KB, [512], F32) for i in range(2)]
        cosb = [AR.at("cos%d" % i, T0 + 8 * KB + i * 2 * KB, [512], F32) for i in range(2)]
        sinb = [AR.at("sin%d" % i, T0 + 12 * KB + i * 2 * KB, [512], F32) for i in range(2)]

        def load_group(slot, g):
            return wload(slot, w_in[:, :, g * 512:(g + 1) * 512])

        def rotate(sa, sr):
            a = WB[sa][:].rearrange("p (cb two j) -> p cb two j", two=2, j=32)
            r_ = WB[sr][:].rearrange("p (cb two j) -> p cb two j", two=2, j=32)
            op("dve", lambda e: e.tensor_scalar(out=r_[:, :, 0, :], in0=a[:, :, 1, :], scalar1=-1.0, scalar2=None, op0=ALU.mult), r=[("wb", sa)], w=[("wb", sr)])
            op("dve", lambda e: e.tensor_copy(out=r_[:, :, 1, :], in_=a[:, :, 0, :]), r=[("wb", sa)], w=[("wb", sr)])
            return WB[sr][:].rearrange("p (a b) -> p a b", b=512)

        wq = load_group(0, 0)
        wk = load_group(2, 1)
        wqr = rotate(0, 1)
        wkr = rotate(2, 3)
        op("pool", lambda e: e.memset(vaug[:, :, :, 128:130], 1.0), w=["vaug1"])
        kctr = [0]

        def rope_proj(tb, wA, wR, sA, sR, dstT, h, dkey):
            i = kctr[0] % 2
            kctr[0] += 1
            ba, bb = 2 * i, 2 * i + 1
            for c in range(8):
                op("pe", lambda e, c=c: e.matmul(PS[ba][:], lhsT=wA[:, c, h * 128:(h + 1) * 128], rhs=hT[:, c, tb * 512:(tb + 1) * 512], start=(c == 0), stop=(c == 7)),
                   r=[("wb", sA), hk(c, tb)], w=[psk(ba)])
            for c in range(8):
                op("pe", lambda e, c=c: e.matmul(PS[bb][:], lhsT=wR[:, c, h * 128:(h + 1) * 128], rhs=hT[:, c, tb * 512:(tb + 1) * 512], start=(c == 0), stop=(c == 7)),
                   r=[("wb", sR), hk(c, tb)], w=[psk(bb)])
            op("dve", lambda e: e.tensor_tensor(out=t1[i], in0=PS[ba][:], in1=cosb[tb % 2], op=ALU.mult), r=[psk(ba), ("cos", tb % 2)], w=[("t1", i)])
            op("dve", lambda e: e.tensor_tensor(out=t2[i], in0=PS[bb][:], in1=sinb[tb % 2], op=ALU.mult), r=[psk(bb), ("sin", tb % 2)], w=[("t2", i)])
            op("pool", lambda e: e.tensor_tensor(out=dstT[:, h, tb * 512:(tb + 1) * 512], in0=t1[i], in1=t2[i], op=ALU.add), r=[("t1", i), ("t2", i)], w=[(dkey, h, tb)])

        def do_tb(tb):
            op("sp", lambda e: e.dma_start(out=cosb[tb % 2], in_=cos_d[:, tb * 512:(tb + 1) * 512]), w=[("cos", tb % 2)], dma=True)
            op("sp", lambda e: e.dma_start(out=sinb[tb % 2], in_=sin_d[:, tb * 512:(tb + 1) * 512]), w=[("sin", tb % 2)], dma=True)
            for h in range(4):
                rope_proj(tb, wq, wqr, 0, 1, qT, h, "qT")
                rope_proj(tb, wk, wkr, 2, 3, kT, h, "kT")

        for tb in range(4):
            do_tb(tb)
        wv = load_group(0, 2)

        def do_v(j):
            bank = 4 + j % 2
            for c in range(8):
                op("pe", lambda e, c=c: e.matmul(PS[bank][:], lhsT=hT[:, c, j * 128:(j + 1) * 128], rhs=wv[:, c, :], start=(c == 0), stop=(c == 7)),
                   r=[("wb", 0), hk(c, j // 4)], w=[psk(bank)])
            op("act", lambda e: e.activation(out=vaug[:, j, :, 0:128], in_=PS[bank][:].rearrange("p (a b) -> p a b", b=128), func=AF.Copy), r=[psk(bank)], w=[("vaug", j)])

        for j in range(16):
            do_v(j)
        wo = wload(2, w_out[:, 0:4, :])
        dbg_put("qT0", qT[:, 0, 0:256], [("qT", 0, 0)])
        dbg_put("kT0", kT[:, 0, 0:256], [("kT", 0, 0)])
        dbg_put("vaug0", vaug[:, 0, :, :], [("vaug", 0), "vaug1"])
        dbg_put("lam", lam[:, 0:4], ["lam"])
        if cfg.get("attn_stop", 9) <= 2:
            return

        AR.phase()
        hT = AR.at("hT", 0, [8, L], BF16)
        qT = AR.at("qT", 32 * KB, [4, L], BF16)
        kT = AR.at("kT", 48 * KB, [4, L], BF16)
        vaug = AR.at("vaug", 64 * KB, [16, 4, 130], BF16)
        aT = AR.at("aT", T0, [4, L], BF16)
        PT = [AR.at("PT%d" % i, T0 + 16 * KB + i * KB, [512], BF16) for i in range(4)]
        SM = T0 + 20 * KB
        rr = rr_g[:]
        tt_ = [AR.at("tt%d" % i, SM + i * 512, [128], F32) for i in range(2)]
        oo = [AR.at("oo%d" % i, SM + 1024 + i * 512, [128], F32) for i in range(2)]
        onb = [AR.at("onb%d" % i, SM + 2048 + i * 256, [128], BF16) for i in range(4)]
        pctr = [0]
        cctr = [0]
        pending = []

        def accv(comp, qs):
            bank = 2 + 2 * comp + qs // 2
            off = (qs % 2) * 130
            return bank, PS[bank][:, off:off + 129]

        its = [(h_, qb_, comp_, kt_) for h_ in range(4) for qb_ in range(4) for comp_ in range(2) for kt_ in range(16)]
        if cfg.get("attn_stop", 9) == 3:
            its = [x_ for x_ in its if x_[0] == 0 and x_[1] == 0]

        SRING = (0, 1, 6)
        accs = [WB[1][:].bitcast(F32)[:, 512 * (1 + c_):512 * (2 + c_)] for c_ in range(2)]

        qpad = [[WB[0][:, (comp_ * 2 + j_) * 512:(comp_ * 2 + j_ + 1) * 512] for j_ in range(2)] for comp_ in range(2)]
        op("pool", lambda e: e.memset(WB[0][:, 0:2048], 0.0), r=[("wb", 0)], w=[("wb", 0)] + [("qpad", c_, j_) for c_ in range(2) for j_ in range(2)])

        def emit_qpad(gi):
            h, qb = gi // 4, gi % 4
            j = gi % 2
            op("pool", lambda e: e.tensor_copy(out=qpad[0][j][0:64, :], in_=qT[0:64, h, qb * 512:(qb + 1) * 512]), r=[("qT", h, qb)], w=[("qpad", 0, j)])
            op("pool", lambda e: e.tensor_copy(out=qpad[1][j][64:128, :], in_=qT[64:128, h, qb * 512:(qb + 1) * 512]), r=[("qT", h, qb)], w=[("qpad", 1, j)])

        def emit_S(i):
            h, qb, comp, kt = its[i]
            sbank = SRING[i % 3]
            j = (h * 4 + qb) % 2
            op("pe", lambda e: e.matmul(PS[sbank][:], lhsT=kT[:, h, kt * 128:(kt + 1) * 128], rhs=qpad[comp][j], start=True, stop=True),
               r=[("kT", h, kt // 4), ("qpad", comp, j)], w=[psk(sbank)])

        def emit_exp_pv(i):
            h, qb, comp, kt = its[i]
            sbank = SRING[i % 3]
            pi = i % 4
            op("act", lambda e: e.activation(out=PT[pi], in_=PS[sbank][:], func=AF.Exp, scale=0.125), r=[psk(sbank)], w=[("PT", pi)])
            bo, bs = 2 + 2 * comp, 3 + 2 * comp
            op("pe", lambda e: e.matmul(PS[bo][:], lhsT=vaug[:, kt, h, 0:128], rhs=PT[pi], start=(kt == 0), stop=(kt == 15)),
               r=[("PT", pi), ("vaug", kt)], w=[psk(bo)])
            op("pe", lambda e: e.matmul(PS[bs][:], lhsT=ones_b[:], rhs=PT[pi], start=(kt == 0), stop=(kt == 15)),
               r=[("PT", pi), "ones_b"], w=[psk(bs)])

        def attend_tail(h, qb):
            fo = WB[3][:].bitcast(F32)
            r0, r1, t_, o_ = fo[:, 0:512], fo[:, 512:1024], fo[:, 1024:1536], fo[:, 1536:2048]
            sqb = WB[1][:, 0:512]
            K3 = [("wb", 3)]
            if dbg and h == 0 and qb == 0:
                dbg_put("acc0", PS[2][:, 0:256], [psk(2)])
                dbg_put("acc1", PS[4][:, 0:256], [psk(4)])
            if cfg.get("attn_stop", 9) <= 3:
                return
            op("dve", lambda e: e.reciprocal(out=r0, in_=PS[3][:]), r=[psk(3)], w=K3)
            op("dve", lambda e: e.reciprocal(out=r1, in_=PS[5][:]), r=[psk(5)] + K3, w=K3)
            op("dve", lambda e: e.tensor_tensor(out=t_, in0=PS[4][:], in1=r1, op=ALU.mult), r=[psk(4)] + K3, w=K3)
            op("dve", lambda e: e.tensor_tensor(out=o_, in0=PS[2][:], in1=r0, op=ALU.mult), r=[psk(2)] + K3, w=K3)
            op("dve", lambda e: e.scalar_tensor_tensor(out=o_, in0=t_, scalar=nlam, in1=o_, op0=ALU.mult, op1=ALU.add), r=K3 + ["lam"], w=K3)
            op("act", lambda e: e.activation(out=sqb, in_=o_, func=AF.Square), r=K3, w=[("wb", 1)])

            def fin():
                op("pe", lambda e: e.matmul(PS[7][:], lhsT=ones_b[:], rhs=sqb, start=True, stop=True), r=[("wb", 1), "ones_b"], w=[psk(7)])
                op("act", lambda e: e.activation(out=r0, in_=PS[7][:], func=AF.Sqrt, scale=1.0 / 128, bias=EPS), r=[psk(7)] + K3, w=K3)
                op("dve", lambda e: e.reciprocal(out=r0, in_=r0), r=K3, w=K3)
                op("dve", lambda e: e.scalar_tensor_tensor(out=aT[:, h, qb * 512:(qb + 1) * 512], in0=o_, scalar=sg[:, 0:1], in1=r0, op0=ALU.mult, op1=ALU.mult),
                   r=K3 + ["sg"], w=[("aT", h, qb)])
            pending.append(fin)

        emit_qpad(0)
        LOOK = 2
        for i in range(LOOK):
            emit_S(i)
        for i in range(len(its)):
            if its[i][2] == 0 and its[i][3] == 0:
                gi_ = its[i][0] * 4 + its[i][1]
                if gi_ + 1 < 16 and cfg.get("attn_stop", 9) != 3:
                    emit_qpad(gi_ + 1)
            if i + LOOK < len(its):
                emit_S(i + LOOK)
            emit_exp_pv(i)
            h_, qb_, comp_, kt_ = its[i]
            if kt_ == 15 and comp_ == 0 and pending:
                for f in pending:
                    f()
                del pending[:]
            if kt_ == 15 and comp_ == 1:
                attend_tail(h_, qb_)
        for f in pending:
            f()
        del pending[:]
        dbg_put("aT", aT[:, :, 0:512], [("aT", h_, 0) for h_ in range(4)])

        def oproj(tb, dc, kq):
            bank = kq % 2
            for hc in range(4):
                op("pe", lambda e, hc=hc: e.matmul(PS[bank][:], lhsT=wo[:, hc, dc * 128:(dc + 1) * 128], rhs=aT[:, hc, tb * 512:(tb + 1) * 512], start=(hc == 0), stop=(hc == 3)),
                   r=[("wb", 2), ("aT", hc, tb)], w=[psk(bank)])
            xs = xT[:, dc, tb * 512:(tb + 1) * 512]
            op("dve", lambda e: e.tensor_tensor(out=xs, in0=xs, in1=PS[bank][:], op=ALU.add), r=[psk(bank), xk(dc, tb)], w=[xk(dc, tb)])

        if not dbg:
            kq = 0
            for tb in range(4):
                for dc in range(8):
                    oproj(tb, dc, kq)
                    kq += 1


    def s5_phase():
        w_in = w_in_even_d.rearrange("(c p) f -> p c f", p=128)
        w_out = w_out_even_d.rearrange("(c p) d -> p c d", p=128)
        AR.phase()
        hT = AR.at("hT", 0, [8, L], BF16)
        if not ("l0attn" in parts or "l0mix" in parts):
            sq = [AR.at("sq%d" % i, 81 * KB + i * 4 * KB, [L], BF16) for i in range(2)]
            rstd = AR.at("rstd", 89 * KB, [L], F32)
            norm_to_hT(0, hT, rstd, sq)
        uT = AR.at("uT", 32 * KB, [4, L], BF16)
        wu = wload(0, w_in[:, :, 1536:2048])

        def uproj(tb, cc, kq):
            bank = kq % 4
            for c in range(8):
                op("pe", lambda e, c=c: e.matmul(PS[bank][:], lhsT=wu[:, c, cc * 128:(cc + 1) * 128], rhs=hT[:, c, tb * 512:(tb + 1) * 512], start=(c == 0), stop=(c == 7)),
                   r=[("wb", 0), hk(c, tb)], w=[psk(bank)])
            op("act", lambda e: e.activation(out=uT[:, cc, tb * 512:(tb + 1) * 512], in_=PS[bank][:], func=AF.Copy), r=[psk(bank)], w=[("uT", cc)])

        kq = 0
        for tb in range(4):
            for cc in range(4):
                uproj(tb, cc, kq)
                kq += 1
        wglu = wload(1, w_glu_d.rearrange("(c p) f -> p c f", p=128))
        wo2 = wload(2, w_out[:, 4:8, :])

        AR.phase()
        VX = AR.at("VX", 0, [2, 32, 256], BF16)
        uT = AR.at("uT", 32 * KB, [4, L], BF16)
        U = AR.at("U", 48 * KB, [32, 256], BF16)
        sm_off = [64 * KB]

        def sm(name, shape=(32,)):
            n = 1
            for v_ in shape:
                n *= v_
            t_ = AR.at(name, sm_off[0], list(shape), F32)
            sm_off[0] += n * 4
            return t_

        bm = sm("bm", (2, 32, 16))
        names = ["lr", "dt", "lrdt", "ang", "mg", "sn", "cs", "r_", "i_", "t0", "t1", "t2", "den", "am1", "cr", "ci", "vr", "vi"]
        sv = {n_: sm(n_) for n_ in names}
        par = sm("par", (3, 32))
        cm = sm("cm", (2, 32, 16))
        bb = sm("bb", (2, 32, 16))
        pw = sm("pw", (2, 32, 8))
        pwi = sm("pwi", (2, 32, 8))
        P1s = sm("P1s", (2, 32))
        P2s = sm("P2s", (2, 32))
        Xst = sm("Xst", (2, 32))
        S1 = sm("S1", (2, 32))
        T1 = sm("T1", (2, 32))
        T2 = sm("T2", (2, 32))
        dsk = sm("dsk", (4,))
        bgl = sm("bgl", (4,))
        SM_END = sm_off[0]
        assert SM_END <= 85 * KB, SM_END
        AB = [AR.at("AB%d" % i, 85 * KB + i * 2 * KB, [8, 128], BF16) for i in range(2)]
        ABT = [AR.at("ABT%d" % i, 89 * KB + i * 2 * KB, [8, 128], BF16) for i in range(2)]
        tA = AR.at("tA", 93 * KB, [512], F32)
        tB = AR.at("tB", 95 * KB, [512], F32)
        Zt = AR.at("Zt", 97 * KB, [8, 240], BF16)
        s5m = AR.at("s5m", 97 * KB + 3840, [2, 128], F32)
        PP = ["pp"]

        def vop(fn, eng="dve", r=(), w=()):
            op(eng, fn, r=PP + list(r), w=PP + list(w))

        def tt(o_, a_, b_, o, **kw):
            vop(lambda e: e.tensor_tensor(out=o_, in0=a_, in1=b_, op=o), **kw)

        def ts(o_, a_, s1, o1, s2=None, o2=None, **kw):
            if o2 is None:
                vop(lambda e: e.tensor_scalar(out=o_, in0=a_, scalar1=s1, scalar2=None, op0=o1), **kw)
            else:
                vop(lambda e: e.tensor_scalar(out=o_, in0=a_, scalar1=s1, scalar2=s2, op0=o1, op1=o2), **kw)

        def cmul(outr, outi, ar_, ai_, br_, bi_, x1, x2, **kw):
            tt(x1, ar_, br_, ALU.mult, **kw)
            tt(x2, ai_, bi_, ALU.mult, **kw)
            tt(outr, x1, x2, ALU.subtract, **kw)
            tt(x1, ar_, bi_, ALU.mult, **kw)
            tt(x2, ai_, br_, ALU.mult, **kw)
            tt(outi, x1, x2, ALU.add, **kw)

        op("sp", lambda e: e.dma_start(out=par, in_=s5s_d), w=PP, dma=True, dkey="s5s")
        op("sp", lambda e: e.dma_start(out=bm, in_=s5b_d), w=PP, dma=True, dkey="s5b")
        op("sp", lambda e: e.dma_start(out=cm, in_=s5c_d), w=PP, dma=True, dkey="s5c")
        op("sp", lambda e: e.dma_start(out=dsk, in_=s5d_d), w=PP, dma=True, dkey="s5d")
        op("sp", lambda e: e.dma_start(out=bgl, in_=s5bg_d), w=PP, dma=True, dkey="s5bg")
        op("sp", lambda e: e.dma_start(out=s5m, in_=s5m_d), w=["s5m"], dma=True)
        op("pool", lambda e: e.memset(Zt, 0.0), w=["Zt"])
        op("pool", lambda e: e.tensor_copy(out=Zt[:, :, 112:128], in_=ident_b[:].rearrange("p (a b) -> p a b", b=16)), r=["ident_b"], w=["Zt"])
        lam_re, lam_im, lstep = par[:, 0, :], par[:, 1, :], par[:, 2, :]
        v_ = sv
        ts(v_["lr"], lam_re, -1e-4, ALU.min)
        vop(lambda e: e.activation(out=v_["dt"], in_=lstep, func=AF.Exp), eng="act")
        tt(v_["lrdt"], v_["lr"], v_["dt"], ALU.mult)
        tt(v_["ang"], lam_im, v_["dt"], ALU.mult)
        vop(lambda e: e.activation(out=v_["mg"], in_=v_["lrdt"], func=AF.Exp, scale=1.0 / 32), eng="act")
        vop(lambda e: e.activation(out=v_["sn"], in_=v_["ang"], func=AF.Sin, scale=1.0 / 32), eng="act")
        ts(v_["t0"], v_["ang"], 1.0 / 32, ALU.mult, math.pi / 2, ALU.add)
        vop(lambda e: e.activation(out=v_["cs"], in_=v_["t0"], func=AF.Sin), eng="act")
        tt(v_["r_"], v_["mg"], v_["cs"], ALU.mult)
        tt(v_["i_"], v_["mg"], v_["sn"], ALU.mult)
        for _ in range(5):
            tt(v_["t0"], v_["r_"], v_["r_"], ALU.mult)
            tt(v_["t1"], v_["i_"], v_["i_"], ALU.mult)
            tt(v_["t2"], v_["r_"], v_["i_"], ALU.mult)
            tt(v_["r_"], v_["t0"], v_["t1"], ALU.subtract)
            ts(v_["i_"], v_["t2"], 2.0, ALU.mult)
        ar, ai = v_["r_"], v_["i_"]
        tt(v_["t0"], v_["lr"], v_["lr"], ALU.mult)
        tt(v_["t1"], lam_im, lam_im, ALU.mult)
        tt(v_["den"], v_["t0"], v_["t1"], ALU.add)
        vop(lambda e: e.reciprocal(out=v_["den"], in_=v_["den"]))
        ts(v_["am1"], ar, -1.0, ALU.add)
        tt(v_["t0"], v_["am1"], v_["lr"], ALU.mult)
        tt(v_["t1"], ai, lam_im, ALU.mult)
        tt(v_["t0"], v_["t0"], v_["t1"], ALU.add)
        tt(v_["cr"], v_["t0"], v_["den"], ALU.mult)
        tt(v_["t0"], ai, v_["lr"], ALU.mult)
        tt(v_["t1"], v_["am1"], lam_im, ALU.mult)
        tt(v_["t0"], v_["t0"], v_["t1"], ALU.subtract)
        tt(v_["ci"], v_["t0"], v_["den"], ALU.mult)
        tt(v_["t0"], ar, ar, ALU.mult)
        tt(v_["t1"], ai, ai, ALU.mult)
        tt(v_["t0"], v_["t0"], v_["t1"], ALU.add)
        vop(lambda e: e.reciprocal(out=v_["t0"], in_=v_["t0"]))
        tt(v_["vr"], ar, v_["t0"], ALU.mult)
        tt(v_["t1"], ai, v_["t0"], ALU.mult)
        ts(v_["vi"], v_["t1"], -1.0, ALU.mult)
        x1 = tA[:, 0:128].rearrange("p (a b) -> p a b", b=4)
        x2 = tB[:, 0:128].rearrange("p (a b) -> p a b", b=4)
        for (tab, br_, bi_) in ((pw, ar, ai), (pwi, v_["vr"], v_["vi"])):
            tr_, ti_ = tab[:, 0, :, :], tab[:, 1, :, :]
            vop(lambda e, tr_=tr_, br_=br_: e.tensor_copy(out=tr_[:, :, 0:1], in_=br_.unsqueeze(2)))
            vop(lambda e, ti_=ti_, bi_=bi_: e.tensor_copy(out=ti_[:, :, 0:1], in_=bi_.unsqueeze(2)))
            for n_ in (1, 2, 4):
                cmul(tr_[:, :, n_:2 * n_], ti_[:, :, n_:2 * n_], tr_[:, :, 0:n_], ti_[:, :, 0:n_],
                     tr_[:, :, n_ - 1:n_].broadcast_to([128, 32, n_]), ti_[:, :, n_ - 1:n_].broadcast_to([128, 32, n_]), x1[:, :, 0:n_], x2[:, :, 0:n_])
        cmul(bb[:, 0, :, :], bb[:, 1, :, :], v_["cr"].unsqueeze(2).broadcast_to([128, 32, 16]), v_["ci"].unsqueeze(2).broadcast_to([128, 32, 16]),
             bm[:, 0, :, :], bm[:, 1, :, :], tA.rearrange("p (a b) -> p a b", b=16), tB.rearrange("p (a b) -> p a b", b=16))
        vop(lambda e: e.tensor_copy(out=P1s[:, 0, :], in_=pw[:, 0, :, 7]))
        vop(lambda e: e.tensor_copy(out=P1s[:, 1, :], in_=pw[:, 0, :, 7]))
        ts(P2s[:, 0, :], pw[:, 1, :, 7], -1.0, ALU.mult)
        vop(lambda e: e.tensor_copy(out=P2s[:, 1, :], in_=pw[:, 1, :, 7]))

        def gen_AB(ABt, g0, gl0):
            for (lo, hi, rev) in ((0, 64, False), (64, 128, True)):
                sl = slice(None, None, -1) if rev else slice(None)
                pr = pwi[lo:hi, 0, g0:g0 + 4, sl].unsqueeze(3).broadcast_to([64, 4, 8, 16])
                pi = pwi[lo:hi, 1, g0:g0 + 4, sl].unsqueeze(3).broadcast_to([64, 4, 8, 16])
                br_ = bb[lo:hi, 0, g0:g0 + 4, :].unsqueeze(2).broadcast_to([64, 4, 8, 16])
                bi_ = bb[lo:hi, 1, g0:g0 + 4, :].unsqueeze(2).broadcast_to([64, 4, 8, 16])
                o_r = ABt[0][lo:hi, gl0:gl0 + 4, :].rearrange("p g (s m) -> p g s m", m=16)
                o_i = ABt[1][lo:hi, gl0:gl0 + 4, :].rearrange("p g (s m) -> p g s m", m=16)
                ta = tA[lo:hi, :].rearrange("p (g s m) -> p g s m", s=8, m=16)
                tb_ = tB[lo:hi, :].rearrange("p (g s m) -> p g s m", s=8, m=16)
                cmul(o_r, o_i, pr, pi, br_, bi_, ta, tb_, w=["AB"])

        def gen_CA(g0, gl0, CAf, CAb):
            for (lo, hi, rev, dst) in ((0, 64, False, CAf), (64, 128, True, CAb)):
                sl = slice(None, None, -1) if rev else slice(None)
                pr = pw[lo:hi, 0, g0:g0 + 4, sl].unsqueeze(3).broadcast_to([64, 4, 8, 16])
                pi = pw[lo:hi, 1, g0:g0 + 4, sl].unsqueeze(3).broadcast_to([64, 4, 8, 16])
                c_r = cm[lo:hi, 0, g0:g0 + 4, :].unsqueeze(2).broadcast_to([64, 4, 8, 16])
                c_i = cm[lo:hi, 1, g0:g0 + 4, :].unsqueeze(2).broadcast_to([64, 4, 8, 16])
                o_r = dst[0][lo:hi, gl0:gl0 + 4, :].rearrange("p g (s m) -> p g s m", m=16)
                o_i = dst[1][lo:hi, gl0:gl0 + 4, :].rearrange("p g (s m) -> p g s m", m=16)
                ta = tA[lo:hi, :].rearrange("p (g s m) -> p g s m", s=8, m=16)
                tb_ = tB[lo:hi, :].rearrange("p (g s m) -> p g s m", s=8, m=16)
                kw = dict(w=["CA"])
                tt(ta, c_r, pr, ALU.mult, **kw)
                tt(tb_, c_i, pi, ALU.mult, **kw)
                tt(o_r, ta, tb_, ALU.subtract, **kw)
                tt(ta, c_r, pi, ALU.mult, **kw)
                tt(tb_, c_i, pr, ALU.mult, **kw)
                vop(lambda e, o_i=o_i, ta=ta, tb_=tb_: e.scalar_tensor_tensor(out=o_i, in0=ta, scalar=-1.0, in1=tb_, op0=ALU.mult, op1=ALU.subtract), **kw)

        def shuffle_group(cc, gl):
            g = 8 * cc + gl
            bank = g % 2
            for s_ in range(8):
                op("pe", lambda e, s_=s_: e.matmul(PS[bank][:, 0:256], lhsT=Zt[:, gl, (7 - s_) * 16:(7 - s_) * 16 + 128],
                                                   rhs=uT[:, cc, :].rearrange("p (b s) -> p b s", s=8)[:, :, s_], start=(s_ == 0), stop=(s_ == 7)),
                   r=["Zt", ("uT", cc)], w=[psk(bank)])
            op("act", lambda e: e.activation(out=U[:, g, :], in_=PS[bank][:, 0:256], func=AF.Copy), r=[psk(bank)], w=[("U", g)])

        def abt_chunk():
            for ri in range(2):
                bank = 2 + ri
                pb = PS[bank][:].bitcast(BF16)
                for gl in range(8):
                    op("pe", lambda e, gl=gl, pb=pb, ri=ri: e.transpose(out=pb[:, gl * 128:(gl + 1) * 128], in_=AB[ri][:, gl, :], identity=ident_b[:]), r=["AB", "pp", "ident_b"], w=[psk(bank)])
                op("dve", lambda e, pb=pb, ri=ri: e.tensor_copy(out=ABT[ri], in_=pb.rearrange("p (a b) -> p a b", b=128)), r=[psk(bank)], w=["ABT"])

        def vprime_group(cc, gl):
            g = 8 * cc + gl
            for ri in range(2):
                bank = 4 + ri
                op("pe", lambda e, ri=ri, bank=bank: e.matmul(PS[bank][:, 0:256], lhsT=ABT[ri][:, gl, :], rhs=U[:, g, :], start=True, stop=True), r=["ABT", ("U", g)], w=[psk(bank)])
                op("act", lambda e, ri=ri, bank=bank: e.activation(out=VX[:, ri, g, :], in_=PS[bank][:, 0:256], func=AF.Copy), r=[psk(bank)], w=[("VX", g)])

        for cc in range(4):
            for gl in range(8):
                shuffle_group(cc, gl)
            gen_AB(AB, 8 * cc, 0)
            gen_AB(AB, 8 * cc + 4, 4)
            abt_chunk()
            for gl in range(8):
                vprime_group(cc, gl)
        dbg_put("U0", U[:, 0, 0:64], [("U", 0)])
        dbg_put("VX0", VX[:, :, 0, 0:64], [("VX", 0)])
        dbg_put("pw", pw[:, :, 0, :], PP)
        dbg_put("pwi", pwi[:, :, 0, :], PP)
        dbg_put("bb", bb[:, :, 0, :], PP)

        VXK = [("VX", g) for g in range(32)]
        op("dve", lambda e: e.memset(Xst, 0.0), w=["sc0x", "sc64x"])

        def scan_step(lo, hi, b):
            key = "sc%d" % lo
            xs, s1, t1_, t2_ = Xst[lo:hi], S1[lo:hi], T1[lo:hi], T2[lo:hi]
            vx = VX[lo:hi, :, :, b]
            s1sw = bass.AP(s1.tensor, s1.offset + 32, [list(s1.ap[0]), [-32, 2], [1, 32]])
            op("dve", lambda e: e.tensor_tensor(out=s1, in0=xs, in1=vx, op=ALU.add), r=VXK + [key + "x"], w=[key + "s"])
            op("dve", lambda e: e.tensor_tensor(out=t1_, in0=P1s[lo:hi], in1=s1, op=ALU.mult), r=[key + "s", "pp"], w=[key + "a"])
            op("dve", lambda e: e.tensor_tensor(out=t2_, in0=P2s[lo:hi], in1=s1sw, op=ALU.mult), r=[key + "s", "pp"], w=[key + "b"])
            op("dve", lambda e: e.tensor_tensor(out=xs, in0=t1_, in1=t2_, op=ALU.add), r=[key + "a", key + "b"], w=[key + "x"])
            op("pool", lambda e: e.tensor_copy(out=vx, in_=xs), r=[key + "x"], w=[key + "c"])

        for b in range(256):
            scan_step(0, 64, b)
            scan_step(64, 128, 255 - b)
        dbg_put("X0", VX[:, :, 0, 0:64], ["sc0c", "sc64c"])

        AR.phase()
        AR.at("keep", 0, [85 * KB // 2], BF16)
        AB2 = [AR.at("AB2_%d" % i, 85 * KB + i * 2 * KB, [8, 128], BF16) for i in range(2)]
        CAf = [AR.at("CAf%d" % i, 89 * KB + i * 2 * KB, [8, 128], BF16) for i in range(2)]
        AR.at("keep2", 93 * KB, [(104 - 93) * KB // 2], BF16)
        CAb = [bm.rearrange("p a b c -> p (a b c)")[:, i * 512:(i + 1) * 512].bitcast(BF16).rearrange("p (a b) -> p a b", b=128) for i in range(2)]
        W0 = sv["lr"].tensor and AR.base[:, (64 * KB + 4096) // 2:(64 * KB + 4096 + 2048) // 2].rearrange("p (a b) -> p a b", b=128)
        for i in range(2):
            op("pool", lambda e, i=i: e.memset(CAf[i][64:128], 0.0), w=["CA"])
            op("pool", lambda e, i=i: e.memset(CAb[i][0:64], 0.0), w=["CA"])
        mF_ = s5m[:, 0, :]
        mB_ = s5m[:, 1, :]

        def w0_chunk():
            for half in range(2):
                pf, pb_ = PS[2], PS[3]
                for jj in range(4):
                    gl = half * 4 + jj
                    o1 = pf[:, jj * 128:(jj + 1) * 128]
                    o2 = pb_[:, jj * 128:(jj + 1) * 128]
                    op("pe", lambda e, gl=gl, o1=o1: e.matmul(o1, lhsT=AB2[0][:, gl, :], rhs=CAf[0][:, gl, :], start=True, stop=False), r=["AB", "CA", "pp"], w=[psk(2)])
                    op("pe", lambda e, gl=gl, o1=o1: e.matmul(o1, lhsT=AB2[1][:, gl, :], rhs=CAf[1][:, gl, :], start=False, stop=True), r=["AB", "CA", "pp"], w=[psk(2)])
                    op("pe", lambda e, gl=gl, o2=o2: e.matmul(o2, lhsT=AB2[0][:, gl, :], rhs=CAb[0][:, gl, :], start=True, stop=False), r=["AB", "CA", "pp"], w=[psk(3)])
                    op("pe", lambda e, gl=gl, o2=o2: e.matmul(o2, lhsT=AB2[1][:, gl, :], rhs=CAb[1][:, gl, :], start=False, stop=True), r=["AB", "CA", "pp"], w=[psk(3)])
                t3 = tA.rearrange("p (a b) -> p a b", b=128)
                op("dve", lambda e, t3=t3: e.tensor_tensor(out=t3, in0=PS[2][:].rearrange("p (a b) -> p a b", b=128), in1=mF_.unsqueeze(1).broadcast_to([128, 4, 128]), op=ALU.mult),
                   r=[psk(2), "s5m", "pp"], w=["pp"])
                op("dve", lambda e: e.tensor_tensor(out=tB.rearrange("p (a b) -> p a b", b=128), in0=PS[3][:].rearrange("p (a b) -> p a b", b=128), in1=mB_.unsqueeze(1).broadcast_to([128, 4, 128]), op=ALU.mult),
                   r=[psk(3), "s5m", "pp"], w=["pp"])
                op("dve", lambda e, half=half: e.tensor_tensor(out=W0[:, half * 4:(half + 1) * 4, :], in0=tA.rearrange("p (a b) -> p a b", b=128), in1=tB.rearrange("p (a b) -> p a b", b=128), op=ALU.add),
                   r=["pp"], w=["W0", "pp"])

        SCK = ["sc0c", "sc64c"]

        def y_group(cc, gl):
            g = 8 * cc + gl
            bank = 4 + g % 2
            o_ = PS[bank]
            rk = ["W0", "CA", ("U", g), ("VX", g)] + SCK
            op("pe", lambda e: e.matmul(o_[:, 0:256], lhsT=W0[:, gl, :], rhs=U[:, g, :], start=True, stop=False), r=rk, w=[psk(bank)])
            op("pe", lambda e: e.matmul(o_[:, 1:256], lhsT=CAf[0][:, gl, :], rhs=VX[:, 0, g, 0:255], start=False, stop=False), r=rk, w=[psk(bank)])
            op("pe", lambda e: e.matmul(o_[:, 1:256], lhsT=CAf[1][:, gl, :], rhs=VX[:, 1, g, 0:255], start=False, stop=False), r=rk, w=[psk(bank)])
            op("pe", lambda e: e.matmul(o_[:, 0:255], lhsT=CAb[0][:, gl, :], rhs=VX[:, 0, g, 1:256], start=False, stop=False), r=rk, w=[psk(bank)])
            op("pe", lambda e: e.matmul(o_[:, 0:255], lhsT=CAb[1][:, gl, :], rhs=VX[:, 1, g, 1:256], start=False, stop=True), r=rk, w=[psk(bank)])
            op("act", lambda e: e.activation(out=U[:, g, :], in_=o_[:, 0:256], func=AF.Copy), r=[psk(bank)], w=[("U", g)])

        def unshuffle(cc, tl):
            bank = tl % 2
            for gl in range(8):
                g = 8 * cc + gl
                op("pe", lambda e, gl=gl, g=g: e.matmul(PS[bank][:, 0:256], lhsT=Zt[:, tl, (7 - gl) * 16:(7 - gl) * 16 + 128], rhs=U[:, g, :], start=(gl == 0), stop=(gl == 7)),
                   r=["Zt", ("U", g)], w=[psk(bank)])
            uv = uT[:, cc, :].rearrange("p (b s) -> p b s", s=8)[:, :, tl]
            op("dve", lambda e: e.scalar_tensor_tensor(out=uv, in0=uv, scalar=dsk[:, cc:cc + 1], in1=PS[bank][:, 0:256], op0=ALU.mult, op1=ALU.add),
               r=[psk(bank), ("uT", cc), "pp"], w=[("uT", cc)])

        for cc in range(4):
            for hh in range(2):
                gen_AB(AB2, 8 * cc + 4 * hh, 4 * hh)
                gen_CA(8 * cc + 4 * hh, 4 * hh, CAf, CAb)
            w0_chunk()
            for gl in range(8):
                y_group(cc, gl)
            for tl in range(8):
                unshuffle(cc, tl)
        dbg_put("y", uT[:, :, 0:512], [("uT", cc) for cc in range(4)])

        AR.phase()
        AR.at("uT", 32 * KB, [4, L], BF16)
        bT = AR.at("bT", 0, [4, L], BF16)
        g1 = AR.at("g1", 16 * KB, [L], F32)
        g2 = AR.at("g2", 24 * KB, [L], F32)
        CG = math.sqrt(2.0 / math.pi)

        def gelu_chunk(cc):
            y_ = uT[:, cc, :]
            op("act", lambda e: e.activation(out=g1, in_=y_, func=AF.Square), r=[("uT", cc)], w=["g1"])
            op("dve", lambda e: e.tensor_scalar(out=g1, in0=g1, scalar1=0.044715, scalar2=1.0, op0=ALU.mult, op1=ALU.add), r=["g1"], w=["g1"])
            op("dve", lambda e: e.tensor_tensor(out=g1, in0=g1, in1=y_, op=ALU.mult), r=["g1", ("uT", cc)], w=["g1"])
            op("act", lambda e: e.activation(out=g2, in_=g1, func=AF.Sigmoid, scale=2.0 * CG), r=["g1"], w=["g2"])
            op("dve", lambda e: e.tensor_tensor(out=y_, in0=y_, in1=g2, op=ALU.mult), r=["g2", ("uT", cc)], w=[("uT", cc)])

        for cc in range(4):
            gelu_chunk(cc)

        def glu(tb, oc, kq):
            bank = kq % 2
            for kc in range(4):
                op("pe", lambda e, kc=kc: e.matmul(PS[bank][:], lhsT=wglu[:, kc, oc * 128:(oc + 1) * 128], rhs=uT[:, kc, tb * 512:(tb + 1) * 512], start=(kc == 0), stop=(kc == 3)),
                   r=[("wb", 1), ("uT", kc)], w=[psk(bank)])
            gsl = g1[:, (kq % 4) * 512:(kq % 4 + 1) * 512]
            op("act", lambda e: e.activation(out=gsl, in_=PS[bank][:], func=AF.Sigmoid, bias=bgl[:, oc:oc + 1]), r=[psk(bank), "pp"], w=[("gs", kq % 4)])
            op("dve", lambda e: e.tensor_tensor(out=bT[:, oc, tb * 512:(tb + 1) * 512], in0=uT[:, oc, tb * 512:(tb + 1) * 512], in1=gsl, op=ALU.mult),
               r=[("gs", kq % 4), ("uT", oc)], w=[("bT", oc, tb)])

        P.barrier()
        kq = 0
        for tb in range(4):
            for oc in range(4):
                glu(tb, oc, kq)
                kq += 1
        dbg_put("bT", bT[:, :, 0:512], [("bT", oc, 0) for oc in range(4)])

        def oproj2(tb, dc, kq):
            bank = 2 + kq % 2
            for hc in range(4):
                op("pe", lambda e, hc=hc: e.matmul(PS[bank][:], lhsT=wo2[:, hc, dc * 128:(dc + 1) * 128], rhs=bT[:, hc, tb * 512:(tb + 1) * 512], start=(hc == 0), stop=(hc == 3)),
                   r=[("wb", 2), ("bT", hc, tb)], w=[psk(bank)])
            xs = xT[:, dc, tb * 512:(tb + 1) * 512]
            op("dve", lambda e: e.tensor_tensor(out=xs, in0=xs, in1=PS[bank][:], op=ALU.add), r=[psk(bank), xk(dc, tb)], w=[xk(dc, tb)])

        if not dbg:
            kq = 0
            for tb in range(4):
                for dc in range(8):
                    oproj2(tb, dc, kq)
                    kq += 1

    def hgrn_phase(gi):
        w_in = w_in_odd_d.rearrange("(c p) f -> p c f", p=128)
        w_out = w_out_odd_d.rearrange("(h p) d -> p h d", p=128)
        AR.phase()
        hT = AR.at("hT", 0, [8, L], BF16)
        sq = [AR.at("sq%d" % i, 44 * KB + i * 4 * KB, [L], BF16) for i in range(2)]
        rstd = AR.at("rstd", 52 * KB, [L], F32)
        norm_to_hT(gi, hT, rstd, sq)
        AR.phase()
        hT = AR.at("hT", 0, [8, L], BF16)
        qT = AR.at("qT", 32 * KB, [L], BF16)
        sigG = AR.at("sigG", 36 * KB, [L], BF16)
        vtok = AR.at("vtok", 40 * KB, [16, 128], BF16)
        HS = []
        for i_ in range(2):
            b0 = 44 * KB + i_ * 22 * KB
            HS.append(dict(i=i_,
                           A=AR.at("A%d" % i_, b0, [1024], F32), B=AR.at("B%d" % i_, b0 + 4 * KB, [1024], F32),
                           kk=AR.at("kk%d" % i_, b0 + 8 * KB, [1024], BF16), qdec=AR.at("qdec%d" % i_, b0 + 10 * KB, [1024], BF16),
                           kinv=AR.at("kinv%d" % i_, b0 + 12 * KB, [1024], BF16), kend=AR.at("kend%d" % i_, b0 + 14 * KB, [8, 128], BF16),
                           scT=AR.at("scT%d" % i_, b0 + 16 * KB, [8, 128], BF16), Sb=AR.at("Sb%d" % i_, b0 + 18 * KB, [16, 128], BF16),
                           pbank=(0, 1) if i_ == 0 else (2, 3), xbank=4 + i_))
        oacc = AR.at("oacc", 88 * KB, [16, 128], F32)
        mF = AR.at("mF", 96 * KB, [L], BF16)
        decs = [AR.at("dec%d" % i, 100 * KB + i * 64, [16], F32) for i in range(2)]
        ssq = AR.at("ssq", 100 * KB + 128, [16], F32)
        SstD = [[AR.at("Sst%d_%d" % (d_, i), 100 * KB + 256 + (2 * d_ + i) * 512, [128], F32) for i in range(2)] for d_ in range(2)]
        on_tok = AR.base[:, 44 * KB // 2: 48 * KB // 2].rearrange("p (a b) -> p a b", b=128)
        mT = AR.base[:, 48 * KB // 2: 52 * KB // 2]

        op("sp", lambda e: e.dma_start(out=lbl[:], in_=lbl_d), w=["lbl"], dma=True)
        op("sp", lambda e: e.dma_start(out=hg[:], in_=hg_d), w=["hg"], dma=True)
        op("sp", lambda e: e.dma_start(out=hmask[:], in_=hmask_d), w=["hmask"], dma=True)
        op("dve", lambda e: e.tensor_tensor(out=lb[:], in0=lbl[:, 1, :], in1=lbl[:, 0, :], op=ALU.subtract), r=["lbl"], w=["lb"])
        op("act", lambda e: e.activation(out=lb[:], in_=lb[:], func=AF.Sigmoid), r=["lb"], w=["lb"])
        op("dve", lambda e: e.tensor_scalar(out=oml[:], in0=lb[:], scalar1=-1.0, scalar2=1.0, op0=ALU.mult, op1=ALU.add), r=["lb"], w=["oml"])
        op("dve", lambda e: e.tensor_scalar(out=noml[:], in0=oml[:], scalar1=-1.0, scalar2=None, op0=ALU.mult), r=["oml"], w=["noml"])
        op("pool", lambda e: e.memset(mF, 1.0), w=["mF"])
        op("pool", lambda e: e.memset(mF.rearrange("p (a b) -> p a b", b=64)[:, :, 0:1], 0.0), w=["mF"])

        def load_head(h):
            sa, sb_ = (0, 1) if h % 2 == 0 else (2, 3)
            secs = []
            for i, sec in enumerate((0, 1, 2, 3)):
                dst = WB[sa][:, i * 1024:(i + 1) * 1024].rearrange("p (a b) -> p a b", b=128)
                op("pool", lambda e, dst=dst, sec=sec, h=h: e.dma_start(out=dst, in_=w_in[:, :, sec * 1024 + h * 128: sec * 1024 + (h + 1) * 128]),
                   w=[("wb", sa)], dma=True, dkey=("wbs", sa, i))
                secs.append(dst)
            dst = WB[sb_][:, 0:1024].rearrange("p (a b) -> p a b", b=128)
            op("pool", lambda e, dst=dst, h=h: e.dma_start(out=dst, in_=w_in[:, :, 4 * 1024 + h * 128: 4 * 1024 + (h + 1) * 128]),
               w=[("wb", sb_)], dma=True, dkey=("wbs", sb_, 0))
            secs.append(dst)
            wo = WB[sb_][:, 1024:2048]
            op("pool", lambda e, wo=wo, h=h: e.dma_start(out=wo, in_=w_out[:, h, :]), w=[("wb", sb_)], dma=True, dkey=("wbs", sb_, 1))
            return secs, wo, sa, sb_

        def proj_fm(wsec, slot, consume):
            for tb in range(4):
                bank = tb
                for c in range(8):
                    op("pe", lambda e, bank=bank, c=c, tb=tb: e.matmul(PS[bank][:], lhsT=wsec[:, c, :], rhs=hT[:, c, tb * 512:(tb + 1) * 512], start=(c == 0), stop=(c == 7)),
                       r=[("wb", slot), hk(c, tb)], w=[psk(bank)])
                consume(tb, bank)

        def do_head(h, cur):
            (wq, wi_, wff, wfb, wg), wo, sa, sb_ = cur
            proj_fm(wq, sa, lambda tb, bank: op("act", lambda e: e.activation(out=qT[:, tb * 512:(tb + 1) * 512], in_=PS[bank][:], func=AF.Copy), r=[psk(bank)], w=["qT"]))
            proj_fm(wg, sb_, lambda tb, bank: op("act", lambda e: e.activation(out=sigG[:, tb * 512:(tb + 1) * 512], in_=PS[bank][:], func=AF.Sigmoid), r=[psk(bank)], w=["sigG"]))
            for q4 in range(4):
                bank = 4 + q4 % 2
                for jj in range(4):
                    j = q4 * 4 + jj
                    for c in range(8):
                        op("pe", lambda e, bank=bank, jj=jj, j=j, c=c: e.matmul(PS[bank][:, jj * 128:(jj + 1) * 128], lhsT=hT[:, c, j * 128:(j + 1) * 128], rhs=wi_[:, c, :], start=(c == 0), stop=(c == 7)),
                           r=[("wb", sa), hk(c, j // 4)], w=[psk(bank)])
                op("dve", lambda e, bank=bank, q4=q4: e.tensor_copy(out=vtok[:, q4 * 4:(q4 + 1) * 4, :], in_=PS[bank][:].rearrange("p (a b) -> p a b", b=128)), r=[psk(bank)], w=["vtok"])
            if h == 0:
                dbg_put("qT", qT[:, 0:256], ["qT"])
                dbg_put("sigG", sigG[:, 0:256], ["sigG"])
                dbg_put("vtok", vtok[:, 0:2, :], ["vtok"])
            sidx = [0, 0]

            def do_dir(d, hf, S):
                si = S["i"]
                A, B, kk, qdec, kinv, kend, scT, Sb = S["A"], S["B"], S["kk"], S["qdec"], S["kinv"], S["kend"], S["scT"], S["Sb"]
                dec = decs[si]
                kA, kB, kK, kQ, kI, kE, kS, kD = ["%s%d" % (n_, si) for n_ in ("hgA", "hgB", "kk", "qdec", "kinv", "kend", "scT", "dec")]
                wf = wff if d == 0 else wfb
                T0 = hf * 1024
                for t2 in range(2):
                    tb = 2 * hf + t2
                    bank = S["pbank"][t2]
                    for c in range(8):
                        op("pe", lambda e, c=c, tb=tb, bank=bank: e.matmul(PS[bank][:], lhsT=wf[:, c, :], rhs=hT[:, c, tb * 512:(tb + 1) * 512], start=(c == 0), stop=(c == 7)),
                           r=[("wb", sa), hk(c, tb)], w=[psk(bank)])
                    op("act", lambda e, t2=t2, bank=bank: e.activation(out=A[:, t2 * 512:(t2 + 1) * 512], in_=PS[bank][:], func=AF.Sigmoid), r=[psk(bank)], w=[kA])
                    yield
                op("act", lambda e: e.activation(out=kk, in_=A, func=AF.Identity, scale=noml[:, h:h + 1], bias=oml[:, h:h + 1]), r=[kA, "noml", "oml"], w=[kK])
                op("act", lambda e: e.activation(out=A, in_=A, func=AF.Ln, scale=oml[:, h:h + 1], bias=lb[:, h:h + 1]), r=[kA, "lb", "oml"], w=[kA])
                yield
                if d == 0:
                    op("dve", lambda e: e.tensor_tensor_scan(out=B, data0=mF[:, 0:1024], data1=A, initial=0.0, op0=ALU.mult, op1=ALU.add), r=[kA, "mF"], w=[kB])
                else:
                    op("dve", lambda e: e.tensor_tensor_scan(out=B[:, ::-1], data0=mF[:, 0:1024], data1=A[:, ::-1], initial=0.0, op0=ALU.mult, op1=ALU.add), r=[kA, "mF"], w=[kB])
                yield
                op("act", lambda e: e.activation(out=A, in_=B, func=AF.Exp), r=[kB], w=[kA])
                yield
                op("dve", lambda e: e.tensor_tensor(out=qdec, in0=qT[:, T0:T0 + 1024], in1=A, op=ALU.mult), r=[kA, "qT"], w=[kQ])
                yield
                op("act", lambda e: e.activation(out=A, in_=B, func=AF.Exp, scale=-1.0), r=[kB, kQ], w=[kA])
                bl = B.rearrange("p (a b) -> p a b", b=64)[:, :, 63:64] if d == 0 else B.rearrange("p (a b) -> p a b", b=64)[:, :, 0:1]
                op("act", lambda e: e.activation(out=dec.rearrange("p (a b) -> p a b", b=1), in_=bl, func=AF.Exp), r=[kB], w=[kD])
                yield
                op("dve", lambda e: e.tensor_tensor(out=A, in0=kk, in1=A, op=ALU.mult), r=[kA, kK], w=[kA])
                yield
                op("act", lambda e: e.activation(out=kinv, in_=A, func=AF.Copy), r=[kA], w=[kI])
                dec_bc = bass.AP(dec.tensor, dec.offset, [list(dec.ap[0]), [1, 16], [0, 64]])
                op("dve", lambda e: e.tensor_tensor(out=kk.rearrange("p (a b) -> p a b", b=64), in0=A.rearrange("p (a b) -> p a b", b=64), in1=dec_bc, op=ALU.mult),
                   r=[kA, kD], w=[kK])
                yield
                xb = S["xbank"]
                pb = PS[xb][:].bitcast(BF16)
                for jj in range(8):
                    op("pe", lambda e, jj=jj: e.transpose(out=pb[:, jj * 128:(jj + 1) * 128], in_=kk[:, jj * 128:(jj + 1) * 128], identity=ident_b[:]),
                       r=[kK, "ident_b"], w=[psk(xb)])
                op("act", lambda e: e.activation(out=kend, in_=pb.rearrange("p (a b) -> p a b", b=128), func=AF.Copy), r=[psk(xb)], w=[kE])
                yield
                hm_ = hmask[:, d, :]
                mk = bass.AP(hm_.tensor, hm_.offset, [list(hm_.ap[0]), [0, 4], [1, 128]])
                for q4 in range(2):
                    bank = S["pbank"][q4]
                    for jj in range(4):
                        jl = q4 * 4 + jj
                        op("pe", lambda e, bank=bank, jj=jj, jl=jl: e.matmul(PS[bank][:, jj * 128:(jj + 1) * 128], lhsT=kinv[:, jl * 128:(jl + 1) * 128], rhs=qdec[:, jl * 128:(jl + 1) * 128], start=True, stop=True),
                           r=[kI, kQ], w=[psk(bank)])
                    op("dve", lambda e, bank=bank, q4=q4: e.tensor_tensor(out=scT[:, q4 * 4:(q4 + 1) * 4, :], in0=PS[bank][:].rearrange("p (a b) -> p a b", b=128), in1=mk, op=ALU.mult),
                       r=[psk(bank), "hmask"], w=[kS])
                    yield
                order = list(range(16)) if d == 0 else list(range(15, -1, -1))
                for cl in order:
                    n = sidx[d]
                    sidx[d] += 1
                    cur, new = SstD[d][n % 2], SstD[d][(n + 1) % 2]
                    ci = hf * 16 + cl
                    par = ci % 2
                    bank = 6 + par
                    op("act", lambda e, cur=cur, cl=cl: e.activation(out=Sb[:, cl, :], in_=cur, func=AF.Copy), r=[("Sst", d, n % 2)], w=[("Sb", si, cl)])
                    op("pe", lambda e, bank=bank, cl=cl, par=par: e.matmul(PS[bank][:, 0:128], lhsT=kend[par * 64:(par + 1) * 64, cl // 2, :], rhs=vtok[par * 64:(par + 1) * 64, hf * 8 + cl // 2, :], start=True, stop=True),
                       r=[kE, "vtok"], w=[psk(bank)])
                    op("dve", lambda e, bank=bank, cur=cur, new=new, cl=cl: e.scalar_tensor_tensor(out=new, in0=cur, scalar=dec[:, cl:cl + 1], in1=PS[bank][:, 0:128], op0=ALU.mult, op1=ALU.add),
                       r=[psk(bank), ("Sst", d, n % 2), kD], w=[("Sst", d, (n + 1) % 2)])
                    if cl % 2 == 1:
                        yield
                first = (d == 0 and hf == 0) or (d == 1 and hf == 1)
                for q4 in range(2):
                    for jj in range(4):
                        jl = q4 * 4 + jj
                        j = hf * 8 + jl
                        o_ = PS[xb][:, jj * 128:(jj + 1) * 128]
                        op("pe", lambda e, o_=o_, jl=jl, j=j: e.matmul(o_, lhsT=scT[:, jl, :], rhs=vtok[:, j, :], start=True, stop=False), r=[kS, "vtok"], w=[psk(xb)])
                        op("pe", lambda e, jj=jj, jl=jl: e.matmul(PS[xb][0:64, jj * 128:(jj + 1) * 128], lhsT=qdec[:, jl * 128:jl * 128 + 64], rhs=Sb[:, 2 * jl, :], start=False, stop=False),
                           r=[kQ, ("Sb", si, 2 * jl)], w=[psk(xb)])
                        op("pe", lambda e, jj=jj, jl=jl: e.matmul(PS[xb][64:128, jj * 128:(jj + 1) * 128], lhsT=qdec[:, jl * 128 + 64:jl * 128 + 128], rhs=Sb[:, 2 * jl + 1, :], start=False, stop=True),
                           r=[kQ, ("Sb", si, 2 * jl + 1)], w=[psk(xb)])
                    ov = oacc[:, hf * 8 + q4 * 4:hf * 8 + (q4 + 1) * 4, :]
                    pv = PS[xb][:].rearrange("p (a b) -> p a b", b=128)
                    if first:
                        op("dve", lambda e, ov=ov, pv=pv: e.tensor_copy(out=ov, in_=pv), r=[psk(xb)], w=[("oacc", hf)])
                    else:
                        op("dve", lambda e, ov=ov, pv=pv: e.tensor_tensor(out=ov, in0=ov, in1=pv, op=ALU.add), r=[psk(xb), ("oacc", hf)], w=[("oacc", hf)])
                    yield

            def interleave(gens):
                alive = list(gens)
                while alive:
                    for g_ in list(alive):
                        try:
                            next(g_)
                        except StopIteration:
                            alive.remove(g_)

            op("pool", lambda e: e.memset(SstD[0][0], 0.0), w=[("Sst", 0, 0)])
            op("pool", lambda e: e.memset(SstD[1][0], 0.0), w=[("Sst", 1, 0)])
            interleave([do_dir(0, 0, HS[0]), do_dir(1, 1, HS[1])])
            interleave([do_dir(0, 1, HS[0]), do_dir(1, 0, HS[1])])
            OACC = [("oacc", 0), ("oacc", 1)]
            HGA = ["hgA0"]
            HGB = ["hgB0"]
            for j in range(16):
                op("act", lambda e, j=j: e.activation(out=on_tok[:, j, :], in_=oacc[:, j, :], func=AF.Square, accum_out=ssq[:, j:j + 1]), r=OACC + HGB, w=HGA + ["ssq"])
            op("act", lambda e: e.activation(out=ssq, in_=ssq, func=AF.Sqrt, scale=1.0 / 128, bias=EPS), r=["ssq"], w=["ssq"])
            op("dve", lambda e: e.reciprocal(out=ssq, in_=ssq), r=["ssq"], w=["ssq"])
            for j in range(16):
                op("dve", lambda e, j=j: e.tensor_scalar(out=on_tok[:, j, :], in0=oacc[:, j, :], scalar1=ssq[:, j:j + 1], scalar2=None, op0=ALU.mult), r=OACC + ["ssq"] + HGA, w=HGA)
            for half in range(2):
                bank = half
                pb = PS[bank][:].bitcast(BF16)
                for jj in range(8):
                    j = half * 8 + jj
                    op("pe", lambda e, pb=pb, jj=jj, j=j: e.transpose(out=pb[:, jj * 128:(jj + 1) * 128], in_=on_tok[:, j, :], identity=ident_b[:]), r=HGA + ["ident_b"], w=[psk(bank)])
                op("dve", lambda e, pb=pb, half=half: e.scalar_tensor_tensor(out=mT[:, half * 1024:(half + 1) * 1024], in0=pb, scalar=hg[:, h:h + 1], in1=sigG[:, half * 1024:(half + 1) * 1024], op0=ALU.mult, op1=ALU.mult),
                   r=[psk(bank), "hg", "sigG"] + HGA, w=HGB)
            if h == 0:
                dbg_put("mT", mT[:, 0:256], HGB)
            if dbg:
                return
            kq = 0
            for tb in range(4):
                for dc in range(8):
                    bank = 2 + kq % 2
                    op("pe", lambda e, bank=bank, dc=dc, tb=tb: e.matmul(PS[bank][:], lhsT=wo[:, dc * 128:(dc + 1) * 128], rhs=mT[:, tb * 512:(tb + 1) * 512], start=True, stop=True),
                       r=[("wb", sb_)] + HGB, w=[psk(bank)])
                    xs = xT[:, dc, tb * 512:(tb + 1) * 512]
                    op("dve", lambda e, xs=xs, bank=bank: e.tensor_tensor(out=xs, in0=xs, in1=PS[bank][:], op=ALU.add), r=[psk(bank), xk(dc, tb)], w=[xk(dc, tb)])
                    kq += 1

        nxt = load_head(0)
        for h in range(8):
            cur = nxt
            if h + 1 < 8:
                nxt = load_head(h + 1)
            do_head(h, cur)
            if dbg:
                break

    if "l0attn" in parts or "l0mix" in parts:
        attn_phase(0)
    if "l0s5" in parts or "l0mix" in parts:
        s5_phase()
    if "l0ffn" in parts:
        ffn_phase(0, 1)
    if "l1mix" in parts:
        hgrn_phase(2)
    if "l1ffn" in parts:
        ffn_phase(1, 3)

    if dbg:
        P.barrier()
        op("sp", lambda e: e.dma_start(out=out_d.rearrange("(p a) d -> p (a d)", p=128), in_=xT_flat), dma=True, dkey="dbgout")
        P.emit()
        return nc
    AR.phase()
    sq = [AR.at("sq%d" % i, i * 4 * KB, [L], BF16) for i in range(2)]
    rstd = AR.at("rstd", 8 * KB, [L], F32)
    stage = [AR.at("stage%d" % i, 16 * KB + i * 4 * KB, [D], F32) for i in range(2)]
    ftmp = [AR.at("ftmp%d" % i, 24 * KB + i * 512, [128], F32) for i in range(4)]
    do_final = "final" in parts
    if do_final:
        rmsnorm_rstd(rstd, sq)
    k = 0
    for t in range(16):
        st = stage[t % 2]
        for half in range(2):
            bank = (2 * t + half) % 4
            for j in range(4):
                c = half * 4 + j
                src = xT[:, c, t * 128:(t + 1) * 128]
                if do_final:
                    ft = ftmp[k % 4]
                    op("dve", lambda e, ft=ft, src=src, c=c, t=t: e.scalar_tensor_tensor(
                        out=ft, in0=src, scalar=gains[:, 4, c:c + 1], in1=rstd[:, t * 128:(t + 1) * 128], op0=ALU.mult, op1=ALU.mult),
                       r=[xk(c, t // 4), ("rstd", t // 4), "gains"], w=[("ftmp", k % 4)])
                    op("pe", lambda e, bank=bank, j=j, ft=ft: e.transpose(out=PS[bank][:, j * 128:(j + 1) * 128], in_=ft, identity=ident_f[:]),
                       r=[("ftmp", k % 4), "ident_f"], w=[psk(bank)])
                    k += 1
                else:
                    op("pe", lambda e, bank=bank, j=j, src=src: e.transpose(out=PS[bank][:, j * 128:(j + 1) * 128], in_=src, identity=ident_f[:]),
                       r=[xk(c, t // 4), "ident_f"], w=[psk(bank)])
            dst = st[:, half * 512:(half + 1) * 512]
            op("act", lambda e, dst=dst, bank=bank: e.activation(out=dst, in_=PS[bank][:], func=AF.Copy), r=[psk(bank)], w=[("stage", t % 2, half)])
        op("sp", lambda e, st=st, t=t: e.dma_start(out=out_d[t * 128:(t + 1) * 128, :], in_=st), r=[("stage", t % 2, 0), ("stage", t % 2, 1)], dma=True, dkey=("st", t % 2))
    P.emit()
    return nc


def host_consts():
    c = {}
    c["ident"] = np.eye(128, dtype=np.float32)
    inv = (10000.0 ** (-np.arange(0, 64, 2, dtype=np.float32) / np.float32(64))).astype(np.float32)
    ang = (np.arange(L, dtype=np.float32)[None, :] * inv[:, None]).astype(np.float32)
    c["rope_cos"] = np.ascontiguousarray(np.tile(np.cos(ang).astype(np.float32), (4, 1)))
    c["rope_sin"] = np.ascontiguousarray(np.tile(np.sin(ang).astype(np.float32), (4, 1)))
    s_ = np.arange(128)[:, None]
    c_ = np.arange(128)[None, :]
    same = (s_ // 64) == (c_ // 64)
    hm = np.stack([(same & (s_ <= c_)), (same & (s_ >= c_))], axis=1).astype(np.float32)
    c["hmask"] = np.ascontiguousarray(hm)
    sl_ = (np.arange(128) // 16)[:, None]
    tl_ = (np.arange(128) // 16)[None, :]
    c["s5m"] = np.ascontiguousarray(np.stack([(tl_ >= sl_), (tl_ <= sl_)], axis=1).astype(np.float32))
    return c


def make_in_maps(inputs, parts=None):
    consts = host_consts()
    gains = np.concatenate([inputs["norm_mix_g"][0:1], inputs["norm_mlp_g"][0:1], inputs["norm_mix_g"][1:2],
                            inputs["norm_mlp_g"][1:2], inputs["final_norm_g"][None, :]], axis=0).astype(np.float32)
    shared = dict(consts)
    shared["gains"] = np.ascontiguousarray(gains.reshape(5, 8, 128).transpose(2, 0, 1))
    shared["w_in_odd"] = np.ascontiguousarray(inputs["w_in_odd"][0], dtype=np.float32)
    shared["w_out_odd"] = np.ascontiguousarray(inputs["w_out_odd"][0], dtype=np.float32)
    shared["lbl"] = np.ascontiguousarray(np.asarray(inputs["hgrn_lb_logits"], dtype=np.float32).reshape(2, 8, 128).transpose(2, 0, 1))
    shared["hg"] = np.ascontiguousarray(np.asarray(inputs["hgrn_norm_g"][0], dtype=np.float32).reshape(8, 128).T)
    shared["w_in_even"] = np.ascontiguousarray(inputs["w_in_even"][0], dtype=np.float32)
    shared["w_out_even"] = np.ascontiguousarray(inputs["w_out_even"][0], dtype=np.float32)
    shared["diff_lambda"] = np.ascontiguousarray(np.asarray(inputs["diff_lambda"][0], dtype=np.float32).reshape(1, 256))
    shared["subln_g"] = np.ascontiguousarray(np.asarray(inputs["diff_subln_g"][0], dtype=np.float32).reshape(128, 1))
    f32 = np.float32
    lre = np.asarray(inputs["s5_lam_re"][0], f32); lim = np.asarray(inputs["s5_lam_im"][0], f32); lst = np.asarray(inputs["s5_log_step"][0], f32)
    s5s = np.stack([lre.transpose(0, 2, 1).reshape(128, 32), lim.transpose(0, 2, 1).reshape(128, 32),
                    np.repeat(lst[:, None, :], 64, axis=1).reshape(128, 32)], axis=1)
    shared["s5s"] = np.ascontiguousarray(s5s, dtype=f32)
    shared["s5b"] = np.ascontiguousarray(np.stack([np.asarray(inputs[k][0], f32).transpose(0, 2, 1, 3).reshape(128, 32, 16) for k in ("s5_b_re", "s5_b_im")], axis=1))
    shared["s5c"] = np.ascontiguousarray(np.stack([np.asarray(inputs[k][0], f32).transpose(0, 3, 1, 2).reshape(128, 32, 16) for k in ("s5_c_re", "s5_c_im")], axis=1))
    shared["s5d"] = np.ascontiguousarray(np.asarray(inputs["s5_d"][0], f32).reshape(4, 128).T)
    shared["s5bg"] = np.ascontiguousarray(np.asarray(inputs["s5_b_glu"][0], f32).reshape(4, 128).T)
    shared["w_glu"] = np.ascontiguousarray(inputs["s5_w_glu"][0], dtype=f32)
    shared["w_ff_in"] = np.ascontiguousarray(inputs["w_ff_in"], dtype=np.float32)
    shared["w_ff_out"] = np.ascontiguousarray(inputs["w_ff_out"], dtype=np.float32)
    x = np.asarray(inputs["x"], dtype=np.float32)
    maps = []
    for b in range(x.shape[0]):
        m = dict(shared)
        m["x"] = np.ascontiguousarray(x[b])
        maps.append(m)
    return maps


ALL_PARTS = {"l0mix", "l0ffn", "l1mix", "l1ffn", "final"}
_NC_CACHE = {}


def kernel(**inputs):
    inputs = {k: np.asarray(v) for k, v in inputs.items()}
    key = "full"
    if key not in _NC_CACHE:
        _NC_CACHE[key] = build({"parts": ALL_PARTS})
    nc = _NC_CACHE[key]
    maps = make_in_maps(inputs)
    res = run_bass_kernel_spmd(nc, maps, core_ids=list(range(8)))
    out = np.stack([np.asarray(r["out"]) for r in res.results], axis=0)
    return out.astype(np.float32)
```

```python
import math
from contextlib import ExitStack

import numpy as np
import concourse.bass as bass
import concourse.mybir as mybir
from concourse.bass_utils import run_bass_kernel_spmd

F32 = mybir.dt.float32
BF16 = mybir.dt.bfloat16
AF = mybir.ActivationFunctionType
ALU = mybir.AluOpType
AX = mybir.AxisListType

ENGS = ("pe", "act", "dve", "pool", "sp")

L = 2048
D = 1024
NTB = 4
EPS = 1e-6
LAMBDA_INIT0 = 0.8 - 0.6 * math.exp(-0.3 * 0)


class Op:
    __slots__ = ("eng", "fn", "deps", "is_dma", "dkey", "signal", "sem", "val", "idx")


class Prog:
    def __init__(self, nc):
        self.nc = nc
        self.ops = []
        self.last_w = {}
        self.readers = {}
        self.stack = ExitStack()
        self.pending_barrier = {}

    def sb(self, name, shape, dt):
        return self.stack.enter_context(self.nc.sbuf_tensor("sb_" + name, list(shape), dt))

    def ps(self, name, shape, dt=F32):
        return self.stack.enter_context(self.nc.psum_tensor("pp_" + name, list(shape), dt))

    def barrier(self):
        deps = set()
        last = {}
        for o in self.ops:
            if o.is_dma:
                deps.add(o.idx)
            else:
                last[o.eng] = o.idx
        deps.update(last.values())
        self.pending_barrier = {e: set(deps) for e in ENGS}
        self.last_w = {}
        self.readers = {}

    def op(self, eng, fn, r=(), w=(), dma=False, dkey=None):
        o = Op()
        o.eng, o.fn, o.is_dma, o.signal = eng, fn, dma, False
        o.idx = len(self.ops)
        deps = set()
        for k in list(r) + list(w):
            if k in self.last_w:
                deps.add(self.last_w[k])
        for k in w:
            for rd in self.readers.get(k, ()):
                deps.add(rd)
        if self.pending_barrier.get(eng):
            deps |= self.pending_barrier[eng]
            self.pending_barrier[eng] = set()
        deps.discard(o.idx)
        o.deps = deps
        o.dkey = (dkey if dkey is not None else (w[0] if len(w) else r[0])) if dma else None
        for k in r:
            self.readers.setdefault(k, []).append(o.idx)
        for k in w:
            self.last_w[k] = o.idx
            self.readers[k] = []
        self.ops.append(o)
        return o.idx

    def _skip(self, od, o):
        return (not od.is_dma) and (not o.is_dma) and od.eng == o.eng and o.eng == "pe"

    def emit(self, final_eng="sp"):
        nc = self.nc
        ops = self.ops
        final_deps = [o.idx for o in ops if o.is_dma]
        for o in ops:
            for d in o.deps:
                if not self._skip(ops[d], o):
                    ops[d].signal = True
        for d in final_deps:
            ops[d].signal = True
        sems = {}

        def get_sem(key):
            if key not in sems:
                sems[key] = self.stack.enter_context(nc.semaphore("s%d" % len(sems)))
            return sems[key]

        cnt = {}
        for o in ops:
            if not o.signal:
                continue
            key = ("dma", o.dkey) if o.is_dma else ("eng", o.eng)
            inc = 16 if o.is_dma else 1
            cnt[key] = cnt.get(key, 0) + inc
            o.sem = get_sem(key)
            o.val = cnt[key]
            o.dkey = key
        self.n_sems = len(sems)
        per_eng = {e: [] for e in ENGS}
        for o in ops:
            per_eng[o.eng].append(o)

        def run(eng_name, e):
            waited = {}
            for o in per_eng[eng_name]:
                need = {}
                for d in o.deps:
                    od = ops[d]
                    if (not od.signal) or self._skip(od, o):
                        continue
                    if waited.get(od.dkey, 0) >= od.val:
                        continue
                    if need.get(od.dkey, (None, 0))[1] < od.val:
                        need[od.dkey] = (od.sem, od.val)
                for k, (sem, val) in need.items():
                    e.wait_ge(sem, val)
                    waited[k] = val
                ins = o.fn(e)
                if o.signal:
                    ins.then_inc(o.sem, 16 if o.is_dma else 1)
            if eng_name == final_eng:
                need = {}
                for d in final_deps:
                    od = ops[d]
                    if waited.get(od.dkey, 0) >= od.val:
                        continue
                    if need.get(od.dkey, (None, 0))[1] < od.val:
                        need[od.dkey] = (od.sem, od.val)
                for k, (sem, val) in need.items():
                    e.wait_ge(sem, val)

        with nc.Block() as block:
            @block.tensor
            def _(e):
                run("pe", e)

            @block.scalar
            def _(e):
                run("act", e)

            @block.vector
            def _(e):
                run("dve", e)

            @block.gpsimd
            def _(e):
                run("pool", e)

            @block.sync
            def _(e):
                run("sp", e)
        self.stack.close()


class Arena:
    def __init__(self, P, nbytes):
        self.P = P
        self.nbytes = nbytes
        self.base = P.sb("arena", [128, nbytes // 2], BF16)
        self.live = []

    def phase(self):
        self.P.barrier()
        self.live = []

    def at(self, name, off, shape, dt):
        n = 1
        for s in shape:
            n *= s
        nb = n * (4 if dt == F32 else 2)
        assert off % 4 == 0 and off + nb <= self.nbytes, (name, off, nb, self.nbytes)
        for (a, b, nm) in self.live:
            assert off >= b or off + nb <= a, ("arena overlap", name, nm)
        self.live.append((off, off + nb, name))
        ap = self.base[:, off // 2:(off + nb) // 2]
        if dt == F32:
            ap = ap.bitcast(F32)
        if len(shape) > 1:
            names = "abcd"[:len(shape)]
            pat = "p (" + " ".join(names) + ") -> p " + " ".join(names)
            ap = ap.rearrange(pat, **{names[i]: shape[i] for i in range(len(shape))})
        return ap

    def drop(self, name):
        self.live = [x for x in self.live if x[2] != name]


KB = 1024


def build(cfg):
    parts = cfg["parts"]
    nc = bass.Bass("TRN2", target_bir_lowering=False)

    def din(name, shape):
        return nc.dram_tensor(name, list(shape), F32, kind="ExternalInput").ap()

    x_d = din("x", [L, D])
    out_d = nc.dram_tensor("out", [L, D], F32, kind="ExternalOutput").ap()
    gains_d = din("gains", [128, 5, 8])
    w_ff_in_d = din("w_ff_in", [2, D, 4 * D])
    w_ff_out_d = din("w_ff_out", [2, 4 * D, D])
    ident_d = din("ident", [128, 128])
    w_in_odd_d = din("w_in_odd", [D, 5 * D])
    w_in_even_d = din("w_in_even", [D, 2 * D])
    w_out_even_d = din("w_out_even", [D, D])
    cos_d = din("rope_cos", [128, L])
    sin_d = din("rope_sin", [128, L])
    dl_d = din("diff_lambda", [1, 256])
    sg_d = din("subln_g", [128, 1])
    w_glu_d = din("w_glu", [512, 512])
    s5s_d = din("s5s", [128, 3, 32])
    s5b_d = din("s5b", [128, 2, 32, 16])
    s5c_d = din("s5c", [128, 2, 32, 16])
    s5d_d = din("s5d", [128, 4])
    s5bg_d = din("s5bg", [128, 4])
    s5m_d = din("s5m", [128, 2, 128])
    w_out_odd_d = din("w_out_odd", [D, D])
    lbl_d = din("lbl", [128, 2, 8])
    hg_d = din("hg", [128, 8])
    hmask_d = din("hmask", [128, 2, 128])

    P = Prog(nc)
    op = P.op

    xT = P.sb("xT", [128, 8, L], F32)
    WB = [P.sb("wb%d" % i, [128, 4096], BF16) for i in range(4)]
    ident_f = P.sb("ident_f", [128, 128], F32)
    ident_b = P.sb("ident_b", [128, 128], BF16)
    ones_b = P.sb("ones_b", [128, 128], BF16)
    ones_f = P.sb("ones_f", [128, 128], F32)
    gains = P.sb("gains", [128, 5, 8], F32)
    PS = [P.ps("ps%d" % i, [128, 512], F32) for i in range(8)]
    lbl = P.sb("lbl", [128, 2, 8], F32)
    dl = P.sb("dl", [128, 256], F32)
    dlp = P.sb("dlp", [128, 128], F32)
    lam = P.sb("lam", [128, 4], F32)
    sg = P.sb("sg", [128, 1], F32)
    rr_g = P.sb("rr_g", [128, 8], F32)
    hg = P.sb("hg", [128, 8], F32)
    hmask = P.sb("hmask", [128, 2, 128], F32)
    lb = P.sb("lb", [128, 8], F32)
    oml = P.sb("oml", [128, 8], F32)
    noml = P.sb("noml", [128, 8], F32)
    AR = Arena(P, 104 * KB)

    def psk(i):
        return ("ps", i)

    dbg = cfg.get("dbg", False)
    dbg_tab = cfg.setdefault("dbg_tab", {})
    xT_flat = xT[:].rearrange("p a b -> p (a b)")
    dbg_off = [0]

    def dbg_put(name, ap, rkeys):
        if not dbg:
            return
        shp = list(ap.shape)
        n = 1
        for v_ in shp[1:]:
            n *= v_
        o = dbg_off[0]
        dst = xT_flat[:, o:o + n]
        if len(shp) == 3:
            dst = dst.rearrange("p (a b) -> p a b", b=shp[2])
        op("dve", lambda e: e.tensor_copy(out=dst, in_=ap), r=list(rkeys), w=["dbg"])
        dbg_tab[name] = (o, shp)
        dbg_off[0] = o + n

    def xk(c, tb):
        return ("xT", c, tb)

    def hk(c, tb):
        return ("hT", c, tb)

    XT_ALL = [xk(c, tb) for c in range(8) for tb in range(4)]
    HT_ALL = [hk(c, tb) for c in range(8) for tb in range(4)]

    op("sp", lambda e: e.dma_start(out=ident_f[:], in_=ident_d), w=["ident_f"], dma=True)
    op("dve", lambda e: e.tensor_copy(out=ident_b[:], in_=ident_f[:]), r=["ident_f"], w=["ident_b"])
    op("pool", lambda e: e.memset(ones_b[:], 1.0), w=["ones_b"])
    op("pool", lambda e: e.memset(ones_f[:], 1.0), w=["ones_f"])
    op("sp", lambda e: e.dma_start(out=gains[:], in_=gains_d), w=["gains"], dma=True)

    AR.phase()
    xin = [AR.at("xin%d" % i, i * 4 * KB, [D], F32) for i in range(2)]
    ev = 0
    for t in range(16):
        b = xin[t % 2]
        op("sp", lambda e, b=b, t=t: e.dma_start(out=b, in_=x_d[t * 128:(t + 1) * 128, :]), w=[("xin", t % 2)], dma=True)
        for half in range(2):
            bank = (2 * t + half) % 4
            for j in range(4):
                c = half * 4 + j
                op("pe", lambda e, bank=bank, j=j, c=c, b=b: e.transpose(out=PS[bank][:, j * 128:(j + 1) * 128], in_=b[:, c * 128:(c + 1) * 128], identity=ident_f[:]),
                   r=[("xin", t % 2), "ident_f"], w=[psk(bank)])
            dst = xT[:, half * 4:(half + 1) * 4, t * 128:(t + 1) * 128]
            src = PS[bank][:].rearrange("p (a b) -> p a b", b=128)
            wk = [xk(half * 4 + j, t // 4) for j in range(4)]
            if ev % 2 == 0:
                op("dve", lambda e, dst=dst, src=src: e.tensor_copy(out=dst, in_=src), r=[psk(bank)], w=wk)
            else:
                op("act", lambda e, dst=dst, src=src: e.activation(out=dst, in_=src, func=AF.Copy), r=[psk(bank)], w=wk)
            ev += 1

    def rmsnorm_rstd(rstd, sq):
        for c in range(8):
            s = sq[c % 2]
            op("act", lambda e, s=s, c=c: e.activation(out=s, in_=xT[:, c, :], func=AF.Square),
               r=[xk(c, tb) for tb in range(4)], w=[("sq", c % 2)])
            for tb in range(4):
                op("pe", lambda e, s=s, c=c, tb=tb: e.matmul(PS[tb][:], lhsT=ones_b[:], rhs=s[:, tb * 512:(tb + 1) * 512], start=(c == 0), stop=(c == 7)),
                   r=[("sq", c % 2), "ones_b"], w=[psk(tb)])
        for tb in range(4):
            sl = rstd[:, tb * 512:(tb + 1) * 512]
            op("act", lambda e, sl=sl, tb=tb: e.activation(out=sl, in_=PS[tb][:], func=AF.Sqrt, scale=1.0 / D, bias=EPS),
               r=[psk(tb)], w=[("rstd", tb)])
            op("dve", lambda e, sl=sl: e.reciprocal(out=sl, in_=sl), r=[("rstd", tb)], w=[("rstd", tb)])

    def norm_to_hT(gi, hT, rstd, sq):
        rmsnorm_rstd(rstd, sq)
        for c in range(8):
            for tb in range(4):
                op("dve", lambda e, c=c, tb=tb: e.scalar_tensor_tensor(
                    out=hT[:, c, tb * 512:(tb + 1) * 512], in0=xT[:, c, tb * 512:(tb + 1) * 512], scalar=gains[:, gi, c:c + 1],
                    in1=rstd[:, tb * 512:(tb + 1) * 512], op0=ALU.mult, op1=ALU.mult),
                   r=[xk(c, tb), ("rstd", tb), "gains"], w=[hk(c, tb)])

    def wload(slot, src, r=()):
        a, b = src.shape[1], src.shape[2]
        dst = WB[slot][:, 0:a * b].rearrange("p (a b) -> p a b", b=b)
        op("pool", lambda e: e.dma_start(out=dst, in_=src), r=list(r), w=[("wb", slot)], dma=True)
        return dst

    def ffn(l, hT, actT, rl):
        w_in = w_ff_in_d[l].rearrange("(c p) f -> p c f", p=128)
        w_out = w_ff_out_d[l].rearrange("(c p) d -> p c d", p=128)
        views = {}

        def load(fg):
            views[fg] = (wload((2 * fg) % 4, w_in[:, :, fg * 512:(fg + 1) * 512]),
                         wload((2 * fg + 1) % 4, w_out[:, fg * 4:(fg + 1) * 4, :]))

        load(0)
        load(1)
        k = 0
        for fg in range(8):
            wi, wo = views[fg]
            sa, sbk = (2 * fg) % 4, (2 * fg + 1) % 4
            at = actT[fg % 2]
            for tb in range(4):
                for fc in range(4):
                    bank = k % 3
                    for c in range(8):
                        op("pe", lambda e, bank=bank, wi=wi, c=c, fc=fc, tb=tb: e.matmul(
                            PS[bank][:], lhsT=wi[:, c, fc * 128:(fc + 1) * 128], rhs=hT[:, c, tb * 512:(tb + 1) * 512], start=(c == 0), stop=(c == 7)),
                           r=[("wb", sa), hk(c, tb)], w=[psk(bank)])
                    r_ = rl[k % 2]
                    op("act", lambda e, bank=bank, r_=r_: e.activation(out=r_, in_=PS[bank][:], func=AF.Relu), r=[psk(bank)], w=[("rl", k % 2)])
                    op("act", lambda e, r_=r_, at=at, fc=fc, tb=tb: e.activation(out=at[:, fc, tb * 512:(tb + 1) * 512], in_=r_, func=AF.Square),
                       r=[("rl", k % 2)], w=[("actT", fg % 2, fc, tb)])
                    k += 1
            kk = 0
            for tb in range(4):
                for dc in range(8):
                    bank = 3 + kk % 3
                    for fc in range(4):
                        op("pe", lambda e, bank=bank, wo=wo, fc=fc, dc=dc, tb=tb, at=at: e.matmul(
                            PS[bank][:], lhsT=wo[:, fc, dc * 128:(dc + 1) * 128], rhs=at[:, fc, tb * 512:(tb + 1) * 512], start=(fc == 0), stop=(fc == 3)),
                           r=[("wb", sbk), ("actT", fg % 2, fc, tb)], w=[psk(bank)])
                    xs = xT[:, dc, tb * 512:(tb + 1) * 512]
                    op("dve", lambda e, xs=xs, bank=bank: e.tensor_tensor(out=xs, in0=xs, in1=PS[bank][:], op=ALU.add), r=[psk(bank), xk(dc, tb)], w=[xk(dc, tb)])
                    kk += 1
            if fg + 2 < 8:
                load(fg + 2)

    def ffn_phase(l, gi):
        AR.phase()
        hT = AR.at("hT", 0, [8, L], BF16)
        actT = [AR.at("actT%d" % i, 32 * KB + i * 16 * KB, [4, L], BF16) for i in range(2)]
        rl = [AR.at("rl%d" % i, 64 * KB + i * 2 * KB, [512], F32) for i in range(2)]
        sq = [AR.at("sq%d" % i, 68 * KB + i * 4 * KB, [L], BF16) for i in range(2)]
        rstd = AR.at("rstd", 76 * KB, [L], F32)
        norm_to_hT(gi, hT, rstd, sq)
        ffn(l, hT, actT, rl)


    def attn_phase(gi):
        w_in = w_in_even_d.rearrange("(c p) f -> p c f", p=128)
        w_out = w_out_even_d.rearrange("(c p) d -> p c d", p=128)
        T0 = 81 * KB
        AR.phase()
        hT = AR.at("hT", 0, [8, L], BF16)
        sq = [AR.at("sq%d" % i, T0 + i * 4 * KB, [L], BF16) for i in range(2)]
        rstd = AR.at("rstd", T0 + 8 * KB, [L], F32)
        norm_to_hT(gi, hT, rstd, sq)
        op("sp", lambda e: e.dma_start(out=dl[:], in_=bass.AP(dl_d.tensor, 0, [[0, 128], [1, 256]])), w=["dl"], dma=True)
        op("sp", lambda e: e.dma_start(out=sg[:], in_=sg_d), w=["sg"], dma=True)
        op("dve", lambda e: e.tensor_tensor(out=dlp[:, 0:64], in0=dl[:, 0:64], in1=dl[:, 64:128], op=ALU.mult), r=["dl"], w=["dlp"])
        op("dve", lambda e: e.tensor_tensor(out=dlp[:, 64:128], in0=dl[:, 128:192], in1=dl[:, 192:256], op=ALU.mult), r=["dl", "dlp"], w=["dlp"])
        op("dve", lambda e: e.reduce_sum(out=lam[:, 0:2], in_=dlp[:].rearrange("p (a b) -> p a b", b=64), axis=AX.X), r=["dlp"], w=["lam"])
        op("act", lambda e: e.activation(out=lam[:, 0:2], in_=lam[:, 0:2], func=AF.Exp), r=["lam"], w=["lam"])
        op("dve", lambda e: e.tensor_tensor(out=lam[:, 2:3], in0=lam[:, 1:2], in1=lam[:, 0:1], op=ALU.subtract), r=["lam"], w=["lam"])
        op("dve", lambda e: e.tensor_scalar(out=lam[:, 3:4], in0=lam[:, 2:3], scalar1=-LAMBDA_INIT0, scalar2=None, op0=ALU.add), r=["lam"], w=["lam"])
        op("dve", lambda e: e.tensor_scalar(out=sg[:], in0=sg[:], scalar1=1.0 - LAMBDA_INIT0, scalar2=None, op0=ALU.mult), r=["sg"], w=["sg"])
        nlam = lam[:, 3:4]
        if cfg.get("attn_stop", 9) <= 1:
            return

        AR.phase()
        hT = AR.at("hT", 0, [8, L], BF16)
        qT = AR.at("qT", 32 * KB, [4, L], BF16)
        kT = AR.at("kT", 48 * KB, [4, L], BF16)
        vaug = AR.at("vaug", 64 * KB, [16, 4, 130], BF16)
        t1 = [AR.at("t1_%d" % i, T0 + i * 2 * KB, [512], F32) for i in range(2)]
        t2 = [AR.at("t2_%d" % i, T0 + 4 * KB + i * 2 * KB, [512], F32) for i in range(2)]
        cosb = [AR.at("cos%d" % i, T0 + 8 * KB + i * 2 * KB, [512], F32) for i in range(2)]
        sinb = [AR.at("sin%d" % i, T0 + 12 * KB + i * 2 * KB, [512], F32) for i in range(2)]

        def load_group(slot, g):
            return wload(slot, w_in[:, :, g * 512:(g + 1) * 512])

        def rotate(sa, sr):
            a = WB[sa][:].rearrange("p (cb two j) -> p cb two j", two=2, j=32)
            r_ = WB[sr][:].rearrange("p (cb two j) -> p cb two j", two=2, j=32)
            op("dve", lambda e: e.tensor_scalar(out=r_[:, :, 0, :], in0=a[:, :, 1, :], scalar1=-1.0, scalar2=None, op0=ALU.mult), r=[("wb", sa)], w=[("wb", sr)])
            op("dve", lambda e: e.tensor_copy(out=r_[:, :, 1, :], in_=a[:, :, 0, :]), r=[("wb", sa)], w=[("wb", sr)])
            return WB[sr][:].rearrange("p (a b) -> p a b", b=512)

        wq = load_group(0, 0)
        wk = load_group(2, 1)
        wqr = rotate(0, 1)
        wkr = rotate(2, 3)
        op("pool", lambda e: e.memset(vaug[:, :, :, 128:130], 1.0), w=["vaug1"])
        kctr = [0]

        def rope_proj(tb, wA, wR, sA, sR, dstT, h, dkey):
            i = kctr[0] % 2
            kctr[0] += 1
            ba, bb = 2 * i, 2 * i + 1
            for c in range(8):
                op("pe", lambda e, c=c: e.matmul(PS[ba][:], lhsT=wA[:, c, h * 128:(h + 1) * 128], rhs=hT[:, c, tb * 512:(tb + 1) * 512], start=(c == 0), stop=(c == 7)),
                   r=[("wb", sA), hk(c, tb)], w=[psk(ba)])
            for c in range(8):
                op("pe", lambda e, c=c: e.matmul(PS[bb][:], lhsT=wR[:, c, h * 128:(h + 1) * 128], rhs=hT[:, c, tb * 512:(tb + 1) * 512], start=(c == 0), stop=(c == 7)),
                   r=[("wb", sR), hk(c, tb)], w=[psk(bb)])
            op("dve", lambda e: e.tensor_tensor(out=t1[i], in0=PS[ba][:], in1=cosb[tb % 2], op=ALU.mult), r=[psk(ba), ("cos", tb % 2)], w=[("t1", i)])
            op("dve", lambda e: e.tensor_tensor(out=t2[i], in0=PS[bb][:], in1=sinb[tb % 2], op=ALU.mult), r=[psk(bb), ("sin", tb % 2)], w=[("t2", i)])
            op("pool", lambda e: e.tensor_tensor(out=dstT[:, h, tb * 512:(tb + 1) * 512], in0=t1[i], in1=t2[i], op=ALU.add), r=[("t1", i), ("t2", i)], w=[(dkey, h, tb)])

        def do_tb(tb):
            op("sp", lambda e: e.dma_start(out=cosb[tb % 2], in_=cos_d[:, tb * 512:(tb + 1) * 512]), w=[("cos", tb % 2)], dma=True)
            op("sp", lambda e: e.dma_start(out=sinb[tb % 2], in_=sin_d[:, tb * 512:(tb + 1) * 512]), w=[("sin", tb % 2)], dma=True)
            for h in range(4):
                rope_proj(tb, wq, wqr, 0, 1, qT, h, "qT")
                rope_proj(tb, wk, wkr, 2, 3, kT, h, "kT")

        for tb in range(4):
            do_tb(tb)
        wv = load_group(0, 2)

        def do_v(j):
            bank = 4 + j % 2
            for c in range(8):
                op("pe", lambda e, c=c: e.matmul(PS[bank][:], lhsT=hT[:, c, j * 128:(j + 1) * 128], rhs=wv[:, c, :], start=(c == 0), stop=(c == 7)),
                   r=[("wb", 0), hk(c, j // 4)], w=[psk(bank)])
            op("act", lambda e: e.activation(out=vaug[:, j, :, 0:128], in_=PS[bank][:].rearrange("p (a b) -> p a b", b=128), func=AF.Copy), r=[psk(bank)], w=[("vaug", j)])

        for j in range(16):
            do_v(j)
        wo = wload(2, w_out[:, 0:4, :])
        dbg_put("qT0", qT[:, 0, 0:256], [("qT", 0, 0)])
        dbg_put("kT0", kT[:, 0, 0:256], [("kT", 0, 0)])
        dbg_put("vaug0", vaug[:, 0, :, :], [("vaug", 0), "vaug1"])
        dbg_put("lam", lam[:, 0:4], ["lam"])
        if cfg.get("attn_stop", 9) <= 2:
            return

        AR.phase()
        hT = AR.at("hT", 0, [8, L], BF16)
        qT = AR.at("qT", 32 * KB, [4, L], BF16)
        kT = AR.at("kT", 48 * KB, [4, L], BF16)
        vaug = AR.at("vaug", 64 * KB, [16, 4, 130], BF16)
        aT = AR.at("aT", T0, [4, L], BF16)
        PT = [AR.at("PT%d" % i, T0 + 16 * KB + i * KB, [512], BF16) for i in range(4)]
        SM = T0 + 20 * KB
        rr = rr_g[:]
        tt_ = [AR.at("tt%d" % i, SM + i * 512, [128], F32) for i in range(2)]
        oo = [AR.at("oo%d" % i, SM + 1024 + i * 512, [128], F32) for i in range(2)]
        onb = [AR.at("onb%d" % i, SM + 2048 + i * 256, [128], BF16) for i in range(4)]
        pctr = [0]
        cctr = [0]
        pending = []

        def accv(comp, qs):
            bank = 2 + 2 * comp + qs // 2
            off = (qs % 2) * 130
            return bank, PS[bank][:, off:off + 129]

        its = [(h_, qb_, comp_, kt_) for h_ in range(4) for qb_ in range(4) for comp_ in range(2) for kt_ in range(16)]
        if cfg.get("attn_stop", 9) == 3:
            its = [x_ for x_ in its if x_[0] == 0 and x_[1] == 0]

        SRING = (0, 1, 6)
        accs = [WB[1][:].bitcast(F32)[:, 512 * (1 + c_):512 * (2 + c_)] for c_ in range(2)]

        qpad = [[WB[0][:, (comp_ * 2 + j_) * 512:(comp_ * 2 + j_ + 1) * 512] for j_ in range(2)] for comp_ in range(2)]
        op("pool", lambda e: e.memset(WB[0][:, 0:2048], 0.0), r=[("wb", 0)], w=[("wb", 0)] + [("qpad", c_, j_) for c_ in range(2) for j_ in range(2)])

        def emit_qpad(gi):
            h, qb = gi // 4, gi % 4
            j = gi % 2
            op("pool", lambda e: e.tensor_copy(out=qpad[0][j][0:64, :], in_=qT[0:64, h, qb * 512:(qb + 1) * 512]), r=[("qT", h, qb)], w=[("qpad", 0, j)])
            op("pool", lambda e: e.tensor_copy(out=qpad[1][j][64:128, :], in_=qT[64:128, h, qb * 512:(qb + 1) * 512]), r=[("qT", h, qb)], w=[("qpad", 1, j)])

        def emit_S(i):
            h, qb, comp, kt = its[i]
            sbank = SRING[i % 3]
            j = (h * 4 + qb) % 2
            op("pe", lambda e: e.matmul(PS[sbank][:], lhsT=kT[:, h, kt * 128:(kt + 1) * 128], rhs=qpad[comp][j], start=True, stop=True),
               r=[("kT", h, kt // 4), ("qpad", comp, j)], w=[psk(sbank)])

        def emit_exp_pv(i):
            h, qb, comp, kt = its[i]
            sbank = SRING[i % 3]
            pi = i % 4
            op("act", lambda e: e.activation(out=PT[pi], in_=PS[sbank][:], func=AF.Exp, scale=0.125), r=[psk(sbank)], w=[("PT", pi)])
            bo, bs = 2 + 2 * comp, 3 + 2 * comp
            op("pe", lambda e: e.matmul(PS[bo][:], lhsT=vaug[:, kt, h, 0:128], rhs=PT[pi], start=(kt == 0), stop=(kt == 15)),
               r=[("PT", pi), ("vaug", kt)], w=[psk(bo)])
            op("pe", lambda e: e.matmul(PS[bs][:], lhsT=ones_b[:], rhs=PT[pi], start=(kt == 0), stop=(kt == 15)),
               r=[("PT", pi), "ones_b"], w=[psk(bs)])

        def attend_tail(h, qb):
            fo = WB[3][:].bitcast(F32)
            r0, r1, t_, o_ = fo[:, 0:512], fo[:, 512:1024], fo[:, 1024:1536], fo[:, 1536:2048]
            sqb = WB[1][:, 0:512]
            K3 = [("wb", 3)]
            if dbg and h == 0 and qb == 0:
                dbg_put("acc0", PS[2][:, 0:256], [psk(2)])
                dbg_put("acc1", PS[4][:, 0:256], [psk(4)])
            if cfg.get("attn_stop", 9) <= 3:
                return
            op("dve", lambda e: e.reciprocal(out=r0, in_=PS[3][:]), r=[psk(3)], w=K3)
            op("dve", lambda e: e.reciprocal(out=r1, in_=PS[5][:]), r=[psk(5)] + K3, w=K3)
            op("dve", lambda e: e.tensor_tensor(out=t_, in0=PS[4][:], in1=r1, op=ALU.mult), r=[psk(4)] + K3, w=K3)
            op("dve", lambda e: e.tensor_tensor(out=o_, in0=PS[2][:], in1=r0, op=ALU.mult), r=[psk(2)] + K3, w=K3)
            op("dve", lambda e: e.scalar_tensor_tensor(out=o_, in0=t_, scalar=nlam, in1=o_, op0=ALU.mult, op1=ALU.add), r=K3 + ["lam"], w=K3)
            op("act", lambda e: e.activation(out=sqb, in_=o_, func=AF.Square), r=K3, w=[("wb", 1)])

            def fin():
                op("pe", lambda e: e.matmul(PS[7][:], lhsT=ones_b[:], rhs=sqb, start=True, stop=True), r=[("wb", 1), "ones_b"], w=[psk(7)])
                op("act", lambda e: e.activation(out=r0, in_=PS[7][:], func=AF.Sqrt, scale=1.0 / 128, bias=EPS), r=[psk(7)] + K3, w=K3)
                op("dve", lambda e: e.reciprocal(out=r0, in_=r0), r=K3, w=K3)
                op("dve", lambda e: e.scalar_tensor_tensor(out=aT[:, h, qb * 512:(qb + 1) * 512], in0=o_, scalar=sg[:, 0:1], in1=r0, op0=ALU.mult, op1=ALU.mult),
                   r=K3 + ["sg"], w=[("aT", h, qb)])
            pending.append(fin)

        emit_qpad(0)
        LOOK = 2
        for i in range(LOOK):
            emit_S(i)
        for i in range(len(its)):
            if its[i][2] == 0 and its[i][3] == 0:
                gi_ = its[i][0] * 4 + its[i][1]
                if gi_ + 1 < 16 and cfg.get("attn_stop", 9) != 3:
                    emit_qpad(gi_ + 1)
            if i + LOOK < len(its):
                emit_S(i + LOOK)
            emit_exp_pv(i)
            h_, qb_, comp_, kt_ = its[i]
            if kt_ == 15 and comp_ == 0 and pending:
                for f in pending:
                    f()
                del pending[:]
            if kt_ == 15 and comp_ == 1:
                attend_tail(h_, qb_)
        for f in pending:
            f()
        del pending[:]
        dbg_put("aT", aT[:, :, 0:512], [("aT", h_, 0) for h_ in range(4)])

        def oproj(tb, dc, kq):
            bank = kq % 2
            for hc in range(4):
                op("pe", lambda e, hc=hc: e.matmul(PS[bank][:], lhsT=wo[:, hc, dc * 128:(dc + 1) * 128], rhs=aT[:, hc, tb * 512:(tb + 1) * 512], start=(hc == 0), stop=(hc == 3)),
                   r=[("wb", 2), ("aT", hc, tb)], w=[psk(bank)])
            xs = xT[:, dc, tb * 512:(tb + 1) * 512]
            op("dve", lambda e: e.tensor_tensor(out=xs, in0=xs, in1=PS[bank][:], op=ALU.add), r=[psk(bank), xk(dc, tb)], w=[xk(dc, tb)])

        if not dbg:
            kq = 0
            for tb in range(4):
                for dc in range(8):
                    oproj(tb, dc, kq)
                    kq += 1


    def s5_phase():
        w_in = w_in_even_d.rearrange("(c p) f -> p c f", p=128)
        w_out = w_out_even_d.rearrange("(c p) d -> p c d", p=128)
        AR.phase()
        hT = AR.at("hT", 0, [8, L], BF16)
        if not ("l0attn" in parts or "l0mix" in parts):
            sq = [AR.at("sq%d" % i, 81 * KB + i * 4 * KB, [L], BF16) for i in range(2)]
            rstd = AR.at("rstd", 89 * KB, [L], F32)
            norm_to_hT(0, hT, rstd, sq)
        uT = AR.at("uT", 32 * KB, [4, L], BF16)
        wu = wload(0, w_in[:, :, 1536:2048])

        def uproj(tb, cc, kq):
            bank = kq % 4
            for c in range(8):
                op("pe", lambda e, c=c: e.matmul(PS[bank][:], lhsT=wu[:, c, cc * 128:(cc + 1) * 128], rhs=hT[:, c, tb * 512:(tb + 1) * 512], start=(c == 0), stop=(c == 7)),
                   r=[("wb", 0), hk(c, tb)], w=[psk(bank)])
            op("act", lambda e: e.activation(out=uT[:, cc, tb * 512:(tb + 1) * 512], in_=PS[bank][:], func=AF.Copy), r=[psk(bank)], w=[("uT", cc)])

        kq = 0
        for tb in range(4):
            for cc in range(4):
                uproj(tb, cc, kq)
                kq += 1
        wglu = wload(1, w_glu_d.rearrange("(c p) f -> p c f", p=128))
        wo2 = wload(2, w_out[:, 4:8, :])

        AR.phase()
        VX = AR.at("VX", 0, [2, 32, 256], BF16)
        uT = AR.at("uT", 32 * KB, [4, L], BF16)
        U = AR.at("U", 48 * KB, [32, 256], BF16)
        sm_off = [64 * KB]

        def sm(name, shape=(32,)):
            n = 1
            for v_ in shape:
                n *= v_
            t_ = AR.at(name, sm_off[0], list(shape), F32)
            sm_off[0] += n * 4
            return t_

        bm = sm("bm", (2, 32, 16))
        names = ["lr", "dt", "lrdt", "ang", "mg", "sn", "cs", "r_", "i_", "t0", "t1", "t2", "den", "am1", "cr", "ci", "vr", "vi"]
        sv = {n_: sm(n_) for n_ in names}
        par = sm("par", (3, 32))
        cm = sm("cm", (2, 32, 16))
        bb = sm("bb", (2, 32, 16))
        pw = sm("pw", (2, 32, 8))
        pwi = sm("pwi", (2, 32, 8))
        P1s = sm("P1s", (2, 32))
        P2s = sm("P2s", (2, 32))
        Xst = sm("Xst", (2, 32))
        S1 = sm("S1", (2, 32))
        T1 = sm("T1", (2, 32))
        T2 = sm("T2", (2, 32))
        dsk = sm("dsk", (4,))
        bgl = sm("bgl", (4,))
        SM_END = sm_off[0]
        assert SM_END <= 85 * KB, SM_END
        AB = [AR.at("AB%d" % i, 85 * KB + i * 2 * KB, [8, 128], BF16) for i in range(2)]
        ABT = [AR.at("ABT%d" % i, 89 * KB + i * 2 * KB, [8, 128], BF16) for i in range(2)]
        tA = AR.at("tA", 93 * KB, [512], F32)
        tB = AR.at("tB", 95 * KB, [512], F32)
        Zt = AR.at("Zt", 97 * KB, [8, 240], BF16)
        s5m = AR.at("s5m", 97 * KB + 3840, [2, 128], F32)
        PP = ["pp"]

        def vop(fn, eng="dve", r=(), w=()):
            op(eng, fn, r=PP + list(r), w=PP + list(w))

        def tt(o_, a_, b_, o, **kw):
            vop(lambda e: e.tensor_tensor(out=o_, in0=a_, in1=b_, op=o), **kw)

        def ts(o_, a_, s1, o1, s2=None, o2=None, **kw):
            if o2 is None:
                vop(lambda e: e.tensor_scalar(out=o_, in0=a_, scalar1=s1, scalar2=None, op0=o1), **kw)
            else:
                vop(lambda e: e.tensor_scalar(out=o_, in0=a_, scalar1=s1, scalar2=s2, op0=o1, op1=o2), **kw)

        def cmul(outr, outi, ar_, ai_, br_, bi_, x1, x2, **kw):
            tt(x1, ar_, br_, ALU.mult, **kw)
            tt(x2, ai_, bi_, ALU.mult, **kw)
            tt(outr, x1, x2, ALU.subtract, **kw)
            tt(x1, ar_, bi_, ALU.mult, **kw)
            tt(x2, ai_, br_, ALU.mult, **kw)
            tt(outi, x1, x2, ALU.add, **kw)

        op("sp", lambda e: e.dma_start(out=par, in_=s5s_d), w=PP, dma=True, dkey="s5s")
        op("sp", lambda e: e.dma_start(out=bm, in_=s5b_d), w=PP, dma=True, dkey="s5b")
        op("sp", lambda e: e.dma_start(out=cm, in_=s5c_d), w=PP, dma=True, dkey="s5c")
        op("sp", lambda e: e.dma_start(out=dsk, in_=s5d_d), w=PP, dma=True, dkey="s5d")
        op("sp", lambda e: e.dma_start(out=bgl, in_=s5bg_d), w=PP, dma=True, dkey="s5bg")
        op("sp", lambda e: e.dma_start(out=s5m, in_=s5m_d), w=["s5m"], dma=True)
        op("pool", lambda e: e.memset(Zt, 0.0), w=["Zt"])
        op("pool", lambda e: e.tensor_copy(out=Zt[:, :, 112:128], in_=ident_b[:].rearrange("p (a b) -> p a b", b=16)), r=["ident_b"], w=["Zt"])
        lam_re, lam_im, lstep = par[:, 0, :], par[:, 1, :], par[:, 2, :]
        v_ = sv
        ts(v_["lr"], lam_re, -1e-4, ALU.min)
        vop(lambda e: e.activation(out=v_["dt"], in_=lstep, func=AF.Exp), eng="act")
        tt(v_["lrdt"], v_["lr"], v_["dt"], ALU.mult)
        tt(v_["ang"], lam_im, v_["dt"], ALU.mult)
        vop(lambda e: e.activation(out=v_["mg"], in_=v_["lrdt"], func=AF.Exp, scale=1.0 / 32), eng="act")
        vop(lambda e: e.activation(out=v_["sn"], in_=v_["ang"], func=AF.Sin, scale=1.0 / 32), eng="act")
        ts(v_["t0"], v_["ang"], 1.0 / 32, ALU.mult, math.pi / 2, ALU.add)
        vop(lambda e: e.activation(out=v_["cs"], in_=v_["t0"], func=AF.Sin), eng="act")
        tt(v_["r_"], v_["mg"], v_["cs"], ALU.mult)
        tt(v_["i_"], v_["mg"], v_["sn"], ALU.mult)
        for _ in range(5):
            tt(v_["t0"], v_["r_"], v_["r_"], ALU.mult)
            tt(v_["t1"], v_["i_"], v_["i_"], ALU.mult)
            tt(v_["t2"], v_["r_"], v_["i_"], ALU.mult)
            tt(v_["r_"], v_["t0"], v_["t1"], ALU.subtract)
            ts(v_["i_"], v_["t2"], 2.0, ALU.mult)
        ar, ai = v_["r_"], v_["i_"]
        tt(v_["t0"], v_["lr"], v_["lr"], ALU.mult)
        tt(v_["t1"], lam_im, lam_im, ALU.mult)
        tt(v_["den"], v_["t0"], v_["t1"], ALU.add)
        vop(lambda e: e.reciprocal(out=v_["den"], in_=v_["den"]))
        ts(v_["am1"], ar, -1.0, ALU.add)
        tt(v_["t0"], v_["am1"], v_["lr"], ALU.mult)
        tt(v_["t1"], ai, lam_im, ALU.mult)
        tt(v_["t0"], v_["t0"], v_["t1"], ALU.add)
        tt(v_["cr"], v_["t0"], v_["den"], ALU.mult)
        tt(v_["t0"], ai, v_["lr"], ALU.mult)
        tt(v_["t1"], v_["am1"], lam_im, ALU.mult)
        tt(v_["t0"], v_["t0"], v_["t1"], ALU.subtract)
        tt(v_["ci"], v_["t0"], v_["den"], ALU.mult)
        tt(v_["t0"], ar, ar, ALU.mult)
        tt(v_["t1"], ai, ai, ALU.mult)
        tt(v_["t0"], v_["t0"], v_["t1"], ALU.add)
        vop(lambda e: e.reciprocal(out=v_["t0"], in_=v_["t0"]))
        tt(v_["vr"], ar, v_["t0"], ALU.mult)
        tt(v_["t1"], ai, v_["t0"], ALU.mult)
        ts(v_["vi"], v_["t1"], -1.0, ALU.mult)
        x1 = tA[:, 0:128].rearrange("p (a b) -> p a b", b=4)
        x2 = tB[:, 0:128].rearrange("p (a b) -> p a b", b=4)
        for (tab, br_, bi_) in ((pw, ar, ai), (pwi, v_["vr"], v_["vi"])):
            tr_, ti_ = tab[:, 0, :, :], tab[:, 1, :, :]
            vop(lambda e, tr_=tr_, br_=br_: e.tensor_copy(out=tr_[:, :, 0:1], in_=br_.unsqueeze(2)))
            vop(lambda e, ti_=ti_, bi_=bi_: e.tensor_copy(out=ti_[:, :, 0:1], in_=bi_.unsqueeze(2)))
            for n_ in (1, 2, 4):
                cmul(tr_[:, :, n_:2 * n_], ti_[:, :, n_:2 * n_], tr_[:, :, 0:n_], ti_[:, :, 0:n_],
                     tr_[:, :, n_ - 1:n_].broadcast_to([128, 32, n_]), ti_[:, :, n_ - 1:n_].broadcast_to([128, 32, n_]), x1[:, :, 0:n_], x2[:, :, 0:n_])
        cmul(bb[:, 0, :, :], bb[:, 1, :, :], v_["cr"].unsqueeze(2).broadcast_to([128, 32, 16]), v_["ci"].unsqueeze(2).broadcast_to([128, 32, 16]),
             bm[:, 0, :, :], bm[:, 1, :, :], tA.rearrange("p (a b) -> p a b", b=16), tB.rearrange("p (a b) -> p a b", b=16))
        vop(lambda e: e.tensor_copy(out=P1s[:, 0, :], in_=pw[:, 0, :, 7]))
        vop(lambda e: e.tensor_copy(out=P1s[:, 1, :], in_=pw[:, 0, :, 7]))
        ts(P2s[:, 0, :], pw[:, 1, :, 7], -1.0, ALU.mult)
        vop(lambda e: e.tensor_copy(out=P2s[:, 1, :], in_=pw[:, 1, :, 7]))

        for tab in (pw, pwi):
            flat = tab[64:128].rearrange("p a b c -> p (a b c)")
            vop(lambda e, flat=flat: e.tensor_copy(out=tA[64:128, :], in_=flat))
            vop(lambda e, tab=tab: e.tensor_copy(out=tab[64:128].rearrange("p a b c -> p (a b) c"), in_=tA[64:128, :].rearrange("p (ab c) -> p ab c", c=8)[:, :, ::-1]))

        def gen_AB(ABt, g0, gl0):
            for (lo, hi) in ((0, 128),):
                np_ = hi - lo
                pr = pwi[lo:hi, 0, g0:g0 + 4, :].unsqueeze(3).broadcast_to([np_, 4, 8, 16])
                pi = pwi[lo:hi, 1, g0:g0 + 4, :].unsqueeze(3).broadcast_to([np_, 4, 8, 16])
                br_ = bb[lo:hi, 0, g0:g0 + 4, :].unsqueeze(2).broadcast_to([np_, 4, 8, 16])
                bi_ = bb[lo:hi, 1, g0:g0 + 4, :].unsqueeze(2).broadcast_to([np_, 4, 8, 16])
                o_r = ABt[0][lo:hi, gl0:gl0 + 4, :].rearrange("p g (s m) -> p g s m", m=16)
                o_i = ABt[1][lo:hi, gl0:gl0 + 4, :].rearrange("p g (s m) -> p g s m", m=16)
                ta = tA[lo:hi, :].rearrange("p (g s m) -> p g s m", s=8, m=16)
                tb_ = tB[lo:hi, :].rearrange("p (g s m) -> p g s m", s=8, m=16)
                cmul(o_r, o_i, pr, pi, br_, bi_, ta, tb_, w=["AB"])

        def gen_CA(g0, gl0, CAf, CAb):
            for (lo, hi, dst) in ((0, 64, CAf), (64, 128, CAb)):
                sl = slice(None)
                pr = pw[lo:hi, 0, g0:g0 + 4, sl].unsqueeze(3).broadcast_to([64, 4, 8, 16])
                pi = pw[lo:hi, 1, g0:g0 + 4, sl].unsqueeze(3).broadcast_to([64, 4, 8, 16])
                c_r = cm[lo:hi, 0, g0:g0 + 4, :].unsqueeze(2).broadcast_to([64, 4, 8, 16])
                c_i = cm[lo:hi, 1, g0:g0 + 4, :].unsqueeze(2).broadcast_to([64, 4, 8, 16])
                o_r = dst[0][lo:hi, gl0:gl0 + 4, :].rearrange("p g (s m) -> p g s m", m=16)
                o_i = dst[1][lo:hi, gl0:gl0 + 4, :].rearrange("p g (s m) -> p g s m", m=16)
                ta = tA[lo:hi, :].rearrange("p (g s m) -> p g s m", s=8, m=16)
                tb_ = tB[lo:hi, :].rearrange("p (g s m) -> p g s m", s=8, m=16)
                kw = dict(w=["CA"])
                tt(ta, c_r, pr, ALU.mult, **kw)
                tt(tb_, c_i, pi, ALU.mult, **kw)
                tt(o_r, ta, tb_, ALU.subtract, **kw)
                tt(ta, c_r, pi, ALU.mult, **kw)
                tt(tb_, c_i, pr, ALU.mult, **kw)
                vop(lambda e, o_i=o_i, ta=ta, tb_=tb_: e.scalar_tensor_tensor(out=o_i, in0=ta, scalar=-1.0, in1=tb_, op0=ALU.mult, op1=ALU.subtract), **kw)

        def shuffle_group(cc, gl):
            g = 8 * cc + gl
            bank = g % 2
            for s_ in range(8):
                op("pe", lambda e, s_=s_: e.matmul(PS[bank][:, 0:256], lhsT=Zt[:, gl, (7 - s_) * 16:(7 - s_) * 16 + 128],
                                                   rhs=uT[:, cc, :].rearrange("p (b s) -> p b s", s=8)[:, :, s_], start=(s_ == 0), stop=(s_ == 7)),
                   r=["Zt", ("uT", cc)], w=[psk(bank)])
            op("act", lambda e: e.activation(out=U[:, g, :], in_=PS[bank][:, 0:256], func=AF.Copy), r=[psk(bank)], w=[("U", g)])

        def abt_chunk():
            for ri in range(2):
                bank = 2 + ri
                pb = PS[bank][:].bitcast(BF16)
                for gl in range(8):
                    op("pe", lambda e, gl=gl, pb=pb, ri=ri: e.transpose(out=pb[:, gl * 128:(gl + 1) * 128], in_=AB[ri][:, gl, :], identity=ident_b[:]), r=["AB", "pp", "ident_b"], w=[psk(bank)])
                op("dve", lambda e, pb=pb, ri=ri: e.tensor_copy(out=ABT[ri], in_=pb.rearrange("p (a b) -> p a b", b=128)), r=[psk(bank)], w=["ABT"])

        def vprime_group(cc, gl):
            g = 8 * cc + gl
            for ri in range(2):
                bank = 4 + ri
                op("pe", lambda e, ri=ri, bank=bank: e.matmul(PS[bank][:, 0:256], lhsT=ABT[ri][:, gl, :], rhs=U[:, g, :], start=True, stop=True), r=["ABT", ("U", g)], w=[psk(bank)])
                op("act", lambda e, ri=ri, bank=bank: e.activation(out=VX[:, ri, g, :], in_=PS[bank][:, 0:256], func=AF.Copy), r=[psk(bank)], w=[("VX", g)])

        for cc in range(4):
            for gl in range(8):
                shuffle_group(cc, gl)
            gen_AB(AB, 8 * cc, 0)
            gen_AB(AB, 8 * cc + 4, 4)
            abt_chunk()
            for gl in range(8):
                vprime_group(cc, gl)
        dbg_put("U0", U[:, 0, 0:64], [("U", 0)])
        dbg_put("VX0", VX[:, :, 0, 0:64], [("VX", 0)])
        dbg_put("pw", pw[:, :, 0, :], PP)
        dbg_put("pwi", pwi[:, :, 0, :], PP)
        dbg_put("bb", bb[:, :, 0, :], PP)

        VXK = [("VX", g) for g in range(32)]
        op("dve", lambda e: e.memset(Xst, 0.0), w=["sc0x", "sc64x"])

        def scan_step(lo, hi, b):
            key = "sc%d" % lo
            xs, s1, t1_, t2_ = Xst[lo:hi], S1[lo:hi], T1[lo:hi], T2[lo:hi]
            vx = VX[lo:hi, :, :, b]
            s1sw = bass.AP(s1.tensor, s1.offset + 32, [list(s1.ap[0]), [-32, 2], [1, 32]])
            op("dve", lambda e: e.tensor_tensor(out=s1, in0=xs, in1=vx, op=ALU.add), r=VXK + [key + "x"], w=[key + "s"])
            op("dve", lambda e: e.tensor_tensor(out=t1_, in0=P1s[lo:hi], in1=s1, op=ALU.mult), r=[key + "s", "pp"], w=[key + "a"])
            op("dve", lambda e: e.tensor_tensor(out=t2_, in0=P2s[lo:hi], in1=s1sw, op=ALU.mult), r=[key + "s", "pp"], w=[key + "b"])
            op("dve", lambda e: e.tensor_tensor(out=xs, in0=t1_, in1=t2_, op=ALU.add), r=[key + "a", key + "b"], w=[key + "x"])
            op("pool", lambda e: e.tensor_copy(out=vx, in_=xs), r=[key + "x"], w=[key + "c"])

        for b in range(256):
            scan_step(0, 64, b)
            scan_step(64, 128, 255 - b)
        dbg_put("X0", VX[:, :, 0, 0:64], ["sc0c", "sc64c"])

        AR.phase()
        AR.at("keep", 0, [85 * KB // 2], BF16)
        AB2 = [AR.at("AB2_%d" % i, 85 * KB + i * 2 * KB, [8, 128], BF16) for i in range(2)]
        CAf = [AR.at("CAf%d" % i, 89 * KB + i * 2 * KB, [8, 128], BF16) for i in range(2)]
        AR.at("keep2", 93 * KB, [(104 - 93) * KB // 2], BF16)
        CAb = [bm.rearrange("p a b c -> p (a b c)")[:, i * 512:(i + 1) * 512].bitcast(BF16).rearrange("p (a b) -> p a b", b=128) for i in range(2)]
        W0 = sv["lr"].tensor and AR.base[:, (64 * KB + 4096) // 2:(64 * KB + 4096 + 2048) // 2].rearrange("p (a b) -> p a b", b=128)
        for i in range(2):
            op("pool", lambda e, i=i: e.memset(CAf[i][64:128], 0.0), w=["CA"])
            op("pool", lambda e, i=i: e.memset(CAb[i][0:64], 0.0), w=["CA"])
        mF_ = s5m[:, 0, :]
        mB_ = s5m[:, 1, :]

        def w0_chunk():
            for half in range(2):
                pf, pb_ = PS[2], PS[3]
                for jj in range(4):
                    gl = half * 4 + jj
                    o1 = pf[:, jj * 128:(jj + 1) * 128]
                    o2 = pb_[:, jj * 128:(jj + 1) * 128]
                    op("pe", lambda e, gl=gl, o1=o1: e.matmul(o1, lhsT=AB2[0][:, gl, :], rhs=CAf[0][:, gl, :], start=True, stop=False), r=["AB", "CA", "pp"], w=[psk(2)])
                    op("pe", lambda e, gl=gl, o1=o1: e.matmul(o1, lhsT=AB2[1][:, gl, :], rhs=CAf[1][:, gl, :], start=False, stop=True), r=["AB", "CA", "pp"], w=[psk(2)])
                    op("pe", lambda e, gl=gl, o2=o2: e.matmul(o2, lhsT=AB2[0][:, gl, :], rhs=CAb[0][:, gl, :], start=True, stop=False), r=["AB", "CA", "pp"], w=[psk(3)])
                    op("pe", lambda e, gl=gl, o2=o2: e.matmul(o2, lhsT=AB2[1][:, gl, :], rhs=CAb[1][:, gl, :], start=False, stop=True), r=["AB", "CA", "pp"], w=[psk(3)])
                t3 = tA.rearrange("p (a b) -> p a b", b=128)
                op("dve", lambda e, t3=t3: e.tensor_tensor(out=t3, in0=PS[2][:].rearrange("p (a b) -> p a b", b=128), in1=mF_.unsqueeze(1).broadcast_to([128, 4, 128]), op=ALU.mult),
                   r=[psk(2), "s5m", "pp"], w=["pp"])
                op("dve", lambda e: e.tensor_tensor(out=tB.rearrange("p (a b) -> p a b", b=128), in0=PS[3][:].rearrange("p (a b) -> p a b", b=128), in1=mB_.unsqueeze(1).broadcast_to([128, 4, 128]), op=ALU.mult),
                   r=[psk(3), "s5m", "pp"], w=["pp"])
                op("dve", lambda e, half=half: e.tensor_tensor(out=W0[:, half * 4:(half + 1) * 4, :], in0=tA.rearrange("p (a b) -> p a b", b=128), in1=tB.rearrange("p (a b) -> p a b", b=128), op=ALU.add),
                   r=["pp"], w=["W0", "pp"])

        SCK = ["sc0c", "sc64c"]

        def y_group(cc, gl):
            g = 8 * cc + gl
            bank = 4 + g % 2
            o_ = PS[bank]
            rk = ["W0", "CA", ("U", g), ("VX", g)] + SCK
            op("pe", lambda e: e.matmul(o_[:, 0:256], lhsT=W0[:, gl, :], rhs=U[:, g, :], start=True, stop=False), r=rk, w=[psk(bank)])
            op("pe", lambda e: e.matmul(o_[:, 1:256], lhsT=CAf[0][:, gl, :], rhs=VX[:, 0, g, 0:255], start=False, stop=False), r=rk, w=[psk(bank)])
            op("pe", lambda e: e.matmul(o_[:, 1:256], lhsT=CAf[1][:, gl, :], rhs=VX[:, 1, g, 0:255], start=False, stop=False), r=rk, w=[psk(bank)])
            op("pe", lambda e: e.matmul(o_[:, 0:255], lhsT=CAb[0][:, gl, :], rhs=VX[:, 0, g, 1:256], start=False, stop=False), r=rk, w=[psk(bank)])
            op("pe", lambda e: e.matmul(o_[:, 0:255], lhsT=CAb[1][:, gl, :], rhs=VX[:, 1, g, 1:256], start=False, stop=True), r=rk, w=[psk(bank)])
            op("act", lambda e: e.activation(out=U[:, g, :], in_=o_[:, 0:256], func=AF.Copy), r=[psk(bank)], w=[("U", g)])

        def unshuffle(cc, tl):
            bank = tl % 2
            for gl in range(8):
                g = 8 * cc + gl
                op("pe", lambda e, gl=gl, g=g: e.matmul(PS[bank][:, 0:256], lhsT=Zt[:, tl, (7 - gl) * 16:(7 - gl) * 16 + 128], rhs=U[:, g, :], start=(gl == 0), stop=(gl == 7)),
                   r=["Zt", ("U", g)], w=[psk(bank)])
            uv = uT[:, cc, :].rearrange("p (b s) -> p b s", s=8)[:, :, tl]
            op("dve", lambda e: e.scalar_tensor_tensor(out=uv, in0=uv, scalar=dsk[:, cc:cc + 1], in1=PS[bank][:, 0:256], op0=ALU.mult, op1=ALU.add),
               r=[psk(bank), ("uT", cc), "pp"], w=[("uT", cc)])

        for cc in range(4):
            for hh in range(2):
                gen_AB(AB2, 8 * cc + 4 * hh, 4 * hh)
                gen_CA(8 * cc + 4 * hh, 4 * hh, CAf, CAb)
            w0_chunk()
            for gl in range(8):
                y_group(cc, gl)
            for tl in range(8):
                unshuffle(cc, tl)
        dbg_put("y", uT[:, :, 0:512], [("uT", cc) for cc in range(4)])

        AR.phase()
        AR.at("uT", 32 * KB, [4, L], BF16)
        bT = AR.at("bT", 0, [4, L], BF16)
        g1 = AR.at("g1", 16 * KB, [L], F32)
        g2 = AR.at("g2", 24 * KB, [L], F32)
        CG = math.sqrt(2.0 / math.pi)

        def gelu_chunk(cc):
            y_ = uT[:, cc, :]
            op("act", lambda e: e.activation(out=g1, in_=y_, func=AF.Square), r=[("uT", cc)], w=["g1"])
            op("dve", lambda e: e.tensor_scalar(out=g1, in0=g1, scalar1=0.044715, scalar2=1.0, op0=ALU.mult, op1=ALU.add), r=["g1"], w=["g1"])
            op("dve", lambda e: e.tensor_tensor(out=g1, in0=g1, in1=y_, op=ALU.mult), r=["g1", ("uT", cc)], w=["g1"])
            op("act", lambda e: e.activation(out=g2, in_=g1, func=AF.Sigmoid, scale=2.0 * CG), r=["g1"], w=["g2"])
            op("dve", lambda e: e.tensor_tensor(out=y_, in0=y_, in1=g2, op=ALU.mult), r=["g2", ("uT", cc)], w=[("uT", cc)])

        for cc in range(4):
            gelu_chunk(cc)

        def glu(tb, oc, kq):
            bank = kq % 2
            for kc in range(4):
                op("pe", lambda e, kc=kc: e.matmul(PS[bank][:], lhsT=wglu[:, kc, oc * 128:(oc + 1) * 128], rhs=uT[:, kc, tb * 512:(tb + 1) * 512], start=(kc == 0), stop=(kc == 3)),
                   r=[("wb", 1), ("uT", kc)], w=[psk(bank)])
            gsl = g1[:, (kq % 4) * 512:(kq % 4 + 1) * 512]
            op("act", lambda e: e.activation(out=gsl, in_=PS[bank][:], func=AF.Sigmoid, bias=bgl[:, oc:oc + 1]), r=[psk(bank), "pp"], w=[("gs", kq % 4)])
            op("dve", lambda e: e.tensor_tensor(out=bT[:, oc, tb * 512:(tb + 1) * 512], in0=uT[:, oc, tb * 512:(tb + 1) * 512], in1=gsl, op=ALU.mult),
               r=[("gs", kq % 4), ("uT", oc)], w=[("bT", oc, tb)])

        P.barrier()
        kq = 0
        for tb in range(4):
            for oc in range(4):
                glu(tb, oc, kq)
                kq += 1
        dbg_put("bT", bT[:, :, 0:512], [("bT", oc, 0) for oc in range(4)])

        def oproj2(tb, dc, kq):
            bank = 2 + kq % 2
            for hc in range(4):
                op("pe", lambda e, hc=hc: e.matmul(PS[bank][:], lhsT=wo2[:, hc, dc * 128:(dc + 1) * 128], rhs=bT[:, hc, tb * 512:(tb + 1) * 512], start=(hc == 0), stop=(hc == 3)),
                   r=[("wb", 2), ("bT", hc, tb)], w=[psk(bank)])
            xs = xT[:, dc, tb * 512:(tb + 1) * 512]
            op("dve", lambda e: e.tensor_tensor(out=xs, in0=xs, in1=PS[bank][:], op=ALU.add), r=[psk(bank), xk(dc, tb)], w=[xk(dc, tb)])

        if not dbg:
            kq = 0
            for tb in range(4):
                for dc in range(8):
                    oproj2(tb, dc, kq)
                    kq += 1

    def hgrn_phase(gi):
        w_in = w_in_odd_d.rearrange("(c p) f -> p c f", p=128)
        w_out = w_out_odd_d.rearrange("(h p) d -> p h d", p=128)
        AR.phase()
        hT = AR.at("hT", 0, [8, L], BF16)
        sq = [AR.at("sq%d" % i, 44 * KB + i * 4 * KB, [L], BF16) for i in range(2)]
        rstd = AR.at("rstd", 52 * KB, [L], F32)
        norm_to_hT(gi, hT, rstd, sq)
        AR.phase()
        hT = AR.at("hT", 0, [8, L], BF16)
        qT = AR.at("qT", 32 * KB, [L], BF16)
        sigG = AR.at("sigG", 36 * KB, [L], BF16)
        vtok = AR.at("vtok", 40 * KB, [16, 128], BF16)
        HS = []
        for i_ in range(2):
            b0 = 44 * KB + i_ * 22 * KB
            HS.append(dict(i=i_,
                           A=AR.at("A%d" % i_, b0, [1024], F32), B=AR.at("B%d" % i_, b0 + 4 * KB, [1024], F32),
                           kk=AR.at("kk%d" % i_, b0 + 8 * KB, [1024], BF16), qdec=AR.at("qdec%d" % i_, b0 + 10 * KB, [1024], BF16),
                           kinv=AR.at("kinv%d" % i_, b0 + 12 * KB, [1024], BF16), kend=AR.at("kend%d" % i_, b0 + 14 * KB, [8, 128], BF16),
                           scT=AR.at("scT%d" % i_, b0 + 16 * KB, [8, 128], BF16), Sb=AR.at("Sb%d" % i_, b0 + 18 * KB, [16, 128], BF16),
                           pbank=(0, 1) if i_ == 0 else (2, 3), xbank=4 + i_))
        oacc = AR.at("oacc", 88 * KB, [16, 128], F32)
        mF = AR.at("mF", 96 * KB, [L], BF16)
        decs = [AR.at("dec%d" % i, 100 * KB + i * 64, [16], F32) for i in range(2)]
        ssq = AR.at("ssq", 100 * KB + 128, [16], F32)
        SstD = [[AR.at("Sst%d_%d" % (d_, i), 100 * KB + 256 + (2 * d_ + i) * 512, [128], F32) for i in range(2)] for d_ in range(2)]
        on_tok = AR.base[:, 44 * KB // 2: 48 * KB // 2].rearrange("p (a b) -> p a b", b=128)
        mT = AR.base[:, 48 * KB // 2: 52 * KB // 2]

        op("sp", lambda e: e.dma_start(out=lbl[:], in_=lbl_d), w=["lbl"], dma=True)
        op("sp", lambda e: e.dma_start(out=hg[:], in_=hg_d), w=["hg"], dma=True)
        op("sp", lambda e: e.dma_start(out=hmask[:], in_=hmask_d), w=["hmask"], dma=True)
        op("dve", lambda e: e.tensor_tensor(out=lb[:], in0=lbl[:, 1, :], in1=lbl[:, 0, :], op=ALU.subtract), r=["lbl"], w=["lb"])
        op("act", lambda e: e.activation(out=lb[:], in_=lb[:], func=AF.Sigmoid), r=["lb"], w=["lb"])
        op("dve", lambda e: e.tensor_scalar(out=oml[:], in0=lb[:], scalar1=-1.0, scalar2=1.0, op0=ALU.mult, op1=ALU.add), r=["lb"], w=["oml"])
        op("dve", lambda e: e.tensor_scalar(out=noml[:], in0=oml[:], scalar1=-1.0, scalar2=None, op0=ALU.mult), r=["oml"], w=["noml"])
        op("pool", lambda e: e.memset(mF, 1.0), w=["mF"])
        op("pool", lambda e: e.memset(mF.rearrange("p (a b) -> p a b", b=64)[:, :, 0:1], 0.0), w=["mF"])

        def load_head(h):
            sa, sb_ = (0, 1) if h % 2 == 0 else (2, 3)
            secs = []
            for i, sec in enumerate((0, 1, 2, 3)):
                dst = WB[sa][:, i * 1024:(i + 1) * 1024].rearrange("p (a b) -> p a b", b=128)
                op("pool", lambda e, dst=dst, sec=sec, h=h: e.dma_start(out=dst, in_=w_in[:, :, sec * 1024 + h * 128: sec * 1024 + (h + 1) * 128]),
                   w=[("wb", sa)], dma=True, dkey=("wbs", sa, i))
                secs.append(dst)
            dst = WB[sb_][:, 0:1024].rearrange("p (a b) -> p a b", b=128)
            op("pool", lambda e, dst=dst, h=h: e.dma_start(out=dst, in_=w_in[:, :, 4 * 1024 + h * 128: 4 * 1024 + (h + 1) * 128]),
               w=[("wb", sb_)], dma=True, dkey=("wbs", sb_, 0))
            secs.append(dst)
            wo = WB[sb_][:, 1024:2048]
            op("pool", lambda e, wo=wo, h=h: e.dma_start(out=wo, in_=w_out[:, h, :]), w=[("wb", sb_)], dma=True, dkey=("wbs", sb_, 1))
            return secs, wo, sa, sb_

        def proj_fm(wsec, slot, consume):
            for tb in range(4):
                bank = tb
                for c in range(8):
                    op("pe", lambda e, bank=bank, c=c, tb=tb: e.matmul(PS[bank][:], lhsT=wsec[:, c, :], rhs=hT[:, c, tb * 512:(tb + 1) * 512], start=(c == 0), stop=(c == 7)),
                       r=[("wb", slot), hk(c, tb)], w=[psk(bank)])
                consume(tb, bank)

        pend_oproj = []

        def do_head(h, cur):
            (wq, wi_, wff, wfb, wg), wo, sa, sb_ = cur
            mTh = WB[sb_][:, 2048:4096]
            proj_fm(wq, sa, lambda tb, bank: op("act", lambda e: e.activation(out=qT[:, tb * 512:(tb + 1) * 512], in_=PS[bank][:], func=AF.Copy), r=[psk(bank)], w=["qT"]))
            proj_fm(wg, sb_, lambda tb, bank: op("act", lambda e: e.activation(out=sigG[:, tb * 512:(tb + 1) * 512], in_=PS[bank][:], func=AF.Sigmoid), r=[psk(bank)], w=["sigG"]))
            for q4 in range(4):
                bank = 4 + q4 % 2
                for jj in range(4):
                    j = q4 * 4 + jj
                    for c in range(8):
                        op("pe", lambda e, bank=bank, jj=jj, j=j, c=c: e.matmul(PS[bank][:, jj * 128:(jj + 1) * 128], lhsT=hT[:, c, j * 128:(j + 1) * 128], rhs=wi_[:, c, :], start=(c == 0), stop=(c == 7)),
                           r=[("wb", sa), hk(c, j // 4)], w=[psk(bank)])
                op("dve", lambda e, bank=bank, q4=q4: e.tensor_copy(out=vtok[:, q4 * 4:(q4 + 1) * 4, :], in_=PS[bank][:].rearrange("p (a b) -> p a b", b=128)), r=[psk(bank)], w=["vtok"])
            if h == 0:
                dbg_put("qT", qT[:, 0:256], ["qT"])
                dbg_put("sigG", sigG[:, 0:256], ["sigG"])
                dbg_put("vtok", vtok[:, 0:2, :], ["vtok"])
            sidx = [0, 0]

            def do_dir(d, hf, S):
                si = S["i"]
                A, B, kk, qdec, kinv, kend, scT, Sb = S["A"], S["B"], S["kk"], S["qdec"], S["kinv"], S["kend"], S["scT"], S["Sb"]
                dec = decs[si]
                kA, kB, kK, kQ, kI, kE, kS, kD = ["%s%d" % (n_, si) for n_ in ("hgA", "hgB", "kk", "qdec", "kinv", "kend", "scT", "dec")]
                wf = wff if d == 0 else wfb
                T0 = hf * 1024
                for t2 in range(2):
                    tb = 2 * hf + t2
                    bank = S["pbank"][t2]
                    for c in range(8):
                        op("pe", lambda e, c=c, tb=tb, bank=bank: e.matmul(PS[bank][:], lhsT=wf[:, c, :], rhs=hT[:, c, tb * 512:(tb + 1) * 512], start=(c == 0), stop=(c == 7)),
                           r=[("wb", sa), hk(c, tb)], w=[psk(bank)])
                    op("act", lambda e, t2=t2, bank=bank: e.activation(out=A[:, t2 * 512:(t2 + 1) * 512], in_=PS[bank][:], func=AF.Sigmoid), r=[psk(bank)], w=[kA])
                    yield
                op("act", lambda e: e.activation(out=kk, in_=A, func=AF.Identity, scale=noml[:, h:h + 1], bias=oml[:, h:h + 1]), r=[kA, "noml", "oml"], w=[kK])
                op("act", lambda e: e.activation(out=A, in_=A, func=AF.Ln, scale=oml[:, h:h + 1], bias=lb[:, h:h + 1]), r=[kA, "lb", "oml"], w=[kA])
                yield
                if d == 0:
                    op("dve", lambda e: e.tensor_tensor_scan(out=B, data0=mF[:, 0:1024], data1=A, initial=0.0, op0=ALU.mult, op1=ALU.add), r=[kA, "mF"], w=[kB])
                else:
                    op("dve", lambda e: e.tensor_tensor_scan(out=B[:, ::-1], data0=mF[:, 0:1024], data1=A[:, ::-1], initial=0.0, op0=ALU.mult, op1=ALU.add), r=[kA, "mF"], w=[kB])
                yield
                op("act", lambda e: e.activation(out=A, in_=B, func=AF.Exp), r=[kB], w=[kA])
                yield
                op("dve", lambda e: e.tensor_tensor(out=qdec, in0=qT[:, T0:T0 + 1024], in1=A, op=ALU.mult), r=[kA, "qT"], w=[kQ])
                yield
                op("act", lambda e: e.activation(out=A, in_=B, func=AF.Exp, scale=-1.0), r=[kB, kQ], w=[kA])
                bl = B.rearrange("p (a b) -> p a b", b=64)[:, :, 63:64] if d == 0 else B.rearrange("p (a b) -> p a b", b=64)[:, :, 0:1]
                op("act", lambda e: e.activation(out=dec.rearrange("p (a b) -> p a b", b=1), in_=bl, func=AF.Exp), r=[kB], w=[kD])
                yield
                op("dve", lambda e: e.tensor_tensor(out=A, in0=kk, in1=A, op=ALU.mult), r=[kA, kK], w=[kA])
                yield
                op("act", lambda e: e.activation(out=kinv, in_=A, func=AF.Copy), r=[kA], w=[kI])
                dec_bc = bass.AP(dec.tensor, dec.offset, [list(dec.ap[0]), [1, 16], [0, 64]])
                op("dve", lambda e: e.tensor_tensor(out=kk.rearrange("p (a b) -> p a b", b=64), in0=A.rearrange("p (a b) -> p a b", b=64), in1=dec_bc, op=ALU.mult),
                   r=[kA, kD], w=[kK])
                yield
                xb = S["xbank"]
                pb = PS[xb][:].bitcast(BF16)
                for jj in range(8):
                    op("pe", lambda e, jj=jj: e.transpose(out=pb[:, jj * 128:(jj + 1) * 128], in_=kk[:, jj * 128:(jj + 1) * 128], identity=ident_b[:]),
                       r=[kK, "ident_b"], w=[psk(xb)])
                op("act", lambda e: e.activation(out=kend, in_=pb.rearrange("p (a b) -> p a b", b=128), func=AF.Copy), r=[psk(xb)], w=[kE])
                yield
                hm_ = hmask[:, d, :]
                mk = bass.AP(hm_.tensor, hm_.offset, [list(hm_.ap[0]), [0, 4], [1, 128]])
                for q4 in range(2):
                    bank = S["pbank"][q4]
                    for jj in range(4):
                        jl = q4 * 4 + jj
                        op("pe", lambda e, bank=bank, jj=jj, jl=jl: e.matmul(PS[bank][:, jj * 128:(jj + 1) * 128], lhsT=kinv[:, jl * 128:(jl + 1) * 128], rhs=qdec[:, jl * 128:(jl + 1) * 128], start=True, stop=True),
                           r=[kI, kQ], w=[psk(bank)])
                    op("dve", lambda e, bank=bank, q4=q4: e.tensor_tensor(out=scT[:, q4 * 4:(q4 + 1) * 4, :], in0=PS[bank][:].rearrange("p (a b) -> p a b", b=128), in1=mk, op=ALU.mult),
                       r=[psk(bank), "hmask"], w=[kS])
                    yield
                order = list(range(16)) if d == 0 else list(range(15, -1, -1))
                for cl in order:
                    n = sidx[d]
                    sidx[d] += 1
                    cur, new = SstD[d][n % 2], SstD[d][(n + 1) % 2]
                    ci = hf * 16 + cl
                    par = ci % 2
                    bank = 6 + par
                    op("act", lambda e, cur=cur, cl=cl: e.activation(out=Sb[:, cl, :], in_=cur, func=AF.Copy), r=[("Sst", d, n % 2)], w=[("Sb", si, cl)])
                    op("pe", lambda e, bank=bank, cl=cl, par=par: e.matmul(PS[bank][:, 0:128], lhsT=kend[par * 64:(par + 1) * 64, cl // 2, :], rhs=vtok[par * 64:(par + 1) * 64, hf * 8 + cl // 2, :], start=True, stop=True),
                       r=[kE, "vtok"], w=[psk(bank)])
                    op("dve", lambda e, bank=bank, cur=cur, new=new, cl=cl: e.scalar_tensor_tensor(out=new, in0=cur, scalar=dec[:, cl:cl + 1], in1=PS[bank][:, 0:128], op0=ALU.mult, op1=ALU.add),
                       r=[psk(bank), ("Sst", d, n % 2), kD], w=[("Sst", d, (n + 1) % 2)])
                    if cl % 2 == 1:
                        yield
                first = (d == 0 and hf == 0) or (d == 1 and hf == 1)
                for q4 in range(2):
                    for jj in range(4):
                        jl = q4 * 4 + jj
                        j = hf * 8 + jl
                        o_ = PS[xb][:, jj * 128:(jj + 1) * 128]
                        op("pe", lambda e, o_=o_, jl=jl, j=j: e.matmul(o_, lhsT=scT[:, jl, :], rhs=vtok[:, j, :], start=True, stop=False), r=[kS, "vtok"], w=[psk(xb)])
                        op("pe", lambda e, jj=jj, jl=jl: e.matmul(PS[xb][0:64, jj * 128:(jj + 1) * 128], lhsT=qdec[:, jl * 128:jl * 128 + 64], rhs=Sb[:, 2 * jl, :], start=False, stop=True),
                           r=[kQ, ("Sb", si, 2 * jl)], w=[psk(xb)])
                        op("pe", lambda e, jj=jj, jl=jl: e.matmul(PS[xb][64:128, jj * 128:(jj + 1) * 128], lhsT=qdec[:, jl * 128 + 64:jl * 128 + 128], rhs=Sb[:, 2 * jl + 1, :], start=False, stop=True),
                           r=[kQ, ("Sb", si, 2 * jl + 1)], w=[psk(xb)])
                    ov = oacc[:, hf * 8 + q4 * 4:hf * 8 + (q4 + 1) * 4, :]
                    pv = PS[xb][:].rearrange("p (a b) -> p a b", b=128)
                    if first:
                        op("dve", lambda e, ov=ov, pv=pv: e.tensor_copy(out=ov, in_=pv), r=[psk(xb)], w=[("oacc", hf)])
                    else:
                        op("dve", lambda e, ov=ov, pv=pv: e.tensor_tensor(out=ov, in0=ov, in1=pv, op=ALU.add), r=[psk(xb), ("oacc", hf)], w=[("oacc", hf)])
                    yield

            def interleave(gens):
                alive = list(gens)
                while alive:
                    for g_ in list(alive):
                        try:
                            next(g_)
                        except StopIteration:
                            alive.remove(g_)

            op("pool", lambda e: e.memset(SstD[0][0], 0.0), w=[("Sst", 0, 0)])
            op("pool", lambda e: e.memset(SstD[1][0], 0.0), w=[("Sst", 1, 0)])
            prev = list(pend_oproj)
            del pend_oproj[:]
            interleave([do_dir(0, 0, HS[0]), do_dir(1, 1, HS[1])] + prev)
            interleave([do_dir(0, 1, HS[0]), do_dir(1, 0, HS[1])])
            OACC = [("oacc", 0), ("oacc", 1)]
            HGA = ["hgA0"]
            HGB = ["hgB0"]
            for j in range(16):
                op("act", lambda e, j=j: e.activation(out=on_tok[:, j, :], in_=oacc[:, j, :], func=AF.Square, accum_out=ssq[:, j:j + 1]), r=OACC + HGB, w=HGA + ["ssq"])
            op("act", lambda e: e.activation(out=ssq, in_=ssq, func=AF.Sqrt, scale=1.0 / 128, bias=EPS), r=["ssq"], w=["ssq"])
            op("dve", lambda e: e.reciprocal(out=ssq, in_=ssq), r=["ssq"], w=["ssq"])
            for j in range(16):
                op("dve", lambda e, j=j: e.tensor_scalar(out=on_tok[:, j, :], in0=oacc[:, j, :], scalar1=ssq[:, j:j + 1], scalar2=None, op0=ALU.mult), r=OACC + ["ssq"] + HGA, w=HGA)
            for half in range(2):
                bank = half
                pb = PS[bank][:].bitcast(BF16)
                for jj in range(8):
                    j = half * 8 + jj
                    op("pe", lambda e, pb=pb, jj=jj, j=j: e.transpose(out=pb[:, jj * 128:(jj + 1) * 128], in_=on_tok[:, j, :], identity=ident_b[:]), r=HGA + ["ident_b"], w=[psk(bank)])
                op("dve", lambda e, pb=pb, half=half: e.scalar_tensor_tensor(out=mTh[:, half * 1024:(half + 1) * 1024], in0=pb, scalar=hg[:, h:h + 1], in1=sigG[:, half * 1024:(half + 1) * 1024], op0=ALU.mult, op1=ALU.mult),
                   r=[psk(bank), "hg", "sigG"] + HGA, w=[("mT", sb_)])
            if h == 0:
                dbg_put("mT", mTh[:, 0:256], [("mT", sb_)])
            if dbg:
                return
            def oproj_gen():
                for tb in range(4):
                    for dc in range(8):
                        bank = 2 + (tb * 8 + dc) % 2
                        op("pe", lambda e, dc=dc, tb=tb, bank=bank: e.matmul(PS[bank][:], lhsT=wo[:, dc * 128:(dc + 1) * 128], rhs=mTh[:, tb * 512:(tb + 1) * 512], start=True, stop=True),
                           r=[("wb", sb_), ("mT", sb_)], w=[psk(bank)])
                        xs = xT[:, dc, tb * 512:(tb + 1) * 512]
                        op("dve", lambda e, xs=xs, bank=bank: e.tensor_tensor(out=xs, in0=xs, in1=PS[bank][:], op=ALU.add), r=[psk(bank), xk(dc, tb)], w=[xk(dc, tb)])
                        yield
            for _ in oproj_gen():
                pass

        nxt_holder = [load_head(0)]
        for h in range(8):
            cur = nxt_holder[0]
            if h + 1 < 8:
                nxt_holder[0] = load_head(h + 1)
            do_head(h, cur)
            if dbg:
                break
        for g_ in pend_oproj:
            for _ in g_:
                pass

    if "l0attn" in parts or "l0mix" in parts:
        attn_phase(0)
    if "l0s5" in parts or "l0mix" in parts:
        s5_phase()
    if "l0ffn" in parts:
        ffn_phase(0, 1)
    if "l1mix" in parts:
        hgrn_phase(2)
    if "l1ffn" in parts:
        ffn_phase(1, 3)

    if dbg:
        P.barrier()
        op("sp", lambda e: e.dma_start(out=out_d.rearrange("(p a) d -> p (a d)", p=128), in_=xT_flat), dma=True, dkey="dbgout")
        P.emit()
        return nc
    AR.phase()
    sq = [AR.at("sq%d" % i, i * 4 * KB, [L], BF16) for i in range(2)]
    rstd = AR.at("rstd", 8 * KB, [L], F32)
    stage = [AR.at("stage%d" % i, 16 * KB + i * 4 * KB, [D], F32) for i in range(2)]
    ftmp = [AR.at("ftmp%d" % i, 24 * KB + i * 512, [128], F32) for i in range(4)]
    do_final = "final" in parts
    if do_final:
        rmsnorm_rstd(rstd, sq)
    k = 0
    for t in range(16):
        st = stage[t % 2]
        for half in range(2):
            bank = (2 * t + half) % 4
            for j in range(4):
                c = half * 4 + j
                src = xT[:, c, t * 128:(t + 1) * 128]
                if do_final:
                    ft = ftmp[k % 4]
                    op("dve", lambda e, ft=ft, src=src, c=c, t=t: e.scalar_tensor_tensor(
                        out=ft, in0=src, scalar=gains[:, 4, c:c + 1], in1=rstd[:, t * 128:(t + 1) * 128], op0=ALU.mult, op1=ALU.mult),
                       r=[xk(c, t // 4), ("rstd", t // 4), "gains"], w=[("ftmp", k % 4)])
                    op("pe", lambda e, bank=bank, j=j, ft=ft: e.transpose(out=PS[bank][:, j * 128:(j + 1) * 128], in_=ft, identity=ident_f[:]),
                       r=[("ftmp", k % 4), "ident_f"], w=[psk(bank)])
                    k += 1
                else:
                    op("pe", lambda e, bank=bank, j=j, src=src: e.transpose(out=PS[bank][:, j * 128:(j + 1) * 128], in_=src, identity=ident_f[:]),
                       r=[xk(c, t // 4), "ident_f"], w=[psk(bank)])
            dst = st[:, half * 512:(half + 1) * 512]
            op("act", lambda e, dst=dst, bank=bank: e.activation(out=dst, in_=PS[bank][:], func=AF.Copy), r=[psk(bank)], w=[("stage", t % 2, half)])
        op("sp", lambda e, st=st, t=t: e.dma_start(out=out_d[t * 128:(t + 1) * 128, :], in_=st), r=[("stage", t % 2, 0), ("stage", t % 2, 1)], dma=True, dkey=("st", t % 2))
    P.emit()
    return nc


def host_consts():
    c = {}
    c["ident"] = np.eye(128, dtype=np.float32)
    inv = (10000.0 ** (-np.arange(0, 64, 2, dtype=np.float32) / np.float32(64))).astype(np.float32)
    ang = (np.arange(L, dtype=np.float32)[None, :] * inv[:, None]).astype(np.float32)
    c["rope_cos"] = np.ascontiguousarray(np.tile(np.cos(ang).astype(np.float32), (4, 1)))
    c["rope_sin"] = np.ascontiguousarray(np.tile(np.sin(ang).astype(np.float32), (4, 1)))
    s_ = np.arange(128)[:, None]
    c_ = np.arange(128)[None, :]
    same = (s_ // 64) == (c_ // 64)
    hm = np.stack([(same & (s_ <= c_)), (same & (s_ >= c_))], axis=1).astype(np.float32)
    c["hmask"] = np.ascontiguousarray(hm)
    sl_ = (np.arange(128) // 16)[:, None]
    tl_ = (np.arange(128) // 16)[None, :]
    c["s5m"] = np.ascontiguousarray(np.stack([(tl_ >= sl_), (tl_ <= sl_)], axis=1).astype(np.float32))
    return c


def make_in_maps(inputs, parts=None):
    consts = host_consts()
    gains = np.concatenate([inputs["norm_mix_g"][0:1], inputs["norm_mlp_g"][0:1], inputs["norm_mix_g"][1:2],
                            inputs["norm_mlp_g"][1:2], inputs["final_norm_g"][None, :]], axis=0).astype(np.float32)
    shared = dict(consts)
    shared["gains"] = np.ascontiguousarray(gains.reshape(5, 8, 128).transpose(2, 0, 1))
    shared["w_in_odd"] = np.ascontiguousarray(inputs["w_in_odd"][0], dtype=np.float32)
    shared["w_out_odd"] = np.ascontiguousarray(inputs["w_out_odd"][0], dtype=np.float32)
    shared["lbl"] = np.ascontiguousarray(np.asarray(inputs["hgrn_lb_logits"], dtype=np.float32).reshape(2, 8, 128).transpose(2, 0, 1))
    shared["hg"] = np.ascontiguousarray(np.asarray(inputs["hgrn_norm_g"][0], dtype=np.float32).reshape(8, 128).T)
    shared["w_in_even"] = np.ascontiguousarray(inputs["w_in_even"][0], dtype=np.float32)
    shared["w_out_even"] = np.ascontiguousarray(inputs["w_out_even"][0], dtype=np.float32)
    shared["diff_lambda"] = np.ascontiguousarray(np.asarray(inputs["diff_lambda"][0], dtype=np.float32).reshape(1, 256))
    shared["subln_g"] = np.ascontiguousarray(np.asarray(inputs["diff_subln_g"][0], dtype=np.float32).reshape(128, 1))
    f32 = np.float32
    lre = np.asarray(inputs["s5_lam_re"][0], f32); lim = np.asarray(inputs["s5_lam_im"][0], f32); lst = np.asarray(inputs["s5_log_step"][0], f32)
    s5s = np.stack([lre.transpose(0, 2, 1).reshape(128, 32), lim.transpose(0, 2, 1).reshape(128, 32),
                    np.repeat(lst[:, None, :], 64, axis=1).reshape(128, 32)], axis=1)
    shared["s5s"] = np.ascontiguousarray(s5s, dtype=f32)
    shared["s5b"] = np.ascontiguousarray(np.stack([np.asarray(inputs[k][0], f32).transpose(0, 2, 1, 3).reshape(128, 32, 16) for k in ("s5_b_re", "s5_b_im")], axis=1))
    shared["s5c"] = np.ascontiguousarray(np.stack([np.asarray(inputs[k][0], f32).transpose(0, 3, 1, 2).reshape(128, 32, 16) for k in ("s5_c_re", "s5_c_im")], axis=1))
    shared["s5d"] = np.ascontiguousarray(np.asarray(inputs["s5_d"][0], f32).reshape(4, 128).T)
    shared["s5bg"] = np.ascontiguousarray(np.asarray(inputs["s5_b_glu"][0], f32).reshape(4, 128).T)
    shared["w_glu"] = np.ascontiguousarray(inputs["s5_w_glu"][0], dtype=f32)
    shared["w_ff_in"] = np.ascontiguousarray(inputs["w_ff_in"], dtype=np.float32)
    shared["w_ff_out"] = np.ascontiguousarray(inputs["w_ff_out"], dtype=np.float32)
    x = np.asarray(inputs["x"], dtype=np.float32)
    maps = []
    for b in range(x.shape[0]):
        m = dict(shared)
        m["x"] = np.ascontiguousarray(x[b])
        maps.append(m)
    return maps


ALL_PARTS = {"l0mix", "l0ffn", "l1mix", "l1ffn", "final"}
_NC_CACHE = {}


def kernel(**inputs):
    inputs = {k: np.asarray(v) for k, v in inputs.items()}
    key = "full"
    if key not in _NC_CACHE:
        _NC_CACHE[key] = build({"parts": ALL_PARTS})
    nc = _NC_CACHE[key]
    maps = make_in_maps(inputs)
    res = run_bass_kernel_spmd(nc, maps, core_ids=list(range(8)))
    out = np.stack([np.asarray(r["out"]) for r in res.results], axis=0)
    return out.astype(np.float32)
```

```python
import math
from contextlib import ExitStack

import numpy as np
import concourse.bass as bass
import concourse.mybir as mybir
from concourse.bass_utils import run_bass_kernel_spmd

F32 = mybir.dt.float32
BF16 = mybir.dt.bfloat16
AF = mybir.ActivationFunctionType
ALU = mybir.AluOpType
AX = mybir.AxisListType

ENGS = ("pe", "act", "dve", "pool", "sp")

L = 2048
D = 1024
NTB = 4
EPS = 1e-6
LAMBDA_INIT0 = 0.8 - 0.6 * math.exp(-0.3 * 0)


class Op:
    __slots__ = ("eng", "fn", "deps", "is_dma", "dkey", "signal", "sem", "val", "idx")


class Prog:
    def __init__(self, nc):
        self.nc = nc
        self.ops = []
        self.last_w = {}
        self.readers = {}
        self.stack = ExitStack()
        self.pending_barrier = {}

    def sb(self, name, shape, dt):
        return self.stack.enter_context(self.nc.sbuf_tensor("sb_" + name, list(shape), dt))

    def ps(self, name, shape, dt=F32):
        return self.stack.enter_context(self.nc.psum_tensor("pp_" + name, list(shape), dt))

    def barrier(self):
        deps = set()
        last = {}
        for o in self.ops:
            if o.is_dma:
                deps.add(o.idx)
            else:
                last[o.eng] = o.idx
        deps.update(last.values())
        self.pending_barrier = {e: set(deps) for e in ENGS}
        self.last_w = {}
        self.readers = {}

    def op(self, eng, fn, r=(), w=(), dma=False, dkey=None):
        o = Op()
        o.eng, o.fn, o.is_dma, o.signal = eng, fn, dma, False
        o.idx = len(self.ops)
        deps = set()
        for k in list(r) + list(w):
            if k in self.last_w:
                deps.add(self.last_w[k])
        for k in w:
            for rd in self.readers.get(k, ()):
                deps.add(rd)
        if self.pending_barrier.get(eng):
            deps |= self.pending_barrier[eng]
            self.pending_barrier[eng] = set()
        deps.discard(o.idx)
        o.deps = deps
        o.dkey = (dkey if dkey is not None else (w[0] if len(w) else r[0])) if dma else None
        for k in r:
            self.readers.setdefault(k, []).append(o.idx)
        for k in w:
            self.last_w[k] = o.idx
            self.readers[k] = []
        self.ops.append(o)
        return o.idx

    def _skip(self, od, o):
        return (not od.is_dma) and (not o.is_dma) and od.eng == o.eng and o.eng == "pe"

    def emit(self, final_eng="sp"):
        nc = self.nc
        ops = self.ops
        final_deps = [o.idx for o in ops if o.is_dma]
        for o in ops:
            for d in o.deps:
                if not self._skip(ops[d], o):
                    ops[d].signal = True
        for d in final_deps:
            ops[d].signal = True
        sems = {}

        def get_sem(key):
            if key not in sems:
                sems[key] = self.stack.enter_context(nc.semaphore("s%d" % len(sems)))
            return sems[key]

        cnt = {}
        for o in ops:
            if not o.signal:
                continue
            key = ("dma", o.dkey) if o.is_dma else ("eng", o.eng)
            inc = 16 if o.is_dma else 1
            cnt[key] = cnt.get(key, 0) + inc
            o.sem = get_sem(key)
            o.val = cnt[key]
            o.dkey = key
        self.n_sems = len(sems)
        per_eng = {e: [] for e in ENGS}
        for o in ops:
            per_eng[o.eng].append(o)

        def run(eng_name, e):
            waited = {}
            for o in per_eng[eng_name]:
                need = {}
                for d in o.deps:
                    od = ops[d]
                    if (not od.signal) or self._skip(od, o):
                        continue
                    if waited.get(od.dkey, 0) >= od.val:
                        continue
                    if need.get(od.dkey, (None, 0))[1] < od.val:
                        need[od.dkey] = (od.sem, od.val)
                for k, (sem, val) in need.items():
                    e.wait_ge(sem, val)
                    waited[k] = val
                ins = o.fn(e)
                if o.signal:
                    ins.then_inc(o.sem, 16 if o.is_dma else 1)
            if eng_name == final_eng:
                need = {}
                for d in final_deps:
                    od = ops[d]
                    if waited.get(od.dkey, 0) >= od.val:
                        continue
                    if need.get(od.dkey, (None, 0))[1] < od.val:
                        need[od.dkey] = (od.sem, od.val)
                for k, (sem, val) in need.items():
                    e.wait_ge(sem, val)

        with nc.Block() as block:
            @block.tensor
            def _(e):
                run("pe", e)

            @block.scalar
            def _(e):
                run("act", e)

            @block.vector
            def _(e):
                run("dve", e)

            @block.gpsimd
            def _(e):
                run("pool", e)

            @block.sync
            def _(e):
                run("sp", e)
        self.stack.close()


class Arena:
    def __init__(self, P, nbytes):
        self.P = P
        self.nbytes = nbytes
        self.base = P.sb("arena", [128, nbytes // 2], BF16)
        self.live = []

    def phase(self):
        self.P.barrier()
        self.live = []

    def at(self, name, off, shape, dt):
        n = 1
        for s in shape:
            n *= s
        nb = n * (4 if dt == F32 else 2)
        assert off % 4 == 0 and off + nb <= self.nbytes, (name, off, nb, self.nbytes)
        for (a, b, nm) in self.live:
            assert off >= b or off + nb <= a, ("arena overlap", name, nm)
        self.live.append((off, off + nb, name))
        ap = self.base[:, off // 2:(off + nb) // 2]
        if dt == F32:
            ap = ap.bitcast(F32)
        if len(shape) > 1:
            names = "abcd"[:len(shape)]
            pat = "p (" + " ".join(names) + ") -> p " + " ".join(names)
            ap = ap.rearrange(pat, **{names[i]: shape[i] for i in range(len(shape))})
        return ap

    def drop(self, name):
        self.live = [x for x in self.live if x[2] != name]


KB = 1024


def build(cfg):
    parts = cfg["parts"]
    nc = bass.Bass("TRN2", target_bir_lowering=False)

    def din(name, shape):
        return nc.dram_tensor(name, list(shape), F32, kind="ExternalInput").ap()

    x_d = din("x", [L, D])
    out_d = nc.dram_tensor("out", [L, D], F32, kind="ExternalOutput").ap()
    gains_d = din("gains", [128, 5, 8])
    w_ff_in_d = din("w_ff_in", [2, D, 4 * D])
    w_ff_out_d = din("w_ff_out", [2, 4 * D, D])
    ident_d = din("ident", [128, 128])
    w_in_odd_d = din("w_in_odd", [D, 5 * D])
    w_in_even_d = din("w_in_even", [D, 2 * D])
    w_out_even_d = din("w_out_even", [D, D])
    cos_d = din("rope_cos", [128, L])
    sin_d = din("rope_sin", [128, L])
    dl_d = din("diff_lambda", [1, 256])
    sg_d = din("subln_g", [128, 1])
    w_glu_d = din("w_glu", [512, 512])
    s5s_d = din("s5s", [128, 3, 32])
    s5b_d = din("s5b", [128, 2, 32, 16])
    s5c_d = din("s5c", [128, 2, 32, 16])
    s5d_d = din("s5d", [128, 4])
    s5bg_d = din("s5bg", [128, 4])
    s5m_d = din("s5m", [128, 2, 128])
    w_out_odd_d = din("w_out_odd", [D, D])
    lbl_d = din("lbl", [128, 2, 8])
    hg_d = din("hg", [128, 8])
    hmask_d = din("hmask", [128, 2, 128])

    P = Prog(nc)
    op = P.op

    xT = P.sb("xT", [128, 8, L], F32)
    WB = [P.sb("wb%d" % i, [128, 4096], BF16) for i in range(4)]
    ident_f = P.sb("ident_f", [128, 128], F32)
    ident_b = P.sb("ident_b", [128, 128], BF16)
    ones_b = P.sb("ones_b", [128, 128], BF16)
    ones_f = P.sb("ones_f", [128, 128], F32)
    gains = P.sb("gains", [128, 5, 8], F32)
    PS = [P.ps("ps%d" % i, [128, 512], F32) for i in range(8)]
    lbl = P.sb("lbl", [128, 2, 8], F32)
    dl = P.sb("dl", [128, 256], F32)
    dlp = P.sb("dlp", [128, 128], F32)
    lam = P.sb("lam", [128, 4], F32)
    sg = P.sb("sg", [128, 1], F32)
    rr_g = P.sb("rr_g", [128, 8], F32)
    hg = P.sb("hg", [128, 8], F32)
    hmask = P.sb("hmask", [128, 2, 128], F32)
    lb = P.sb("lb", [128, 8], F32)
    oml = P.sb("oml", [128, 8], F32)
    noml = P.sb("noml", [128, 8], F32)
    AR = Arena(P, 104 * KB)

    def psk(i):
        return ("ps", i)

    dbg = cfg.get("dbg", False)
    dbg_tab = cfg.setdefault("dbg_tab", {})
    xT_flat = xT[:].rearrange("p a b -> p (a b)")
    dbg_off = [0]

    def dbg_put(name, ap, rkeys):
        if not dbg:
            return
        shp = list(ap.shape)
        n = 1
        for v_ in shp[1:]:
            n *= v_
        o = dbg_off[0]
        dst = xT_flat[:, o:o + n]
        if len(shp) == 3:
            dst = dst.rearrange("p (a b) -> p a b", b=shp[2])
        op("dve", lambda e: e.tensor_copy(out=dst, in_=ap), r=list(rkeys), w=["dbg"])
        dbg_tab[name] = (o, shp)
        dbg_off[0] = o + n

    def xk(c, tb):
        return ("xT", c, tb)

    def hk(c, tb):
        return ("hT", c, tb)

    XT_ALL = [xk(c, tb) for c in range(8) for tb in range(4)]
    HT_ALL = [hk(c, tb) for c in range(8) for tb in range(4)]

    op("sp", lambda e: e.dma_start(out=ident_f[:], in_=ident_d), w=["ident_f"], dma=True)
    op("dve", lambda e: e.tensor_copy(out=ident_b[:], in_=ident_f[:]), r=["ident_f"], w=["ident_b"])
    op("pool", lambda e: e.memset(ones_b[:], 1.0), w=["ones_b"])
    op("pool", lambda e: e.memset(ones_f[:], 1.0), w=["ones_f"])
    op("sp", lambda e: e.dma_start(out=gains[:], in_=gains_d), w=["gains"], dma=True)

    AR.phase()
    xin = [AR.at("xin%d" % i, i * 4 * KB, [D], F32) for i in range(2)]
    ev = 0
    for t in range(16):
        b = xin[t % 2]
        op("sp", lambda e, b=b, t=t: e.dma_start(out=b, in_=x_d[t * 128:(t + 1) * 128, :]), w=[("xin", t % 2)], dma=True)
        for half in range(2):
            bank = (2 * t + half) % 4
            for j in range(4):
                c = half * 4 + j
                op("pe", lambda e, bank=bank, j=j, c=c, b=b: e.transpose(out=PS[bank][:, j * 128:(j + 1) * 128], in_=b[:, c * 128:(c + 1) * 128], identity=ident_f[:]),
                   r=[("xin", t % 2), "ident_f"], w=[psk(bank)])
            dst = xT[:, half * 4:(half + 1) * 4, t * 128:(t + 1) * 128]
            src = PS[bank][:].rearrange("p (a b) -> p a b", b=128)
            wk = [xk(half * 4 + j, t // 4) for j in range(4)]
            if ev % 2 == 0:
                op("dve", lambda e, dst=dst, src=src: e.tensor_copy(out=dst, in_=src), r=[psk(bank)], w=wk)
            else:
                op("act", lambda e, dst=dst, src=src: e.activation(out=dst, in_=src, func=AF.Copy), r=[psk(bank)], w=wk)
            ev += 1

    def rmsnorm_rstd(rstd, sq):
        for c in range(8):
            s = sq[c % 2]
            op("act", lambda e, s=s, c=c: e.activation(out=s, in_=xT[:, c, :], func=AF.Square),
               r=[xk(c, tb) for tb in range(4)], w=[("sq", c % 2)])
            for tb in range(4):
                op("pe", lambda e, s=s, c=c, tb=tb: e.matmul(PS[tb][:], lhsT=ones_b[:], rhs=s[:, tb * 512:(tb + 1) * 512], start=(c == 0), stop=(c == 7)),
                   r=[("sq", c % 2), "ones_b"], w=[psk(tb)])
        for tb in range(4):
            sl = rstd[:, tb * 512:(tb + 1) * 512]
            op("act", lambda e, sl=sl, tb=tb: e.activation(out=sl, in_=PS[tb][:], func=AF.Sqrt, scale=1.0 / D, bias=EPS),
               r=[psk(tb)], w=[("rstd", tb)])
            op("dve", lambda e, sl=sl: e.reciprocal(out=sl, in_=sl), r=[("rstd", tb)], w=[("rstd", tb)])

    def norm_to_hT(gi, hT, rstd, sq):
        rmsnorm_rstd(rstd, sq)
        for c in range(8):
            for tb in range(4):
                op("dve", lambda e, c=c, tb=tb: e.scalar_tensor_tensor(
                    out=hT[:, c, tb * 512:(tb + 1) * 512], in0=xT[:, c, tb * 512:(tb + 1) * 512], scalar=gains[:, gi, c:c + 1],
                    in1=rstd[:, tb * 512:(tb + 1) * 512], op0=ALU.mult, op1=ALU.mult),
                   r=[xk(c, tb), ("rstd", tb), "gains"], w=[hk(c, tb)])

    def wload(slot, src, r=()):
        a, b = src.shape[1], src.shape[2]
        dst = WB[slot][:, 0:a * b].rearrange("p (a b) -> p a b", b=b)
        op("pool", lambda e: e.dma_start(out=dst, in_=src), r=list(r), w=[("wb", slot)], dma=True)
        return dst

    def ffn_loader(l):
        w_in = w_ff_in_d[l].rearrange("(c p) f -> p c f", p=128)
        w_out = w_ff_out_d[l].rearrange("(c p) d -> p c d", p=128)
        views = {}

        def load(fg):
            views[fg] = (wload((2 * fg) % 4, w_in[:, :, fg * 512:(fg + 1) * 512]),
                         wload((2 * fg + 1) % 4, w_out[:, fg * 4:(fg + 1) * 4, :]))
        return views, load

    def ffn(l, hT, actT, rl, views, load):
        k = 0
        for fg in range(8):
            wi, wo = views[fg]
            sa, sbk = (2 * fg) % 4, (2 * fg + 1) % 4
            at = actT[fg % 2]
            for tb in range(4):
                for fc in range(4):
                    bank = k % 3
                    for c in range(8):
                        op("pe", lambda e, bank=bank, wi=wi, c=c, fc=fc, tb=tb: e.matmul(
                            PS[bank][:], lhsT=wi[:, c, fc * 128:(fc + 1) * 128], rhs=hT[:, c, tb * 512:(tb + 1) * 512], start=(c == 0), stop=(c == 7)),
                           r=[("wb", sa), hk(c, tb)], w=[psk(bank)])
                    r_ = rl[k % 2]
                    op("act", lambda e, bank=bank, r_=r_: e.activation(out=r_, in_=PS[bank][:], func=AF.Relu), r=[psk(bank)], w=[("rl", k % 2)])
                    op("act", lambda e, r_=r_, at=at, fc=fc, tb=tb: e.activation(out=at[:, fc, tb * 512:(tb + 1) * 512], in_=r_, func=AF.Square),
                       r=[("rl", k % 2)], w=[("actT", fg % 2, fc, tb)])
                    k += 1
            kk = 0
            for tb in range(4):
                for dc in range(8):
                    bank = 3 + kk % 3
                    for fc in range(4):
                        op("pe", lambda e, bank=bank, wo=wo, fc=fc, dc=dc, tb=tb, at=at: e.matmul(
                            PS[bank][:], lhsT=wo[:, fc, dc * 128:(dc + 1) * 128], rhs=at[:, fc, tb * 512:(tb + 1) * 512], start=(fc == 0), stop=(fc == 3)),
                           r=[("wb", sbk), ("actT", fg % 2, fc, tb)], w=[psk(bank)])
                    xs = xT[:, dc, tb * 512:(tb + 1) * 512]
                    op("dve", lambda e, xs=xs, bank=bank: e.tensor_tensor(out=xs, in0=xs, in1=PS[bank][:], op=ALU.add), r=[psk(bank), xk(dc, tb)], w=[xk(dc, tb)])
                    kk += 1
            if fg + 2 < 8:
                load(fg + 2)

    def ffn_phase(l, gi):
        AR.phase()
        hT = AR.at("hT", 0, [8, L], BF16)
        actT = [AR.at("actT%d" % i, 32 * KB + i * 16 * KB, [4, L], BF16) for i in range(2)]
        rl = [AR.at("rl%d" % i, 64 * KB + i * 2 * KB, [512], F32) for i in range(2)]
        sq = [AR.at("sq%d" % i, 68 * KB + i * 4 * KB, [L], BF16) for i in range(2)]
        rstd = AR.at("rstd", 76 * KB, [L], F32)
        views, load = ffn_loader(l)
        load(0)
        load(1)
        norm_to_hT(gi, hT, rstd, sq)
        ffn(l, hT, actT, rl, views, load)


    def attn_phase(gi):
        w_in = w_in_even_d.rearrange("(c p) f -> p c f", p=128)
        w_out = w_out_even_d.rearrange("(c p) d -> p c d", p=128)
        T0 = 81 * KB
        AR.phase()
        hT = AR.at("hT", 0, [8, L], BF16)
        sq = [AR.at("sq%d" % i, T0 + i * 4 * KB, [L], BF16) for i in range(2)]
        rstd = AR.at("rstd", T0 + 8 * KB, [L], F32)
        wq = wload(0, w_in[:, :, 0:512])
        wk = wload(2, w_in[:, :, 512:1024])
        norm_to_hT(gi, hT, rstd, sq)
        op("sp", lambda e: e.dma_start(out=dl[:], in_=bass.AP(dl_d.tensor, 0, [[0, 128], [1, 256]])), w=["dl"], dma=True)
        op("sp", lambda e: e.dma_start(out=sg[:], in_=sg_d), w=["sg"], dma=True)
        op("dve", lambda e: e.tensor_tensor(out=dlp[:, 0:64], in0=dl[:, 0:64], in1=dl[:, 64:128], op=ALU.mult), r=["dl"], w=["dlp"])
        op("dve", lambda e: e.tensor_tensor(out=dlp[:, 64:128], in0=dl[:, 128:192], in1=dl[:, 192:256], op=ALU.mult), r=["dl", "dlp"], w=["dlp"])
        op("dve", lambda e: e.reduce_sum(out=lam[:, 0:2], in_=dlp[:].rearrange("p (a b) -> p a b", b=64), axis=AX.X), r=["dlp"], w=["lam"])
        op("act", lambda e: e.activation(out=lam[:, 0:2], in_=lam[:, 0:2], func=AF.Exp), r=["lam"], w=["lam"])
        op("dve", lambda e: e.tensor_tensor(out=lam[:, 2:3], in0=lam[:, 1:2], in1=lam[:, 0:1], op=ALU.subtract), r=["lam"], w=["lam"])
        op("dve", lambda e: e.tensor_scalar(out=lam[:, 3:4], in0=lam[:, 2:3], scalar1=-LAMBDA_INIT0, scalar2=None, op0=ALU.add), r=["lam"], w=["lam"])
        op("dve", lambda e: e.tensor_scalar(out=sg[:], in0=sg[:], scalar1=1.0 - LAMBDA_INIT0, scalar2=None, op0=ALU.mult), r=["sg"], w=["sg"])
        nlam = lam[:, 3:4]
        if cfg.get("attn_stop", 9) <= 1:
            return

        AR.phase()
        hT = AR.at("hT", 0, [8, L], BF16)
        qT = AR.at("qT", 32 * KB, [4, L], BF16)
        kT = AR.at("kT", 48 * KB, [4, L], BF16)
        vaug = AR.at("vaug", 64 * KB, [16, 4, 130], BF16)
        t1 = [AR.at("t1_%d" % i, T0 + i * 2 * KB, [512], F32) for i in range(2)]
        t2 = [AR.at("t2_%d" % i, T0 + 4 * KB + i * 2 * KB, [512], F32) for i in range(2)]
        cosb = [AR.at("cos%d" % i, T0 + 8 * KB + i * 2 * KB, [512], F32) for i in range(2)]
        sinb = [AR.at("sin%d" % i, T0 + 12 * KB + i * 2 * KB, [512], F32) for i in range(2)]

        def load_group(slot, g):
            return wload(slot, w_in[:, :, g * 512:(g + 1) * 512])

        def rotate(sa, sr):
            a = WB[sa][:].rearrange("p (cb two j) -> p cb two j", two=2, j=32)
            r_ = WB[sr][:].rearrange("p (cb two j) -> p cb two j", two=2, j=32)
            op("dve", lambda e: e.tensor_scalar(out=r_[:, :, 0, :], in0=a[:, :, 1, :], scalar1=-1.0, scalar2=None, op0=ALU.mult), r=[("wb", sa)], w=[("wb", sr)])
            op("dve", lambda e: e.tensor_copy(out=r_[:, :, 1, :], in_=a[:, :, 0, :]), r=[("wb", sa)], w=[("wb", sr)])
            return WB[sr][:].rearrange("p (a b) -> p a b", b=512)

        wqr = rotate(0, 1)
        wkr = rotate(2, 3)
        op("pool", lambda e: e.memset(vaug[:, :, :, 128:130], 1.0), w=["vaug1"])
        kctr = [0]

        def rope_proj(tb, wA, wR, sA, sR, dstT, h, dkey):
            i = kctr[0] % 2
            kctr[0] += 1
            ba, bb = 2 * i, 2 * i + 1
            for c in range(8):
                op("pe", lambda e, c=c: e.matmul(PS[ba][:], lhsT=wA[:, c, h * 128:(h + 1) * 128], rhs=hT[:, c, tb * 512:(tb + 1) * 512], start=(c == 0), stop=(c == 7)),
                   r=[("wb", sA), hk(c, tb)], w=[psk(ba)])
            for c in range(8):
                op("pe", lambda e, c=c: e.matmul(PS[bb][:], lhsT=wR[:, c, h * 128:(h + 1) * 128], rhs=hT[:, c, tb * 512:(tb + 1) * 512], start=(c == 0), stop=(c == 7)),
                   r=[("wb", sR), hk(c, tb)], w=[psk(bb)])
            op("dve", lambda e: e.tensor_tensor(out=t1[i], in0=PS[ba][:], in1=cosb[tb % 2], op=ALU.mult), r=[psk(ba), ("cos", tb % 2)], w=[("t1", i)])
            op("dve", lambda e: e.tensor_tensor(out=t2[i], in0=PS[bb][:], in1=sinb[tb % 2], op=ALU.mult), r=[psk(bb), ("sin", tb % 2)], w=[("t2", i)])
            op("pool", lambda e: e.tensor_tensor(out=dstT[:, h, tb * 512:(tb + 1) * 512], in0=t1[i], in1=t2[i], op=ALU.add), r=[("t1", i), ("t2", i)], w=[(dkey, h, tb)])

        def do_tb(tb):
            op("sp", lambda e: e.dma_start(out=cosb[tb % 2], in_=cos_d[:, tb * 512:(tb + 1) * 512]), w=[("cos", tb % 2)], dma=True)
            op("sp", lambda e: e.dma_start(out=sinb[tb % 2], in_=sin_d[:, tb * 512:(tb + 1) * 512]), w=[("sin", tb % 2)], dma=True)
            for h in range(4):
                rope_proj(tb, wq, wqr, 0, 1, qT, h, "qT")
                rope_proj(tb, wk, wkr, 2, 3, kT, h, "kT")

        for tb in range(4):
            do_tb(tb)
        wv = load_group(0, 2)

        def do_v(j):
            bank = 4 + j % 2
            for c in range(8):
                op("pe", lambda e, c=c: e.matmul(PS[bank][:], lhsT=hT[:, c, j * 128:(j + 1) * 128], rhs=wv[:, c, :], start=(c == 0), stop=(c == 7)),
                   r=[("wb", 0), hk(c, j // 4)], w=[psk(bank)])
            op("act", lambda e: e.activation(out=vaug[:, j, :, 0:128], in_=PS[bank][:].rearrange("p (a b) -> p a b", b=128), func=AF.Copy), r=[psk(bank)], w=[("vaug", j)])

        for j in range(16):
            do_v(j)
        wo = wload(2, w_out[:, 0:4, :])
        dbg_put("qT0", qT[:, 0, 0:256], [("qT", 0, 0)])
        dbg_put("kT0", kT[:, 0, 0:256], [("kT", 0, 0)])
        dbg_put("vaug0", vaug[:, 0, :, :], [("vaug", 0), "vaug1"])
        dbg_put("lam", lam[:, 0:4], ["lam"])
        if cfg.get("attn_stop", 9) <= 2:
            return

        AR.phase()
        hT = AR.at("hT", 0, [8, L], BF16)
        qT = AR.at("qT", 32 * KB, [4, L], BF16)
        kT = AR.at("kT", 48 * KB, [4, L], BF16)
        vaug = AR.at("vaug", 64 * KB, [16, 4, 130], BF16)
        aT = AR.at("aT", T0, [4, L], BF16)
        PT = [AR.at("PT%d" % i, T0 + 16 * KB + i * KB, [512], BF16) for i in range(4)]
        SM = T0 + 20 * KB
        rr = rr_g[:]
        tt_ = [AR.at("tt%d" % i, SM + i * 512, [128], F32) for i in range(2)]
        oo = [AR.at("oo%d" % i, SM + 1024 + i * 512, [128], F32) for i in range(2)]
        onb = [AR.at("onb%d" % i, SM + 2048 + i * 256, [128], BF16) for i in range(4)]
        pctr = [0]
        cctr = [0]
        pending = []

        def accv(comp, qs):
            bank = 2 + 2 * comp + qs // 2
            off = (qs % 2) * 130
            return bank, PS[bank][:, off:off + 129]

        its = [(h_, qb_, comp_, kt_) for h_ in range(4) for qb_ in range(4) for comp_ in range(2) for kt_ in range(16)]
        if cfg.get("attn_stop", 9) == 3:
            its = [x_ for x_ in its if x_[0] == 0 and x_[1] == 0]

        SRING = (0, 1, 6)
        accs = [WB[1][:].bitcast(F32)[:, 512 * (1 + c_):512 * (2 + c_)] for c_ in range(2)]

        qpad = [[WB[0][:, (comp_ * 2 + j_) * 512:(comp_ * 2 + j_ + 1) * 512] for j_ in range(2)] for comp_ in range(2)]
        op("pool", lambda e: e.memset(WB[0][:, 0:2048], 0.0), r=[("wb", 0)], w=[("wb", 0)] + [("qpad", c_, j_) for c_ in range(2) for j_ in range(2)])

        def emit_qpad(gi):
            h, qb = gi // 4, gi % 4
            j = gi % 2
            op("pool", lambda e: e.tensor_copy(out=qpad[0][j][0:64, :], in_=qT[0:64, h, qb * 512:(qb + 1) * 512]), r=[("qT", h, qb)], w=[("qpad", 0, j)])
            op("pool", lambda e: e.tensor_copy(out=qpad[1][j][64:128, :], in_=qT[64:128, h, qb * 512:(qb + 1) * 512]), r=[("qT", h, qb)], w=[("qpad", 1, j)])

        def emit_S(i):
            h, qb, comp, kt = its[i]
            sbank = SRING[i % 3]
            j = (h * 4 + qb) % 2
            op("pe", lambda e: e.matmul(PS[sbank][:], lhsT=kT[:, h, kt * 128:(kt + 1) * 128], rhs=qpad[comp][j], start=True, stop=True),
               r=[("kT", h, kt // 4), ("qpad", comp, j)], w=[psk(sbank)])

        def emit_exp_pv(i):
            h, qb, comp, kt = its[i]
            sbank = SRING[i % 3]
            pi = i % 4
            op("act", lambda e: e.activation(out=PT[pi], in_=PS[sbank][:], func=AF.Exp, scale=0.125), r=[psk(sbank)], w=[("PT", pi)])
            bo, bs = 2 + 2 * comp, 3 + 2 * comp
            op("pe", lambda e: e.matmul(PS[bo][:], lhsT=vaug[:, kt, h, 0:128], rhs=PT[pi], start=(kt == 0), stop=(kt == 15)),
               r=[("PT", pi), ("vaug", kt)], w=[psk(bo)])
            op("pe", lambda e: e.matmul(PS[bs][:], lhsT=ones_b[:], rhs=PT[pi], start=(kt == 0), stop=(kt == 15)),
               r=[("PT", pi), "ones_b"], w=[psk(bs)])

        def attend_tail(h, qb):
            fo = WB[3][:].bitcast(F32)
            r0, r1, t_, o_ = fo[:, 0:512], fo[:, 512:1024], fo[:, 1024:1536], fo[:, 1536:2048]
            sqb = WB[1][:, 0:512]
            K3 = [("wb", 3)]
            if dbg and h == 0 and qb == 0:
                dbg_put("acc0", PS[2][:, 0:256], [psk(2)])
                dbg_put("acc1", PS[4][:, 0:256], [psk(4)])
            if cfg.get("attn_stop", 9) <= 3:
                return
            op("dve", lambda e: e.reciprocal(out=r0, in_=PS[3][:]), r=[psk(3)], w=K3)
            op("dve", lambda e: e.reciprocal(out=r1, in_=PS[5][:]), r=[psk(5)] + K3, w=K3)
            op("dve", lambda e: e.tensor_tensor(out=t_, in0=PS[4][:], in1=r1, op=ALU.mult), r=[psk(4)] + K3, w=K3)
            op("dve", lambda e: e.tensor_tensor(out=o_, in0=PS[2][:], in1=r0, op=ALU.mult), r=[psk(2)] + K3, w=K3)
            op("dve", lambda e: e.scalar_tensor_tensor(out=o_, in0=t_, scalar=nlam, in1=o_, op0=ALU.mult, op1=ALU.add), r=K3 + ["lam"], w=K3)
            op("act", lambda e: e.activation(out=sqb, in_=o_, func=AF.Square), r=K3, w=[("wb", 1)])

            def fin():
                op("pe", lambda e: e.matmul(PS[7][:], lhsT=ones_b[:], rhs=sqb, start=True, stop=True), r=[("wb", 1), "ones_b"], w=[psk(7)])
                op("act", lambda e: e.activation(out=r0, in_=PS[7][:], func=AF.Sqrt, scale=1.0 / 128, bias=EPS), r=[psk(7)] + K3, w=K3)
                op("dve", lambda e: e.reciprocal(out=r0, in_=r0), r=K3, w=K3)
                op("dve", lambda e: e.scalar_tensor_tensor(out=aT[:, h, qb * 512:(qb + 1) * 512], in0=o_, scalar=sg[:, 0:1], in1=r0, op0=ALU.mult, op1=ALU.mult),
                   r=K3 + ["sg"], w=[("aT", h, qb)])
            pending.append(fin)

        emit_qpad(0)
        LOOK = 2
        for i in range(LOOK):
            emit_S(i)
        for i in range(len(its)):
            if its[i][2] == 0 and its[i][3] == 0:
                gi_ = its[i][0] * 4 + its[i][1]
                if gi_ + 1 < 16 and cfg.get("attn_stop", 9) != 3:
                    emit_qpad(gi_ + 1)
            if i + LOOK < len(its):
                emit_S(i + LOOK)
            emit_exp_pv(i)
            h_, qb_, comp_, kt_ = its[i]
            if kt_ == 15 and comp_ == 0 and pending:
                for f in pending:
                    f()
                del pending[:]
            if kt_ == 15 and comp_ == 1:
                attend_tail(h_, qb_)
        for f in pending:
            f()
        del pending[:]
        dbg_put("aT", aT[:, :, 0:512], [("aT", h_, 0) for h_ in range(4)])

        def oproj(tb, dc, kq):
            bank = kq % 2
            for hc in range(4):
                op("pe", lambda e, hc=hc: e.matmul(PS[bank][:], lhsT=wo[:, hc, dc * 128:(dc + 1) * 128], rhs=aT[:, hc, tb * 512:(tb + 1) * 512], start=(hc == 0), stop=(hc == 3)),
                   r=[("wb", 2), ("aT", hc, tb)], w=[psk(bank)])
            xs = xT[:, dc, tb * 512:(tb + 1) * 512]
            op("dve", lambda e: e.tensor_tensor(out=xs, in0=xs, in1=PS[bank][:], op=ALU.add), r=[psk(bank), xk(dc, tb)], w=[xk(dc, tb)])

        if not dbg:
            kq = 0
            for tb in range(4):
                for dc in range(8):
                    oproj(tb, dc, kq)
                    kq += 1


    def s5_phase():
        w_in = w_in_even_d.rearrange("(c p) f -> p c f", p=128)
        w_out = w_out_even_d.rearrange("(c p) d -> p c d", p=128)
        AR.phase()
        hT = AR.at("hT", 0, [8, L], BF16)
        if not ("l0attn" in parts or "l0mix" in parts):
            sq = [AR.at("sq%d" % i, 81 * KB + i * 4 * KB, [L], BF16) for i in range(2)]
            rstd = AR.at("rstd", 89 * KB, [L], F32)
            norm_to_hT(0, hT, rstd, sq)
        uT = AR.at("uT", 32 * KB, [4, L], BF16)
        wu = wload(0, w_in[:, :, 1536:2048])

        def uproj(tb, cc, kq):
            bank = kq % 4
            for c in range(8):
                op("pe", lambda e, c=c: e.matmul(PS[bank][:], lhsT=wu[:, c, cc * 128:(cc + 1) * 128], rhs=hT[:, c, tb * 512:(tb + 1) * 512], start=(c == 0), stop=(c == 7)),
                   r=[("wb", 0), hk(c, tb)], w=[psk(bank)])
            op("act", lambda e: e.activation(out=uT[:, cc, tb * 512:(tb + 1) * 512], in_=PS[bank][:], func=AF.Copy), r=[psk(bank)], w=[("uT", cc)])

        kq = 0
        for tb in range(4):
            for cc in range(4):
                uproj(tb, cc, kq)
                kq += 1
        wglu = wload(1, w_glu_d.rearrange("(c p) f -> p c f", p=128))
        wo2 = wload(2, w_out[:, 4:8, :])

        AR.phase()
        VX = AR.at("VX", 0, [2, 32, 256], BF16)
        uT = AR.at("uT", 32 * KB, [4, L], BF16)
        U = AR.at("U", 48 * KB, [32, 256], BF16)
        sm_off = [64 * KB]

        def sm(name, shape=(32,)):
            n = 1
            for v_ in shape:
                n *= v_
            t_ = AR.at(name, sm_off[0], list(shape), F32)
            sm_off[0] += n * 4
            return t_

        bm = sm("bm", (2, 32, 16))
        names = ["lr", "dt", "lrdt", "ang", "mg", "sn", "cs", "r_", "i_", "t0", "t1", "t2", "den", "am1", "cr", "ci", "vr", "vi"]
        sv = {n_: sm(n_) for n_ in names}
        par = sm("par", (3, 32))
        cm = sm("cm", (2, 32, 16))
        bb = sm("bb", (2, 32, 16))
        pw = sm("pw", (2, 32, 8))
        pwi = sm("pwi", (2, 32, 8))
        P1s = sm("P1s", (2, 32))
        P2s = sm("P2s", (2, 32))
        Xst = sm("Xst", (2, 32))
        S1 = sm("S1", (2, 32))
        T1 = sm("T1", (2, 32))
        T2 = sm("T2", (2, 32))
        dsk = sm("dsk", (4,))
        bgl = sm("bgl", (4,))
        SM_END = sm_off[0]
        assert SM_END <= 85 * KB, SM_END
        AB = [AR.at("AB%d" % i, 85 * KB + i * 2 * KB, [8, 128], BF16) for i in range(2)]
        ABT = [AR.at("ABT%d" % i, 89 * KB + i * 2 * KB, [8, 128], BF16) for i in range(2)]
        tA = AR.at("tA", 93 * KB, [512], F32)
        tB = AR.at("tB", 95 * KB, [512], F32)
        Zt = AR.at("Zt", 97 * KB, [8, 240], BF16)
        s5m = AR.at("s5m", 97 * KB + 3840, [2, 128], F32)
        PP = ["pp"]

        def vop(fn, eng="dve", r=(), w=()):
            op(eng, fn, r=PP + list(r), w=PP + list(w))

        def tt(o_, a_, b_, o, **kw):
            vop(lambda e: e.tensor_tensor(out=o_, in0=a_, in1=b_, op=o), **kw)

        def ts(o_, a_, s1, o1, s2=None, o2=None, **kw):
            if o2 is None:
                vop(lambda e: e.tensor_scalar(out=o_, in0=a_, scalar1=s1, scalar2=None, op0=o1), **kw)
            else:
                vop(lambda e: e.tensor_scalar(out=o_, in0=a_, scalar1=s1, scalar2=s2, op0=o1, op1=o2), **kw)

        def cmul(outr, outi, ar_, ai_, br_, bi_, x1, x2, **kw):
            tt(x1, ar_, br_, ALU.mult, **kw)
            tt(x2, ai_, bi_, ALU.mult, **kw)
            tt(outr, x1, x2, ALU.subtract, **kw)
            tt(x1, ar_, bi_, ALU.mult, **kw)
            tt(x2, ai_, br_, ALU.mult, **kw)
            tt(outi, x1, x2, ALU.add, **kw)

        op("sp", lambda e: e.dma_start(out=par, in_=s5s_d), w=PP, dma=True, dkey="s5s")
        op("sp", lambda e: e.dma_start(out=bm, in_=s5b_d), w=PP, dma=True, dkey="s5b")
        op("sp", lambda e: e.dma_start(out=cm, in_=s5c_d), w=PP, dma=True, dkey="s5c")
        op("sp", lambda e: e.dma_start(out=dsk, in_=s5d_d), w=PP, dma=True, dkey="s5d")
        op("sp", lambda e: e.dma_start(out=bgl, in_=s5bg_d), w=PP, dma=True, dkey="s5bg")
        op("sp", lambda e: e.dma_start(out=s5m, in_=s5m_d), w=["s5m"], dma=True)
        op("pool", lambda e: e.memset(Zt, 0.0), w=["Zt"])
        op("pool", lambda e: e.tensor_copy(out=Zt[:, :, 112:128], in_=ident_b[:].rearrange("p (a b) -> p a b", b=16)), r=["ident_b"], w=["Zt"])
        lam_re, lam_im, lstep = par[:, 0, :], par[:, 1, :], par[:, 2, :]
        v_ = sv
        ts(v_["lr"], lam_re, -1e-4, ALU.min)
        vop(lambda e: e.activation(out=v_["dt"], in_=lstep, func=AF.Exp), eng="act")
        tt(v_["lrdt"], v_["lr"], v_["dt"], ALU.mult)
        tt(v_["ang"], lam_im, v_["dt"], ALU.mult)
        vop(lambda e: e.activation(out=v_["mg"], in_=v_["lrdt"], func=AF.Exp, scale=1.0 / 32), eng="act")
        vop(lambda e: e.activation(out=v_["sn"], in_=v_["ang"], func=AF.Sin, scale=1.0 / 32), eng="act")
        ts(v_["t0"], v_["ang"], 1.0 / 32, ALU.mult, math.pi / 2, ALU.add)
        vop(lambda e: e.activation(out=v_["cs"], in_=v_["t0"], func=AF.Sin), eng="act")
        tt(v_["r_"], v_["mg"], v_["cs"], ALU.mult)
        tt(v_["i_"], v_["mg"], v_["sn"], ALU.mult)
        for _ in range(5):
            tt(v_["t0"], v_["r_"], v_["r_"], ALU.mult)
            tt(v_["t1"], v_["i_"], v_["i_"], ALU.mult)
            tt(v_["t2"], v_["r_"], v_["i_"], ALU.mult)
            tt(v_["r_"], v_["t0"], v_["t1"], ALU.subtract)
            ts(v_["i_"], v_["t2"], 2.0, ALU.mult)
        ar, ai = v_["r_"], v_["i_"]
        tt(v_["t0"], v_["lr"], v_["lr"], ALU.mult)
        tt(v_["t1"], lam_im, lam_im, ALU.mult)
        tt(v_["den"], v_["t0"], v_["t1"], ALU.add)
        vop(lambda e: e.reciprocal(out=v_["den"], in_=v_["den"]))
        ts(v_["am1"], ar, -1.0, ALU.add)
        tt(v_["t0"], v_["am1"], v_["lr"], ALU.mult)
        tt(v_["t1"], ai, lam_im, ALU.mult)
        tt(v_["t0"], v_["t0"], v_["t1"], ALU.add)
        tt(v_["cr"], v_["t0"], v_["den"], ALU.mult)
        tt(v_["t0"], ai, v_["lr"], ALU.mult)
        tt(v_["t1"], v_["am1"], lam_im, ALU.mult)
        tt(v_["t0"], v_["t0"], v_["t1"], ALU.subtract)
        tt(v_["ci"], v_["t0"], v_["den"], ALU.mult)
        tt(v_["t0"], ar, ar, ALU.mult)
        tt(v_["t1"], ai, ai, ALU.mult)
        tt(v_["t0"], v_["t0"], v_["t1"], ALU.add)
        vop(lambda e: e.reciprocal(out=v_["t0"], in_=v_["t0"]))
        tt(v_["vr"], ar, v_["t0"], ALU.mult)
        tt(v_["t1"], ai, v_["t0"], ALU.mult)
        ts(v_["vi"], v_["t1"], -1.0, ALU.mult)
        x1 = tA[:, 0:128].rearrange("p (a b) -> p a b", b=4)
        x2 = tB[:, 0:128].rearrange("p (a b) -> p a b", b=4)
        for (tab, br_, bi_) in ((pw, ar, ai), (pwi, v_["vr"], v_["vi"])):
            tr_, ti_ = tab[:, 0, :, :], tab[:, 1, :, :]
            vop(lambda e, tr_=tr_, br_=br_: e.tensor_copy(out=tr_[:, :, 0:1], in_=br_.unsqueeze(2)))
            vop(lambda e, ti_=ti_, bi_=bi_: e.tensor_copy(out=ti_[:, :, 0:1], in_=bi_.unsqueeze(2)))
            for n_ in (1, 2, 4):
                cmul(tr_[:, :, n_:2 * n_], ti_[:, :, n_:2 * n_], tr_[:, :, 0:n_], ti_[:, :, 0:n_],
                     tr_[:, :, n_ - 1:n_].broadcast_to([128, 32, n_]), ti_[:, :, n_ - 1:n_].broadcast_to([128, 32, n_]), x1[:, :, 0:n_], x2[:, :, 0:n_])
        cmul(bb[:, 0, :, :], bb[:, 1, :, :], v_["cr"].unsqueeze(2).broadcast_to([128, 32, 16]), v_["ci"].unsqueeze(2).broadcast_to([128, 32, 16]),
             bm[:, 0, :, :], bm[:, 1, :, :], tA.rearrange("p (a b) -> p a b", b=16), tB.rearrange("p (a b) -> p a b", b=16))
        vop(lambda e: e.tensor_copy(out=P1s[:, 0, :], in_=pw[:, 0, :, 7]))
        vop(lambda e: e.tensor_copy(out=P1s[:, 1, :], in_=pw[:, 0, :, 7]))
        ts(P2s[:, 0, :], pw[:, 1, :, 7], -1.0, ALU.mult)
        vop(lambda e: e.tensor_copy(out=P2s[:, 1, :], in_=pw[:, 1, :, 7]))

        for tab in (pw, pwi):
            flat = tab[64:128].rearrange("p a b c -> p (a b c)")
            vop(lambda e, flat=flat: e.tensor_copy(out=tA[64:128, :], in_=flat))
            vop(lambda e, tab=tab: e.tensor_copy(out=tab[64:128].rearrange("p a b c -> p (a b) c"), in_=tA[64:128, :].rearrange("p (ab c) -> p ab c", c=8)[:, :, ::-1]))

        def gen_AB(ABt, g0, gl0):
            for (lo, hi) in ((0, 128),):
                np_ = hi - lo
                pr = pwi[lo:hi, 0, g0:g0 + 4, :].unsqueeze(3).broadcast_to([np_, 4, 8, 16])
                pi = pwi[lo:hi, 1, g0:g0 + 4, :].unsqueeze(3).broadcast_to([np_, 4, 8, 16])
                br_ = bb[lo:hi, 0, g0:g0 + 4, :].unsqueeze(2).broadcast_to([np_, 4, 8, 16])
                bi_ = bb[lo:hi, 1, g0:g0 + 4, :].unsqueeze(2).broadcast_to([np_, 4, 8, 16])
                o_r = ABt[0][lo:hi, gl0:gl0 + 4, :].rearrange("p g (s m) -> p g s m", m=16)
                o_i = ABt[1][lo:hi, gl0:gl0 + 4, :].rearrange("p g (s m) -> p g s m", m=16)
                ta = tA[lo:hi, :].rearrange("p (g s m) -> p g s m", s=8, m=16)
                tb_ = tB[lo:hi, :].rearrange("p (g s m) -> p g s m", s=8, m=16)
                cmul(o_r, o_i, pr, pi, br_, bi_, ta, tb_, w=["AB"])

        def gen_CA(g0, gl0, CAf, CAb):
            for (lo, hi, dst) in ((0, 64, CAf), (64, 128, CAb)):
                sl = slice(None)
                pr = pw[lo:hi, 0, g0:g0 + 4, sl].unsqueeze(3).broadcast_to([64, 4, 8, 16])
                pi = pw[lo:hi, 1, g0:g0 + 4, sl].unsqueeze(3).broadcast_to([64, 4, 8, 16])
                c_r = cm[lo:hi, 0, g0:g0 + 4, :].unsqueeze(2).broadcast_to([64, 4, 8, 16])
                c_i = cm[lo:hi, 1, g0:g0 + 4, :].unsqueeze(2).broadcast_to([64, 4, 8, 16])
                o_r = dst[0][lo:hi, gl0:gl0 + 4, :].rearrange("p g (s m) -> p g s m", m=16)
                o_i = dst[1][lo:hi, gl0:gl0 + 4, :].rearrange("p g (s m) -> p g s m", m=16)
                ta = tA[lo:hi, :].rearrange("p (g s m) -> p g s m", s=8, m=16)
                tb_ = tB[lo:hi, :].rearrange("p (g s m) -> p g s m", s=8, m=16)
                kw = dict(w=["CA"])
                tt(ta, c_r, pr, ALU.mult, **kw)
                tt(tb_, c_i, pi, ALU.mult, **kw)
                tt(o_r, ta, tb_, ALU.subtract, **kw)
                tt(ta, c_r, pi, ALU.mult, **kw)
                tt(tb_, c_i, pr, ALU.mult, **kw)
                vop(lambda e, o_i=o_i, ta=ta, tb_=tb_: e.scalar_tensor_tensor(out=o_i, in0=ta, scalar=-1.0, in1=tb_, op0=ALU.mult, op1=ALU.subtract), **kw)

        def shuffle_group(cc, gl):
            g = 8 * cc + gl
            bank = g % 2
            for s_ in range(8):
                op("pe", lambda e, s_=s_: e.matmul(PS[bank][:, 0:256], lhsT=Zt[:, gl, (7 - s_) * 16:(7 - s_) * 16 + 128],
                                                   rhs=uT[:, cc, :].rearrange("p (b s) -> p b s", s=8)[:, :, s_], start=(s_ == 0), stop=(s_ == 7)),
                   r=["Zt", ("uT", cc)], w=[psk(bank)])
            op("act", lambda e: e.activation(out=U[:, g, :], in_=PS[bank][:, 0:256], func=AF.Copy), r=[psk(bank)], w=[("U", g)])

        def abt_chunk():
            for ri in range(2):
                bank = 2 + ri
                pb = PS[bank][:].bitcast(BF16)
                for gl in range(8):
                    op("pe", lambda e, gl=gl, pb=pb, ri=ri: e.transpose(out=pb[:, gl * 128:(gl + 1) * 128], in_=AB[ri][:, gl, :], identity=ident_b[:]), r=["AB", "pp", "ident_b"], w=[psk(bank)])
                op("dve", lambda e, pb=pb, ri=ri: e.tensor_copy(out=ABT[ri], in_=pb.rearrange("p (a b) -> p a b", b=128)), r=[psk(bank)], w=["ABT"])

        def vprime_group(cc, gl):
            g = 8 * cc + gl
            for ri in range(2):
                bank = 4 + ri
                op("pe", lambda e, ri=ri, bank=bank: e.matmul(PS[bank][:, 0:256], lhsT=ABT[ri][:, gl, :], rhs=U[:, g, :], start=True, stop=True), r=["ABT", ("U", g)], w=[psk(bank)])
                op("act", lambda e, ri=ri, bank=bank: e.activation(out=VX[:, ri, g, :], in_=PS[bank][:, 0:256], func=AF.Copy), r=[psk(bank)], w=[("VX", g)])

        for cc in range(4):
            for gl in range(8):
                shuffle_group(cc, gl)
            gen_AB(AB, 8 * cc, 0)
            gen_AB(AB, 8 * cc + 4, 4)
            abt_chunk()
            for gl in range(8):
                vprime_group(cc, gl)
        dbg_put("U0", U[:, 0, 0:64], [("U", 0)])
        dbg_put("VX0", VX[:, :, 0, 0:64], [("VX", 0)])
        dbg_put("pw", pw[:, :, 0, :], PP)
        dbg_put("pwi", pwi[:, :, 0, :], PP)
        dbg_put("bb", bb[:, :, 0, :], PP)

        VXK = [("VX", g) for g in range(32)]
        op("dve", lambda e: e.memset(Xst, 0.0), w=["sc0x", "sc64x"])

        def scan_step(lo, hi, b):
            key = "sc%d" % lo
            xs, s1, t1_, t2_ = Xst[lo:hi], S1[lo:hi], T1[lo:hi], T2[lo:hi]
            vx = VX[lo:hi, :, :, b]
            s1sw = bass.AP(s1.tensor, s1.offset + 32, [list(s1.ap[0]), [-32, 2], [1, 32]])
            op("dve", lambda e: e.tensor_tensor(out=s1, in0=xs, in1=vx, op=ALU.add), r=VXK + [key + "x"], w=[key + "s"])
            op("dve", lambda e: e.tensor_tensor(out=t1_, in0=P1s[lo:hi], in1=s1, op=ALU.mult), r=[key + "s", "pp"], w=[key + "a"])
            op("dve", lambda e: e.tensor_tensor(out=t2_, in0=P2s[lo:hi], in1=s1sw, op=ALU.mult), r=[key + "s", "pp"], w=[key + "b"])
            op("dve", lambda e: e.tensor_tensor(out=xs, in0=t1_, in1=t2_, op=ALU.add), r=[key + "a", key + "b"], w=[key + "x"])
            op("pool", lambda e: e.tensor_copy(out=vx, in_=xs), r=[key + "x"], w=[key + "c"])

        for b in range(256):
            scan_step(0, 64, b)
            scan_step(64, 128, 255 - b)
        dbg_put("X0", VX[:, :, 0, 0:64], ["sc0c", "sc64c"])

        AR.phase()
        AR.at("keep", 0, [85 * KB // 2], BF16)
        AB2 = [AR.at("AB2_%d" % i, 85 * KB + i * 2 * KB, [8, 128], BF16) for i in range(2)]
        CAf = [AR.at("CAf%d" % i, 89 * KB + i * 2 * KB, [8, 128], BF16) for i in range(2)]
        AR.at("keep2", 93 * KB, [(104 - 93) * KB // 2], BF16)
        CAb = [bm.rearrange("p a b c -> p (a b c)")[:, i * 512:(i + 1) * 512].bitcast(BF16).rearrange("p (a b) -> p a b", b=128) for i in range(2)]
        W0 = sv["lr"].tensor and AR.base[:, (64 * KB + 4096) // 2:(64 * KB + 4096 + 2048) // 2].rearrange("p (a b) -> p a b", b=128)
        for i in range(2):
            op("pool", lambda e, i=i: e.memset(CAf[i][64:128], 0.0), w=["CA"])
            op("pool", lambda e, i=i: e.memset(CAb[i][0:64], 0.0), w=["CA"])
        mF_ = s5m[:, 0, :]
        mB_ = s5m[:, 1, :]

        def w0_chunk():
            for half in range(2):
                pf, pb_ = PS[2], PS[3]
                for jj in range(4):
                    gl = half * 4 + jj
                    o1 = pf[:, jj * 128:(jj + 1) * 128]
                    o2 = pb_[:, jj * 128:(jj + 1) * 128]
                    op("pe", lambda e, gl=gl, o1=o1: e.matmul(o1, lhsT=AB2[0][:, gl, :], rhs=CAf[0][:, gl, :], start=True, stop=False), r=["AB", "CA", "pp"], w=[psk(2)])
                    op("pe", lambda e, gl=gl, o1=o1: e.matmul(o1, lhsT=AB2[1][:, gl, :], rhs=CAf[1][:, gl, :], start=False, stop=True), r=["AB", "CA", "pp"], w=[psk(2)])
                    op("pe", lambda e, gl=gl, o2=o2: e.matmul(o2, lhsT=AB2[0][:, gl, :], rhs=CAb[0][:, gl, :], start=True, stop=False), r=["AB", "CA", "pp"], w=[psk(3)])
                    op("pe", lambda e, gl=gl, o2=o2: e.matmul(o2, lhsT=AB2[1][:, gl, :], rhs=CAb[1][:, gl, :], start=False, stop=True), r=["AB", "CA", "pp"], w=[psk(3)])
                t3 = tA.rearrange("p (a b) -> p a b", b=128)
                op("dve", lambda e, t3=t3: e.tensor_tensor(out=t3, in0=PS[2][:].rearrange("p (a b) -> p a b", b=128), in1=mF_.unsqueeze(1).broadcast_to([128, 4, 128]), op=ALU.mult),
                   r=[psk(2), "s5m", "pp"], w=["pp"])
                op("dve", lambda e: e.tensor_tensor(out=tB.rearrange("p (a b) -> p a b", b=128), in0=PS[3][:].rearrange("p (a b) -> p a b", b=128), in1=mB_.unsqueeze(1).broadcast_to([128, 4, 128]), op=ALU.mult),
                   r=[psk(3), "s5m", "pp"], w=["pp"])
                op("dve", lambda e, half=half: e.tensor_tensor(out=W0[:, half * 4:(half + 1) * 4, :], in0=tA.rearrange("p (a b) -> p a b", b=128), in1=tB.rearrange("p (a b) -> p a b", b=128), op=ALU.add),
                   r=["pp"], w=["W0", "pp"])

        SCK = ["sc0c", "sc64c"]

        def y_group(cc, gl):
            g = 8 * cc + gl
            bank = 4 + g % 2
            o_ = PS[bank]
            rk = ["W0", "CA", ("U", g), ("VX", g)] + SCK
            op("pe", lambda e: e.matmul(o_[:, 0:256], lhsT=W0[:, gl, :], rhs=U[:, g, :], start=True, stop=False), r=rk, w=[psk(bank)])
            op("pe", lambda e: e.matmul(o_[:, 1:256], lhsT=CAf[0][:, gl, :], rhs=VX[:, 0, g, 0:255], start=False, stop=False), r=rk, w=[psk(bank)])
            op("pe", lambda e: e.matmul(o_[:, 1:256], lhsT=CAf[1][:, gl, :], rhs=VX[:, 1, g, 0:255], start=False, stop=False), r=rk, w=[psk(bank)])
            op("pe", lambda e: e.matmul(o_[:, 0:255], lhsT=CAb[0][:, gl, :], rhs=VX[:, 0, g, 1:256], start=False, stop=False), r=rk, w=[psk(bank)])
            op("pe", lambda e: e.matmul(o_[:, 0:255], lhsT=CAb[1][:, gl, :], rhs=VX[:, 1, g, 1:256], start=False, stop=True), r=rk, w=[psk(bank)])
            op("act", lambda e: e.activation(out=U[:, g, :], in_=o_[:, 0:256], func=AF.Copy), r=[psk(bank)], w=[("U", g)])

        def unshuffle(cc, tl):
            bank = tl % 2
            for gl in range(8):
                g = 8 * cc + gl
                op("pe", lambda e, gl=gl, g=g: e.matmul(PS[bank][:, 0:256], lhsT=Zt[:, tl, (7 - gl) * 16:(7 - gl) * 16 + 128], rhs=U[:, g, :], start=(gl == 0), stop=(gl == 7)),
                   r=["Zt", ("U", g)], w=[psk(bank)])
            uv = uT[:, cc, :].rearrange("p (b s) -> p b s", s=8)[:, :, tl]
            op("dve", lambda e: e.scalar_tensor_tensor(out=uv, in0=uv, scalar=dsk[:, cc:cc + 1], in1=PS[bank][:, 0:256], op0=ALU.mult, op1=ALU.add),
               r=[psk(bank), ("uT", cc), "pp"], w=[("uT", cc)])

        for cc in range(4):
            for hh in range(2):
                gen_AB(AB2, 8 * cc + 4 * hh, 4 * hh)
                gen_CA(8 * cc + 4 * hh, 4 * hh, CAf, CAb)
            w0_chunk()
            for gl in range(8):
                y_group(cc, gl)
            for tl in range(8):
                unshuffle(cc, tl)
        dbg_put("y", uT[:, :, 0:512], [("uT", cc) for cc in range(4)])

        AR.phase()
        AR.at("uT", 32 * KB, [4, L], BF16)
        bT = AR.at("bT", 0, [4, L], BF16)
        g1 = AR.at("g1", 16 * KB, [L], F32)
        g2 = AR.at("g2", 24 * KB, [L], F32)
        CG = math.sqrt(2.0 / math.pi)

        def gelu_chunk(cc):
            y_ = uT[:, cc, :]
            op("act", lambda e: e.activation(out=g1, in_=y_, func=AF.Square), r=[("uT", cc)], w=["g1"])
            op("dve", lambda e: e.tensor_scalar(out=g1, in0=g1, scalar1=0.044715, scalar2=1.0, op0=ALU.mult, op1=ALU.add), r=["g1"], w=["g1"])
            op("dve", lambda e: e.tensor_tensor(out=g1, in0=g1, in1=y_, op=ALU.mult), r=["g1", ("uT", cc)], w=["g1"])
            op("act", lambda e: e.activation(out=g2, in_=g1, func=AF.Sigmoid, scale=2.0 * CG), r=["g1"], w=["g2"])
            op("dve", lambda e: e.tensor_tensor(out=y_, in0=y_, in1=g2, op=ALU.mult), r=["g2", ("uT", cc)], w=[("uT", cc)])

        for cc in range(4):
            gelu_chunk(cc)

        def glu(tb, oc, kq):
            bank = kq % 2
            for kc in range(4):
                op("pe", lambda e, kc=kc: e.matmul(PS[bank][:], lhsT=wglu[:, kc, oc * 128:(oc + 1) * 128], rhs=uT[:, kc, tb * 512:(tb + 1) * 512], start=(kc == 0), stop=(kc == 3)),
                   r=[("wb", 1), ("uT", kc)], w=[psk(bank)])
            gsl = g1[:, (kq % 4) * 512:(kq % 4 + 1) * 512]
            op("act", lambda e: e.activation(out=gsl, in_=PS[bank][:], func=AF.Sigmoid, bias=bgl[:, oc:oc + 1]), r=[psk(bank), "pp"], w=[("gs", kq % 4)])
            op("dve", lambda e: e.tensor_tensor(out=bT[:, oc, tb * 512:(tb + 1) * 512], in0=uT[:, oc, tb * 512:(tb + 1) * 512], in1=gsl, op=ALU.mult),
               r=[("gs", kq % 4), ("uT", oc)], w=[("bT", oc, tb)])

        P.barrier()
        kq = 0
        for tb in range(4):
            for oc in range(4):
                glu(tb, oc, kq)
                kq += 1
        dbg_put("bT", bT[:, :, 0:512], [("bT", oc, 0) for oc in range(4)])

        def oproj2(tb, dc, kq):
            bank = 2 + kq % 2
            for hc in range(4):
                op("pe", lambda e, hc=hc: e.matmul(PS[bank][:], lhsT=wo2[:, hc, dc * 128:(dc + 1) * 128], rhs=bT[:, hc, tb * 512:(tb + 1) * 512], start=(hc == 0), stop=(hc == 3)),
                   r=[("wb", 2), ("bT", hc, tb)], w=[psk(bank)])
            xs = xT[:, dc, tb * 512:(tb + 1) * 512]
            op("dve", lambda e: e.tensor_tensor(out=xs, in0=xs, in1=PS[bank][:], op=ALU.add), r=[psk(bank), xk(dc, tb)], w=[xk(dc, tb)])

        if not dbg:
            kq = 0
            for tb in range(4):
                for dc in range(8):
                    oproj2(tb, dc, kq)
                    kq += 1

    def hgrn_phase(gi):
        w_in = w_in_odd_d.rearrange("(c p) f -> p c f", p=128)
        w_out = w_out_odd_d.rearrange("(h p) d -> p h d", p=128)
        AR.phase()
        hT = AR.at("hT", 0, [8, L], BF16)
        sq = [AR.at("sq%d" % i, 44 * KB + i * 4 * KB, [L], BF16) for i in range(2)]
        rstd = AR.at("rstd", 52 * KB, [L], F32)
        def load_head(h):
            sa, sb_ = (0, 1) if h % 2 == 0 else (2, 3)
            secs = []
            for i, sec in enumerate((0, 1, 2, 3)):
                dst = WB[sa][:, i * 1024:(i + 1) * 1024].rearrange("p (a b) -> p a b", b=128)
                op("pool", lambda e, dst=dst, sec=sec, h=h: e.dma_start(out=dst, in_=w_in[:, :, sec * 1024 + h * 128: sec * 1024 + (h + 1) * 128]),
                   w=[("wb", sa)], dma=True, dkey=("wbs", sa, i))
                secs.append(dst)
            dst = WB[sb_][:, 0:1024].rearrange("p (a b) -> p a b", b=128)
            op("pool", lambda e, dst=dst, h=h: e.dma_start(out=dst, in_=w_in[:, :, 4 * 1024 + h * 128: 4 * 1024 + (h + 1) * 128]),
               w=[("wb", sb_)], dma=True, dkey=("wbs", sb_, 0))
            secs.append(dst)
            wo = WB[sb_][:, 1024:2048]
            op("pool", lambda e, wo=wo, h=h: e.dma_start(out=wo, in_=w_out[:, h, :]), w=[("wb", sb_)], dma=True, dkey=("wbs", sb_, 1))
            return secs, wo, sa, sb_

        first_head = load_head(0)
        norm_to_hT(gi, hT, rstd, sq)
        AR.phase()
        hT = AR.at("hT", 0, [8, L], BF16)
        qT = AR.at("qT", 32 * KB, [L], BF16)
        sigG = AR.at("sigG", 36 * KB, [L], BF16)
        vtok = AR.at("vtok", 40 * KB, [16, 128], BF16)
        HS = []
        for i_ in range(2):
            b0 = 44 * KB + i_ * 22 * KB
            HS.append(dict(i=i_,
                           A=AR.at("A%d" % i_, b0, [1024], F32), B=AR.at("B%d" % i_, b0 + 4 * KB, [1024], F32),
                           kk=AR.at("kk%d" % i_, b0 + 8 * KB, [1024], BF16), qdec=AR.at("qdec%d" % i_, b0 + 10 * KB, [1024], BF16),
                           kinv=AR.at("kinv%d" % i_, b0 + 12 * KB, [1024], BF16), kend=AR.at("kend%d" % i_, b0 + 14 * KB, [8, 128], BF16),
                           scT=AR.at("scT%d" % i_, b0 + 16 * KB, [8, 128], BF16), Sb=AR.at("Sb%d" % i_, b0 + 18 * KB, [16, 128], BF16),
                           pbank=(0, 1) if i_ == 0 else (2, 3), xbank=4 + i_))
        oacc = AR.at("oacc", 88 * KB, [16, 128], F32)
        mF = AR.at("mF", 96 * KB, [L], BF16)
        decs = [AR.at("dec%d" % i, 100 * KB + i * 64, [16], F32) for i in range(2)]
        ssq = AR.at("ssq", 100 * KB + 128, [16], F32)
        SstD = [[AR.at("Sst%d_%d" % (d_, i), 100 * KB + 256 + (2 * d_ + i) * 512, [128], F32) for i in range(2)] for d_ in range(2)]
        on_tok = AR.base[:, 44 * KB // 2: 48 * KB // 2].rearrange("p (a b) -> p a b", b=128)
        mT = AR.base[:, 48 * KB // 2: 52 * KB // 2]

        op("sp", lambda e: e.dma_start(out=lbl[:], in_=lbl_d), w=["lbl"], dma=True)
        op("sp", lambda e: e.dma_start(out=hg[:], in_=hg_d), w=["hg"], dma=True)
        op("sp", lambda e: e.dma_start(out=hmask[:], in_=hmask_d), w=["hmask"], dma=True)
        op("dve", lambda e: e.tensor_tensor(out=lb[:], in0=lbl[:, 1, :], in1=lbl[:, 0, :], op=ALU.subtract), r=["lbl"], w=["lb"])
        op("act", lambda e: e.activation(out=lb[:], in_=lb[:], func=AF.Sigmoid), r=["lb"], w=["lb"])
        op("dve", lambda e: e.tensor_scalar(out=oml[:], in0=lb[:], scalar1=-1.0, scalar2=1.0, op0=ALU.mult, op1=ALU.add), r=["lb"], w=["oml"])
        op("dve", lambda e: e.tensor_scalar(out=noml[:], in0=oml[:], scalar1=-1.0, scalar2=None, op0=ALU.mult), r=["oml"], w=["noml"])
        op("pool", lambda e: e.memset(mF, 1.0), w=["mF"])
        op("pool", lambda e: e.memset(mF.rearrange("p (a b) -> p a b", b=64)[:, :, 0:1], 0.0), w=["mF"])

        def proj_fm(wsec, slot, consume):
            for tb in range(4):
                bank = tb
                for c in range(8):
                    op("pe", lambda e, bank=bank, c=c, tb=tb: e.matmul(PS[bank][:], lhsT=wsec[:, c, :], rhs=hT[:, c, tb * 512:(tb + 1) * 512], start=(c == 0), stop=(c == 7)),
                       r=[("wb", slot), hk(c, tb)], w=[psk(bank)])
                consume(tb, bank)

        pend_oproj = []

        def front_gen(h, cur):
            (wq, wi_, wff, wfb, wg), wo, sa, sb_ = cur
            for (wsec, slot, dst, func, key) in ((wq, sa, qT, AF.Copy, "qT"), (wg, sb_, sigG, AF.Sigmoid, "sigG")):
                for tb in range(4):
                    bank = tb % 2
                    for c in range(8):
                        op("pe", lambda e, bank=bank, c=c, tb=tb, wsec=wsec: e.matmul(PS[bank][:], lhsT=wsec[:, c, :], rhs=hT[:, c, tb * 512:(tb + 1) * 512], start=(c == 0), stop=(c == 7)),
                           r=[("wb", slot), hk(c, tb)], w=[psk(bank)])
                    op("act", lambda e, bank=bank, tb=tb, dst=dst, func=func: e.activation(out=dst[:, tb * 512:(tb + 1) * 512], in_=PS[bank][:], func=func), r=[psk(bank)], w=[key])
                    yield
            for q4 in range(4):
                bank = 4 + q4 % 2
                for jj in range(4):
                    j = q4 * 4 + jj
                    for c in range(8):
                        op("pe", lambda e, bank=bank, jj=jj, j=j, c=c: e.matmul(PS[bank][:, jj * 128:(jj + 1) * 128], lhsT=hT[:, c, j * 128:(j + 1) * 128], rhs=wi_[:, c, :], start=(c == 0), stop=(c == 7)),
                           r=[("wb", sa), hk(c, j // 4)], w=[psk(bank)])
                op("dve", lambda e, bank=bank, q4=q4: e.tensor_copy(out=vtok[:, q4 * 4:(q4 + 1) * 4, :], in_=PS[bank][:].rearrange("p (a b) -> p a b", b=128)), r=[psk(bank)], w=["vtok"])
                yield

        def do_head(h, cur):
            (wq, wi_, wff, wfb, wg), wo, sa, sb_ = cur
            mTh = WB[sb_][:, 2048:4096]
            if h == 0:
                dbg_put("qT", qT[:, 0:256], ["qT"])
                dbg_put("sigG", sigG[:, 0:256], ["sigG"])
                dbg_put("vtok", vtok[:, 0:2, :], ["vtok"])
            sidx = [0, 0]

            def do_dir(d, hf, S):
                si = S["i"]
                A, B, kk, qdec, kinv, kend, scT, Sb = S["A"], S["B"], S["kk"], S["qdec"], S["kinv"], S["kend"], S["scT"], S["Sb"]
                dec = decs[si]
                kA, kB, kK, kQ, kI, kE, kS, kD = ["%s%d" % (n_, si) for n_ in ("hgA", "hgB", "kk", "qdec", "kinv", "kend", "scT", "dec")]
                wf = wff if d == 0 else wfb
                T0 = hf * 1024
                for t2 in range(2):
                    tb = 2 * hf + t2
                    bank = S["pbank"][t2]
                    for c in range(8):
                        op("pe", lambda e, c=c, tb=tb, bank=bank: e.matmul(PS[bank][:], lhsT=wf[:, c, :], rhs=hT[:, c, tb * 512:(tb + 1) * 512], start=(c == 0), stop=(c == 7)),
                           r=[("wb", sa), hk(c, tb)], w=[psk(bank)])
                    op("act", lambda e, t2=t2, bank=bank: e.activation(out=A[:, t2 * 512:(t2 + 1) * 512], in_=PS[bank][:], func=AF.Sigmoid), r=[psk(bank)], w=[kA])
                    yield
                op("act", lambda e: e.activation(out=kk, in_=A, func=AF.Identity, scale=noml[:, h:h + 1], bias=oml[:, h:h + 1]), r=[kA, "noml", "oml"], w=[kK])
                op("act", lambda e: e.activation(out=A, in_=A, func=AF.Ln, scale=oml[:, h:h + 1], bias=lb[:, h:h + 1]), r=[kA, "lb", "oml"], w=[kA])
                yield
                if d == 0:
                    op("dve", lambda e: e.tensor_tensor_scan(out=B, data0=mF[:, 0:1024], data1=A, initial=0.0, op0=ALU.mult, op1=ALU.add), r=[kA, "mF"], w=[kB])
                else:
                    op("dve", lambda e: e.tensor_tensor_scan(out=B[:, ::-1], data0=mF[:, 0:1024], data1=A[:, ::-1], initial=0.0, op0=ALU.mult, op1=ALU.add), r=[kA, "mF"], w=[kB])
                yield
                op("act", lambda e: e.activation(out=A, in_=B, func=AF.Exp), r=[kB], w=[kA])
                yield
                op("dve", lambda e: e.tensor_tensor(out=qdec, in0=qT[:, T0:T0 + 1024], in1=A, op=ALU.mult), r=[kA, "qT"], w=[kQ])
                yield
                op("act", lambda e: e.activation(out=A, in_=B, func=AF.Exp, scale=-1.0), r=[kB, kQ], w=[kA])
                bl = B.rearrange("p (a b) -> p a b", b=64)[:, :, 63:64] if d == 0 else B.rearrange("p (a b) -> p a b", b=64)[:, :, 0:1]
                op("act", lambda e: e.activation(out=dec.rearrange("p (a b) -> p a b", b=1), in_=bl, func=AF.Exp), r=[kB], w=[kD])
                yield
                op("dve", lambda e: e.tensor_tensor(out=A, in0=kk, in1=A, op=ALU.mult), r=[kA, kK], w=[kA])
                yield
                op("act", lambda e: e.activation(out=kinv, in_=A, func=AF.Copy), r=[kA], w=[kI])
                dec_bc = bass.AP(dec.tensor, dec.offset, [list(dec.ap[0]), [1, 16], [0, 64]])
                op("dve", lambda e: e.tensor_tensor(out=kk.rearrange("p (a b) -> p a b", b=64), in0=A.rearrange("p (a b) -> p a b", b=64), in1=dec_bc, op=ALU.mult),
                   r=[kA, kD], w=[kK])
                yield
                xb = S["xbank"]
                pb = PS[xb][:].bitcast(BF16)
                for jj in range(8):
                    op("pe", lambda e, jj=jj: e.transpose(out=pb[:, jj * 128:(jj + 1) * 128], in_=kk[:, jj * 128:(jj + 1) * 128], identity=ident_b[:]),
                       r=[kK, "ident_b"], w=[psk(xb)])
                op("act", lambda e: e.activation(out=kend, in_=pb.rearrange("p (a b) -> p a b", b=128), func=AF.Copy), r=[psk(xb)], w=[kE])
                yield
                hm_ = hmask[:, d, :]
                mk = bass.AP(hm_.tensor, hm_.offset, [list(hm_.ap[0]), [0, 4], [1, 128]])
                for q4 in range(2):
                    bank = S["pbank"][q4]
                    for jj in range(4):
                        jl = q4 * 4 + jj
                        op("pe", lambda e, bank=bank, jj=jj, jl=jl: e.matmul(PS[bank][:, jj * 128:(jj + 1) * 128], lhsT=kinv[:, jl * 128:(jl + 1) * 128], rhs=qdec[:, jl * 128:(jl + 1) * 128], start=True, stop=True),
                           r=[kI, kQ], w=[psk(bank)])
                    op("dve", lambda e, bank=bank, q4=q4: e.tensor_tensor(out=scT[:, q4 * 4:(q4 + 1) * 4, :], in0=PS[bank][:].rearrange("p (a b) -> p a b", b=128), in1=mk, op=ALU.mult),
                       r=[psk(bank), "hmask"], w=[kS])
                    yield
                order = list(range(16)) if d == 0 else list(range(15, -1, -1))
                for cl in order:
                    n = sidx[d]
                    sidx[d] += 1
                    cur, new = SstD[d][n % 2], SstD[d][(n + 1) % 2]
                    ci = hf * 16 + cl
                    par = ci % 2
                    bank = 6 + par
                    op("act", lambda e, cur=cur, cl=cl: e.activation(out=Sb[:, cl, :], in_=cur, func=AF.Copy), r=[("Sst", d, n % 2)], w=[("Sb", si, cl)])
                    op("pe", lambda e, bank=bank, cl=cl, par=par: e.matmul(PS[bank][:, 0:128], lhsT=kend[par * 64:(par + 1) * 64, cl // 2, :], rhs=vtok[par * 64:(par + 1) * 64, hf * 8 + cl // 2, :], start=True, stop=True),
                       r=[kE, "vtok"], w=[psk(bank)])
                    op("dve", lambda e, bank=bank, cur=cur, new=new, cl=cl: e.scalar_tensor_tensor(out=new, in0=cur, scalar=dec[:, cl:cl + 1], in1=PS[bank][:, 0:128], op0=ALU.mult, op1=ALU.add),
                       r=[psk(bank), ("Sst", d, n % 2), kD], w=[("Sst", d, (n + 1) % 2)])
                    if cl % 2 == 1:
                        yield
                first = (d == 0 and hf == 0) or (d == 1 and hf == 1)
                for q4 in range(2):
                    for jj in range(4):
                        jl = q4 * 4 + jj
                        j = hf * 8 + jl
                        o_ = PS[xb][:, jj * 128:(jj + 1) * 128]
                        op("pe", lambda e, o_=o_, jl=jl, j=j: e.matmul(o_, lhsT=scT[:, jl, :], rhs=vtok[:, j, :], start=True, stop=False), r=[kS, "vtok"], w=[psk(xb)])
                        op("pe", lambda e, jj=jj, jl=jl: e.matmul(PS[xb][0:64, jj * 128:(jj + 1) * 128], lhsT=qdec[:, jl * 128:jl * 128 + 64], rhs=Sb[:, 2 * jl, :], start=False, stop=True),
                           r=[kQ, ("Sb", si, 2 * jl)], w=[psk(xb)])
                        op("pe", lambda e, jj=jj, jl=jl: e.matmul(PS[xb][64:128, jj * 128:(jj + 1) * 128], lhsT=qdec[:, jl * 128 + 64:jl * 128 + 128], rhs=Sb[:, 2 * jl + 1, :], start=False, stop=True),
                           r=[kQ, ("Sb", si, 2 * jl + 1)], w=[psk(xb)])
                    ov = oacc[:, hf * 8 + q4 * 4:hf * 8 + (q4 + 1) * 4, :]
                    pv = PS[xb][:].rearrange("p (a b) -> p a b", b=128)
                    if first:
                        op("dve", lambda e, ov=ov, pv=pv: e.tensor_copy(out=ov, in_=pv), r=[psk(xb)], w=[("oacc", hf)])
                    else:
                        op("dve", lambda e, ov=ov, pv=pv: e.tensor_tensor(out=ov, in0=ov, in1=pv, op=ALU.add), r=[psk(xb), ("oacc", hf)], w=[("oacc", hf)])
                    yield

            def interleave(gens):
                alive = list(gens)
                while alive:
                    for g_ in list(alive):
                        try:
                            next(g_)
                        except StopIteration:
                            alive.remove(g_)

            op("pool", lambda e: e.memset(SstD[0][0], 0.0), w=[("Sst", 0, 0)])
            op("pool", lambda e: e.memset(SstD[1][0], 0.0), w=[("Sst", 1, 0)])
            prev = list(pend_oproj)
            del pend_oproj[:]
            interleave([do_dir(0, 0, HS[0]), do_dir(1, 1, HS[1])] + prev)
            interleave([do_dir(0, 1, HS[0]), do_dir(1, 0, HS[1])])
            OACC = [("oacc", 0), ("oacc", 1)]
            HGA = ["hgA0"]
            HGB = ["hgB0"]
            for j in range(16):
                op("act", lambda e, j=j: e.activation(out=on_tok[:, j, :], in_=oacc[:, j, :], func=AF.Square, accum_out=ssq[:, j:j + 1]), r=OACC + HGB, w=HGA + ["ssq"])
            op("act", lambda e: e.activation(out=ssq, in_=ssq, func=AF.Sqrt, scale=1.0 / 128, bias=EPS), r=["ssq"], w=["ssq"])
            op("dve", lambda e: e.reciprocal(out=ssq, in_=ssq), r=["ssq"], w=["ssq"])
            for j in range(16):
                op("dve", lambda e, j=j: e.tensor_scalar(out=on_tok[:, j, :], in0=oacc[:, j, :], scalar1=ssq[:, j:j + 1], scalar2=None, op0=ALU.mult), r=OACC + ["ssq"] + HGA, w=HGA)
            for half in range(2):
                bank = half
                pb = PS[bank][:].bitcast(BF16)
                for jj in range(8):
                    j = half * 8 + jj
                    op("pe", lambda e, pb=pb, jj=jj, j=j: e.transpose(out=pb[:, jj * 128:(jj + 1) * 128], in_=on_tok[:, j, :], identity=ident_b[:]), r=HGA + ["ident_b"], w=[psk(bank)])
                op("dve", lambda e, pb=pb, half=half: e.scalar_tensor_tensor(out=mTh[:, half * 1024:(half + 1) * 1024], in0=pb, scalar=hg[:, h:h + 1], in1=sigG[:, half * 1024:(half + 1) * 1024], op0=ALU.mult, op1=ALU.mult),
                   r=[psk(bank), "hg", "sigG"] + HGA, w=[("mT", sb_)])
            if h == 0:
                dbg_put("mT", mTh[:, 0:256], [("mT", sb_)])
            if dbg:
                return None
            def oproj_gen():
                for tb in range(4):
                    for dc in range(8):
                        bank = 2 + (tb * 8 + dc) % 2
                        op("pe", lambda e, dc=dc, tb=tb, bank=bank: e.matmul(PS[bank][:], lhsT=wo[:, dc * 128:(dc + 1) * 128], rhs=mTh[:, tb * 512:(tb + 1) * 512], start=True, stop=True),
                           r=[("wb", sb_), ("mT", sb_)], w=[psk(bank)])
                        xs = xT[:, dc, tb * 512:(tb + 1) * 512]
                        op("dve", lambda e, xs=xs, bank=bank: e.tensor_tensor(out=xs, in0=xs, in1=PS[bank][:], op=ALU.add), r=[psk(bank), xk(dc, tb)], w=[xk(dc, tb)])
                        if (tb * 8 + dc) % 3 == 2:
                            yield
            return oproj_gen()

        def run_all(gens):
            alive = list(gens)
            while alive:
                for g_ in list(alive):
                    try:
                        next(g_)
                    except StopIteration:
                        alive.remove(g_)

        nxt_holder = [first_head]
        run_all([front_gen(0, nxt_holder[0])])
        for h in range(8):
            cur = nxt_holder[0]
            if h + 1 < 8:
                nxt_holder[0] = load_head(h + 1)
            og = do_head(h, cur)
            if dbg:
                break
            run_all(([front_gen(h + 1, nxt_holder[0])] if h + 1 < 8 else []) + [og])
        for g_ in pend_oproj:
            for _ in g_:
                pass

    if "l0attn" in parts or "l0mix" in parts:
        attn_phase(0)
    if "l0s5" in parts or "l0mix" in parts:
        s5_phase()
    if "l0ffn" in parts:
        ffn_phase(0, 1)
    if "l1mix" in parts:
        hgrn_phase(2)
    if "l1ffn" in parts:
        ffn_phase(1, 3)

    if dbg:
        P.barrier()
        op("sp", lambda e: e.dma_start(out=out_d.rearrange("(p a) d -> p (a d)", p=128), in_=xT_flat), dma=True, dkey="dbgout")
        P.emit()
        return nc
    AR.phase()
    sq = [AR.at("sq%d" % i, i * 4 * KB, [L], BF16) for i in range(2)]
    rstd = AR.at("rstd", 8 * KB, [L], F32)
    stage = [AR.at("stage%d" % i, 16 * KB + i * 4 * KB, [D], F32) for i in range(2)]
    ftmp = [AR.at("ftmp%d" % i, 24 * KB + i * 512, [128], F32) for i in range(4)]
    do_final = "final" in parts
    if do_final:
        rmsnorm_rstd(rstd, sq)
    k = 0
    for t in range(16):
        st = stage[t % 2]
        for half in range(2):
            bank = (2 * t + half) % 4
            for j in range(4):
                c = half * 4 + j
                src = xT[:, c, t * 128:(t + 1) * 128]
                if do_final:
                    ft = ftmp[k % 4]
                    op("dve", lambda e, ft=ft, src=src, c=c, t=t: e.scalar_tensor_tensor(
                        out=ft, in0=src, scalar=gains[:, 4, c:c + 1], in1=rstd[:, t * 128:(t + 1) * 128], op0=ALU.mult, op1=ALU.mult),
                       r=[xk(c, t // 4), ("rstd", t // 4), "gains"], w=[("ftmp", k % 4)])
                    op("pe", lambda e, bank=bank, j=j, ft=ft: e.transpose(out=PS[bank][:, j * 128:(j + 1) * 128], in_=ft, identity=ident_f[:]),
                       r=[("ftmp", k % 4), "ident_f"], w=[psk(bank)])
                    k += 1
                else:
                    op("pe", lambda e, bank=bank, j=j, src=src: e.transpose(out=PS[bank][:, j * 128:(j + 1) * 128], in_=src, identity=ident_f[:]),
                       r=[xk(c, t // 4), "ident_f"], w=[psk(bank)])
            dst = st[:, half * 512:(half + 1) * 512]
            op("act", lambda e, dst=dst, bank=bank: e.activation(out=dst, in_=PS[bank][:], func=AF.Copy), r=[psk(bank)], w=[("stage", t % 2, half)])
        op("sp", lambda e, st=st, t=t: e.dma_start(out=out_d[t * 128:(t + 1) * 128, :], in_=st), r=[("stage", t % 2, 0), ("stage", t % 2, 1)], dma=True, dkey=("st", t % 2))
    P.emit()
    return nc


def host_consts():
    c = {}
    c["ident"] = np.eye(128, dtype=np.float32)
    inv = (10000.0 ** (-np.arange(0, 64, 2, dtype=np.float32) / np.float32(64))).astype(np.float32)
    ang = (np.arange(L, dtype=np.float32)[None, :] * inv[:, None]).astype(np.float32)
    c["rope_cos"] = np.ascontiguousarray(np.tile(np.cos(ang).astype(np.float32), (4, 1)))
    c["rope_sin"] = np.ascontiguousarray(np.tile(np.sin(ang).astype(np.float32), (4, 1)))
    s_ = np.arange(128)[:, None]
    c_ = np.arange(128)[None, :]
    same = (s_ // 64) == (c_ // 64)
    hm = np.stack([(same & (s_ <= c_)), (same & (s_ >= c_))], axis=1).astype(np.float32)
    c["hmask"] = np.ascontiguousarray(hm)
    sl_ = (np.arange(128) // 16)[:, None]
    tl_ = (np.arange(128) // 16)[None, :]
    c["s5m"] = np.ascontiguousarray(np.stack([(tl_ >= sl_), (tl_ <= sl_)], axis=1).astype(np.float32))
    return c


def make_in_maps(inputs, parts=None):
    consts = host_consts()
    gains = np.concatenate([inputs["norm_mix_g"][0:1], inputs["norm_mlp_g"][0:1], inputs["norm_mix_g"][1:2],
                            inputs["norm_mlp_g"][1:2], inputs["final_norm_g"][None, :]], axis=0).astype(np.float32)
    shared = dict(consts)
    shared["gains"] = np.ascontiguousarray(gains.reshape(5, 8, 128).transpose(2, 0, 1))
    shared["w_in_odd"] = np.ascontiguousarray(inputs["w_in_odd"][0], dtype=np.float32)
    shared["w_out_odd"] = np.ascontiguousarray(inputs["w_out_odd"][0], dtype=np.float32)
    shared["lbl"] = np.ascontiguousarray(np.asarray(inputs["hgrn_lb_logits"], dtype=np.float32).reshape(2, 8, 128).transpose(2, 0, 1))
    shared["hg"] = np.ascontiguousarray(np.asarray(inputs["hgrn_norm_g"][0], dtype=np.float32).reshape(8, 128).T)
    shared["w_in_even"] = np.ascontiguousarray(inputs["w_in_even"][0], dtype=np.float32)
    shared["w_out_even"] = np.ascontiguousarray(inputs["w_out_even"][0], dtype=np.float32)
    shared["diff_lambda"] = np.ascontiguousarray(np.asarray(inputs["diff_lambda"][0], dtype=np.float32).reshape(1, 256))
    shared["subln_g"] = np.ascontiguousarray(np.asarray(inputs["diff_subln_g"][0], dtype=np.float32).reshape(128, 1))
    f32 = np.float32
    lre = np.asarray(inputs["s5_lam_re"][0], f32); lim = np.asarray(inputs["s5_lam_im"][0], f32); lst = np.asarray(inputs["s5_log_step"][0], f32)
    s5s = np.stack([lre.transpose(0, 2, 1).reshape(128, 32), lim.transpose(0, 2, 1).reshape(128, 32),
                    np.repeat(lst[:, None, :], 64, axis=1).reshape(128, 32)], axis=1)
    shared["s5s"] = np.ascontiguousarray(s5s, dtype=f32)
    shared["s5b"] = np.ascontiguousarray(np.stack([np.asarray(inputs[k][0], f32).transpose(0, 2, 1, 3).reshape(128, 32, 16) for k in ("s5_b_re", "s5_b_im")], axis=1))
    shared["s5c"] = np.ascontiguousarray(np.stack([np.asarray(inputs[k][0], f32).transpose(0, 3, 1, 2).reshape(128, 32, 16) for k in ("s5_c_re", "s5_c_im")], axis=1))
    shared["s5d"] = np.ascontiguousarray(np.asarray(inputs["s5_d"][0], f32).reshape(4, 128).T)
    shared["s5bg"] = np.ascontiguousarray(np.asarray(inputs["s5_b_glu"][0], f32).reshape(4, 128).T)
    shared["w_glu"] = np.ascontiguousarray(inputs["s5_w_glu"][0], dtype=f32)
    shared["w_ff_in"] = np.ascontiguousarray(inputs["w_ff_in"], dtype=np.float32)
    shared["w_ff_out"] = np.ascontiguousarray(inputs["w_ff_out"], dtype=np.float32)
    x = np.asarray(inputs["x"], dtype=np.float32)
    maps = []
    for b in range(x.shape[0]):
        m = dict(shared)
        m["x"] = np.ascontiguousarray(x[b])
        maps.append(m)
    return maps


ALL_PARTS = {"l0mix", "l0ffn", "l1mix", "l1ffn", "final"}
_NC_CACHE = {}


def kernel(**inputs):
    inputs = {k: np.asarray(v) for k, v in inputs.items()}
    key = "full"
    if key not in _NC_CACHE:
        _NC_CACHE[key] = build({"parts": ALL_PARTS})
    nc = _NC_CACHE[key]
    maps = make_in_maps(inputs)
    res = run_bass_kernel_spmd(nc, maps, core_ids=list(range(8)))
    out = np.stack([np.asarray(r["out"]) for r in res.results], axis=0)
    return out.astype(np.float32)
```

```python
import math
from contextlib import ExitStack

import numpy as np
import concourse.bass as bass
import concourse.mybir as mybir
from concourse.bass_utils import run_bass_kernel_spmd

F32 = mybir.dt.float32
BF16 = mybir.dt.bfloat16
AF = mybir.ActivationFunctionType
ALU = mybir.AluOpType
AX = mybir.AxisListType

ENGS = ("pe", "act", "dve", "pool", "sp")

L = 2048
D = 1024
NTB = 4
EPS = 1e-6
LAMBDA_INIT0 = 0.8 - 0.6 * math.exp(-0.3 * 0)


class Op:
    __slots__ = ("eng", "fn", "deps", "is_dma", "dkey", "signal", "sem", "val", "idx", "chain")


class Prog:
    def __init__(self, nc):
        self.nc = nc
        self.ops = []
        self.last_w = {}
        self.readers = {}
        self.stack = ExitStack()
        self.pending_barrier = {}

    def sb(self, name, shape, dt):
        return self.stack.enter_context(self.nc.sbuf_tensor("sb_" + name, list(shape), dt))

    def ps(self, name, shape, dt=F32):
        return self.stack.enter_context(self.nc.psum_tensor("pp_" + name, list(shape), dt))

    def barrier(self):
        deps = set()
        last = {}
        for o in self.ops:
            if o.is_dma:
                deps.add(o.idx)
            else:
                last[o.eng] = o.idx
        deps.update(last.values())
        self.pending_barrier = {e: set(deps) for e in ENGS}
        self.last_w = {}
        self.readers = {}

    def op(self, eng, fn, r=(), w=(), dma=False, dkey=None, chain=None):
        o = Op()
        o.eng, o.fn, o.is_dma, o.signal = eng, fn, dma, False
        o.chain = chain
        o.idx = len(self.ops)
        deps = set()
        for k in list(r) + list(w):
            if k in self.last_w:
                deps.add(self.last_w[k])
        for k in w:
            for rd in self.readers.get(k, ()):
                deps.add(rd)
        if self.pending_barrier.get(eng):
            deps |= self.pending_barrier[eng]
            self.pending_barrier[eng] = set()
        deps.discard(o.idx)
        o.deps = deps
        o.dkey = (dkey if dkey is not None else (w[0] if len(w) else r[0])) if dma else None
        for k in r:
            self.readers.setdefault(k, []).append(o.idx)
        for k in w:
            self.last_w[k] = o.idx
            self.readers[k] = []
        self.ops.append(o)
        return o.idx

    def _skip(self, od, o):
        if od.is_dma or o.is_dma or od.eng != o.eng:
            return False
        return o.eng == "pe" or (o.chain is not None and o.chain == od.chain)

    def emit(self, final_eng="sp"):
        nc = self.nc
        ops = self.ops
        final_deps = [o.idx for o in ops if o.is_dma]
        for o in ops:
            for d in o.deps:
                if not self._skip(ops[d], o):
                    ops[d].signal = True
        for d in final_deps:
            ops[d].signal = True
        sems = {}

        def get_sem(key):
            if key not in sems:
                sems[key] = self.stack.enter_context(nc.semaphore("s%d" % len(sems)))
            return sems[key]

        cnt = {}
        for o in ops:
            if not o.signal:
                continue
            key = ("dma", o.dkey) if o.is_dma else ("eng", o.eng)
            inc = 16 if o.is_dma else 1
            cnt[key] = cnt.get(key, 0) + inc
            o.sem = get_sem(key)
            o.val = cnt[key]
            o.dkey = key
        self.n_sems = len(sems)
        per_eng = {e: [] for e in ENGS}
        for o in ops:
            per_eng[o.eng].append(o)

        def run(eng_name, e):
            waited = {}
            for o in per_eng[eng_name]:
                need = {}
                for d in o.deps:
                    od = ops[d]
                    if (not od.signal) or self._skip(od, o):
                        continue
                    if waited.get(od.dkey, 0) >= od.val:
                        continue
                    if need.get(od.dkey, (None, 0))[1] < od.val:
                        need[od.dkey] = (od.sem, od.val)
                for k, (sem, val) in need.items():
                    e.wait_ge(sem, val)
                    waited[k] = val
                ins = o.fn(e)
                if o.signal:
                    ins.then_inc(o.sem, 16 if o.is_dma else 1)
            if eng_name == final_eng:
                need = {}
                for d in final_deps:
                    od = ops[d]
                    if waited.get(od.dkey, 0) >= od.val:
                        continue
                    if need.get(od.dkey, (None, 0))[1] < od.val:
                        need[od.dkey] = (od.sem, od.val)
                for k, (sem, val) in need.items():
                    e.wait_ge(sem, val)

        with nc.Block() as block:
            @block.tensor
            def _(e):
                run("pe", e)

            @block.scalar
            def _(e):
                run("act", e)

            @block.vector
            def _(e):
                run("dve", e)

            @block.gpsimd
            def _(e):
                run("pool", e)

            @block.sync
            def _(e):
                run("sp", e)
        self.stack.close()


class Arena:
    def __init__(self, P, nbytes):
        self.P = P
        self.nbytes = nbytes
        self.base = P.sb("arena", [128, nbytes // 2], BF16)
        self.live = []

    def phase(self):
        self.P.barrier()
        self.live = []

    def at(self, name, off, shape, dt):
        n = 1
        for s in shape:
            n *= s
        nb = n * (4 if dt == F32 else 2)
        assert off % 4 == 0 and off + nb <= self.nbytes, (name, off, nb, self.nbytes)
        for (a, b, nm) in self.live:
            assert off >= b or off + nb <= a, ("arena overlap", name, nm)
        self.live.append((off, off + nb, name))
        ap = self.base[:, off // 2:(off + nb) // 2]
        if dt == F32:
            ap = ap.bitcast(F32)
        if len(shape) > 1:
            names = "abcd"[:len(shape)]
            pat = "p (" + " ".join(names) + ") -> p " + " ".join(names)
            ap = ap.rearrange(pat, **{names[i]: shape[i] for i in range(len(shape))})
        return ap

    def drop(self, name):
        self.live = [x for x in self.live if x[2] != name]


KB = 1024


def build(cfg):
    parts = cfg["parts"]
    nc = bass.Bass("TRN2", target_bir_lowering=False)

    def din(name, shape):
        return nc.dram_tensor(name, list(shape), F32, kind="ExternalInput").ap()

    x_d = din("x", [L, D])
    out_d = nc.dram_tensor("out", [L, D], F32, kind="ExternalOutput").ap()
    gains_d = din("gains", [128, 5, 8])
    w_ff_in_d = din("w_ff_in", [2, D, 4 * D])
    w_ff_out_d = din("w_ff_out", [2, 4 * D, D])
    ident_d = din("ident", [128, 128])
    w_in_odd_d = din("w_in_odd", [D, 5 * D])
    w_in_even_d = din("w_in_even", [D, 2 * D])
    w_out_even_d = din("w_out_even", [D, D])
    cos_d = din("rope_cos", [128, L])
    sin_d = din("rope_sin", [128, L])
    dl_d = din("diff_lambda", [1, 256])
    sg_d = din("subln_g", [128, 1])
    w_glu_d = din("w_glu", [512, 512])
    s5s_d = din("s5s", [128, 3, 32])
    s5b_d = din("s5b", [128, 2, 32, 16])
    s5c_d = din("s5c", [128, 2, 32, 16])
    s5d_d = din("s5d", [128, 4])
    s5bg_d = din("s5bg", [128, 4])
    s5m_d = din("s5m", [128, 2, 128])
    w_out_odd_d = din("w_out_odd", [D, D])
    lbl_d = din("lbl", [128, 2, 8])
    hg_d = din("hg", [128, 8])
    hmask_d = din("hmask", [128, 2, 128])

    P = Prog(nc)
    op = P.op

    xT = P.sb("xT", [128, 8, L], F32)
    WB = [P.sb("wb%d" % i, [128, 4096], BF16) for i in range(4)]
    ident_f = P.sb("ident_f", [128, 128], F32)
    ident_b = P.sb("ident_b", [128, 128], BF16)
    ones_b = P.sb("ones_b", [128, 128], BF16)
    ones_f = P.sb("ones_f", [128, 128], F32)
    gains = P.sb("gains", [128, 5, 8], F32)
    PS = [P.ps("ps%d" % i, [128, 512], F32) for i in range(8)]
    lbl = P.sb("lbl", [128, 2, 8], F32)
    dl = P.sb("dl", [128, 256], F32)
    dlp = P.sb("dlp", [128, 128], F32)
    lam = P.sb("lam", [128, 4], F32)
    sg = P.sb("sg", [128, 1], F32)
    rr_g = P.sb("rr_g", [128, 8], F32)
    hg = P.sb("hg", [128, 8], F32)
    hmask = P.sb("hmask", [128, 2, 128], F32)
    lb = P.sb("lb", [128, 8], F32)
    oml = P.sb("oml", [128, 8], F32)
    noml = P.sb("noml", [128, 8], F32)
    AR = Arena(P, 104 * KB)

    def psk(i):
        return ("ps", i)

    dbg = cfg.get("dbg", False)
    dbg_tab = cfg.setdefault("dbg_tab", {})
    xT_flat = xT[:].rearrange("p a b -> p (a b)")
    dbg_off = [0]

    def dbg_put(name, ap, rkeys):
        if not dbg:
            return
        shp = list(ap.shape)
        n = 1
        for v_ in shp[1:]:
            n *= v_
        o = dbg_off[0]
        dst = xT_flat[:, o:o + n]
        if len(shp) == 3:
            dst = dst.rearrange("p (a b) -> p a b", b=shp[2])
        op("dve", lambda e: e.tensor_copy(out=dst, in_=ap), r=list(rkeys), w=["dbg"])
        dbg_tab[name] = (o, shp)
        dbg_off[0] = o + n

    def xk(c, tb):
        return ("xT", c, tb)

    def hk(c, tb):
        return ("hT", c, tb)

    XT_ALL = [xk(c, tb) for c in range(8) for tb in range(4)]
    HT_ALL = [hk(c, tb) for c in range(8) for tb in range(4)]

    op("sp", lambda e: e.dma_start(out=ident_f[:], in_=ident_d), w=["ident_f"], dma=True)
    op("dve", lambda e: e.tensor_copy(out=ident_b[:], in_=ident_f[:]), r=["ident_f"], w=["ident_b"])
    op("pool", lambda e: e.memset(ones_b[:], 1.0), w=["ones_b"])
    op("pool", lambda e: e.memset(ones_f[:], 1.0), w=["ones_f"])
    op("sp", lambda e: e.dma_start(out=gains[:], in_=gains_d), w=["gains"], dma=True)

    AR.phase()
    xin = [AR.at("xin%d" % i, i * 4 * KB, [D], F32) for i in range(2)]
    ev = 0
    for t in range(16):
        b = xin[t % 2]
        op("sp", lambda e, b=b, t=t: e.dma_start(out=b, in_=x_d[t * 128:(t + 1) * 128, :]), w=[("xin", t % 2)], dma=True)
        for half in range(2):
            bank = (2 * t + half) % 4
            for j in range(4):
                c = half * 4 + j
                op("pe", lambda e, bank=bank, j=j, c=c, b=b: e.transpose(out=PS[bank][:, j * 128:(j + 1) * 128], in_=b[:, c * 128:(c + 1) * 128], identity=ident_f[:]),
                   r=[("xin", t % 2), "ident_f"], w=[psk(bank)])
            dst = xT[:, half * 4:(half + 1) * 4, t * 128:(t + 1) * 128]
            src = PS[bank][:].rearrange("p (a b) -> p a b", b=128)
            wk = [xk(half * 4 + j, t // 4) for j in range(4)]
            if ev % 2 == 0:
                op("dve", lambda e, dst=dst, src=src: e.tensor_copy(out=dst, in_=src), r=[psk(bank)], w=wk)
            else:
                op("act", lambda e, dst=dst, src=src: e.activation(out=dst, in_=src, func=AF.Copy), r=[psk(bank)], w=wk)
            ev += 1

    def rmsnorm_rstd(rstd, sq):
        for c in range(8):
            s = sq[c % 2]
            op("act", lambda e, s=s, c=c: e.activation(out=s, in_=xT[:, c, :], func=AF.Square),
               r=[xk(c, tb) for tb in range(4)], w=[("sq", c % 2)])
            for tb in range(4):
                op("pe", lambda e, s=s, c=c, tb=tb: e.matmul(PS[tb][:], lhsT=ones_b[:], rhs=s[:, tb * 512:(tb + 1) * 512], start=(c == 0), stop=(c == 7)),
                   r=[("sq", c % 2), "ones_b"], w=[psk(tb)])
        for tb in range(4):
            sl = rstd[:, tb * 512:(tb + 1) * 512]
            op("act", lambda e, sl=sl, tb=tb: e.activation(out=sl, in_=PS[tb][:], func=AF.Sqrt, scale=1.0 / D, bias=EPS),
               r=[psk(tb)], w=[("rstd", tb)])
            op("dve", lambda e, sl=sl: e.reciprocal(out=sl, in_=sl), r=[("rstd", tb)], w=[("rstd", tb)])

    def norm_to_hT(gi, hT, rstd, sq):
        rmsnorm_rstd(rstd, sq)
        for c in range(8):
            for tb in range(4):
                op("dve", lambda e, c=c, tb=tb: e.scalar_tensor_tensor(
                    out=hT[:, c, tb * 512:(tb + 1) * 512], in0=xT[:, c, tb * 512:(tb + 1) * 512], scalar=gains[:, gi, c:c + 1],
                    in1=rstd[:, tb * 512:(tb + 1) * 512], op0=ALU.mult, op1=ALU.mult),
                   r=[xk(c, tb), ("rstd", tb), "gains"], w=[hk(c, tb)])

    def wload(slot, src, r=()):
        a, b = src.shape[1], src.shape[2]
        dst = WB[slot][:, 0:a * b].rearrange("p (a b) -> p a b", b=b)
        op("pool", lambda e: e.dma_start(out=dst, in_=src), r=list(r), w=[("wb", slot)], dma=True)
        return dst

    def ffn_loader(l):
        w_in = w_ff_in_d[l].rearrange("(c p) f -> p c f", p=128)
        w_out = w_ff_out_d[l].rearrange("(c p) d -> p c d", p=128)
        views = {}

        def load(fg):
            views[fg] = (wload((2 * fg) % 4, w_in[:, :, fg * 512:(fg + 1) * 512]),
                         wload((2 * fg + 1) % 4, w_out[:, fg * 4:(fg + 1) * 4, :]))
        return views, load

    def ffn(l, hT, actT, rl, views, load):
        k = 0
        for fg in range(8):
            wi, wo = views[fg]
            sa, sbk = (2 * fg) % 4, (2 * fg + 1) % 4
            at = actT[fg % 2]
            for tb in range(4):
                for fc in range(4):
                    bank = k % 3
                    for c in range(8):
                        op("pe", lambda e, bank=bank, wi=wi, c=c, fc=fc, tb=tb: e.matmul(
                            PS[bank][:], lhsT=wi[:, c, fc * 128:(fc + 1) * 128], rhs=hT[:, c, tb * 512:(tb + 1) * 512], start=(c == 0), stop=(c == 7)),
                           r=[("wb", sa), hk(c, tb)], w=[psk(bank)])
                    r_ = rl[k % 2]
                    op("act", lambda e, bank=bank, r_=r_: e.activation(out=r_, in_=PS[bank][:], func=AF.Relu), r=[psk(bank)], w=[("rl", k % 2)])
                    op("act", lambda e, r_=r_, at=at, fc=fc, tb=tb: e.activation(out=at[:, fc, tb * 512:(tb + 1) * 512], in_=r_, func=AF.Square),
                       r=[("rl", k % 2)], w=[("actT", fg % 2, fc, tb)])
                    k += 1
            kk = 0
            for tb in range(4):
                for dc in range(8):
                    bank = 3 + kk % 3
                    for fc in range(4):
                        op("pe", lambda e, bank=bank, wo=wo, fc=fc, dc=dc, tb=tb, at=at: e.matmul(
                            PS[bank][:], lhsT=wo[:, fc, dc * 128:(dc + 1) * 128], rhs=at[:, fc, tb * 512:(tb + 1) * 512], start=(fc == 0), stop=(fc == 3)),
                           r=[("wb", sbk), ("actT", fg % 2, fc, tb)], w=[psk(bank)])
                    xs = xT[:, dc, tb * 512:(tb + 1) * 512]
                    op("dve", lambda e, xs=xs, bank=bank: e.tensor_tensor(out=xs, in0=xs, in1=PS[bank][:], op=ALU.add), r=[psk(bank), xk(dc, tb)], w=[xk(dc, tb)])
                    kk += 1
            if fg + 2 < 8:
                load(fg + 2)

    def ffn_phase(l, gi):
        AR.phase()
        hT = AR.at("hT", 0, [8, L], BF16)
        actT = [AR.at("actT%d" % i, 32 * KB + i * 16 * KB, [4, L], BF16) for i in range(2)]
        rl = [AR.at("rl%d" % i, 64 * KB + i * 2 * KB, [512], F32) for i in range(2)]
        sq = [AR.at("sq%d" % i, 68 * KB + i * 4 * KB, [L], BF16) for i in range(2)]
        rstd = AR.at("rstd", 76 * KB, [L], F32)
        views, load = ffn_loader(l)
        load(0)
        load(1)
        norm_to_hT(gi, hT, rstd, sq)
        ffn(l, hT, actT, rl, views, load)


    def attn_phase(gi):
        w_in = w_in_even_d.rearrange("(c p) f -> p c f", p=128)
        w_out = w_out_even_d.rearrange("(c p) d -> p c d", p=128)
        T0 = 81 * KB
        AR.phase()
        hT = AR.at("hT", 0, [8, L], BF16)
        sq = [AR.at("sq%d" % i, T0 + i * 4 * KB, [L], BF16) for i in range(2)]
        rstd = AR.at("rstd", T0 + 8 * KB, [L], F32)
        wq = wload(0, w_in[:, :, 0:512])
        wk = wload(2, w_in[:, :, 512:1024])
        norm_to_hT(gi, hT, rstd, sq)
        op("sp", lambda e: e.dma_start(out=dl[:], in_=bass.AP(dl_d.tensor, 0, [[0, 128], [1, 256]])), w=["dl"], dma=True)
        op("sp", lambda e: e.dma_start(out=sg[:], in_=sg_d), w=["sg"], dma=True)
        op("dve", lambda e: e.tensor_tensor(out=dlp[:, 0:64], in0=dl[:, 0:64], in1=dl[:, 64:128], op=ALU.mult), r=["dl"], w=["dlp"])
        op("dve", lambda e: e.tensor_tensor(out=dlp[:, 64:128], in0=dl[:, 128:192], in1=dl[:, 192:256], op=ALU.mult), r=["dl", "dlp"], w=["dlp"])
        op("dve", lambda e: e.reduce_sum(out=lam[:, 0:2], in_=dlp[:].rearrange("p (a b) -> p a b", b=64), axis=AX.X), r=["dlp"], w=["lam"])
        op("act", lambda e: e.activation(out=lam[:, 0:2], in_=lam[:, 0:2], func=AF.Exp), r=["lam"], w=["lam"])
        op("dve", lambda e: e.tensor_tensor(out=lam[:, 2:3], in0=lam[:, 1:2], in1=lam[:, 0:1], op=ALU.subtract), r=["lam"], w=["lam"])
        op("dve", lambda e: e.tensor_scalar(out=lam[:, 3:4], in0=lam[:, 2:3], scalar1=-LAMBDA_INIT0, scalar2=None, op0=ALU.add), r=["lam"], w=["lam"])
        op("dve", lambda e: e.tensor_scalar(out=sg[:], in0=sg[:], scalar1=1.0 - LAMBDA_INIT0, scalar2=None, op0=ALU.mult), r=["sg"], w=["sg"])
        nlam = lam[:, 3:4]
        if cfg.get("attn_stop", 9) <= 1:
            return

        AR.phase()
        hT = AR.at("hT", 0, [8, L], BF16)
        qT = AR.at("qT", 32 * KB, [4, L], BF16)
        kT = AR.at("kT", 48 * KB, [4, L], BF16)
        vaug = AR.at("vaug", 64 * KB, [16, 4, 130], BF16)
        t1 = [AR.at("t1_%d" % i, T0 + i * 2 * KB, [512], F32) for i in range(2)]
        t2 = [AR.at("t2_%d" % i, T0 + 4 * KB + i * 2 * KB, [512], F32) for i in range(2)]
        cosb = [AR.at("cos%d" % i, T0 + 8 * KB + i * 2 * KB, [512], F32) for i in range(2)]
        sinb = [AR.at("sin%d" % i, T0 + 12 * KB + i * 2 * KB, [512], F32) for i in range(2)]

        def load_group(slot, g):
            return wload(slot, w_in[:, :, g * 512:(g + 1) * 512])

        def rotate(sa, sr):
            a = WB[sa][:].rearrange("p (cb two j) -> p cb two j", two=2, j=32)
            r_ = WB[sr][:].rearrange("p (cb two j) -> p cb two j", two=2, j=32)
            op("dve", lambda e: e.tensor_scalar(out=r_[:, :, 0, :], in0=a[:, :, 1, :], scalar1=-1.0, scalar2=None, op0=ALU.mult), r=[("wb", sa)], w=[("wb", sr)])
            op("dve", lambda e: e.tensor_copy(out=r_[:, :, 1, :], in_=a[:, :, 0, :]), r=[("wb", sa)], w=[("wb", sr)])
            return WB[sr][:].rearrange("p (a b) -> p a b", b=512)

        wqr = rotate(0, 1)
        wkr = rotate(2, 3)
        op("pool", lambda e: e.memset(vaug[:, :, :, 128:130], 1.0), w=["vaug1"])
        kctr = [0]

        def rope_proj(tb, wA, wR, sA, sR, dstT, h, dkey):
            i = kctr[0] % 2
            kctr[0] += 1
            ba, bb = 2 * i, 2 * i + 1
            for c in range(8):
                op("pe", lambda e, c=c: e.matmul(PS[ba][:], lhsT=wA[:, c, h * 128:(h + 1) * 128], rhs=hT[:, c, tb * 512:(tb + 1) * 512], start=(c == 0), stop=(c == 7)),
                   r=[("wb", sA), hk(c, tb)], w=[psk(ba)])
            for c in range(8):
                op("pe", lambda e, c=c: e.matmul(PS[bb][:], lhsT=wR[:, c, h * 128:(h + 1) * 128], rhs=hT[:, c, tb * 512:(tb + 1) * 512], start=(c == 0), stop=(c == 7)),
                   r=[("wb", sR), hk(c, tb)], w=[psk(bb)])
            op("dve", lambda e: e.tensor_tensor(out=t1[i], in0=PS[ba][:], in1=cosb[tb % 2], op=ALU.mult), r=[psk(ba), ("cos", tb % 2)], w=[("t1", i)])
            op("dve", lambda e: e.tensor_tensor(out=t2[i], in0=PS[bb][:], in1=sinb[tb % 2], op=ALU.mult), r=[psk(bb), ("sin", tb % 2)], w=[("t2", i)])
            op("pool", lambda e: e.tensor_tensor(out=dstT[:, h, tb * 512:(tb + 1) * 512], in0=t1[i], in1=t2[i], op=ALU.add), r=[("t1", i), ("t2", i)], w=[(dkey, h, tb)])

        def do_tb(tb):
            op("sp", lambda e: e.dma_start(out=cosb[tb % 2], in_=cos_d[:, tb * 512:(tb + 1) * 512]), w=[("cos", tb % 2)], dma=True)
            op("sp", lambda e: e.dma_start(out=sinb[tb % 2], in_=sin_d[:, tb * 512:(tb + 1) * 512]), w=[("sin", tb % 2)], dma=True)
            for h in range(4):
                rope_proj(tb, wq, wqr, 0, 1, qT, h, "qT")
                rope_proj(tb, wk, wkr, 2, 3, kT, h, "kT")

        for tb in range(4):
            do_tb(tb)
        wv = load_group(0, 2)

        def do_v(j):
            bank = 4 + j % 2
            for c in range(8):
                op("pe", lambda e, c=c: e.matmul(PS[bank][:], lhsT=hT[:, c, j * 128:(j + 1) * 128], rhs=wv[:, c, :], start=(c == 0), stop=(c == 7)),
                   r=[("wb", 0), hk(c, j // 4)], w=[psk(bank)])
            op("act", lambda e: e.activation(out=vaug[:, j, :, 0:128], in_=PS[bank][:].rearrange("p (a b) -> p a b", b=128), func=AF.Copy), r=[psk(bank)], w=[("vaug", j)])

        for j in range(16):
            do_v(j)
        wo = wload(2, w_out[:, 0:4, :])
        dbg_put("qT0", qT[:, 0, 0:256], [("qT", 0, 0)])
        dbg_put("kT0", kT[:, 0, 0:256], [("kT", 0, 0)])
        dbg_put("vaug0", vaug[:, 0, :, :], [("vaug", 0), "vaug1"])
        dbg_put("lam", lam[:, 0:4], ["lam"])
        if cfg.get("attn_stop", 9) <= 2:
            return

        AR.phase()
        hT = AR.at("hT", 0, [8, L], BF16)
        qT = AR.at("qT", 32 * KB, [4, L], BF16)
        kT = AR.at("kT", 48 * KB, [4, L], BF16)
        vaug = AR.at("vaug", 64 * KB, [16, 4, 130], BF16)
        aT = AR.at("aT", T0, [4, L], BF16)
        PT = [AR.at("PT%d" % i, T0 + 16 * KB + i * KB, [512], BF16) for i in range(4)]
        SM = T0 + 20 * KB
        rr = rr_g[:]
        tt_ = [AR.at("tt%d" % i, SM + i * 512, [128], F32) for i in range(2)]
        oo = [AR.at("oo%d" % i, SM + 1024 + i * 512, [128], F32) for i in range(2)]
        onb = [AR.at("onb%d" % i, SM + 2048 + i * 256, [128], BF16) for i in range(4)]
        pctr = [0]
        cctr = [0]
        pending = []

        def accv(comp, qs):
            bank = 2 + 2 * comp + qs // 2
            off = (qs % 2) * 130
            return bank, PS[bank][:, off:off + 129]

        its = [(h_, qb_, comp_, kt_) for h_ in range(4) for qb_ in range(4) for comp_ in range(2) for kt_ in range(16)]
        if cfg.get("attn_stop", 9) == 3:
            its = [x_ for x_ in its if x_[0] == 0 and x_[1] == 0]

        SRING = (0, 1, 6)
        accs = [WB[1][:].bitcast(F32)[:, 512 * (1 + c_):512 * (2 + c_)] for c_ in range(2)]

        qpad = [[WB[0][:, (comp_ * 2 + j_) * 512:(comp_ * 2 + j_ + 1) * 512] for j_ in range(2)] for comp_ in range(2)]
        op("pool", lambda e: e.memset(WB[0][:, 0:2048], 0.0), r=[("wb", 0)], w=[("wb", 0)] + [("qpad", c_, j_) for c_ in range(2) for j_ in range(2)])

        def emit_qpad(gi):
            h, qb = gi // 4, gi % 4
            j = gi % 2
            op("pool", lambda e: e.tensor_copy(out=qpad[0][j][0:64, :], in_=qT[0:64, h, qb * 512:(qb + 1) * 512]), r=[("qT", h, qb)], w=[("qpad", 0, j)])
            op("pool", lambda e: e.tensor_copy(out=qpad[1][j][64:128, :], in_=qT[64:128, h, qb * 512:(qb + 1) * 512]), r=[("qT", h, qb)], w=[("qpad", 1, j)])

        def emit_S(i):
            h, qb, comp, kt = its[i]
            sbank = SRING[i % 3]
            j = (h * 4 + qb) % 2
            op("pe", lambda e: e.matmul(PS[sbank][:], lhsT=kT[:, h, kt * 128:(kt + 1) * 128], rhs=qpad[comp][j], start=True, stop=True),
               r=[("kT", h, kt // 4), ("qpad", comp, j)], w=[psk(sbank)])

        def emit_exp_pv(i):
            h, qb, comp, kt = its[i]
            sbank = SRING[i % 3]
            pi = i % 4
            op("act", lambda e: e.activation(out=PT[pi], in_=PS[sbank][:], func=AF.Exp, scale=0.125), r=[psk(sbank)], w=[("PT", pi)])
            bo, bs = 2 + 2 * comp, 3 + 2 * comp
            op("pe", lambda e: e.matmul(PS[bo][:], lhsT=vaug[:, kt, h, 0:128], rhs=PT[pi], start=(kt == 0), stop=(kt == 15)),
               r=[("PT", pi), ("vaug", kt)], w=[psk(bo)])
            op("pe", lambda e: e.matmul(PS[bs][:], lhsT=ones_b[:], rhs=PT[pi], start=(kt == 0), stop=(kt == 15)),
               r=[("PT", pi), "ones_b"], w=[psk(bs)])

        def attend_tail(h, qb):
            fo = WB[3][:].bitcast(F32)
            r0, r1, t_, o_ = fo[:, 0:512], fo[:, 512:1024], fo[:, 1024:1536], fo[:, 1536:2048]
            sqb = WB[1][:, 0:512]
            K3 = [("wb", 3)]
            if dbg and h == 0 and qb == 0:
                dbg_put("acc0", PS[2][:, 0:256], [psk(2)])
                dbg_put("acc1", PS[4][:, 0:256], [psk(4)])
            if cfg.get("attn_stop", 9) <= 3:
                return
            op("dve", lambda e: e.reciprocal(out=r0, in_=PS[3][:]), r=[psk(3)], w=K3)
            op("dve", lambda e: e.reciprocal(out=r1, in_=PS[5][:]), r=[psk(5)] + K3, w=K3)
            op("dve", lambda e: e.tensor_tensor(out=t_, in0=PS[4][:], in1=r1, op=ALU.mult), r=[psk(4)] + K3, w=K3)
            op("dve", lambda e: e.tensor_tensor(out=o_, in0=PS[2][:], in1=r0, op=ALU.mult), r=[psk(2)] + K3, w=K3)
            op("dve", lambda e: e.scalar_tensor_tensor(out=o_, in0=t_, scalar=nlam, in1=o_, op0=ALU.mult, op1=ALU.add), r=K3 + ["lam"], w=K3)
            op("act", lambda e: e.activation(out=sqb, in_=o_, func=AF.Square), r=K3, w=[("wb", 1)])

            def fin():
                op("pe", lambda e: e.matmul(PS[7][:], lhsT=ones_b[:], rhs=sqb, start=True, stop=True), r=[("wb", 1), "ones_b"], w=[psk(7)])
                op("act", lambda e: e.activation(out=r0, in_=PS[7][:], func=AF.Sqrt, scale=1.0 / 128, bias=EPS), r=[psk(7)] + K3, w=K3)
                op("dve", lambda e: e.reciprocal(out=r0, in_=r0), r=K3, w=K3)
                op("dve", lambda e: e.scalar_tensor_tensor(out=aT[:, h, qb * 512:(qb + 1) * 512], in0=o_, scalar=sg[:, 0:1], in1=r0, op0=ALU.mult, op1=ALU.mult),
                   r=K3 + ["sg"], w=[("aT", h, qb)])
            pending.append(fin)

        emit_qpad(0)
        LOOK = 2
        for i in range(LOOK):
            emit_S(i)
        for i in range(len(its)):
            if its[i][2] == 0 and its[i][3] == 0:
                gi_ = its[i][0] * 4 + its[i][1]
                if gi_ + 1 < 16 and cfg.get("attn_stop", 9) != 3:
                    emit_qpad(gi_ + 1)
            if i + LOOK < len(its):
                emit_S(i + LOOK)
            emit_exp_pv(i)
            h_, qb_, comp_, kt_ = its[i]
            if kt_ == 15 and comp_ == 0 and pending:
                for f in pending:
                    f()
                del pending[:]
            if kt_ == 15 and comp_ == 1:
                attend_tail(h_, qb_)
        for f in pending:
            f()
        del pending[:]
        dbg_put("aT", aT[:, :, 0:512], [("aT", h_, 0) for h_ in range(4)])

        def oproj(tb, dc, kq):
            bank = kq % 2
            for hc in range(4):
                op("pe", lambda e, hc=hc: e.matmul(PS[bank][:], lhsT=wo[:, hc, dc * 128:(dc + 1) * 128], rhs=aT[:, hc, tb * 512:(tb + 1) * 512], start=(hc == 0), stop=(hc == 3)),
                   r=[("wb", 2), ("aT", hc, tb)], w=[psk(bank)])
            xs = xT[:, dc, tb * 512:(tb + 1) * 512]
            op("dve", lambda e: e.tensor_tensor(out=xs, in0=xs, in1=PS[bank][:], op=ALU.add), r=[psk(bank), xk(dc, tb)], w=[xk(dc, tb)])

        if not dbg:
            kq = 0
            for tb in range(4):
                for dc in range(8):
                    oproj(tb, dc, kq)
                    kq += 1


    def s5_phase():
        w_in = w_in_even_d.rearrange("(c p) f -> p c f", p=128)
        w_out = w_out_even_d.rearrange("(c p) d -> p c d", p=128)
        AR.phase()
        hT = AR.at("hT", 0, [8, L], BF16)
        if not ("l0attn" in parts or "l0mix" in parts):
            sq = [AR.at("sq%d" % i, 81 * KB + i * 4 * KB, [L], BF16) for i in range(2)]
            rstd = AR.at("rstd", 89 * KB, [L], F32)
            norm_to_hT(0, hT, rstd, sq)
        uT = AR.at("uT", 32 * KB, [4, L], BF16)
        wu = wload(0, w_in[:, :, 1536:2048])

        def uproj(tb, cc, kq):
            bank = kq % 4
            for c in range(8):
                op("pe", lambda e, c=c: e.matmul(PS[bank][:], lhsT=wu[:, c, cc * 128:(cc + 1) * 128], rhs=hT[:, c, tb * 512:(tb + 1) * 512], start=(c == 0), stop=(c == 7)),
                   r=[("wb", 0), hk(c, tb)], w=[psk(bank)])
            op("act", lambda e: e.activation(out=uT[:, cc, tb * 512:(tb + 1) * 512], in_=PS[bank][:], func=AF.Copy), r=[psk(bank)], w=[("uT", cc)])

        kq = 0
        for tb in range(4):
            for cc in range(4):
                uproj(tb, cc, kq)
                kq += 1
        wglu = wload(1, w_glu_d.rearrange("(c p) f -> p c f", p=128))
        wo2 = wload(2, w_out[:, 4:8, :])

        AR.phase()
        VX = AR.at("VX", 0, [2, 32, 256], BF16)
        uT = AR.at("uT", 32 * KB, [4, L], BF16)
        U = AR.at("U", 48 * KB, [32, 256], BF16)
        sm_off = [64 * KB]

        def sm(name, shape=(32,)):
            n = 1
            for v_ in shape:
                n *= v_
            t_ = AR.at(name, sm_off[0], list(shape), F32)
            sm_off[0] += n * 4
            return t_

        bm = sm("bm", (2, 32, 16))
        names = ["lr", "dt", "lrdt", "ang", "mg", "sn", "cs", "r_", "i_", "t0", "t1", "t2", "den", "am1", "cr", "ci", "vr", "vi"]
        sv = {n_: sm(n_) for n_ in names}
        par = sm("par", (3, 32))
        cm = sm("cm", (2, 32, 16))
        bb = sm("bb", (2, 32, 16))
        pw = sm("pw", (2, 32, 8))
        pwi = sm("pwi", (2, 32, 8))
        P1s = sm("P1s", (2, 32))
        P2s = sm("P2s", (2, 32))
        Xst = sm("Xst", (2, 32))
        Xst2 = sm("Xst2", (2, 32))
        S1 = sm("S1", (2, 32))
        T1 = sm("T1", (2, 32))
        T2 = sm("T2", (2, 32))
        dsk = sm("dsk", (4,))
        bgl = sm("bgl", (4,))
        SM_END = sm_off[0]
        assert SM_END <= 85 * KB, SM_END
        AB = [AR.at("AB%d" % i, 85 * KB + i * 2 * KB, [8, 128], BF16) for i in range(2)]
        ABT = [AR.at("ABT%d" % i, 89 * KB + i * 2 * KB, [8, 128], BF16) for i in range(2)]
        tA = AR.at("tA", 93 * KB, [512], F32)
        tB = AR.at("tB", 95 * KB, [512], F32)
        Zt = AR.at("Zt", 97 * KB, [8, 240], BF16)
        s5m = AR.at("s5m", 97 * KB + 3840, [2, 128], F32)
        PP = ["pp"]

        def vop(fn, eng="dve", r=(), w=()):
            op(eng, fn, r=PP + list(r), w=PP + list(w))

        def tt(o_, a_, b_, o, **kw):
            vop(lambda e: e.tensor_tensor(out=o_, in0=a_, in1=b_, op=o), **kw)

        def ts(o_, a_, s1, o1, s2=None, o2=None, **kw):
            if o2 is None:
                vop(lambda e: e.tensor_scalar(out=o_, in0=a_, scalar1=s1, scalar2=None, op0=o1), **kw)
            else:
                vop(lambda e: e.tensor_scalar(out=o_, in0=a_, scalar1=s1, scalar2=s2, op0=o1, op1=o2), **kw)

        def cmul(outr, outi, ar_, ai_, br_, bi_, x1, x2, **kw):
            tt(x1, ar_, br_, ALU.mult, **kw)
            tt(x2, ai_, bi_, ALU.mult, **kw)
            tt(outr, x1, x2, ALU.subtract, **kw)
            tt(x1, ar_, bi_, ALU.mult, **kw)
            tt(x2, ai_, br_, ALU.mult, **kw)
            tt(outi, x1, x2, ALU.add, **kw)

        op("sp", lambda e: e.dma_start(out=par, in_=s5s_d), w=PP, dma=True, dkey="s5s")
        op("sp", lambda e: e.dma_start(out=bm, in_=s5b_d), w=PP, dma=True, dkey="s5b")
        op("sp", lambda e: e.dma_start(out=cm, in_=s5c_d), w=PP, dma=True, dkey="s5c")
        op("sp", lambda e: e.dma_start(out=dsk, in_=s5d_d), w=PP, dma=True, dkey="s5d")
        op("sp", lambda e: e.dma_start(out=bgl, in_=s5bg_d), w=PP, dma=True, dkey="s5bg")
        op("sp", lambda e: e.dma_start(out=s5m, in_=s5m_d), w=["s5m"], dma=True)
        op("pool", lambda e: e.memset(Zt, 0.0), w=["Zt"])
        op("pool", lambda e: e.tensor_copy(out=Zt[:, :, 112:128], in_=ident_b[:].rearrange("p (a b) -> p a b", b=16)), r=["ident_b"], w=["Zt"])
        lam_re, lam_im, lstep = par[:, 0, :], par[:, 1, :], par[:, 2, :]
        v_ = sv
        ts(v_["lr"], lam_re, -1e-4, ALU.min)
        vop(lambda e: e.activation(out=v_["dt"], in_=lstep, func=AF.Exp), eng="act")
        tt(v_["lrdt"], v_["lr"], v_["dt"], ALU.mult)
        tt(v_["ang"], lam_im, v_["dt"], ALU.mult)
        vop(lambda e: e.activation(out=v_["mg"], in_=v_["lrdt"], func=AF.Exp, scale=1.0 / 32), eng="act")
        vop(lambda e: e.activation(out=v_["sn"], in_=v_["ang"], func=AF.Sin, scale=1.0 / 32), eng="act")
        ts(v_["t0"], v_["ang"], 1.0 / 32, ALU.mult, math.pi / 2, ALU.add)
        vop(lambda e: e.activation(out=v_["cs"], in_=v_["t0"], func=AF.Sin), eng="act")
        tt(v_["r_"], v_["mg"], v_["cs"], ALU.mult)
        tt(v_["i_"], v_["mg"], v_["sn"], ALU.mult)
        for _ in range(5):
            tt(v_["t0"], v_["r_"], v_["r_"], ALU.mult)
            tt(v_["t1"], v_["i_"], v_["i_"], ALU.mult)
            tt(v_["t2"], v_["r_"], v_["i_"], ALU.mult)
            tt(v_["r_"], v_["t0"], v_["t1"], ALU.subtract)
            ts(v_["i_"], v_["t2"], 2.0, ALU.mult)
        ar, ai = v_["r_"], v_["i_"]
        tt(v_["t0"], v_["lr"], v_["lr"], ALU.mult)
        tt(v_["t1"], lam_im, lam_im, ALU.mult)
        tt(v_["den"], v_["t0"], v_["t1"], ALU.add)
        vop(lambda e: e.reciprocal(out=v_["den"], in_=v_["den"]))
        ts(v_["am1"], ar, -1.0, ALU.add)
        tt(v_["t0"], v_["am1"], v_["lr"], ALU.mult)
        tt(v_["t1"], ai, lam_im, ALU.mult)
        tt(v_["t0"], v_["t0"], v_["t1"], ALU.add)
        tt(v_["cr"], v_["t0"], v_["den"], ALU.mult)
        tt(v_["t0"], ai, v_["lr"], ALU.mult)
        tt(v_["t1"], v_["am1"], lam_im, ALU.mult)
        tt(v_["t0"], v_["t0"], v_["t1"], ALU.subtract)
        tt(v_["ci"], v_["t0"], v_["den"], ALU.mult)
        tt(v_["t0"], ar, ar, ALU.mult)
        tt(v_["t1"], ai, ai, ALU.mult)
        tt(v_["t0"], v_["t0"], v_["t1"], ALU.add)
        vop(lambda e: e.reciprocal(out=v_["t0"], in_=v_["t0"]))
        tt(v_["vr"], ar, v_["t0"], ALU.mult)
        tt(v_["t1"], ai, v_["t0"], ALU.mult)
        ts(v_["vi"], v_["t1"], -1.0, ALU.mult)
        x1 = tA[:, 0:128].rearrange("p (a b) -> p a b", b=4)
        x2 = tB[:, 0:128].rearrange("p (a b) -> p a b", b=4)
        for (tab, br_, bi_) in ((pw, ar, ai), (pwi, v_["vr"], v_["vi"])):
            tr_, ti_ = tab[:, 0, :, :], tab[:, 1, :, :]
            vop(lambda e, tr_=tr_, br_=br_: e.tensor_copy(out=tr_[:, :, 0:1], in_=br_.unsqueeze(2)))
            vop(lambda e, ti_=ti_, bi_=bi_: e.tensor_copy(out=ti_[:, :, 0:1], in_=bi_.unsqueeze(2)))
            for n_ in (1, 2, 4):
                cmul(tr_[:, :, n_:2 * n_], ti_[:, :, n_:2 * n_], tr_[:, :, 0:n_], ti_[:, :, 0:n_],
                     tr_[:, :, n_ - 1:n_].broadcast_to([128, 32, n_]), ti_[:, :, n_ - 1:n_].broadcast_to([128, 32, n_]), x1[:, :, 0:n_], x2[:, :, 0:n_])
        cmul(bb[:, 0, :, :], bb[:, 1, :, :], v_["cr"].unsqueeze(2).broadcast_to([128, 32, 16]), v_["ci"].unsqueeze(2).broadcast_to([128, 32, 16]),
             bm[:, 0, :, :], bm[:, 1, :, :], tA.rearrange("p (a b) -> p a b", b=16), tB.rearrange("p (a b) -> p a b", b=16))
        vop(lambda e: e.tensor_copy(out=P1s[:, 0, :], in_=pw[:, 0, :, 7]))
        vop(lambda e: e.tensor_copy(out=P1s[:, 1, :], in_=pw[:, 0, :, 7]))
        ts(P2s[:, 0, :], pw[:, 1, :, 7], -1.0, ALU.mult)
        vop(lambda e: e.tensor_copy(out=P2s[:, 1, :], in_=pw[:, 1, :, 7]))

        for tab in (pw, pwi):
            flat = tab[64:128].rearrange("p a b c -> p (a b c)")
            vop(lambda e, flat=flat: e.tensor_copy(out=tA[64:128, :], in_=flat))
            vop(lambda e, tab=tab: e.tensor_copy(out=tab[64:128].rearrange("p a b c -> p (a b) c"), in_=tA[64:128, :].rearrange("p (ab c) -> p ab c", c=8)[:, :, ::-1]))

        def gen_AB(ABt, g0, gl0):
            for (lo, hi) in ((0, 128),):
                np_ = hi - lo
                pr = pwi[lo:hi, 0, g0:g0 + 4, :].unsqueeze(3).broadcast_to([np_, 4, 8, 16])
                pi = pwi[lo:hi, 1, g0:g0 + 4, :].unsqueeze(3).broadcast_to([np_, 4, 8, 16])
                br_ = bb[lo:hi, 0, g0:g0 + 4, :].unsqueeze(2).broadcast_to([np_, 4, 8, 16])
                bi_ = bb[lo:hi, 1, g0:g0 + 4, :].unsqueeze(2).broadcast_to([np_, 4, 8, 16])
                o_r = ABt[0][lo:hi, gl0:gl0 + 4, :].rearrange("p g (s m) -> p g s m", m=16)
                o_i = ABt[1][lo:hi, gl0:gl0 + 4, :].rearrange("p g (s m) -> p g s m", m=16)
                ta = tA[lo:hi, :].rearrange("p (g s m) -> p g s m", s=8, m=16)
                tb_ = tB[lo:hi, :].rearrange("p (g s m) -> p g s m", s=8, m=16)
                cmul(o_r, o_i, pr, pi, br_, bi_, ta, tb_, w=["AB"])

        def gen_CA(g0, gl0, CAf, CAb):
            for (lo, hi, dst) in ((0, 64, CAf), (64, 128, CAb)):
                sl = slice(None)
                pr = pw[lo:hi, 0, g0:g0 + 4, sl].unsqueeze(3).broadcast_to([64, 4, 8, 16])
                pi = pw[lo:hi, 1, g0:g0 + 4, sl].unsqueeze(3).broadcast_to([64, 4, 8, 16])
                c_r = cm[lo:hi, 0, g0:g0 + 4, :].unsqueeze(2).broadcast_to([64, 4, 8, 16])
                c_i = cm[lo:hi, 1, g0:g0 + 4, :].unsqueeze(2).broadcast_to([64, 4, 8, 16])
                o_r = dst[0][lo:hi, gl0:gl0 + 4, :].rearrange("p g (s m) -> p g s m", m=16)
                o_i = dst[1][lo:hi, gl0:gl0 + 4, :].rearrange("p g (s m) -> p g s m", m=16)
                ta = tA[lo:hi, :].rearrange("p (g s m) -> p g s m", s=8, m=16)
                tb_ = tB[lo:hi, :].rearrange("p (g s m) -> p g s m", s=8, m=16)
                kw = dict(w=["CA"])
                tt(ta, c_r, pr, ALU.mult, **kw)
                tt(tb_, c_i, pi, ALU.mult, **kw)
                tt(o_r, ta, tb_, ALU.subtract, **kw)
                tt(ta, c_r, pi, ALU.mult, **kw)
                tt(tb_, c_i, pr, ALU.mult, **kw)
                vop(lambda e, o_i=o_i, ta=ta, tb_=tb_: e.scalar_tensor_tensor(out=o_i, in0=ta, scalar=-1.0, in1=tb_, op0=ALU.mult, op1=ALU.subtract), **kw)

        def shuffle_group(cc, gl):
            g = 8 * cc + gl
            bank = g % 2
            for s_ in range(8):
                op("pe", lambda e, s_=s_: e.matmul(PS[bank][:, 0:256], lhsT=Zt[:, gl, (7 - s_) * 16:(7 - s_) * 16 + 128],
                                                   rhs=uT[:, cc, :].rearrange("p (b s) -> p b s", s=8)[:, :, s_], start=(s_ == 0), stop=(s_ == 7)),
                   r=["Zt", ("uT", cc)], w=[psk(bank)])
            op("act", lambda e: e.activation(out=U[:, g, :], in_=PS[bank][:, 0:256], func=AF.Copy), r=[psk(bank)], w=[("U", g)])

        def abt_chunk():
            for ri in range(2):
                bank = 2 + ri
                pb = PS[bank][:].bitcast(BF16)
                for gl in range(8):
                    op("pe", lambda e, gl=gl, pb=pb, ri=ri: e.transpose(out=pb[:, gl * 128:(gl + 1) * 128], in_=AB[ri][:, gl, :], identity=ident_b[:]), r=["AB", "pp", "ident_b"], w=[psk(bank)])
                op("dve", lambda e, pb=pb, ri=ri: e.tensor_copy(out=ABT[ri], in_=pb.rearrange("p (a b) -> p a b", b=128)), r=[psk(bank)], w=["ABT"])

        def vprime_group(cc, gl):
            g = 8 * cc + gl
            for ri in range(2):
                bank = 4 + ri
                op("pe", lambda e, ri=ri, bank=bank: e.matmul(PS[bank][:, 0:256], lhsT=ABT[ri][:, gl, :], rhs=U[:, g, :], start=True, stop=True), r=["ABT", ("U", g)], w=[psk(bank)])
                op("act", lambda e, ri=ri, bank=bank: e.activation(out=VX[:, ri, g, :], in_=PS[bank][:, 0:256], func=AF.Copy), r=[psk(bank)], w=[("VX", g)])

        for cc in range(4):
            for gl in range(8):
                shuffle_group(cc, gl)
            gen_AB(AB, 8 * cc, 0)
            gen_AB(AB, 8 * cc + 4, 4)
            abt_chunk()
            for gl in range(8):
                vprime_group(cc, gl)
        dbg_put("U0", U[:, 0, 0:64], [("U", 0)])
        dbg_put("VX0", VX[:, :, 0, 0:64], [("VX", 0)])
        dbg_put("pw", pw[:, :, 0, :], PP)
        dbg_put("pwi", pwi[:, :, 0, :], PP)
        dbg_put("bb", bb[:, :, 0, :], PP)

        VXK = [("VX", g) for g in range(32)]
        op("dve", lambda e: e.memset(Xst, 0.0), w=["sc0x0", "sc64x0"])
        scnt = {0: 0, 64: 0}

        def scan_step(lo, hi, b, eng):
            key = "sc%d" % lo
            n_ = scnt[lo]
            scnt[lo] += 1
            XB = (Xst, Xst2)
            xs_old, xs = XB[n_ % 2][lo:hi], XB[(n_ + 1) % 2][lo:hi]
            kx_old, kx = key + "x%d" % (n_ % 2), key + "x%d" % ((n_ + 1) % 2)
            s1, t1_, t2_ = S1[lo:hi], T1[lo:hi], T2[lo:hi]
            vx = VX[lo:hi, :, :, b]
            ch = ("scan", lo) if (eng == "dve" and cfg.get("scan_chain", True)) else None
            op(eng, lambda e: e.tensor_tensor(out=s1, in0=xs_old, in1=vx, op=ALU.add), r=VXK + [kx_old], w=[key + "s"], chain=ch)
            op(eng, lambda e: e.tensor_tensor(out=t1_, in0=P1s[lo:hi], in1=s1, op=ALU.mult), r=[key + "s", "pp"], w=[key + "a"], chain=ch)
            if eng == "dve":
                s1sw = bass.AP(s1.tensor, s1.offset + 32, [list(s1.ap[0]), [-32, 2], [1, 32]])
                op(eng, lambda e: e.tensor_tensor(out=t2_, in0=P2s[lo:hi], in1=s1sw, op=ALU.mult), r=[key + "s", "pp"], w=[key + "b"], chain=ch)
            else:
                op(eng, lambda e: e.tensor_tensor(out=t2_[:, 0, :], in0=P2s[lo:hi, 0, :], in1=s1[:, 1, :], op=ALU.mult), r=[key + "s", "pp"], w=[key + "b"])
                op(eng, lambda e: e.tensor_tensor(out=t2_[:, 1, :], in0=P2s[lo:hi, 1, :], in1=s1[:, 0, :], op=ALU.mult), r=[key + "s", "pp", key + "b"], w=[key + "b"])
            op(eng, lambda e: e.tensor_tensor(out=xs, in0=t1_, in1=t2_, op=ALU.add), r=[key + "a", key + "b"], w=[kx], chain=ch)
            op("act", lambda e: e.activation(out=vx, in_=xs, func=AF.Copy), r=[kx], w=[key + "c"])

        for b in range(256):
            scan_step(0, 64, b, "dve")
            scan_step(64, 128, 255 - b, "dve")
        dbg_put("X0", VX[:, :, 0, 0:64], ["sc0c", "sc64c"])

        AR.phase()
        AR.at("keep", 0, [85 * KB // 2], BF16)
        AB2 = [AR.at("AB2_%d" % i, 85 * KB + i * 2 * KB, [8, 128], BF16) for i in range(2)]
        CAf = [AR.at("CAf%d" % i, 89 * KB + i * 2 * KB, [8, 128], BF16) for i in range(2)]
        AR.at("keep2", 93 * KB, [(104 - 93) * KB // 2], BF16)
        CAb = [bm.rearrange("p a b c -> p (a b c)")[:, i * 512:(i + 1) * 512].bitcast(BF16).rearrange("p (a b) -> p a b", b=128) for i in range(2)]
        W0 = sv["lr"].tensor and AR.base[:, (64 * KB + 4096) // 2:(64 * KB + 4096 + 2048) // 2].rearrange("p (a b) -> p a b", b=128)
        for i in range(2):
            op("pool", lambda e, i=i: e.memset(CAf[i][64:128], 0.0), w=["CA"])
            op("pool", lambda e, i=i: e.memset(CAb[i][0:64], 0.0), w=["CA"])
        mF_ = s5m[:, 0, :]
        mB_ = s5m[:, 1, :]

        def w0_chunk():
            for half in range(2):
                pf, pb_ = PS[2], PS[3]
                for jj in range(4):
                    gl = half * 4 + jj
                    o1 = pf[:, jj * 128:(jj + 1) * 128]
                    o2 = pb_[:, jj * 128:(jj + 1) * 128]
                    op("pe", lambda e, gl=gl, o1=o1: e.matmul(o1, lhsT=AB2[0][:, gl, :], rhs=CAf[0][:, gl, :], start=True, stop=False), r=["AB", "CA", "pp"], w=[psk(2)])
                    op("pe", lambda e, gl=gl, o1=o1: e.matmul(o1, lhsT=AB2[1][:, gl, :], rhs=CAf[1][:, gl, :], start=False, stop=True), r=["AB", "CA", "pp"], w=[psk(2)])
                    op("pe", lambda e, gl=gl, o2=o2: e.matmul(o2, lhsT=AB2[0][:, gl, :], rhs=CAb[0][:, gl, :], start=True, stop=False), r=["AB", "CA", "pp"], w=[psk(3)])
                    op("pe", lambda e, gl=gl, o2=o2: e.matmul(o2, lhsT=AB2[1][:, gl, :], rhs=CAb[1][:, gl, :], start=False, stop=True), r=["AB", "CA", "pp"], w=[psk(3)])
                t3 = tA.rearrange("p (a b) -> p a b", b=128)
                op("dve", lambda e, t3=t3: e.tensor_tensor(out=t3, in0=PS[2][:].rearrange("p (a b) -> p a b", b=128), in1=mF_.unsqueeze(1).broadcast_to([128, 4, 128]), op=ALU.mult),
                   r=[psk(2), "s5m", "pp"], w=["pp"])
                op("dve", lambda e: e.tensor_tensor(out=tB.rearrange("p (a b) -> p a b", b=128), in0=PS[3][:].rearrange("p (a b) -> p a b", b=128), in1=mB_.unsqueeze(1).broadcast_to([128, 4, 128]), op=ALU.mult),
                   r=[psk(3), "s5m", "pp"], w=["pp"])
                op("dve", lambda e, half=half: e.tensor_tensor(out=W0[:, half * 4:(half + 1) * 4, :], in0=tA.rearrange("p (a b) -> p a b", b=128), in1=tB.rearrange("p (a b) -> p a b", b=128), op=ALU.add),
                   r=["pp"], w=["W0", "pp"])

        SCK = ["sc0c", "sc64c"]

        def y_group(cc, gl):
            g = 8 * cc + gl
            bank = 4 + g % 2
            o_ = PS[bank]
            rk = ["W0", "CA", ("U", g), ("VX", g)] + SCK
            op("pe", lambda e: e.matmul(o_[:, 0:256], lhsT=W0[:, gl, :], rhs=U[:, g, :], start=True, stop=False), r=rk, w=[psk(bank)])
            op("pe", lambda e: e.matmul(o_[:, 1:256], lhsT=CAf[0][:, gl, :], rhs=VX[:, 0, g, 0:255], start=False, stop=False), r=rk, w=[psk(bank)])
            op("pe", lambda e: e.matmul(o_[:, 1:256], lhsT=CAf[1][:, gl, :], rhs=VX[:, 1, g, 0:255], start=False, stop=False), r=rk, w=[psk(bank)])
            op("pe", lambda e: e.matmul(o_[:, 0:255], lhsT=CAb[0][:, gl, :], rhs=VX[:, 0, g, 1:256], start=False, stop=False), r=rk, w=[psk(bank)])
            op("pe", lambda e: e.matmul(o_[:, 0:255], lhsT=CAb[1][:, gl, :], rhs=VX[:, 1, g, 1:256], start=False, stop=True), r=rk, w=[psk(bank)])
            op("act", lambda e: e.activation(out=U[:, g, :], in_=o_[:, 0:256], func=AF.Copy), r=[psk(bank)], w=[("U", g)])

        def unshuffle(cc, tl):
            bank = tl % 2
            for gl in range(8):
                g = 8 * cc + gl
                op("pe", lambda e, gl=gl, g=g: e.matmul(PS[bank][:, 0:256], lhsT=Zt[:, tl, (7 - gl) * 16:(7 - gl) * 16 + 128], rhs=U[:, g, :], start=(gl == 0), stop=(gl == 7)),
                   r=["Zt", ("U", g)], w=[psk(bank)])
            uv = uT[:, cc, :].rearrange("p (b s) -> p b s", s=8)[:, :, tl]
            op("dve", lambda e: e.scalar_tensor_tensor(out=uv, in0=uv, scalar=dsk[:, cc:cc + 1], in1=PS[bank][:, 0:256], op0=ALU.mult, op1=ALU.add),
               r=[psk(bank), ("uT", cc), "pp"], w=[("uT", cc)])

        for cc in range(4):
            for hh in range(2):
                gen_AB(AB2, 8 * cc + 4 * hh, 4 * hh)
                gen_CA(8 * cc + 4 * hh, 4 * hh, CAf, CAb)
            w0_chunk()
            for gl in range(8):
                y_group(cc, gl)
            for tl in range(8):
                unshuffle(cc, tl)
        dbg_put("y", uT[:, :, 0:512], [("uT", cc) for cc in range(4)])

        AR.phase()
        AR.at("uT", 32 * KB, [4, L], BF16)
        bT = AR.at("bT", 0, [4, L], BF16)
        g1 = AR.at("g1", 16 * KB, [L], F32)
        g2 = AR.at("g2", 24 * KB, [L], F32)
        CG = math.sqrt(2.0 / math.pi)

        def gelu_chunk(cc):
            y_ = uT[:, cc, :]
            op("act", lambda e: e.activation(out=g1, in_=y_, func=AF.Square), r=[("uT", cc)], w=["g1"])
            op("dve", lambda e: e.tensor_scalar(out=g1, in0=g1, scalar1=0.044715, scalar2=1.0, op0=ALU.mult, op1=ALU.add), r=["g1"], w=["g1"])
            op("dve", lambda e: e.tensor_tensor(out=g1, in0=g1, in1=y_, op=ALU.mult), r=["g1", ("uT", cc)], w=["g1"])
            op("act", lambda e: e.activation(out=g2, in_=g1, func=AF.Sigmoid, scale=2.0 * CG), r=["g1"], w=["g2"])
            op("dve", lambda e: e.tensor_tensor(out=y_, in0=y_, in1=g2, op=ALU.mult), r=["g2", ("uT", cc)], w=[("uT", cc)])

        for cc in range(4):
            gelu_chunk(cc)

        def glu(tb, oc, kq):
            bank = kq % 2
            for kc in range(4):
                op("pe", lambda e, kc=kc: e.matmul(PS[bank][:], lhsT=wglu[:, kc, oc * 128:(oc + 1) * 128], rhs=uT[:, kc, tb * 512:(tb + 1) * 512], start=(kc == 0), stop=(kc == 3)),
                   r=[("wb", 1), ("uT", kc)], w=[psk(bank)])
            gsl = g1[:, (kq % 4) * 512:(kq % 4 + 1) * 512]
            op("act", lambda e: e.activation(out=gsl, in_=PS[bank][:], func=AF.Sigmoid, bias=bgl[:, oc:oc + 1]), r=[psk(bank), "pp"], w=[("gs", kq % 4)])
            op("dve", lambda e: e.tensor_tensor(out=bT[:, oc, tb * 512:(tb + 1) * 512], in0=uT[:, oc, tb * 512:(tb + 1) * 512], in1=gsl, op=ALU.mult),
               r=[("gs", kq % 4), ("uT", oc)], w=[("bT", oc, tb)])

        P.barrier()
        kq = 0
        for tb in range(4):
            for oc in range(4):
                glu(tb, oc, kq)
                kq += 1
        dbg_put("bT", bT[:, :, 0:512], [("bT", oc, 0) for oc in range(4)])

        def oproj2(tb, dc, kq):
            bank = 2 + kq % 2
            for hc in range(4):
                op("pe", lambda e, hc=hc: e.matmul(PS[bank][:], lhsT=wo2[:, hc, dc * 128:(dc + 1) * 128], rhs=bT[:, hc, tb * 512:(tb + 1) * 512], start=(hc == 0), stop=(hc == 3)),
                   r=[("wb", 2), ("bT", hc, tb)], w=[psk(bank)])
            xs = xT[:, dc, tb * 512:(tb + 1) * 512]
            op("dve", lambda e: e.tensor_tensor(out=xs, in0=xs, in1=PS[bank][:], op=ALU.add), r=[psk(bank), xk(dc, tb)], w=[xk(dc, tb)])

        if not dbg:
            kq = 0
            for tb in range(4):
                for dc in range(8):
                    oproj2(tb, dc, kq)
                    kq += 1

    def hgrn_phase(gi):
        w_in = w_in_odd_d.rearrange("(c p) f -> p c f", p=128)
        w_out = w_out_odd_d.rearrange("(h p) d -> p h d", p=128)
        AR.phase()
        hT = AR.at("hT", 0, [8, L], BF16)
        sq = [AR.at("sq%d" % i, 44 * KB + i * 4 * KB, [L], BF16) for i in range(2)]
        rstd = AR.at("rstd", 52 * KB, [L], F32)
        def load_head(h):
            sa, sb_ = (0, 1) if h % 2 == 0 else (2, 3)
            secs = []
            for i, sec in enumerate((0, 1, 2, 3)):
                dst = WB[sa][:, i * 1024:(i + 1) * 1024].rearrange("p (a b) -> p a b", b=128)
                op("pool", lambda e, dst=dst, sec=sec, h=h: e.dma_start(out=dst, in_=w_in[:, :, sec * 1024 + h * 128: sec * 1024 + (h + 1) * 128]),
                   w=[("wb", sa)], dma=True, dkey=("wbs", sa, i))
                secs.append(dst)
            dst = WB[sb_][:, 0:1024].rearrange("p (a b) -> p a b", b=128)
            op("pool", lambda e, dst=dst, h=h: e.dma_start(out=dst, in_=w_in[:, :, 4 * 1024 + h * 128: 4 * 1024 + (h + 1) * 128]),
               w=[("wb", sb_)], dma=True, dkey=("wbs", sb_, 0))
            secs.append(dst)
            wo = WB[sb_][:, 1024:2048]
            op("pool", lambda e, wo=wo, h=h: e.dma_start(out=wo, in_=w_out[:, h, :]), w=[("wb", sb_)], dma=True, dkey=("wbs", sb_, 1))
            return secs, wo, sa, sb_

        first_head = load_head(0)
        norm_to_hT(gi, hT, rstd, sq)
        AR.phase()
        hT = AR.at("hT", 0, [8, L], BF16)
        qT = AR.at("qT", 32 * KB, [L], BF16)
        sigG = AR.at("sigG", 36 * KB, [L], BF16)
        vtok = AR.at("vtok", 40 * KB, [16, 128], BF16)
        HS = []
        for i_ in range(2):
            b0 = 44 * KB + i_ * 22 * KB
            HS.append(dict(i=i_,
                           A=AR.at("A%d" % i_, b0, [1024], F32), B=AR.at("B%d" % i_, b0 + 4 * KB, [1024], F32),
                           kk=AR.at("kk%d" % i_, b0 + 8 * KB, [1024], BF16), qdec=AR.at("qdec%d" % i_, b0 + 10 * KB, [1024], BF16),
                           kinv=AR.at("kinv%d" % i_, b0 + 12 * KB, [1024], BF16), kend=AR.at("kend%d" % i_, b0 + 14 * KB, [8, 128], BF16),
                           scT=AR.at("scT%d" % i_, b0 + 16 * KB, [8, 128], BF16), Sb=AR.at("Sb%d" % i_, b0 + 18 * KB, [16, 128], BF16),
                           pbank=(0, 1) if i_ == 0 else (2, 3), xbank=4 + i_))
        oacc = AR.at("oacc", 88 * KB, [16, 128], F32)
        mF = AR.at("mF", 96 * KB, [L], BF16)
        decs = [AR.at("dec%d" % i, 100 * KB + i * 64, [16], F32) for i in range(2)]
        ssq = AR.at("ssq", 100 * KB + 128, [16], F32)
        SstD = [[AR.at("Sst%d_%d" % (d_, i), 100 * KB + 256 + (2 * d_ + i) * 512, [128], F32) for i in range(2)] for d_ in range(2)]
        on_tok = AR.base[:, 44 * KB // 2: 48 * KB // 2].rearrange("p (a b) -> p a b", b=128)
        mT = AR.base[:, 48 * KB // 2: 52 * KB // 2]

        op("sp", lambda e: e.dma_start(out=lbl[:], in_=lbl_d), w=["lbl"], dma=True)
        op("sp", lambda e: e.dma_start(out=hg[:], in_=hg_d), w=["hg"], dma=True)
        op("sp", lambda e: e.dma_start(out=hmask[:], in_=hmask_d), w=["hmask"], dma=True)
        op("dve", lambda e: e.tensor_tensor(out=lb[:], in0=lbl[:, 1, :], in1=lbl[:, 0, :], op=ALU.subtract), r=["lbl"], w=["lb"])
        op("act", lambda e: e.activation(out=lb[:], in_=lb[:], func=AF.Sigmoid), r=["lb"], w=["lb"])
        op("dve", lambda e: e.tensor_scalar(out=oml[:], in0=lb[:], scalar1=-1.0, scalar2=1.0, op0=ALU.mult, op1=ALU.add), r=["lb"], w=["oml"])
        op("dve", lambda e: e.tensor_scalar(out=noml[:], in0=oml[:], scalar1=-1.0, scalar2=None, op0=ALU.mult), r=["oml"], w=["noml"])
        op("pool", lambda e: e.memset(mF, 1.0), w=["mF"])
        op("pool", lambda e: e.memset(mF.rearrange("p (a b) -> p a b", b=64)[:, :, 0:1], 0.0), w=["mF"])

        def proj_fm(wsec, slot, consume):
            for tb in range(4):
                bank = tb
                for c in range(8):
                    op("pe", lambda e, bank=bank, c=c, tb=tb: e.matmul(PS[bank][:], lhsT=wsec[:, c, :], rhs=hT[:, c, tb * 512:(tb + 1) * 512], start=(c == 0), stop=(c == 7)),
                       r=[("wb", slot), hk(c, tb)], w=[psk(bank)])
                consume(tb, bank)

        pend_oproj = []

        def front_gen(h, cur):
            (wq, wi_, wff, wfb, wg), wo, sa, sb_ = cur
            for (wsec, slot, dst, func, key) in ((wq, sa, qT, AF.Copy, "qT"), (wg, sb_, sigG, AF.Sigmoid, "sigG")):
                for tb in range(4):
                    bank = tb % 2
                    for c in range(8):
                        op("pe", lambda e, bank=bank, c=c, tb=tb, wsec=wsec: e.matmul(PS[bank][:], lhsT=wsec[:, c, :], rhs=hT[:, c, tb * 512:(tb + 1) * 512], start=(c == 0), stop=(c == 7)),
                           r=[("wb", slot), hk(c, tb)], w=[psk(bank)])
                    op("act", lambda e, bank=bank, tb=tb, dst=dst, func=func: e.activation(out=dst[:, tb * 512:(tb + 1) * 512], in_=PS[bank][:], func=func), r=[psk(bank)], w=[key])
                    yield
            for q4 in range(4):
                bank = 4 + q4 % 2
                for jj in range(4):
                    j = q4 * 4 + jj
                    for c in range(8):
                        op("pe", lambda e, bank=bank, jj=jj, j=j, c=c: e.matmul(PS[bank][:, jj * 128:(jj + 1) * 128], lhsT=hT[:, c, j * 128:(j + 1) * 128], rhs=wi_[:, c, :], start=(c == 0), stop=(c == 7)),
                           r=[("wb", sa), hk(c, j // 4)], w=[psk(bank)])
                op("dve", lambda e, bank=bank, q4=q4: e.tensor_copy(out=vtok[:, q4 * 4:(q4 + 1) * 4, :], in_=PS[bank][:].rearrange("p (a b) -> p a b", b=128)), r=[psk(bank)], w=["vtok"])
                yield

        def do_head(h, cur):
            (wq, wi_, wff, wfb, wg), wo, sa, sb_ = cur
            mTh = WB[sb_][:, 2048:4096]
            if h == 0:
                dbg_put("qT", qT[:, 0:256], ["qT"])
                dbg_put("sigG", sigG[:, 0:256], ["sigG"])
                dbg_put("vtok", vtok[:, 0:2, :], ["vtok"])
            sidx = [0, 0]

            def do_dir(d, hf, S):
                si = S["i"]
                A, B, kk, qdec, kinv, kend, scT, Sb = S["A"], S["B"], S["kk"], S["qdec"], S["kinv"], S["kend"], S["scT"], S["Sb"]
                dec = decs[si]
                kA, kB, kK, kQ, kI, kE, kS, kD = ["%s%d" % (n_, si) for n_ in ("hgA", "hgB", "kk", "qdec", "kinv", "kend", "scT", "dec")]
                wf = wff if d == 0 else wfb
                T0 = hf * 1024
                for t2 in range(2):
                    tb = 2 * hf + t2
                    bank = S["pbank"][t2]
                    for c in range(8):
                        op("pe", lambda e, c=c, tb=tb, bank=bank: e.matmul(PS[bank][:], lhsT=wf[:, c, :], rhs=hT[:, c, tb * 512:(tb + 1) * 512], start=(c == 0), stop=(c == 7)),
                           r=[("wb", sa), hk(c, tb)], w=[psk(bank)])
                    op("act", lambda e, t2=t2, bank=bank: e.activation(out=A[:, t2 * 512:(t2 + 1) * 512], in_=PS[bank][:], func=AF.Sigmoid), r=[psk(bank)], w=[kA])
                    yield
                op("act", lambda e: e.activation(out=kk, in_=A, func=AF.Identity, scale=noml[:, h:h + 1], bias=oml[:, h:h + 1]), r=[kA, "noml", "oml"], w=[kK])
                op("act", lambda e: e.activation(out=A, in_=A, func=AF.Ln, scale=oml[:, h:h + 1], bias=lb[:, h:h + 1]), r=[kA, "lb", "oml"], w=[kA])
                yield
                if d == 0:
                    op("dve", lambda e: e.tensor_tensor_scan(out=B, data0=mF[:, 0:1024], data1=A, initial=0.0, op0=ALU.mult, op1=ALU.add), r=[kA, "mF"], w=[kB])
                else:
                    op("dve", lambda e: e.tensor_tensor_scan(out=B[:, ::-1], data0=mF[:, 0:1024], data1=A[:, ::-1], initial=0.0, op0=ALU.mult, op1=ALU.add), r=[kA, "mF"], w=[kB])
                yield
                op("act", lambda e: e.activation(out=A, in_=B, func=AF.Exp), r=[kB], w=[kA])
                yield
                op("dve", lambda e: e.tensor_tensor(out=qdec, in0=qT[:, T0:T0 + 1024], in1=A, op=ALU.mult), r=[kA, "qT"], w=[kQ])
                yield
                op("act", lambda e: e.activation(out=A, in_=B, func=AF.Exp, scale=-1.0), r=[kB, kQ], w=[kA])
                bl = B.rearrange("p (a b) -> p a b", b=64)[:, :, 63:64] if d == 0 else B.rearrange("p (a b) -> p a b", b=64)[:, :, 0:1]
                op("act", lambda e: e.activation(out=dec.rearrange("p (a b) -> p a b", b=1), in_=bl, func=AF.Exp), r=[kB], w=[kD])
                yield
                op("dve", lambda e: e.tensor_tensor(out=A, in0=kk, in1=A, op=ALU.mult), r=[kA, kK], w=[kA])
                yield
                op("act", lambda e: e.activation(out=kinv, in_=A, func=AF.Copy), r=[kA], w=[kI])
                dec_bc = bass.AP(dec.tensor, dec.offset, [list(dec.ap[0]), [1, 16], [0, 64]])
                op("dve", lambda e: e.tensor_tensor(out=kk.rearrange("p (a b) -> p a b", b=64), in0=A.rearrange("p (a b) -> p a b", b=64), in1=dec_bc, op=ALU.mult),
                   r=[kA, kD], w=[kK])
                yield
                xb = S["xbank"]
                pb = PS[xb][:].bitcast(BF16)
                for jj in range(8):
                    op("pe", lambda e, jj=jj: e.transpose(out=pb[:, jj * 128:(jj + 1) * 128], in_=kk[:, jj * 128:(jj + 1) * 128], identity=ident_b[:]),
                       r=[kK, "ident_b"], w=[psk(xb)])
                op("act", lambda e: e.activation(out=kend, in_=pb.rearrange("p (a b) -> p a b", b=128), func=AF.Copy), r=[psk(xb)], w=[kE])
                yield
                hm_ = hmask[:, d, :]
                mk = bass.AP(hm_.tensor, hm_.offset, [list(hm_.ap[0]), [0, 4], [1, 128]])
                for q4 in range(2):
                    bank = S["pbank"][q4]
                    for jj in range(4):
                        jl = q4 * 4 + jj
                        op("pe", lambda e, bank=bank, jj=jj, jl=jl: e.matmul(PS[bank][:, jj * 128:(jj + 1) * 128], lhsT=kinv[:, jl * 128:(jl + 1) * 128], rhs=qdec[:, jl * 128:(jl + 1) * 128], start=True, stop=True),
                           r=[kI, kQ], w=[psk(bank)])
                    op("dve", lambda e, bank=bank, q4=q4: e.tensor_tensor(out=scT[:, q4 * 4:(q4 + 1) * 4, :], in0=PS[bank][:].rearrange("p (a b) -> p a b", b=128), in1=mk, op=ALU.mult),
                       r=[psk(bank), "hmask"], w=[kS])
                    yield
                order = list(range(16)) if d == 0 else list(range(15, -1, -1))
                for cl in order:
                    n = sidx[d]
                    sidx[d] += 1
                    cur, new = SstD[d][n % 2], SstD[d][(n + 1) % 2]
                    ci = hf * 16 + cl
                    par = ci % 2
                    bank = 6 + par
                    op("act", lambda e, cur=cur, cl=cl: e.activation(out=Sb[:, cl, :], in_=cur, func=AF.Copy), r=[("Sst", d, n % 2)], w=[("Sb", si, cl)])
                    op("pe", lambda e, bank=bank, cl=cl, par=par: e.matmul(PS[bank][:, 0:128], lhsT=kend[par * 64:(par + 1) * 64, cl // 2, :], rhs=vtok[par * 64:(par + 1) * 64, hf * 8 + cl // 2, :], start=True, stop=True),
                       r=[kE, "vtok"], w=[psk(bank)])
                    op("dve", lambda e, bank=bank, cur=cur, new=new, cl=cl: e.scalar_tensor_tensor(out=new, in0=cur, scalar=dec[:, cl:cl + 1], in1=PS[bank][:, 0:128], op0=ALU.mult, op1=ALU.add),
                       r=[psk(bank), ("Sst", d, n % 2), kD], w=[("Sst", d, (n + 1) % 2)])
                    if cl % 2 == 1:
                        yield
                first = (d == 0 and hf == 0) or (d == 1 and hf == 1)
                for q4 in range(2):
                    for jj in range(4):
                        jl = q4 * 4 + jj
                        j = hf * 8 + jl
                        o_ = PS[xb][:, jj * 128:(jj + 1) * 128]
                        op("pe", lambda e, o_=o_, jl=jl, j=j: e.matmul(o_, lhsT=scT[:, jl, :], rhs=vtok[:, j, :], start=True, stop=False), r=[kS, "vtok"], w=[psk(xb)])
                        op("pe", lambda e, jj=jj, jl=jl: e.matmul(PS[xb][0:64, jj * 128:(jj + 1) * 128], lhsT=qdec[:, jl * 128:jl * 128 + 64], rhs=Sb[:, 2 * jl, :], start=False, stop=True),
                           r=[kQ, ("Sb", si, 2 * jl)], w=[psk(xb)])
                        op("pe", lambda e, jj=jj, jl=jl: e.matmul(PS[xb][64:128, jj * 128:(jj + 1) * 128], lhsT=qdec[:, jl * 128 + 64:jl * 128 + 128], rhs=Sb[:, 2 * jl + 1, :], start=False, stop=True),
                           r=[kQ, ("Sb", si, 2 * jl + 1)], w=[psk(xb)])
                    ov = oacc[:, hf * 8 + q4 * 4:hf * 8 + (q4 + 1) * 4, :]
                    pv = PS[xb][:].rearrange("p (a b) -> p a b", b=128)
                    if first:
                        op("dve", lambda e, ov=ov, pv=pv: e.tensor_copy(out=ov, in_=pv), r=[psk(xb)], w=[("oacc", hf)])
                    else:
                        op("dve", lambda e, ov=ov, pv=pv: e.tensor_tensor(out=ov, in0=ov, in1=pv, op=ALU.add), r=[psk(xb), ("oacc", hf)], w=[("oacc", hf)])
                    yield

            def interleave(gens):
                alive = list(gens)
                while alive:
                    for g_ in list(alive):
                        try:
                            next(g_)
                        except StopIteration:
                            alive.remove(g_)

            op("pool", lambda e: e.memset(SstD[0][0], 0.0), w=[("Sst", 0, 0)])
            op("pool", lambda e: e.memset(SstD[1][0], 0.0), w=[("Sst", 1, 0)])
            prev = list(pend_oproj)
            del pend_oproj[:]
            interleave([do_dir(0, 0, HS[0]), do_dir(1, 1, HS[1])] + prev)
            interleave([do_dir(0, 1, HS[0]), do_dir(1, 0, HS[1])])
            OACC = [("oacc", 0), ("oacc", 1)]
            HGA = ["hgA0"]
            HGB = ["hgB0"]
            for j in range(16):
                op("act", lambda e, j=j: e.activation(out=on_tok[:, j, :], in_=oacc[:, j, :], func=AF.Square, accum_out=ssq[:, j:j + 1]), r=OACC + HGB, w=HGA + ["ssq"])
            op("act", lambda e: e.activation(out=ssq, in_=ssq, func=AF.Sqrt, scale=1.0 / 128, bias=EPS), r=["ssq"], w=["ssq"])
            op("dve", lambda e: e.reciprocal(out=ssq, in_=ssq), r=["ssq"], w=["ssq"])
            for j in range(16):
                op("dve", lambda e, j=j: e.tensor_scalar(out=on_tok[:, j, :], in0=oacc[:, j, :], scalar1=ssq[:, j:j + 1], scalar2=None, op0=ALU.mult), r=OACC + ["ssq"] + HGA, w=HGA)
            for half in range(2):
                bank = half
                pb = PS[bank][:].bitcast(BF16)
                for jj in range(8):
                    j = half * 8 + jj
                    op("pe", lambda e, pb=pb, jj=jj, j=j: e.transpose(out=pb[:, jj * 128:(jj + 1) * 128], in_=on_tok[:, j, :], identity=ident_b[:]), r=HGA + ["ident_b"], w=[psk(bank)])
                op("dve", lambda e, pb=pb, half=half: e.scalar_tensor_tensor(out=mTh[:, half * 1024:(half + 1) * 1024], in0=pb, scalar=hg[:, h:h + 1], in1=sigG[:, half * 1024:(half + 1) * 1024], op0=ALU.mult, op1=ALU.mult),
                   r=[psk(bank), "hg", "sigG"] + HGA, w=[("mT", sb_)])
            if h == 0:
                dbg_put("mT", mTh[:, 0:256], [("mT", sb_)])
            if dbg:
                return None
            def oproj_gen():
                for tb in range(4):
                    for dc in range(8):
                        bank = 2 + (tb * 8 + dc) % 2
                        op("pe", lambda e, dc=dc, tb=tb, bank=bank: e.matmul(PS[bank][:], lhsT=wo[:, dc * 128:(dc + 1) * 128], rhs=mTh[:, tb * 512:(tb + 1) * 512], start=True, stop=True),
                           r=[("wb", sb_), ("mT", sb_)], w=[psk(bank)])
                        xs = xT[:, dc, tb * 512:(tb + 1) * 512]
                        op("dve", lambda e, xs=xs, bank=bank: e.tensor_tensor(out=xs, in0=xs, in1=PS[bank][:], op=ALU.add), r=[psk(bank), xk(dc, tb)], w=[xk(dc, tb)])
                        if (tb * 8 + dc) % 3 == 2:
                            yield
            return oproj_gen()

        def run_all(gens):
            alive = list(gens)
            while alive:
                for g_ in list(alive):
                    try:
                        next(g_)
                    except StopIteration:
                        alive.remove(g_)

        nxt_holder = [first_head]
        run_all([front_gen(0, nxt_holder[0])])
        for h in range(8):
            cur = nxt_holder[0]
            if h + 1 < 8:
                nxt_holder[0] = load_head(h + 1)
            og = do_head(h, cur)
            if dbg:
                break
            run_all(([front_gen(h + 1, nxt_holder[0])] if h + 1 < 8 else []) + [og])
        for g_ in pend_oproj:
            for _ in g_:
                pass

    if "l0attn" in parts or "l0mix" in parts:
        attn_phase(0)
    if "l0s5" in parts or "l0mix" in parts:
        s5_phase()
    if "l0ffn" in parts:
        ffn_phase(0, 1)
    if "l1mix" in parts:
        hgrn_phase(2)
    if "l1ffn" in parts:
        ffn_phase(1, 3)

    if dbg:
        P.barrier()
        op("sp", lambda e: e.dma_start(out=out_d.rearrange("(p a) d -> p (a d)", p=128), in_=xT_flat), dma=True, dkey="dbgout")
        P.emit()
        return nc
    AR.phase()
    sq = [AR.at("sq%d" % i, i * 4 * KB, [L], BF16) for i in range(2)]
    rstd = AR.at("rstd", 8 * KB, [L], F32)
    stage = [AR.at("stage%d" % i, 16 * KB + i * 4 * KB, [D], F32) for i in range(2)]
    ftmp = [AR.at("ftmp%d" % i, 24 * KB + i * 512, [128], F32) for i in range(4)]
    do_final = "final" in parts
    if do_final:
        rmsnorm_rstd(rstd, sq)
    k = 0
    for t in range(16):
        st = stage[t % 2]
        for half in range(2):
            bank = (2 * t + half) % 4
            for j in range(4):
                c = half * 4 + j
                src = xT[:, c, t * 128:(t + 1) * 128]
                if do_final:
                    ft = ftmp[k % 4]
                    op("dve", lambda e, ft=ft, src=src, c=c, t=t: e.scalar_tensor_tensor(
                        out=ft, in0=src, scalar=gains[:, 4, c:c + 1], in1=rstd[:, t * 128:(t + 1) * 128], op0=ALU.mult, op1=ALU.mult),
                       r=[xk(c, t // 4), ("rstd", t // 4), "gains"], w=[("ftmp", k % 4)])
                    op("pe", lambda e, bank=bank, j=j, ft=ft: e.transpose(out=PS[bank][:, j * 128:(j + 1) * 128], in_=ft, identity=ident_f[:]),
                       r=[("ftmp", k % 4), "ident_f"], w=[psk(bank)])
                    k += 1
                else:
                    op("pe", lambda e, bank=bank, j=j, src=src: e.transpose(out=PS[bank][:, j * 128:(j + 1) * 128], in_=src, identity=ident_f[:]),
                       r=[xk(c, t // 4), "ident_f"], w=[psk(bank)])
            dst = st[:, half * 512:(half + 1) * 512]
            op("act", lambda e, dst=dst, bank=bank: e.activation(out=dst, in_=PS[bank][:], func=AF.Copy), r=[psk(bank)], w=[("stage", t % 2, half)])
        op("sp", lambda e, st=st, t=t: e.dma_start(out=out_d[t * 128:(t + 1) * 128, :], in_=st), r=[("stage", t % 2, 0), ("stage", t % 2, 1)], dma=True, dkey=("st", t % 2))
    P.emit()
    return nc


def host_consts():
    c = {}
    c["ident"] = np.eye(128, dtype=np.float32)
    inv = (10000.0 ** (-np.arange(0, 64, 2, dtype=np.float32) / np.float32(64))).astype(np.float32)
    ang = (np.arange(L, dtype=np.float32)[None, :] * inv[:, None]).astype(np.float32)
    c["rope_cos"] = np.ascontiguousarray(np.tile(np.cos(ang).astype(np.float32), (4, 1)))
    c["rope_sin"] = np.ascontiguousarray(np.tile(np.sin(ang).astype(np.float32), (4, 1)))
    s_ = np.arange(128)[:, None]
    c_ = np.arange(128)[None, :]
    same = (s_ // 64) == (c_ // 64)
    hm = np.stack([(same & (s_ <= c_)), (same & (s_ >= c_))], axis=1).astype(np.float32)
    c["hmask"] = np.ascontiguousarray(hm)
    sl_ = (np.arange(128) // 16)[:, None]
    tl_ = (np.arange(128) // 16)[None, :]
    c["s5m"] = np.ascontiguousarray(np.stack([(tl_ >= sl_), (tl_ <= sl_)], axis=1).astype(np.float32))
    return c


def make_in_maps(inputs, parts=None):
    consts = host_consts()
    gains = np.concatenate([inputs["norm_mix_g"][0:1], inputs["norm_mlp_g"][0:1], inputs["norm_mix_g"][1:2],
                            inputs["norm_mlp_g"][1:2], inputs["final_norm_g"][None, :]], axis=0).astype(np.float32)
    shared = dict(consts)
    shared["gains"] = np.ascontiguousarray(gains.reshape(5, 8, 128).transpose(2, 0, 1))
    shared["w_in_odd"] = np.ascontiguousarray(inputs["w_in_odd"][0], dtype=np.float32)
    shared["w_out_odd"] = np.ascontiguousarray(inputs["w_out_odd"][0], dtype=np.float32)
    shared["lbl"] = np.ascontiguousarray(np.asarray(inputs["hgrn_lb_logits"], dtype=np.float32).reshape(2, 8, 128).transpose(2, 0, 1))
    shared["hg"] = np.ascontiguousarray(np.asarray(inputs["hgrn_norm_g"][0], dtype=np.float32).reshape(8, 128).T)
    shared["w_in_even"] = np.ascontiguousarray(inputs["w_in_even"][0], dtype=np.float32)
    shared["w_out_even"] = np.ascontiguousarray(inputs["w_out_even"][0], dtype=np.float32)
    shared["diff_lambda"] = np.ascontiguousarray(np.asarray(inputs["diff_lambda"][0], dtype=np.float32).reshape(1, 256))
    shared["subln_g"] = np.ascontiguousarray(np.asarray(inputs["diff_subln_g"][0], dtype=np.float32).reshape(128, 1))
    f32 = np.float32
    lre = np.asarray(inputs["s5_lam_re"][0], f32); lim = np.asarray(inputs["s5_lam_im"][0], f32); lst = np.asarray(inputs["s5_log_step"][0], f32)
    s5s = np.stack([lre.transpose(0, 2, 1).reshape(128, 32), lim.transpose(0, 2, 1).reshape(128, 32),
                    np.repeat(lst[:, None, :], 64, axis=1).reshape(128, 32)], axis=1)
    shared["s5s"] = np.ascontiguousarray(s5s, dtype=f32)
    shared["s5b"] = np.ascontiguousarray(np.stack([np.asarray(inputs[k][0], f32).transpose(0, 2, 1, 3).reshape(128, 32, 16) for k in ("s5_b_re", "s5_b_im")], axis=1))
    shared["s5c"] = np.ascontiguousarray(np.stack([np.asarray(inputs[k][0], f32).transpose(0, 3, 1, 2).reshape(128, 32, 16) for k in ("s5_c_re", "s5_c_im")], axis=1))
    shared["s5d"] = np.ascontiguousarray(np.asarray(inputs["s5_d"][0], f32).reshape(4, 128).T)
    shared["s5bg"] = np.ascontiguousarray(np.asarray(inputs["s5_b_glu"][0], f32).reshape(4, 128).T)
    shared["w_glu"] = np.ascontiguousarray(inputs["s5_w_glu"][0], dtype=f32)
    shared["w_ff_in"] = np.ascontiguousarray(inputs["w_ff_in"], dtype=np.float32)
    shared["w_ff_out"] = np.ascontiguousarray(inputs["w_ff_out"], dtype=np.float32)
    x = np.asarray(inputs["x"], dtype=np.float32)
    maps = []
    for b in range(x.shape[0]):
        m = dict(shared)
        m["x"] = np.ascontiguousarray(x[b])
        maps.append(m)
    return maps


ALL_PARTS = {"l0mix", "l0ffn", "l1mix", "l1ffn", "final"}
_NC_CACHE = {}


def kernel(**inputs):
    inputs = {k: np.asarray(v) for k, v in inputs.items()}
    key = "full"
    if key not in _NC_CACHE:
        _NC_CACHE[key] = build({"parts": ALL_PARTS})
    nc = _NC_CACHE[key]
    maps = make_in_maps(inputs)
    res = run_bass_kernel_spmd(nc, maps, core_ids=list(range(8)))
    out = np.stack([np.asarray(r["out"]) for r in res.results], axis=0)
    return out.astype(np.float32)
```

```python
import math
from contextlib import ExitStack

import numpy as np
import concourse.bass as bass
import concourse.mybir as mybir
from concourse.bass_utils import run_bass_kernel_spmd

F32 = mybir.dt.float32
BF16 = mybir.dt.bfloat16
AF = mybir.ActivationFunctionType
ALU = mybir.AluOpType
AX = mybir.AxisListType

ENGS = ("pe", "act", "dve", "pool", "sp")

L = 2048
D = 1024
NTB = 4
EPS = 1e-6
LAMBDA_INIT0 = 0.8 - 0.6 * math.exp(-0.3 * 0)


class Op:
    __slots__ = ("eng", "fn", "deps", "is_dma", "dkey", "signal", "sem", "val", "idx", "chain")


class Prog:
    def __init__(self, nc):
        self.nc = nc
        self.ops = []
        self.last_w = {}
        self.readers = {}
        self.stack = ExitStack()
        self.pending_barrier = {}

    def sb(self, name, shape, dt):
        return self.stack.enter_context(self.nc.sbuf_tensor("sb_" + name, list(shape), dt))

    def ps(self, name, shape, dt=F32):
        return self.stack.enter_context(self.nc.psum_tensor("pp_" + name, list(shape), dt))

    def barrier(self):
        deps = set()
        last = {}
        for o in self.ops:
            if o.is_dma:
                deps.add(o.idx)
            else:
                last[o.eng] = o.idx
        deps.update(last.values())
        self.pending_barrier = {e: set(deps) for e in ENGS}
        self.last_w = {}
        self.readers = {}

    def op(self, eng, fn, r=(), w=(), dma=False, dkey=None, chain=None):
        o = Op()
        o.eng, o.fn, o.is_dma, o.signal = eng, fn, dma, False
        o.chain = chain
        o.idx = len(self.ops)
        deps = set()
        for k in list(r) + list(w):
            if k in self.last_w:
                deps.add(self.last_w[k])
        for k in w:
            for rd in self.readers.get(k, ()):
                deps.add(rd)
        if self.pending_barrier.get(eng):
            deps |= self.pending_barrier[eng]
            self.pending_barrier[eng] = set()
        deps.discard(o.idx)
        o.deps = deps
        o.dkey = (dkey if dkey is not None else (w[0] if len(w) else r[0])) if dma else None
        for k in r:
            self.readers.setdefault(k, []).append(o.idx)
        for k in w:
            self.last_w[k] = o.idx
            self.readers[k] = []
        self.ops.append(o)
        return o.idx

    def _skip(self, od, o):
        if od.is_dma or o.is_dma or od.eng != o.eng:
            return False
        return o.eng == "pe" or (o.chain is not None and o.chain == od.chain)

    def emit(self, final_eng="sp"):
        nc = self.nc
        ops = self.ops
        final_deps = [o.idx for o in ops if o.is_dma]
        for o in ops:
            for d in o.deps:
                if not self._skip(ops[d], o):
                    ops[d].signal = True
        for d in final_deps:
            ops[d].signal = True
        sems = {}

        def get_sem(key):
            if key not in sems:
                sems[key] = self.stack.enter_context(nc.semaphore("s%d" % len(sems)))
            return sems[key]

        cnt = {}
        for o in ops:
            if not o.signal:
                continue
            key = ("dma", o.dkey) if o.is_dma else ("eng", o.eng)
            inc = 16 if o.is_dma else 1
            cnt[key] = cnt.get(key, 0) + inc
            o.sem = get_sem(key)
            o.val = cnt[key]
            o.dkey = key
        self.n_sems = len(sems)
        per_eng = {e: [] for e in ENGS}
        for o in ops:
            per_eng[o.eng].append(o)

        def run(eng_name, e):
            waited = {}
            for o in per_eng[eng_name]:
                need = {}
                for d in o.deps:
                    od = ops[d]
                    if (not od.signal) or self._skip(od, o):
                        continue
                    if waited.get(od.dkey, 0) >= od.val:
                        continue
                    if need.get(od.dkey, (None, 0))[1] < od.val:
                        need[od.dkey] = (od.sem, od.val)
                for k, (sem, val) in need.items():
                    e.wait_ge(sem, val)
                    waited[k] = val
                ins = o.fn(e)
                if o.signal:
                    ins.then_inc(o.sem, 16 if o.is_dma else 1)
            if eng_name == final_eng:
                need = {}
                for d in final_deps:
                    od = ops[d]
                    if waited.get(od.dkey, 0) >= od.val:
                        continue
                    if need.get(od.dkey, (None, 0))[1] < od.val:
                        need[od.dkey] = (od.sem, od.val)
                for k, (sem, val) in need.items():
                    e.wait_ge(sem, val)

        with nc.Block() as block:
            @block.tensor
            def _(e):
                run("pe", e)

            @block.scalar
            def _(e):
                run("act", e)

            @block.vector
            def _(e):
                run("dve", e)

            @block.gpsimd
            def _(e):
                run("pool", e)

            @block.sync
            def _(e):
                run("sp", e)
        self.stack.close()


class Arena:
    def __init__(self, P, nbytes):
        self.P = P
        self.nbytes = nbytes
        self.base = P.sb("arena", [128, nbytes // 2], BF16)
        self.live = []

    def phase(self):
        self.P.barrier()
        self.live = []

    def at(self, name, off, shape, dt):
        n = 1
        for s in shape:
            n *= s
        nb = n * (4 if dt == F32 else 2)
        assert off % 4 == 0 and off + nb <= self.nbytes, (name, off, nb, self.nbytes)
        for (a, b, nm) in self.live:
            assert off >= b or off + nb <= a, ("arena overlap", name, nm)
        self.live.append((off, off + nb, name))
        ap = self.base[:, off // 2:(off + nb) // 2]
        if dt == F32:
            ap = ap.bitcast(F32)
        if len(shape) > 1:
            names = "abcd"[:len(shape)]
            pat = "p (" + " ".join(names) + ") -> p " + " ".join(names)
            ap = ap.rearrange(pat, **{names[i]: shape[i] for i in range(len(shape))})
        return ap

    def drop(self, name):
        self.live = [x for x in self.live if x[2] != name]


KB = 1024


def build(cfg):
    parts = cfg["parts"]
    nc = bass.Bass("TRN2", target_bir_lowering=False)

    def din(name, shape):
        return nc.dram_tensor(name, list(shape), F32, kind="ExternalInput").ap()

    x_d = din("x", [L, D])
    out_d = nc.dram_tensor("out", [L, D], F32, kind="ExternalOutput").ap()
    gains_d = din("gains", [128, 5, 8])
    w_ff_in_d = din("w_ff_in", [2, D, 4 * D])
    w_ff_out_d = din("w_ff_out", [2, 4 * D, D])
    ident_d = din("ident", [128, 128])
    w_in_odd_d = din("w_in_odd", [D, 5 * D])
    w_in_even_d = din("w_in_even", [D, 2 * D])
    w_out_even_d = din("w_out_even", [D, D])
    cos_d = din("rope_cos", [128, L])
    sin_d = din("rope_sin", [128, L])
    dl_d = din("diff_lambda", [1, 256])
    sg_d = din("subln_g", [128, 1])
    w_glu_d = din("w_glu", [512, 512])
    s5s_d = din("s5s", [128, 3, 32])
    s5b_d = din("s5b", [128, 2, 32, 16])
    s5c_d = din("s5c", [128, 2, 32, 16])
    s5d_d = din("s5d", [128, 4])
    s5bg_d = din("s5bg", [128, 4])
    s5m_d = din("s5m", [128, 2, 128])
    w_out_odd_d = din("w_out_odd", [D, D])
    lbl_d = din("lbl", [128, 2, 8])
    hg_d = din("hg", [128, 8])
    hmask_d = din("hmask", [128, 2, 128])

    P = Prog(nc)
    op = P.op

    xT = P.sb("xT", [128, 8, L], F32)
    WB = [P.sb("wb%d" % i, [128, 4096], BF16) for i in range(4)]
    ident_f = P.sb("ident_f", [128, 128], F32)
    ident_b = P.sb("ident_b", [128, 128], BF16)
    ones_b = P.sb("ones_b", [128, 128], BF16)
    ones_f = P.sb("ones_f", [128, 128], F32)
    gains = P.sb("gains", [128, 5, 8], F32)
    PS = [P.ps("ps%d" % i, [128, 512], F32) for i in range(8)]
    lbl = P.sb("lbl", [128, 2, 8], F32)
    dl = P.sb("dl", [128, 256], F32)
    dlp = P.sb("dlp", [128, 128], F32)
    lam = P.sb("lam", [128, 4], F32)
    sg = P.sb("sg", [128, 1], F32)
    rr_g = P.sb("rr_g", [128, 8], F32)
    hg = P.sb("hg", [128, 8], F32)
    hmask = P.sb("hmask", [128, 2, 128], F32)
    lb = P.sb("lb", [128, 8], F32)
    oml = P.sb("oml", [128, 8], F32)
    noml = P.sb("noml", [128, 8], F32)
    AR = Arena(P, 104 * KB)

    def psk(i):
        return ("ps", i)

    dbg = cfg.get("dbg", False)
    dbg_tab = cfg.setdefault("dbg_tab", {})
    xT_flat = xT[:].rearrange("p a b -> p (a b)")
    dbg_off = [0]

    def dbg_put(name, ap, rkeys):
        if not dbg:
            return
        shp = list(ap.shape)
        n = 1
        for v_ in shp[1:]:
            n *= v_
        o = dbg_off[0]
        dst = xT_flat[:, o:o + n]
        if len(shp) == 3:
            dst = dst.rearrange("p (a b) -> p a b", b=shp[2])
        op("dve", lambda e: e.tensor_copy(out=dst, in_=ap), r=list(rkeys), w=["dbg"])
        dbg_tab[name] = (o, shp)
        dbg_off[0] = o + n

    def xk(c, tb):
        return ("xT", c, tb)

    def hk(c, tb):
        return ("hT", c, tb)

    XT_ALL = [xk(c, tb) for c in range(8) for tb in range(4)]
    HT_ALL = [hk(c, tb) for c in range(8) for tb in range(4)]

    op("sp", lambda e: e.dma_start(out=ident_f[:], in_=ident_d), w=["ident_f"], dma=True)
    op("dve", lambda e: e.tensor_copy(out=ident_b[:], in_=ident_f[:]), r=["ident_f"], w=["ident_b"])
    op("pool", lambda e: e.memset(ones_b[:], 1.0), w=["ones_b"])
    op("pool", lambda e: e.memset(ones_f[:], 1.0), w=["ones_f"])
    op("sp", lambda e: e.dma_start(out=gains[:], in_=gains_d), w=["gains"], dma=True)

    AR.phase()
    xin = [AR.at("xin%d" % i, i * 4 * KB, [D], F32) for i in range(2)]
    ev = 0
    for t in range(16):
        b = xin[t % 2]
        op("sp", lambda e, b=b, t=t: e.dma_start(out=b, in_=x_d[t * 128:(t + 1) * 128, :]), w=[("xin", t % 2)], dma=True)
        for half in range(2):
            bank = (2 * t + half) % 4
            for j in range(4):
                c = half * 4 + j
                op("pe", lambda e, bank=bank, j=j, c=c, b=b: e.transpose(out=PS[bank][:, j * 128:(j + 1) * 128], in_=b[:, c * 128:(c + 1) * 128], identity=ident_f[:]),
                   r=[("xin", t % 2), "ident_f"], w=[psk(bank)])
            dst = xT[:, half * 4:(half + 1) * 4, t * 128:(t + 1) * 128]
            src = PS[bank][:].rearrange("p (a b) -> p a b", b=128)
            wk = [xk(half * 4 + j, t // 4) for j in range(4)]
            if ev % 2 == 0:
                op("dve", lambda e, dst=dst, src=src: e.tensor_copy(out=dst, in_=src), r=[psk(bank)], w=wk)
            else:
                op("act", lambda e, dst=dst, src=src: e.activation(out=dst, in_=src, func=AF.Copy), r=[psk(bank)], w=wk)
            ev += 1

    def rmsnorm_rstd(rstd, sq):
        for c in range(8):
            s = sq[c % 2]
            op("act", lambda e, s=s, c=c: e.activation(out=s, in_=xT[:, c, :], func=AF.Square),
               r=[xk(c, tb) for tb in range(4)], w=[("sq", c % 2)])
            for tb in range(4):
                op("pe", lambda e, s=s, c=c, tb=tb: e.matmul(PS[tb][:], lhsT=ones_b[:], rhs=s[:, tb * 512:(tb + 1) * 512], start=(c == 0), stop=(c == 7)),
                   r=[("sq", c % 2), "ones_b"], w=[psk(tb)])
        for tb in range(4):
            sl = rstd[:, tb * 512:(tb + 1) * 512]
            op("act", lambda e, sl=sl, tb=tb: e.activation(out=sl, in_=PS[tb][:], func=AF.Sqrt, scale=1.0 / D, bias=EPS),
               r=[psk(tb)], w=[("rstd", tb)])
            op("dve", lambda e, sl=sl: e.reciprocal(out=sl, in_=sl), r=[("rstd", tb)], w=[("rstd", tb)])

    def norm_to_hT(gi, hT, rstd, sq):
        rmsnorm_rstd(rstd, sq)
        for c in range(8):
            for tb in range(4):
                op("dve", lambda e, c=c, tb=tb: e.scalar_tensor_tensor(
                    out=hT[:, c, tb * 512:(tb + 1) * 512], in0=xT[:, c, tb * 512:(tb + 1) * 512], scalar=gains[:, gi, c:c + 1],
                    in1=rstd[:, tb * 512:(tb + 1) * 512], op0=ALU.mult, op1=ALU.mult),
                   r=[xk(c, tb), ("rstd", tb), "gains"], w=[hk(c, tb)])

    def wload(slot, src, r=()):
        a, b = src.shape[1], src.shape[2]
        dst = WB[slot][:, 0:a * b].rearrange("p (a b) -> p a b", b=b)
        op("pool", lambda e: e.dma_start(out=dst, in_=src), r=list(r), w=[("wb", slot)], dma=True)
        return dst

    def ffn_loader(l):
        w_in = w_ff_in_d[l].rearrange("(c p) f -> p c f", p=128)
        w_out = w_ff_out_d[l].rearrange("(c p) d -> p c d", p=128)
        views = {}

        def load(fg):
            views[fg] = (wload((2 * fg) % 4, w_in[:, :, fg * 512:(fg + 1) * 512]),
                         wload((2 * fg + 1) % 4, w_out[:, fg * 4:(fg + 1) * 4, :]))
        return views, load

    def ffn(l, hT, actT, rl, views, load):
        k = 0
        for fg in range(8):
            wi, wo = views[fg]
            sa, sbk = (2 * fg) % 4, (2 * fg + 1) % 4
            at = actT[fg % 2]
            for tb in range(4):
                for fc in range(4):
                    bank = k % 3
                    for c in range(8):
                        op("pe", lambda e, bank=bank, wi=wi, c=c, fc=fc, tb=tb: e.matmul(
                            PS[bank][:], lhsT=wi[:, c, fc * 128:(fc + 1) * 128], rhs=hT[:, c, tb * 512:(tb + 1) * 512], start=(c == 0), stop=(c == 7)),
                           r=[("wb", sa), hk(c, tb)], w=[psk(bank)])
                    r_ = rl[k % 2]
                    op("act", lambda e, bank=bank, r_=r_: e.activation(out=r_, in_=PS[bank][:], func=AF.Relu), r=[psk(bank)], w=[("rl", k % 2)])
                    op("act", lambda e, r_=r_, at=at, fc=fc, tb=tb: e.activation(out=at[:, fc, tb * 512:(tb + 1) * 512], in_=r_, func=AF.Square),
                       r=[("rl", k % 2)], w=[("actT", fg % 2, fc, tb)])
                    k += 1
            kk = 0
            for tb in range(4):
                for dc in range(8):
                    bank = 3 + kk % 3
                    for fc in range(4):
                        op("pe", lambda e, bank=bank, wo=wo, fc=fc, dc=dc, tb=tb, at=at: e.matmul(
                            PS[bank][:], lhsT=wo[:, fc, dc * 128:(dc + 1) * 128], rhs=at[:, fc, tb * 512:(tb + 1) * 512], start=(fc == 0), stop=(fc == 3)),
                           r=[("wb", sbk), ("actT", fg % 2, fc, tb)], w=[psk(bank)])
                    xs = xT[:, dc, tb * 512:(tb + 1) * 512]
                    op("dve", lambda e, xs=xs, bank=bank: e.tensor_tensor(out=xs, in0=xs, in1=PS[bank][:], op=ALU.add), r=[psk(bank), xk(dc, tb)], w=[xk(dc, tb)])
                    kk += 1
            if fg + 2 < 8:
                load(fg + 2)

    def ffn_phase(l, gi):
        AR.phase()
        hT = AR.at("hT", 0, [8, L], BF16)
        actT = [AR.at("actT%d" % i, 32 * KB + i * 16 * KB, [4, L], BF16) for i in range(2)]
        rl = [AR.at("rl%d" % i, 64 * KB + i * 2 * KB, [512], F32) for i in range(2)]
        sq = [AR.at("sq%d" % i, 68 * KB + i * 4 * KB, [L], BF16) for i in range(2)]
        rstd = AR.at("rstd", 76 * KB, [L], F32)
        views, load = ffn_loader(l)
        load(0)
        load(1)
        norm_to_hT(gi, hT, rstd, sq)
        ffn(l, hT, actT, rl, views, load)


    def attn_phase(gi):
        w_in = w_in_even_d.rearrange("(c p) f -> p c f", p=128)
        w_out = w_out_even_d.rearrange("(c p) d -> p c d", p=128)
        T0 = 81 * KB
        AR.phase()
        hT = AR.at("hT", 0, [8, L], BF16)
        sq = [AR.at("sq%d" % i, T0 + i * 4 * KB, [L], BF16) for i in range(2)]
        rstd = AR.at("rstd", T0 + 8 * KB, [L], F32)
        wq = wload(0, w_in[:, :, 0:512])
        wk = wload(2, w_in[:, :, 512:1024])
        norm_to_hT(gi, hT, rstd, sq)
        op("sp", lambda e: e.dma_start(out=dl[:], in_=bass.AP(dl_d.tensor, 0, [[0, 128], [1, 256]])), w=["dl"], dma=True)
        op("sp", lambda e: e.dma_start(out=sg[:], in_=sg_d), w=["sg"], dma=True)
        op("dve", lambda e: e.tensor_tensor(out=dlp[:, 0:64], in0=dl[:, 0:64], in1=dl[:, 64:128], op=ALU.mult), r=["dl"], w=["dlp"])
        op("dve", lambda e: e.tensor_tensor(out=dlp[:, 64:128], in0=dl[:, 128:192], in1=dl[:, 192:256], op=ALU.mult), r=["dl", "dlp"], w=["dlp"])
        op("dve", lambda e: e.reduce_sum(out=lam[:, 0:2], in_=dlp[:].rearrange("p (a b) -> p a b", b=64), axis=AX.X), r=["dlp"], w=["lam"])
        op("act", lambda e: e.activation(out=lam[:, 0:2], in_=lam[:, 0:2], func=AF.Exp), r=["lam"], w=["lam"])
        op("dve", lambda e: e.tensor_tensor(out=lam[:, 2:3], in0=lam[:, 1:2], in1=lam[:, 0:1], op=ALU.subtract), r=["lam"], w=["lam"])
        op("dve", lambda e: e.tensor_scalar(out=lam[:, 3:4], in0=lam[:, 2:3], scalar1=-LAMBDA_INIT0, scalar2=None, op0=ALU.add), r=["lam"], w=["lam"])
        op("dve", lambda e: e.tensor_scalar(out=sg[:], in0=sg[:], scalar1=1.0 - LAMBDA_INIT0, scalar2=None, op0=ALU.mult), r=["sg"], w=["sg"])
        nlam = lam[:, 3:4]
        if cfg.get("attn_stop", 9) <= 1:
            return

        AR.phase()
        hT = AR.at("hT", 0, [8, L], BF16)
        qT = AR.at("qT", 32 * KB, [4, L], BF16)
        kT = AR.at("kT", 48 * KB, [4, L], BF16)
        vaug = AR.at("vaug", 64 * KB, [16, 4, 130], BF16)
        t1 = [AR.at("t1_%d" % i, T0 + i * 2 * KB, [512], F32) for i in range(2)]
        t2 = [AR.at("t2_%d" % i, T0 + 4 * KB + i * 2 * KB, [512], F32) for i in range(2)]
        cosb = [AR.at("cos%d" % i, T0 + 8 * KB + i * 2 * KB, [512], F32) for i in range(2)]
        sinb = [AR.at("sin%d" % i, T0 + 12 * KB + i * 2 * KB, [512], F32) for i in range(2)]

        def load_group(slot, g):
            return wload(slot, w_in[:, :, g * 512:(g + 1) * 512])

        def rotate(sa, sr):
            a = WB[sa][:].rearrange("p (cb two j) -> p cb two j", two=2, j=32)
            r_ = WB[sr][:].rearrange("p (cb two j) -> p cb two j", two=2, j=32)
            op("dve", lambda e: e.tensor_scalar(out=r_[:, :, 0, :], in0=a[:, :, 1, :], scalar1=-1.0, scalar2=None, op0=ALU.mult), r=[("wb", sa)], w=[("wb", sr)])
            op("dve", lambda e: e.tensor_copy(out=r_[:, :, 1, :], in_=a[:, :, 0, :]), r=[("wb", sa)], w=[("wb", sr)])
            return WB[sr][:].rearrange("p (a b) -> p a b", b=512)

        wqr = rotate(0, 1)
        wkr = rotate(2, 3)
        op("pool", lambda e: e.memset(vaug[:, :, :, 128:130], 1.0), w=["vaug1"])
        kctr = [0]

        def rope_proj(tb, wA, wR, sA, sR, dstT, h, dkey):
            i = kctr[0] % 2
            kctr[0] += 1
            ba, bb = 2 * i, 2 * i + 1
            for c in range(8):
                op("pe", lambda e, c=c: e.matmul(PS[ba][:], lhsT=wA[:, c, h * 128:(h + 1) * 128], rhs=hT[:, c, tb * 512:(tb + 1) * 512], start=(c == 0), stop=(c == 7)),
                   r=[("wb", sA), hk(c, tb)], w=[psk(ba)])
            for c in range(8):
                op("pe", lambda e, c=c: e.matmul(PS[bb][:], lhsT=wR[:, c, h * 128:(h + 1) * 128], rhs=hT[:, c, tb * 512:(tb + 1) * 512], start=(c == 0), stop=(c == 7)),
                   r=[("wb", sR), hk(c, tb)], w=[psk(bb)])
            op("dve", lambda e: e.tensor_tensor(out=t1[i], in0=PS[ba][:], in1=cosb[tb % 2], op=ALU.mult), r=[psk(ba), ("cos", tb % 2)], w=[("t1", i)])
            op("dve", lambda e: e.tensor_tensor(out=t2[i], in0=PS[bb][:], in1=sinb[tb % 2], op=ALU.mult), r=[psk(bb), ("sin", tb % 2)], w=[("t2", i)])
            op("pool", lambda e: e.tensor_tensor(out=dstT[:, h, tb * 512:(tb + 1) * 512], in0=t1[i], in1=t2[i], op=ALU.add), r=[("t1", i), ("t2", i)], w=[(dkey, h, tb)])

        def do_tb(tb):
            op("sp", lambda e: e.dma_start(out=cosb[tb % 2], in_=cos_d[:, tb * 512:(tb + 1) * 512]), w=[("cos", tb % 2)], dma=True)
            op("sp", lambda e: e.dma_start(out=sinb[tb % 2], in_=sin_d[:, tb * 512:(tb + 1) * 512]), w=[("sin", tb % 2)], dma=True)
            for h in range(4):
                rope_proj(tb, wq, wqr, 0, 1, qT, h, "qT")
                rope_proj(tb, wk, wkr, 2, 3, kT, h, "kT")

        for tb in range(4):
            do_tb(tb)
        wv = load_group(0, 2)

        def do_v(j):
            bank = 4 + j % 2
            for c in range(8):
                op("pe", lambda e, c=c: e.matmul(PS[bank][:], lhsT=hT[:, c, j * 128:(j + 1) * 128], rhs=wv[:, c, :], start=(c == 0), stop=(c == 7)),
                   r=[("wb", 0), hk(c, j // 4)], w=[psk(bank)])
            op("act", lambda e: e.activation(out=vaug[:, j, :, 0:128], in_=PS[bank][:].rearrange("p (a b) -> p a b", b=128), func=AF.Copy), r=[psk(bank)], w=[("vaug", j)])

        for j in range(16):
            do_v(j)
        wo = wload(2, w_out[:, 0:4, :])
        dbg_put("qT0", qT[:, 0, 0:256], [("qT", 0, 0)])
        dbg_put("kT0", kT[:, 0, 0:256], [("kT", 0, 0)])
        dbg_put("vaug0", vaug[:, 0, :, :], [("vaug", 0), "vaug1"])
        dbg_put("lam", lam[:, 0:4], ["lam"])
        if cfg.get("attn_stop", 9) <= 2:
            return

        AR.phase()
        hT = AR.at("hT", 0, [8, L], BF16)
        qT = AR.at("qT", 32 * KB, [4, L], BF16)
        kT = AR.at("kT", 48 * KB, [4, L], BF16)
        vaug = AR.at("vaug", 64 * KB, [16, 4, 130], BF16)
        aT = AR.at("aT", T0, [4, L], BF16)
        PT = [AR.at("PT%d" % i, T0 + 16 * KB + i * KB, [512], BF16) for i in range(4)]
        SM = T0 + 20 * KB
        rr = rr_g[:]
        tt_ = [AR.at("tt%d" % i, SM + i * 512, [128], F32) for i in range(2)]
        oo = [AR.at("oo%d" % i, SM + 1024 + i * 512, [128], F32) for i in range(2)]
        onb = [AR.at("onb%d" % i, SM + 2048 + i * 256, [128], BF16) for i in range(4)]
        pctr = [0]
        cctr = [0]
        pending = []

        def accv(comp, qs):
            bank = 2 + 2 * comp + qs // 2
            off = (qs % 2) * 130
            return bank, PS[bank][:, off:off + 129]

        its = [(h_, qb_, comp_, kt_) for h_ in range(4) for qb_ in range(4) for comp_ in range(2) for kt_ in range(16)]
        if cfg.get("attn_stop", 9) == 3:
            its = [x_ for x_ in its if x_[0] == 0 and x_[1] == 0]

        SRING = (0, 1, 6)
        accs = [WB[1][:].bitcast(F32)[:, 512 * (1 + c_):512 * (2 + c_)] for c_ in range(2)]

        qpad = [[WB[0][:, (comp_ * 2 + j_) * 512:(comp_ * 2 + j_ + 1) * 512] for j_ in range(2)] for comp_ in range(2)]
        op("pool", lambda e: e.memset(WB[0][:, 0:2048], 0.0), r=[("wb", 0)], w=[("wb", 0)] + [("qpad", c_, j_) for c_ in range(2) for j_ in range(2)])

        def emit_qpad(gi):
            h, qb = gi // 4, gi % 4
            j = gi % 2
            op("pool", lambda e: e.tensor_copy(out=qpad[0][j][0:64, :], in_=qT[0:64, h, qb * 512:(qb + 1) * 512]), r=[("qT", h, qb)], w=[("qpad", 0, j)])
            op("pool", lambda e: e.tensor_copy(out=qpad[1][j][64:128, :], in_=qT[64:128, h, qb * 512:(qb + 1) * 512]), r=[("qT", h, qb)], w=[("qpad", 1, j)])

        def emit_S(i):
            h, qb, comp, kt = its[i]
            sbank = SRING[i % 3]
            j = (h * 4 + qb) % 2
            op("pe", lambda e: e.matmul(PS[sbank][:], lhsT=kT[:, h, kt * 128:(kt + 1) * 128], rhs=qpad[comp][j], start=True, stop=True),
               r=[("kT", h, kt // 4), ("qpad", comp, j)], w=[psk(sbank)])

        def emit_exp_pv(i):
            h, qb, comp, kt = its[i]
            sbank = SRING[i % 3]
            pi = i % 4
            op("act", lambda e: e.activation(out=PT[pi], in_=PS[sbank][:], func=AF.Exp, scale=0.125), r=[psk(sbank)], w=[("PT", pi)])
            bo, bs = 2 + 2 * comp, 3 + 2 * comp
            op("pe", lambda e: e.matmul(PS[bo][:], lhsT=vaug[:, kt, h, 0:128], rhs=PT[pi], start=(kt == 0), stop=(kt == 15)),
               r=[("PT", pi), ("vaug", kt)], w=[psk(bo)])
            op("pe", lambda e: e.matmul(PS[bs][:], lhsT=ones_b[:], rhs=PT[pi], start=(kt == 0), stop=(kt == 15)),
               r=[("PT", pi), "ones_b"], w=[psk(bs)])

        def attend_tail(h, qb):
            fo = WB[3][:].bitcast(F32)
            r0, r1, t_, o_ = fo[:, 0:512], fo[:, 512:1024], fo[:, 1024:1536], fo[:, 1536:2048]
            sqb = WB[1][:, 0:512]
            K3 = [("wb", 3)]
            if dbg and h == 0 and qb == 0:
                dbg_put("acc0", PS[2][:, 0:256], [psk(2)])
                dbg_put("acc1", PS[4][:, 0:256], [psk(4)])
            if cfg.get("attn_stop", 9) <= 3:
                return
            op("dve", lambda e: e.reciprocal(out=r0, in_=PS[3][:]), r=[psk(3)], w=K3)
            op("dve", lambda e: e.reciprocal(out=r1, in_=PS[5][:]), r=[psk(5)] + K3, w=K3)
            op("dve", lambda e: e.tensor_tensor(out=t_, in0=PS[4][:], in1=r1, op=ALU.mult), r=[psk(4)] + K3, w=K3)
            op("dve", lambda e: e.tensor_tensor(out=o_, in0=PS[2][:], in1=r0, op=ALU.mult), r=[psk(2)] + K3, w=K3)
            op("dve", lambda e: e.scalar_tensor_tensor(out=o_, in0=t_, scalar=nlam, in1=o_, op0=ALU.mult, op1=ALU.add), r=K3 + ["lam"], w=K3)
            op("act", lambda e: e.activation(out=sqb, in_=o_, func=AF.Square), r=K3, w=[("wb", 1)])

            def fin():
                op("pe", lambda e: e.matmul(PS[7][:], lhsT=ones_b[:], rhs=sqb, start=True, stop=True), r=[("wb", 1), "ones_b"], w=[psk(7)])
                op("act", lambda e: e.activation(out=r0, in_=PS[7][:], func=AF.Sqrt, scale=1.0 / 128, bias=EPS), r=[psk(7)] + K3, w=K3)
                op("dve", lambda e: e.reciprocal(out=r0, in_=r0), r=K3, w=K3)
                op("dve", lambda e: e.scalar_tensor_tensor(out=aT[:, h, qb * 512:(qb + 1) * 512], in0=o_, scalar=sg[:, 0:1], in1=r0, op0=ALU.mult, op1=ALU.mult),
                   r=K3 + ["sg"], w=[("aT", h, qb)])
            pending.append(fin)

        emit_qpad(0)
        LOOK = 2
        for i in range(LOOK):
            emit_S(i)
        for i in range(len(its)):
            if its[i][2] == 0 and its[i][3] == 0:
                gi_ = its[i][0] * 4 + its[i][1]
                if gi_ + 1 < 16 and cfg.get("attn_stop", 9) != 3:
                    emit_qpad(gi_ + 1)
            if i + LOOK < len(its):
                emit_S(i + LOOK)
            emit_exp_pv(i)
            h_, qb_, comp_, kt_ = its[i]
            if kt_ == 15 and comp_ == 0 and pending:
                for f in pending:
                    f()
                del pending[:]
            if kt_ == 15 and comp_ == 1:
                attend_tail(h_, qb_)
        for f in pending:
            f()
        del pending[:]
        dbg_put("aT", aT[:, :, 0:512], [("aT", h_, 0) for h_ in range(4)])

        def oproj(tb, dc, kq):
            bank = kq % 2
            for hc in range(4):
                op("pe", lambda e, hc=hc: e.matmul(PS[bank][:], lhsT=wo[:, hc, dc * 128:(dc + 1) * 128], rhs=aT[:, hc, tb * 512:(tb + 1) * 512], start=(hc == 0), stop=(hc == 3)),
                   r=[("wb", 2), ("aT", hc, tb)], w=[psk(bank)])
            xs = xT[:, dc, tb * 512:(tb + 1) * 512]
            op("dve", lambda e: e.tensor_tensor(out=xs, in0=xs, in1=PS[bank][:], op=ALU.add), r=[psk(bank), xk(dc, tb)], w=[xk(dc, tb)])

        if not dbg:
            kq = 0
            for tb in range(4):
                for dc in range(8):
                    oproj(tb, dc, kq)
                    kq += 1


    def s5_phase():
        w_in = w_in_even_d.rearrange("(c p) f -> p c f", p=128)
        w_out = w_out_even_d.rearrange("(c p) d -> p c d", p=128)
        AR.phase()
        hT = AR.at("hT", 0, [8, L], BF16)
        if not ("l0attn" in parts or "l0mix" in parts):
            sq = [AR.at("sq%d" % i, 81 * KB + i * 4 * KB, [L], BF16) for i in range(2)]
            rstd = AR.at("rstd", 89 * KB, [L], F32)
            norm_to_hT(0, hT, rstd, sq)
        uT = AR.at("uT", 32 * KB, [4, L], BF16)
        wu = wload(0, w_in[:, :, 1536:2048])

        def uproj(tb, cc, kq):
            bank = kq % 4
            for c in range(8):
                op("pe", lambda e, c=c: e.matmul(PS[bank][:], lhsT=wu[:, c, cc * 128:(cc + 1) * 128], rhs=hT[:, c, tb * 512:(tb + 1) * 512], start=(c == 0), stop=(c == 7)),
                   r=[("wb", 0), hk(c, tb)], w=[psk(bank)])
            op("act", lambda e: e.activation(out=uT[:, cc, tb * 512:(tb + 1) * 512], in_=PS[bank][:], func=AF.Copy), r=[psk(bank)], w=[("uT", cc)])

        kq = 0
        for tb in range(4):
            for cc in range(4):
                uproj(tb, cc, kq)
                kq += 1
        wglu = wload(1, w_glu_d.rearrange("(c p) f -> p c f", p=128))
        wo2 = wload(2, w_out[:, 4:8, :])

        AR.phase()
        VX = AR.at("VX", 0, [2, 32, 256], BF16)
        uT = AR.at("uT", 32 * KB, [4, L], BF16)
        U = AR.at("U", 48 * KB, [32, 256], BF16)
        sm_off = [64 * KB]

        def sm(name, shape=(32,)):
            n = 1
            for v_ in shape:
                n *= v_
            t_ = AR.at(name, sm_off[0], list(shape), F32)
            sm_off[0] += n * 4
            return t_

        bm = sm("bm", (2, 32, 16))
        names = ["lr", "dt", "lrdt", "ang", "mg", "sn", "cs", "r_", "i_", "t0", "t1", "t2", "den", "am1", "cr", "ci", "vr", "vi"]
        sv = {n_: sm(n_) for n_ in names}
        par = sm("par", (3, 32))
        cm = sm("cm", (2, 32, 16))
        bb = sm("bb", (2, 32, 16))
        pw = sm("pw", (2, 32, 8))
        pwi = sm("pwi", (2, 32, 8))
        P1s = sm("P1s", (2, 32))
        P2s = sm("P2s", (2, 32))
        Xst = sm("Xst", (2, 32))
        Xst2 = sm("Xst2", (2, 32))
        S1 = sm("S1", (2, 32))
        T1 = sm("T1", (2, 32))
        T2 = sm("T2", (2, 32))
        dsk = sm("dsk", (4,))
        bgl = sm("bgl", (4,))
        SM_END = sm_off[0]
        assert SM_END <= 85 * KB, SM_END
        AB = [AR.at("AB%d" % i, 85 * KB + i * 2 * KB, [8, 128], BF16) for i in range(2)]
        ABT = [AR.at("ABT%d" % i, 89 * KB + i * 2 * KB, [8, 128], BF16) for i in range(2)]
        tA = AR.at("tA", 93 * KB, [512], F32)
        tB = AR.at("tB", 95 * KB, [512], F32)
        Zt = AR.at("Zt", 97 * KB, [8, 240], BF16)
        s5m = AR.at("s5m", 97 * KB + 3840, [2, 128], F32)
        PP = ["pp"]

        def vop(fn, eng="dve", r=(), w=(), chain=None):
            op(eng, fn, r=PP + list(r), w=PP + list(w), chain=chain)

        def tt(o_, a_, b_, o, **kw):
            vop(lambda e: e.tensor_tensor(out=o_, in0=a_, in1=b_, op=o), **kw)

        GCH = "s5gen" if cfg.get("gen_chain", True) else None

        def ts(o_, a_, s1, o1, s2=None, o2=None, **kw):
            if o2 is None:
                vop(lambda e: e.tensor_scalar(out=o_, in0=a_, scalar1=s1, scalar2=None, op0=o1), **kw)
            else:
                vop(lambda e: e.tensor_scalar(out=o_, in0=a_, scalar1=s1, scalar2=s2, op0=o1, op1=o2), **kw)

        def cmul(outr, outi, ar_, ai_, br_, bi_, x1, x2, **kw):
            kw = dict(kw, chain=GCH)
            tt(x1, ar_, br_, ALU.mult, **kw)
            tt(x2, ai_, bi_, ALU.mult, **kw)
            tt(outr, x1, x2, ALU.subtract, **kw)
            tt(x1, ar_, bi_, ALU.mult, **kw)
            tt(x2, ai_, br_, ALU.mult, **kw)
            tt(outi, x1, x2, ALU.add, **kw)

        op("sp", lambda e: e.dma_start(out=par, in_=s5s_d), w=PP, dma=True, dkey="s5s")
        op("sp", lambda e: e.dma_start(out=bm, in_=s5b_d), w=PP, dma=True, dkey="s5b")
        op("sp", lambda e: e.dma_start(out=cm, in_=s5c_d), w=PP, dma=True, dkey="s5c")
        op("sp", lambda e: e.dma_start(out=dsk, in_=s5d_d), w=PP, dma=True, dkey="s5d")
        op("sp", lambda e: e.dma_start(out=bgl, in_=s5bg_d), w=PP, dma=True, dkey="s5bg")
        op("sp", lambda e: e.dma_start(out=s5m, in_=s5m_d), w=["s5m"], dma=True)
        op("pool", lambda e: e.memset(Zt, 0.0), w=["Zt"])
        op("pool", lambda e: e.tensor_copy(out=Zt[:, :, 112:128], in_=ident_b[:].rearrange("p (a b) -> p a b", b=16)), r=["ident_b"], w=["Zt"])
        lam_re, lam_im, lstep = par[:, 0, :], par[:, 1, :], par[:, 2, :]
        v_ = sv
        ts(v_["lr"], lam_re, -1e-4, ALU.min)
        vop(lambda e: e.activation(out=v_["dt"], in_=lstep, func=AF.Exp), eng="act")
        tt(v_["lrdt"], v_["lr"], v_["dt"], ALU.mult)
        tt(v_["ang"], lam_im, v_["dt"], ALU.mult)
        vop(lambda e: e.activation(out=v_["mg"], in_=v_["lrdt"], func=AF.Exp, scale=1.0 / 32), eng="act")
        vop(lambda e: e.activation(out=v_["sn"], in_=v_["ang"], func=AF.Sin, scale=1.0 / 32), eng="act")
        ts(v_["t0"], v_["ang"], 1.0 / 32, ALU.mult, math.pi / 2, ALU.add)
        vop(lambda e: e.activation(out=v_["cs"], in_=v_["t0"], func=AF.Sin), eng="act")
        tt(v_["r_"], v_["mg"], v_["cs"], ALU.mult)
        tt(v_["i_"], v_["mg"], v_["sn"], ALU.mult)
        for _ in range(5):
            tt(v_["t0"], v_["r_"], v_["r_"], ALU.mult)
            tt(v_["t1"], v_["i_"], v_["i_"], ALU.mult)
            tt(v_["t2"], v_["r_"], v_["i_"], ALU.mult)
            tt(v_["r_"], v_["t0"], v_["t1"], ALU.subtract)
            ts(v_["i_"], v_["t2"], 2.0, ALU.mult)
        ar, ai = v_["r_"], v_["i_"]
        tt(v_["t0"], v_["lr"], v_["lr"], ALU.mult)
        tt(v_["t1"], lam_im, lam_im, ALU.mult)
        tt(v_["den"], v_["t0"], v_["t1"], ALU.add)
        vop(lambda e: e.reciprocal(out=v_["den"], in_=v_["den"]))
        ts(v_["am1"], ar, -1.0, ALU.add)
        tt(v_["t0"], v_["am1"], v_["lr"], ALU.mult)
        tt(v_["t1"], ai, lam_im, ALU.mult)
        tt(v_["t0"], v_["t0"], v_["t1"], ALU.add)
        tt(v_["cr"], v_["t0"], v_["den"], ALU.mult)
        tt(v_["t0"], ai, v_["lr"], ALU.mult)
        tt(v_["t1"], v_["am1"], lam_im, ALU.mult)
        tt(v_["t0"], v_["t0"], v_["t1"], ALU.subtract)
        tt(v_["ci"], v_["t0"], v_["den"], ALU.mult)
        tt(v_["t0"], ar, ar, ALU.mult)
        tt(v_["t1"], ai, ai, ALU.mult)
        tt(v_["t0"], v_["t0"], v_["t1"], ALU.add)
        vop(lambda e: e.reciprocal(out=v_["t0"], in_=v_["t0"]))
        tt(v_["vr"], ar, v_["t0"], ALU.mult)
        tt(v_["t1"], ai, v_["t0"], ALU.mult)
        ts(v_["vi"], v_["t1"], -1.0, ALU.mult)
        x1 = tA[:, 0:128].rearrange("p (a b) -> p a b", b=4)
        x2 = tB[:, 0:128].rearrange("p (a b) -> p a b", b=4)
        for (tab, br_, bi_) in ((pw, ar, ai), (pwi, v_["vr"], v_["vi"])):
            tr_, ti_ = tab[:, 0, :, :], tab[:, 1, :, :]
            vop(lambda e, tr_=tr_, br_=br_: e.tensor_copy(out=tr_[:, :, 0:1], in_=br_.unsqueeze(2)))
            vop(lambda e, ti_=ti_, bi_=bi_: e.tensor_copy(out=ti_[:, :, 0:1], in_=bi_.unsqueeze(2)))
            for n_ in (1, 2, 4):
                cmul(tr_[:, :, n_:2 * n_], ti_[:, :, n_:2 * n_], tr_[:, :, 0:n_], ti_[:, :, 0:n_],
                     tr_[:, :, n_ - 1:n_].broadcast_to([128, 32, n_]), ti_[:, :, n_ - 1:n_].broadcast_to([128, 32, n_]), x1[:, :, 0:n_], x2[:, :, 0:n_])
        cmul(bb[:, 0, :, :], bb[:, 1, :, :], v_["cr"].unsqueeze(2).broadcast_to([128, 32, 16]), v_["ci"].unsqueeze(2).broadcast_to([128, 32, 16]),
             bm[:, 0, :, :], bm[:, 1, :, :], tA.rearrange("p (a b) -> p a b", b=16), tB.rearrange("p (a b) -> p a b", b=16))
        vop(lambda e: e.tensor_copy(out=P1s[:, 0, :], in_=pw[:, 0, :, 7]))
        vop(lambda e: e.tensor_copy(out=P1s[:, 1, :], in_=pw[:, 0, :, 7]))
        ts(P2s[:, 0, :], pw[:, 1, :, 7], -1.0, ALU.mult)
        vop(lambda e: e.tensor_copy(out=P2s[:, 1, :], in_=pw[:, 1, :, 7]))

        for tab in (pw, pwi):
            flat = tab[64:128].rearrange("p a b c -> p (a b c)")
            vop(lambda e, flat=flat: e.tensor_copy(out=tA[64:128, :], in_=flat))
            vop(lambda e, tab=tab: e.tensor_copy(out=tab[64:128].rearrange("p a b c -> p (a b) c"), in_=tA[64:128, :].rearrange("p (ab c) -> p ab c", c=8)[:, :, ::-1]))

        def gen_AB(ABt, g0, gl0):
            for (lo, hi) in ((0, 128),):
                np_ = hi - lo
                pr = pwi[lo:hi, 0, g0:g0 + 4, :].unsqueeze(3).broadcast_to([np_, 4, 8, 16])
                pi = pwi[lo:hi, 1, g0:g0 + 4, :].unsqueeze(3).broadcast_to([np_, 4, 8, 16])
                br_ = bb[lo:hi, 0, g0:g0 + 4, :].unsqueeze(2).broadcast_to([np_, 4, 8, 16])
                bi_ = bb[lo:hi, 1, g0:g0 + 4, :].unsqueeze(2).broadcast_to([np_, 4, 8, 16])
                o_r = ABt[0][lo:hi, gl0:gl0 + 4, :].rearrange("p g (s m) -> p g s m", m=16)
                o_i = ABt[1][lo:hi, gl0:gl0 + 4, :].rearrange("p g (s m) -> p g s m", m=16)
                ta = tA[lo:hi, :].rearrange("p (g s m) -> p g s m", s=8, m=16)
                tb_ = tB[lo:hi, :].rearrange("p (g s m) -> p g s m", s=8, m=16)
                cmul(o_r, o_i, pr, pi, br_, bi_, ta, tb_, w=["AB"])

        def gen_CA(g0, gl0, CAf, CAb):
            for (lo, hi, dst) in ((0, 64, CAf), (64, 128, CAb)):
                sl = slice(None)
                pr = pw[lo:hi, 0, g0:g0 + 4, sl].unsqueeze(3).broadcast_to([64, 4, 8, 16])
                pi = pw[lo:hi, 1, g0:g0 + 4, sl].unsqueeze(3).broadcast_to([64, 4, 8, 16])
                c_r = cm[lo:hi, 0, g0:g0 + 4, :].unsqueeze(2).broadcast_to([64, 4, 8, 16])
                c_i = cm[lo:hi, 1, g0:g0 + 4, :].unsqueeze(2).broadcast_to([64, 4, 8, 16])
                o_r = dst[0][lo:hi, gl0:gl0 + 4, :].rearrange("p g (s m) -> p g s m", m=16)
                o_i = dst[1][lo:hi, gl0:gl0 + 4, :].rearrange("p g (s m) -> p g s m", m=16)
                ta = tA[lo:hi, :].rearrange("p (g s m) -> p g s m", s=8, m=16)
                tb_ = tB[lo:hi, :].rearrange("p (g s m) -> p g s m", s=8, m=16)
                kw = dict(w=["CA"], chain=GCH)
                tt(ta, c_r, pr, ALU.mult, **kw)
                tt(tb_, c_i, pi, ALU.mult, **kw)
                tt(o_r, ta, tb_, ALU.subtract, **kw)
                tt(ta, c_r, pi, ALU.mult, **kw)
                tt(tb_, c_i, pr, ALU.mult, **kw)
                vop(lambda e, o_i=o_i, ta=ta, tb_=tb_: e.scalar_tensor_tensor(out=o_i, in0=ta, scalar=-1.0, in1=tb_, op0=ALU.mult, op1=ALU.subtract), **kw)

        def shuffle_group(cc, gl):
            g = 8 * cc + gl
            bank = g % 2
            for s_ in range(8):
                op("pe", lambda e, s_=s_: e.matmul(PS[bank][:, 0:256], lhsT=Zt[:, gl, (7 - s_) * 16:(7 - s_) * 16 + 128],
                                                   rhs=uT[:, cc, :].rearrange("p (b s) -> p b s", s=8)[:, :, s_], start=(s_ == 0), stop=(s_ == 7)),
                   r=["Zt", ("uT", cc)], w=[psk(bank)])
            op("act", lambda e: e.activation(out=U[:, g, :], in_=PS[bank][:, 0:256], func=AF.Copy), r=[psk(bank)], w=[("U", g)])

        def abt_chunk():
            for ri in range(2):
                bank = 2 + ri
                pb = PS[bank][:].bitcast(BF16)
                for gl in range(8):
                    op("pe", lambda e, gl=gl, pb=pb, ri=ri: e.transpose(out=pb[:, gl * 128:(gl + 1) * 128], in_=AB[ri][:, gl, :], identity=ident_b[:]), r=["AB", "pp", "ident_b"], w=[psk(bank)])
                op("dve", lambda e, pb=pb, ri=ri: e.tensor_copy(out=ABT[ri], in_=pb.rearrange("p (a b) -> p a b", b=128)), r=[psk(bank)], w=["ABT"])

        def vprime_group(cc, gl):
            g = 8 * cc + gl
            for ri in range(2):
                bank = 4 + ri
                op("pe", lambda e, ri=ri, bank=bank: e.matmul(PS[bank][:, 0:256], lhsT=ABT[ri][:, gl, :], rhs=U[:, g, :], start=True, stop=True), r=["ABT", ("U", g)], w=[psk(bank)])
                op("act", lambda e, ri=ri, bank=bank: e.activation(out=VX[:, ri, g, :], in_=PS[bank][:, 0:256], func=AF.Copy), r=[psk(bank)], w=[("VX", g)])

        for cc in range(4):
            for gl in range(8):
                shuffle_group(cc, gl)
            gen_AB(AB, 8 * cc, 0)
            gen_AB(AB, 8 * cc + 4, 4)
            abt_chunk()
            for gl in range(8):
                vprime_group(cc, gl)
        dbg_put("U0", U[:, 0, 0:64], [("U", 0)])
        dbg_put("VX0", VX[:, :, 0, 0:64], [("VX", 0)])
        dbg_put("pw", pw[:, :, 0, :], PP)
        dbg_put("pwi", pwi[:, :, 0, :], PP)
        dbg_put("bb", bb[:, :, 0, :], PP)

        VXK = [("VX", g) for g in range(32)]
        op("dve", lambda e: e.memset(Xst, 0.0), w=["sc0x0", "sc64x0"])
        scnt = {0: 0, 64: 0}

        def scan_step(lo, hi, b, eng):
            key = "sc%d" % lo
            n_ = scnt[lo]
            scnt[lo] += 1
            XB = (Xst, Xst2)
            xs_old, xs = XB[n_ % 2][lo:hi], XB[(n_ + 1) % 2][lo:hi]
            kx_old, kx = key + "x%d" % (n_ % 2), key + "x%d" % ((n_ + 1) % 2)
            s1, t1_, t2_ = S1[lo:hi], T1[lo:hi], T2[lo:hi]
            vx = VX[lo:hi, :, :, b]
            ch = ("scan", lo) if (eng == "dve" and cfg.get("scan_chain", True)) else None
            op(eng, lambda e: e.tensor_tensor(out=s1, in0=xs_old, in1=vx, op=ALU.add), r=VXK + [kx_old], w=[key + "s"], chain=ch)
            op(eng, lambda e: e.tensor_tensor(out=t1_, in0=P1s[lo:hi], in1=s1, op=ALU.mult), r=[key + "s", "pp"], w=[key + "a"], chain=ch)
            if eng == "dve":
                s1sw = bass.AP(s1.tensor, s1.offset + 32, [list(s1.ap[0]), [-32, 2], [1, 32]])
                op(eng, lambda e: e.tensor_tensor(out=t2_, in0=P2s[lo:hi], in1=s1sw, op=ALU.mult), r=[key + "s", "pp"], w=[key + "b"], chain=ch)
            else:
                op(eng, lambda e: e.tensor_tensor(out=t2_[:, 0, :], in0=P2s[lo:hi, 0, :], in1=s1[:, 1, :], op=ALU.mult), r=[key + "s", "pp"], w=[key + "b"])
                op(eng, lambda e: e.tensor_tensor(out=t2_[:, 1, :], in0=P2s[lo:hi, 1, :], in1=s1[:, 0, :], op=ALU.mult), r=[key + "s", "pp", key + "b"], w=[key + "b"])
            op(eng, lambda e: e.tensor_tensor(out=xs, in0=t1_, in1=t2_, op=ALU.add), r=[key + "a", key + "b"], w=[kx], chain=ch)
            op("act", lambda e: e.activation(out=vx, in_=xs, func=AF.Copy), r=[kx], w=[key + "c"])

        for b in range(256):
            scan_step(0, 64, b, "dve")
            scan_step(64, 128, 255 - b, "dve")
        dbg_put("X0", VX[:, :, 0, 0:64], ["sc0c", "sc64c"])

        AR.phase()
        AR.at("keep", 0, [85 * KB // 2], BF16)
        AB2 = [AR.at("AB2_%d" % i, 85 * KB + i * 2 * KB, [8, 128], BF16) for i in range(2)]
        CAf = [AR.at("CAf%d" % i, 89 * KB + i * 2 * KB, [8, 128], BF16) for i in range(2)]
        AR.at("keep2", 93 * KB, [(104 - 93) * KB // 2], BF16)
        CAb = [bm.rearrange("p a b c -> p (a b c)")[:, i * 512:(i + 1) * 512].bitcast(BF16).rearrange("p (a b) -> p a b", b=128) for i in range(2)]
        W0 = sv["lr"].tensor and AR.base[:, (64 * KB + 4096) // 2:(64 * KB + 4096 + 2048) // 2].rearrange("p (a b) -> p a b", b=128)
        for i in range(2):
            op("pool", lambda e, i=i: e.memset(CAf[i][64:128], 0.0), w=["CA"])
            op("pool", lambda e, i=i: e.memset(CAb[i][0:64], 0.0), w=["CA"])
        mF_ = s5m[:, 0, :]
        mB_ = s5m[:, 1, :]

        def w0_chunk():
            for half in range(2):
                pf, pb_ = PS[2], PS[3]
                for jj in range(4):
                    gl = half * 4 + jj
                    o1 = pf[:, jj * 128:(jj + 1) * 128]
                    o2 = pb_[:, jj * 128:(jj + 1) * 128]
                    op("pe", lambda e, gl=gl, o1=o1: e.matmul(o1, lhsT=AB2[0][:, gl, :], rhs=CAf[0][:, gl, :], start=True, stop=False), r=["AB", "CA", "pp"], w=[psk(2)])
                    op("pe", lambda e, gl=gl, o1=o1: e.matmul(o1, lhsT=AB2[1][:, gl, :], rhs=CAf[1][:, gl, :], start=False, stop=True), r=["AB", "CA", "pp"], w=[psk(2)])
                    op("pe", lambda e, gl=gl, o2=o2: e.matmul(o2, lhsT=AB2[0][:, gl, :], rhs=CAb[0][:, gl, :], start=True, stop=False), r=["AB", "CA", "pp"], w=[psk(3)])
                    op("pe", lambda e, gl=gl, o2=o2: e.matmul(o2, lhsT=AB2[1][:, gl, :], rhs=CAb[1][:, gl, :], start=False, stop=True), r=["AB", "CA", "pp"], w=[psk(3)])
                t3 = tA.rearrange("p (a b) -> p a b", b=128)
                op("dve", lambda e, t3=t3: e.tensor_tensor(out=t3, in0=PS[2][:].rearrange("p (a b) -> p a b", b=128), in1=mF_.unsqueeze(1).broadcast_to([128, 4, 128]), op=ALU.mult),
                   r=[psk(2), "s5m", "pp"], w=["pp"])
                op("dve", lambda e: e.tensor_tensor(out=tB.rearrange("p (a b) -> p a b", b=128), in0=PS[3][:].rearrange("p (a b) -> p a b", b=128), in1=mB_.unsqueeze(1).broadcast_to([128, 4, 128]), op=ALU.mult),
                   r=[psk(3), "s5m", "pp"], w=["pp"])
                op("dve", lambda e, half=half: e.tensor_tensor(out=W0[:, half * 4:(half + 1) * 4, :], in0=tA.rearrange("p (a b) -> p a b", b=128), in1=tB.rearrange("p (a b) -> p a b", b=128), op=ALU.add),
                   r=["pp"], w=["W0", "pp"])

        SCK = ["sc0c", "sc64c"]

        def y_group(cc, gl):
            g = 8 * cc + gl
            bank = 4 + g % 2
            o_ = PS[bank]
            rk = ["W0", "CA", ("U", g), ("VX", g)] + SCK
            op("pe", lambda e: e.matmul(o_[:, 0:256], lhsT=W0[:, gl, :], rhs=U[:, g, :], start=True, stop=False), r=rk, w=[psk(bank)])
            op("pe", lambda e: e.matmul(o_[:, 1:256], lhsT=CAf[0][:, gl, :], rhs=VX[:, 0, g, 0:255], start=False, stop=False), r=rk, w=[psk(bank)])
            op("pe", lambda e: e.matmul(o_[:, 1:256], lhsT=CAf[1][:, gl, :], rhs=VX[:, 1, g, 0:255], start=False, stop=False), r=rk, w=[psk(bank)])
            op("pe", lambda e: e.matmul(o_[:, 0:255], lhsT=CAb[0][:, gl, :], rhs=VX[:, 0, g, 1:256], start=False, stop=False), r=rk, w=[psk(bank)])
            op("pe", lambda e: e.matmul(o_[:, 0:255], lhsT=CAb[1][:, gl, :], rhs=VX[:, 1, g, 1:256], start=False, stop=True), r=rk, w=[psk(bank)])
            op("act", lambda e: e.activation(out=U[:, g, :], in_=o_[:, 0:256], func=AF.Copy), r=[psk(bank)], w=[("U", g)])

        def unshuffle(cc, tl):
            bank = tl % 2
            for gl in range(8):
                g = 8 * cc + gl
                op("pe", lambda e, gl=gl, g=g: e.matmul(PS[bank][:, 0:256], lhsT=Zt[:, tl, (7 - gl) * 16:(7 - gl) * 16 + 128], rhs=U[:, g, :], start=(gl == 0), stop=(gl == 7)),
                   r=["Zt", ("U", g)], w=[psk(bank)])
            uv = uT[:, cc, :].rearrange("p (b s) -> p b s", s=8)[:, :, tl]
            op("dve", lambda e: e.scalar_tensor_tensor(out=uv, in0=uv, scalar=dsk[:, cc:cc + 1], in1=PS[bank][:, 0:256], op0=ALU.mult, op1=ALU.add),
               r=[psk(bank), ("uT", cc), "pp"], w=[("uT", cc)])

        for cc in range(4):
            for hh in range(2):
                gen_AB(AB2, 8 * cc + 4 * hh, 4 * hh)
                gen_CA(8 * cc + 4 * hh, 4 * hh, CAf, CAb)
            w0_chunk()
            for gl in range(8):
                y_group(cc, gl)
            for tl in range(8):
                unshuffle(cc, tl)
        dbg_put("y", uT[:, :, 0:512], [("uT", cc) for cc in range(4)])

        AR.phase()
        AR.at("uT", 32 * KB, [4, L], BF16)
        bT = AR.at("bT", 0, [4, L], BF16)
        g1 = AR.at("g1", 16 * KB, [L], F32)
        g2 = AR.at("g2", 24 * KB, [L], F32)
        CG = math.sqrt(2.0 / math.pi)

        def gelu_chunk(cc):
            y_ = uT[:, cc, :]
            op("act", lambda e: e.activation(out=g1, in_=y_, func=AF.Square), r=[("uT", cc)], w=["g1"])
            op("dve", lambda e: e.tensor_scalar(out=g1, in0=g1, scalar1=0.044715, scalar2=1.0, op0=ALU.mult, op1=ALU.add), r=["g1"], w=["g1"])
            op("dve", lambda e: e.tensor_tensor(out=g1, in0=g1, in1=y_, op=ALU.mult), r=["g1", ("uT", cc)], w=["g1"])
            op("act", lambda e: e.activation(out=g2, in_=g1, func=AF.Sigmoid, scale=2.0 * CG), r=["g1"], w=["g2"])
            op("dve", lambda e: e.tensor_tensor(out=y_, in0=y_, in1=g2, op=ALU.mult), r=["g2", ("uT", cc)], w=[("uT", cc)])

        for cc in range(4):
            gelu_chunk(cc)

        def glu(tb, oc, kq):
            bank = kq % 2
            for kc in range(4):
                op("pe", lambda e, kc=kc: e.matmul(PS[bank][:], lhsT=wglu[:, kc, oc * 128:(oc + 1) * 128], rhs=uT[:, kc, tb * 512:(tb + 1) * 512], start=(kc == 0), stop=(kc == 3)),
                   r=[("wb", 1), ("uT", kc)], w=[psk(bank)])
            gsl = g1[:, (kq % 4) * 512:(kq % 4 + 1) * 512]
            op("act", lambda e: e.activation(out=gsl, in_=PS[bank][:], func=AF.Sigmoid, bias=bgl[:, oc:oc + 1]), r=[psk(bank), "pp"], w=[("gs", kq % 4)])
            op("dve", lambda e: e.tensor_tensor(out=bT[:, oc, tb * 512:(tb + 1) * 512], in0=uT[:, oc, tb * 512:(tb + 1) * 512], in1=gsl, op=ALU.mult),
               r=[("gs", kq % 4), ("uT", oc)], w=[("bT", oc, tb)])

        P.barrier()
        kq = 0
        for tb in range(4):
            for oc in range(4):
                glu(tb, oc, kq)
                kq += 1
        dbg_put("bT", bT[:, :, 0:512], [("bT", oc, 0) for oc in range(4)])

        def oproj2(tb, dc, kq):
            bank = 2 + kq % 2
            for hc in range(4):
                op("pe", lambda e, hc=hc: e.matmul(PS[bank][:], lhsT=wo2[:, hc, dc * 128:(dc + 1) * 128], rhs=bT[:, hc, tb * 512:(tb + 1) * 512], start=(hc == 0), stop=(hc == 3)),
                   r=[("wb", 2), ("bT", hc, tb)], w=[psk(bank)])
            xs = xT[:, dc, tb * 512:(tb + 1) * 512]
            op("dve", lambda e: e.tensor_tensor(out=xs, in0=xs, in1=PS[bank][:], op=ALU.add), r=[psk(bank), xk(dc, tb)], w=[xk(dc, tb)])

        if not dbg:
            kq = 0
            for tb in range(4):
                for dc in range(8):
                    oproj2(tb, dc, kq)
                    kq += 1

    def hgrn_phase(gi):
        w_in = w_in_odd_d.rearrange("(c p) f -> p c f", p=128)
        w_out = w_out_odd_d.rearrange("(h p) d -> p h d", p=128)
        AR.phase()
        hT = AR.at("hT", 0, [8, L], BF16)
        sq = [AR.at("sq%d" % i, 44 * KB + i * 4 * KB, [L], BF16) for i in range(2)]
        rstd = AR.at("rstd", 52 * KB, [L], F32)
        def load_head(h):
            sa, sb_ = (0, 1) if h % 2 == 0 else (2, 3)
            secs = []
            for i, sec in enumerate((0, 1, 2, 3)):
                dst = WB[sa][:, i * 1024:(i + 1) * 1024].rearrange("p (a b) -> p a b", b=128)
                op("pool", lambda e, dst=dst, sec=sec, h=h: e.dma_start(out=dst, in_=w_in[:, :, sec * 1024 + h * 128: sec * 1024 + (h + 1) * 128]),
                   w=[("wb", sa)], dma=True, dkey=("wbs", sa, i))
                secs.append(dst)
            dst = WB[sb_][:, 0:1024].rearrange("p (a b) -> p a b", b=128)
            op("pool", lambda e, dst=dst, h=h: e.dma_start(out=dst, in_=w_in[:, :, 4 * 1024 + h * 128: 4 * 1024 + (h + 1) * 128]),
               w=[("wb", sb_)], dma=True, dkey=("wbs", sb_, 0))
            secs.append(dst)
            wo = WB[sb_][:, 1024:2048]
            op("pool", lambda e, wo=wo, h=h: e.dma_start(out=wo, in_=w_out[:, h, :]), w=[("wb", sb_)], dma=True, dkey=("wbs", sb_, 1))
            return secs, wo, sa, sb_

        first_head = load_head(0)
        norm_to_hT(gi, hT, rstd, sq)
        AR.phase()
        hT = AR.at("hT", 0, [8, L], BF16)
        qT = AR.at("qT", 32 * KB, [L], BF16)
        sigG = AR.at("sigG", 36 * KB, [L], BF16)
        vtok = AR.at("vtok", 40 * KB, [16, 128], BF16)
        HS = []
        for i_ in range(2):
            b0 = 44 * KB + i_ * 22 * KB
            HS.append(dict(i=i_,
                           A=AR.at("A%d" % i_, b0, [1024], F32), B=AR.at("B%d" % i_, b0 + 4 * KB, [1024], F32),
                           kk=AR.at("kk%d" % i_, b0 + 8 * KB, [1024], BF16), qdec=AR.at("qdec%d" % i_, b0 + 10 * KB, [1024], BF16),
                           kinv=AR.at("kinv%d" % i_, b0 + 12 * KB, [1024], BF16), kend=AR.at("kend%d" % i_, b0 + 14 * KB, [8, 128], BF16),
                           scT=AR.at("scT%d" % i_, b0 + 16 * KB, [8, 128], BF16), Sb=AR.at("Sb%d" % i_, b0 + 18 * KB, [16, 128], BF16),
                           pbank=(0, 1) if i_ == 0 else (2, 3), xbank=4 + i_))
        oacc = AR.at("oacc", 88 * KB, [16, 128], F32)
        mF = AR.at("mF", 96 * KB, [L], BF16)
        decs = [AR.at("dec%d" % i, 100 * KB + i * 64, [16], F32) for i in range(2)]
        ssq = AR.at("ssq", 100 * KB + 128, [16], F32)
        SstD = [[AR.at("Sst%d_%d" % (d_, i), 100 * KB + 256 + (2 * d_ + i) * 512, [128], F32) for i in range(2)] for d_ in range(2)]
        on_tok = AR.base[:, 44 * KB // 2: 48 * KB // 2].rearrange("p (a b) -> p a b", b=128)
        mT = AR.base[:, 48 * KB // 2: 52 * KB // 2]

        op("sp", lambda e: e.dma_start(out=lbl[:], in_=lbl_d), w=["lbl"], dma=True)
        op("sp", lambda e: e.dma_start(out=hg[:], in_=hg_d), w=["hg"], dma=True)
        op("sp", lambda e: e.dma_start(out=hmask[:], in_=hmask_d), w=["hmask"], dma=True)
        op("dve", lambda e: e.tensor_tensor(out=lb[:], in0=lbl[:, 1, :], in1=lbl[:, 0, :], op=ALU.subtract), r=["lbl"], w=["lb"])
        op("act", lambda e: e.activation(out=lb[:], in_=lb[:], func=AF.Sigmoid), r=["lb"], w=["lb"])
        op("dve", lambda e: e.tensor_scalar(out=oml[:], in0=lb[:], scalar1=-1.0, scalar2=1.0, op0=ALU.mult, op1=ALU.add), r=["lb"], w=["oml"])
        op("dve", lambda e: e.tensor_scalar(out=noml[:], in0=oml[:], scalar1=-1.0, scalar2=None, op0=ALU.mult), r=["oml"], w=["noml"])
        op("pool", lambda e: e.memset(mF, 1.0), w=["mF"])
        op("pool", lambda e: e.memset(mF.rearrange("p (a b) -> p a b", b=64)[:, :, 0:1], 0.0), w=["mF"])

        def proj_fm(wsec, slot, consume):
            for tb in range(4):
                bank = tb
                for c in range(8):
                    op("pe", lambda e, bank=bank, c=c, tb=tb: e.matmul(PS[bank][:], lhsT=wsec[:, c, :], rhs=hT[:, c, tb * 512:(tb + 1) * 512], start=(c == 0), stop=(c == 7)),
                       r=[("wb", slot), hk(c, tb)], w=[psk(bank)])
                consume(tb, bank)

        pend_oproj = []

        def front_gen(h, cur):
            (wq, wi_, wff, wfb, wg), wo, sa, sb_ = cur
            for (wsec, slot, dst, func, key) in ((wq, sa, qT, AF.Copy, "qT"), (wg, sb_, sigG, AF.Sigmoid, "sigG")):
                for tb in range(4):
                    bank = tb % 2
                    for c in range(8):
                        op("pe", lambda e, bank=bank, c=c, tb=tb, wsec=wsec: e.matmul(PS[bank][:], lhsT=wsec[:, c, :], rhs=hT[:, c, tb * 512:(tb + 1) * 512], start=(c == 0), stop=(c == 7)),
                           r=[("wb", slot), hk(c, tb)], w=[psk(bank)])
                    op("act", lambda e, bank=bank, tb=tb, dst=dst, func=func: e.activation(out=dst[:, tb * 512:(tb + 1) * 512], in_=PS[bank][:], func=func), r=[psk(bank)], w=[key])
                    yield
            for q4 in range(4):
                bank = 4 + q4 % 2
                for jj in range(4):
                    j = q4 * 4 + jj
                    for c in range(8):
                        op("pe", lambda e, bank=bank, jj=jj, j=j, c=c: e.matmul(PS[bank][:, jj * 128:(jj + 1) * 128], lhsT=hT[:, c, j * 128:(j + 1) * 128], rhs=wi_[:, c, :], start=(c == 0), stop=(c == 7)),
                           r=[("wb", sa), hk(c, j // 4)], w=[psk(bank)])
                op("dve", lambda e, bank=bank, q4=q4: e.tensor_copy(out=vtok[:, q4 * 4:(q4 + 1) * 4, :], in_=PS[bank][:].rearrange("p (a b) -> p a b", b=128)), r=[psk(bank)], w=["vtok"])
                yield

        def do_head(h, cur):
            (wq, wi_, wff, wfb, wg), wo, sa, sb_ = cur
            mTh = WB[sb_][:, 2048:4096]
            if h == 0:
                dbg_put("qT", qT[:, 0:256], ["qT"])
                dbg_put("sigG", sigG[:, 0:256], ["sigG"])
                dbg_put("vtok", vtok[:, 0:2, :], ["vtok"])
            sidx = [0, 0]

            def do_dir(d, hf, S):
                si = S["i"]
                A, B, kk, qdec, kinv, kend, scT, Sb = S["A"], S["B"], S["kk"], S["qdec"], S["kinv"], S["kend"], S["scT"], S["Sb"]
                dec = decs[si]
                kA, kB, kK, kQ, kI, kE, kS, kD = ["%s%d" % (n_, si) for n_ in ("hgA", "hgB", "kk", "qdec", "kinv", "kend", "scT", "dec")]
                wf = wff if d == 0 else wfb
                T0 = hf * 1024
                for t2 in range(2):
                    tb = 2 * hf + t2
                    bank = S["pbank"][t2]
                    for c in range(8):
                        op("pe", lambda e, c=c, tb=tb, bank=bank: e.matmul(PS[bank][:], lhsT=wf[:, c, :], rhs=hT[:, c, tb * 512:(tb + 1) * 512], start=(c == 0), stop=(c == 7)),
                           r=[("wb", sa), hk(c, tb)], w=[psk(bank)])
                    op("act", lambda e, t2=t2, bank=bank: e.activation(out=A[:, t2 * 512:(t2 + 1) * 512], in_=PS[bank][:], func=AF.Sigmoid), r=[psk(bank)], w=[kA])
                    yield
                op("act", lambda e: e.activation(out=kk, in_=A, func=AF.Identity, scale=noml[:, h:h + 1], bias=oml[:, h:h + 1]), r=[kA, "noml", "oml"], w=[kK])
                op("act", lambda e: e.activation(out=A, in_=A, func=AF.Ln, scale=oml[:, h:h + 1], bias=lb[:, h:h + 1]), r=[kA, "lb", "oml"], w=[kA])
                yield
                if d == 0:
                    op("dve", lambda e: e.tensor_tensor_scan(out=B, data0=mF[:, 0:1024], data1=A, initial=0.0, op0=ALU.mult, op1=ALU.add), r=[kA, "mF"], w=[kB])
                else:
                    op("dve", lambda e: e.tensor_tensor_scan(out=B[:, ::-1], data0=mF[:, 0:1024], data1=A[:, ::-1], initial=0.0, op0=ALU.mult, op1=ALU.add), r=[kA, "mF"], w=[kB])
                yield
                op("act", lambda e: e.activation(out=A, in_=B, func=AF.Exp), r=[kB], w=[kA])
                yield
                op("dve", lambda e: e.tensor_tensor(out=qdec, in0=qT[:, T0:T0 + 1024], in1=A, op=ALU.mult), r=[kA, "qT"], w=[kQ])
                yield
                op("act", lambda e: e.activation(out=A, in_=B, func=AF.Exp, scale=-1.0), r=[kB, kQ], w=[kA])
                bl = B.rearrange("p (a b) -> p a b", b=64)[:, :, 63:64] if d == 0 else B.rearrange("p (a b) -> p a b", b=64)[:, :, 0:1]
                op("act", lambda e: e.activation(out=dec.rearrange("p (a b) -> p a b", b=1), in_=bl, func=AF.Exp), r=[kB], w=[kD])
                yield
                op("dve", lambda e: e.tensor_tensor(out=A, in0=kk, in1=A, op=ALU.mult), r=[kA, kK], w=[kA])
                yield
                op("act", lambda e: e.activation(out=kinv, in_=A, func=AF.Copy), r=[kA], w=[kI])
                dec_bc = bass.AP(dec.tensor, dec.offset, [list(dec.ap[0]), [1, 16], [0, 64]])
                op("dve", lambda e: e.tensor_tensor(out=kk.rearrange("p (a b) -> p a b", b=64), in0=A.rearrange("p (a b) -> p a b", b=64), in1=dec_bc, op=ALU.mult),
                   r=[kA, kD], w=[kK])
                yield
                xb = S["xbank"]
                pb = PS[xb][:].bitcast(BF16)
                for jj in range(8):
                    op("pe", lambda e, jj=jj: e.transpose(out=pb[:, jj * 128:(jj + 1) * 128], in_=kk[:, jj * 128:(jj + 1) * 128], identity=ident_b[:]),
                       r=[kK, "ident_b"], w=[psk(xb)])
                op("act", lambda e: e.activation(out=kend, in_=pb.rearrange("p (a b) -> p a b", b=128), func=AF.Copy), r=[psk(xb)], w=[kE])
                yield
                hm_ = hmask[:, d, :]
                mk = bass.AP(hm_.tensor, hm_.offset, [list(hm_.ap[0]), [0, 4], [1, 128]])
                for q4 in range(2):
                    bank = S["pbank"][q4]
                    for jj in range(4):
                        jl = q4 * 4 + jj
                        op("pe", lambda e, bank=bank, jj=jj, jl=jl: e.matmul(PS[bank][:, jj * 128:(jj + 1) * 128], lhsT=kinv[:, jl * 128:(jl + 1) * 128], rhs=qdec[:, jl * 128:(jl + 1) * 128], start=True, stop=True),
                           r=[kI, kQ], w=[psk(bank)])
                    op("dve", lambda e, bank=bank, q4=q4: e.tensor_tensor(out=scT[:, q4 * 4:(q4 + 1) * 4, :], in0=PS[bank][:].rearrange("p (a b) -> p a b", b=128), in1=mk, op=ALU.mult),
                       r=[psk(bank), "hmask"], w=[kS])
                    yield
                order = list(range(16)) if d == 0 else list(range(15, -1, -1))
                for cl in order:
                    n = sidx[d]
                    sidx[d] += 1
                    cur, new = SstD[d][n % 2], SstD[d][(n + 1) % 2]
                    ci = hf * 16 + cl
                    par = ci % 2
                    bank = 6 + par
                    op("act", lambda e, cur=cur, cl=cl: e.activation(out=Sb[:, cl, :], in_=cur, func=AF.Copy), r=[("Sst", d, n % 2)], w=[("Sb", si, cl)])
                    op("pe", lambda e, bank=bank, cl=cl, par=par: e.matmul(PS[bank][:, 0:128], lhsT=kend[par * 64:(par + 1) * 64, cl // 2, :], rhs=vtok[par * 64:(par + 1) * 64, hf * 8 + cl // 2, :], start=True, stop=True),
                       r=[kE, "vtok"], w=[psk(bank)])
                    op("dve", lambda e, bank=bank, cur=cur, new=new, cl=cl: e.scalar_tensor_tensor(out=new, in0=cur, scalar=dec[:, cl:cl + 1], in1=PS[bank][:, 0:128], op0=ALU.mult, op1=ALU.add),
                       r=[psk(bank), ("Sst", d, n % 2), kD], w=[("Sst", d, (n + 1) % 2)], chain=("hgst", d) if cfg.get("gen_chain", True) else None)
                    if cl % 2 == 1:
                        yield
                first = (d == 0 and hf == 0) or (d == 1 and hf == 1)
                for q4 in range(2):
                    for jj in range(4):
                        jl = q4 * 4 + jj
                        j = hf * 8 + jl
                        o_ = PS[xb][:, jj * 128:(jj + 1) * 128]
                        op("pe", lambda e, o_=o_, jl=jl, j=j: e.matmul(o_, lhsT=scT[:, jl, :], rhs=vtok[:, j, :], start=True, stop=False), r=[kS, "vtok"], w=[psk(xb)])
                        op("pe", lambda e, jj=jj, jl=jl: e.matmul(PS[xb][0:64, jj * 128:(jj + 1) * 128], lhsT=qdec[:, jl * 128:jl * 128 + 64], rhs=Sb[:, 2 * jl, :], start=False, stop=True),
                           r=[kQ, ("Sb", si, 2 * jl)], w=[psk(xb)])
                        op("pe", lambda e, jj=jj, jl=jl: e.matmul(PS[xb][64:128, jj * 128:(jj + 1) * 128], lhsT=qdec[:, jl * 128 + 64:jl * 128 + 128], rhs=Sb[:, 2 * jl + 1, :], start=False, stop=True),
                           r=[kQ, ("Sb", si, 2 * jl + 1)], w=[psk(xb)])
                    ov = oacc[:, hf * 8 + q4 * 4:hf * 8 + (q4 + 1) * 4, :]
                    pv = PS[xb][:].rearrange("p (a b) -> p a b", b=128)
                    if first:
                        op("dve", lambda e, ov=ov, pv=pv: e.tensor_copy(out=ov, in_=pv), r=[psk(xb)], w=[("oacc", hf)])
                    else:
                        op("dve", lambda e, ov=ov, pv=pv: e.tensor_tensor(out=ov, in0=ov, in1=pv, op=ALU.add), r=[psk(xb), ("oacc", hf)], w=[("oacc", hf)])
                    yield

            def interleave(gens):
                alive = list(gens)
                while alive:
                    for g_ in list(alive):
                        try:
                            next(g_)
                        except StopIteration:
                            alive.remove(g_)

            op("pool", lambda e: e.memset(SstD[0][0], 0.0), w=[("Sst", 0, 0)])
            op("pool", lambda e: e.memset(SstD[1][0], 0.0), w=[("Sst", 1, 0)])
            prev = list(pend_oproj)
            del pend_oproj[:]
            interleave([do_dir(0, 0, HS[0]), do_dir(1, 1, HS[1])] + prev)
            interleave([do_dir(0, 1, HS[0]), do_dir(1, 0, HS[1])])
            OACC = [("oacc", 0), ("oacc", 1)]
            HGA = ["hgA0"]
            HGB = ["hgB0"]
            for j in range(16):
                op("act", lambda e, j=j: e.activation(out=on_tok[:, j, :], in_=oacc[:, j, :], func=AF.Square, accum_out=ssq[:, j:j + 1]), r=OACC + HGB, w=HGA + ["ssq"])
            op("act", lambda e: e.activation(out=ssq, in_=ssq, func=AF.Sqrt, scale=1.0 / 128, bias=EPS), r=["ssq"], w=["ssq"])
            op("dve", lambda e: e.reciprocal(out=ssq, in_=ssq), r=["ssq"], w=["ssq"])
            for j in range(16):
                op("dve", lambda e, j=j: e.tensor_scalar(out=on_tok[:, j, :], in0=oacc[:, j, :], scalar1=ssq[:, j:j + 1], scalar2=None, op0=ALU.mult), r=OACC + ["ssq"] + HGA, w=HGA)
            for half in range(2):
                bank = half
                pb = PS[bank][:].bitcast(BF16)
                for jj in range(8):
                    j = half * 8 + jj
                    op("pe", lambda e, pb=pb, jj=jj, j=j: e.transpose(out=pb[:, jj * 128:(jj + 1) * 128], in_=on_tok[:, j, :], identity=ident_b[:]), r=HGA + ["ident_b"], w=[psk(bank)])
                op("dve", lambda e, pb=pb, half=half: e.scalar_tensor_tensor(out=mTh[:, half * 1024:(half + 1) * 1024], in0=pb, scalar=hg[:, h:h + 1], in1=sigG[:, half * 1024:(half + 1) * 1024], op0=ALU.mult, op1=ALU.mult),
                   r=[psk(bank), "hg", "sigG"] + HGA, w=[("mT", sb_)])
            if h == 0:
                dbg_put("mT", mTh[:, 0:256], [("mT", sb_)])
            if dbg:
                return None
            def oproj_gen():
                for tb in range(4):
                    for dc in range(8):
                        bank = 2 + (tb * 8 + dc) % 2
                        op("pe", lambda e, dc=dc, tb=tb, bank=bank: e.matmul(PS[bank][:], lhsT=wo[:, dc * 128:(dc + 1) * 128], rhs=mTh[:, tb * 512:(tb + 1) * 512], start=True, stop=True),
                           r=[("wb", sb_), ("mT", sb_)], w=[psk(bank)])
                        xs = xT[:, dc, tb * 512:(tb + 1) * 512]
                        op("dve", lambda e, xs=xs, bank=bank: e.tensor_tensor(out=xs, in0=xs, in1=PS[bank][:], op=ALU.add), r=[psk(bank), xk(dc, tb)], w=[xk(dc, tb)])
                        if (tb * 8 + dc) % 3 == 2:
                            yield
            return oproj_gen()

        def run_all(gens):
            alive = list(gens)
            while alive:
                for g_ in list(alive):
                    try:
                        next(g_)
                    except StopIteration:
                        alive.remove(g_)

        nxt_holder = [first_head]
        run_all([front_gen(0, nxt_holder[0])])
        for h in range(8):
            cur = nxt_holder[0]
            if h + 1 < 8:
                nxt_holder[0] = load_head(h + 1)
            og = do_head(h, cur)
            if dbg:
                break
            run_all(([front_gen(h + 1, nxt_holder[0])] if h + 1 < 8 else []) + [og])
        for g_ in pend_oproj:
            for _ in g_:
                pass

    if "l0attn" in parts or "l0mix" in parts:
        attn_phase(0)
    if "l0s5" in parts or "l0mix" in parts:
        s5_phase()
    if "l0ffn" in parts:
        ffn_phase(0, 1)
    if "l1mix" in parts:
        hgrn_phase(2)
    if "l1ffn" in parts:
        ffn_phase(1, 3)

    if dbg:
        P.barrier()
        op("sp", lambda e: e.dma_start(out=out_d.rearrange("(p a) d -> p (a d)", p=128), in_=xT_flat), dma=True, dkey="dbgout")
        P.emit()
        return nc
    AR.phase()
    sq = [AR.at("sq%d" % i, i * 4 * KB, [L], BF16) for i in range(2)]
    rstd = AR.at("rstd", 8 * KB, [L], F32)
    stage = [AR.at("stage%d" % i, 16 * KB + i * 4 * KB, [D], F32) for i in range(2)]
    ftmp = [AR.at("ftmp%d" % i, 24 * KB + i * 512, [128], F32) for i in range(4)]
    do_final = "final" in parts
    if do_final:
        rmsnorm_rstd(rstd, sq)
    k = 0
    for t in range(16):
        st = stage[t % 2]
        for half in range(2):
            bank = (2 * t + half) % 4
            for j in range(4):
                c = half * 4 + j
                src = xT[:, c, t * 128:(t + 1) * 128]
                if do_final:
                    ft = ftmp[k % 4]
                    op("dve", lambda e, ft=ft, src=src, c=c, t=t: e.scalar_tensor_tensor(
                        out=ft, in0=src, scalar=gains[:, 4, c:c + 1], in1=rstd[:, t * 128:(t + 1) * 128], op0=ALU.mult, op1=ALU.mult),
                       r=[xk(c, t // 4), ("rstd", t // 4), "gains"], w=[("ftmp", k % 4)])
                    op("pe", lambda e, bank=bank, j=j, ft=ft: e.transpose(out=PS[bank][:, j * 128:(j + 1) * 128], in_=ft, identity=ident_f[:]),
                       r=[("ftmp", k % 4), "ident_f"], w=[psk(bank)])
                    k += 1
                else:
                    op("pe", lambda e, bank=bank, j=j, src=src: e.transpose(out=PS[bank][:, j * 128:(j + 1) * 128], in_=src, identity=ident_f[:]),
                       r=[xk(c, t // 4), "ident_f"], w=[psk(bank)])
            dst = st[:, half * 512:(half + 1) * 512]
            op("act", lambda e, dst=dst, bank=bank: e.activation(out=dst, in_=PS[bank][:], func=AF.Copy), r=[psk(bank)], w=[("stage", t % 2, half)])
        op("sp", lambda e, st=st, t=t: e.dma_start(out=out_d[t * 128:(t + 1) * 128, :], in_=st), r=[("stage", t % 2, 0), ("stage", t % 2, 1)], dma=True, dkey=("st", t % 2))
    P.emit()
    return nc


def host_consts():
    c = {}
    c["ident"] = np.eye(128, dtype=np.float32)
    inv = (10000.0 ** (-np.arange(0, 64, 2, dtype=np.float32) / np.float32(64))).astype(np.float32)
    ang = (np.arange(L, dtype=np.float32)[None, :] * inv[:, None]).astype(np.float32)
    c["rope_cos"] = np.ascontiguousarray(np.tile(np.cos(ang).astype(np.float32), (4, 1)))
    c["rope_sin"] = np.ascontiguousarray(np.tile(np.sin(ang).astype(np.float32), (4, 1)))
    s_ = np.arange(128)[:, None]
    c_ = np.arange(128)[None, :]
    same = (s_ // 64) == (c_ // 64)
    hm = np.stack([(same & (s_ <= c_)), (same & (s_ >= c_))], axis=1).astype(np.float32)
    c["hmask"] = np.ascontiguousarray(hm)
    sl_ = (np.arange(128) // 16)[:, None]
    tl_ = (np.arange(128) // 16)[None, :]
    c["s5m"] = np.ascontiguousarray(np.stack([(tl_ >= sl_), (tl_ <= sl_)], axis=1).astype(np.float32))
    return c


def make_in_maps(inputs, parts=None):
    consts = host_consts()
    gains = np.concatenate([inputs["norm_mix_g"][0:1], inputs["norm_mlp_g"][0:1], inputs["norm_mix_g"][1:2],
                            inputs["norm_mlp_g"][1:2], inputs["final_norm_g"][None, :]], axis=0).astype(np.float32)
    shared = dict(consts)
    shared["gains"] = np.ascontiguousarray(gains.reshape(5, 8, 128).transpose(2, 0, 1))
    shared["w_in_odd"] = np.ascontiguousarray(inputs["w_in_odd"][0], dtype=np.float32)
    shared["w_out_odd"] = np.ascontiguousarray(inputs["w_out_odd"][0], dtype=np.float32)
    shared["lbl"] = np.ascontiguousarray(np.asarray(inputs["hgrn_lb_logits"], dtype=np.float32).reshape(2, 8, 128).transpose(2, 0, 1))
    shared["hg"] = np.ascontiguousarray(np.asarray(inputs["hgrn_norm_g"][0], dtype=np.float32).reshape(8, 128).T)
    shared["w_in_even"] = np.ascontiguousarray(inputs["w_in_even"][0], dtype=np.float32)
    shared["w_out_even"] = np.ascontiguousarray(inputs["w_out_even"][0], dtype=np.float32)
    shared["diff_lambda"] = np.ascontiguousarray(np.asarray(inputs["diff_lambda"][0], dtype=np.float32).reshape(1, 256))
    shared["subln_g"] = np.ascontiguousarray(np.asarray(inputs["diff_subln_g"][0], dtype=np.float32).reshape(128, 1))
    f32 = np.float32
    lre = np.asarray(inputs["s5_lam_re"][0], f32); lim = np.asarray(inputs["s5_lam_im"][0], f32); lst = np.asarray(inputs["s5_log_step"][0], f32)
    s5s = np.stack([lre.transpose(0, 2, 1).reshape(128, 32), lim.transpose(0, 2, 1).reshape(128, 32),
                    np.repeat(lst[:, None, :], 64, axis=1).reshape(128, 32)], axis=1)
    shared["s5s"] = np.ascontiguousarray(s5s, dtype=f32)
    shared["s5b"] = np.ascontiguousarray(np.stack([np.asarray(inputs[k][0], f32).transpose(0, 2, 1, 3).reshape(128, 32, 16) for k in ("s5_b_re", "s5_b_im")], axis=1))
    shared["s5c"] = np.ascontiguousarray(np.stack([np.asarray(inputs[k][0], f32).transpose(0, 3, 1, 2).reshape(128, 32, 16) for k in ("s5_c_re", "s5_c_im")], axis=1))
    shared["s5d"] = np.ascontiguousarray(np.asarray(inputs["s5_d"][0], f32).reshape(4, 128).T)
    shared["s5bg"] = np.ascontiguousarray(np.asarray(inputs["s5_b_glu"][0], f32).reshape(4, 128).T)
    shared["w_glu"] = np.ascontiguousarray(inputs["s5_w_glu"][0], dtype=f32)
    shared["w_ff_in"] = np.ascontiguousarray(inputs["w_ff_in"], dtype=np.float32)
    shared["w_ff_out"] = np.ascontiguousarray(inputs["w_ff_out"], dtype=np.float32)
    x = np.asarray(inputs["x"], dtype=np.float32)
    maps = []
    for b in range(x.shape[0]):
        m = dict(shared)
        m["x"] = np.ascontiguousarray(x[b])
        maps.append(m)
    return maps


ALL_PARTS = {"l0mix", "l0ffn", "l1mix", "l1ffn", "final"}
_NC_CACHE = {}


def kernel(**inputs):
    inputs = {k: np.asarray(v) for k, v in inputs.items()}
    key = "full"
    if key not in _NC_CACHE:
        _NC_CACHE[key] = build({"parts": ALL_PARTS})
    nc = _NC_CACHE[key]
    maps = make_in_maps(inputs)
    res = run_bass_kernel_spmd(nc, maps, core_ids=list(range(8)))
    out = np.stack([np.asarray(r["out"]) for r in res.results], axis=0)
    return out.astype(np.float32)
```
